# Optimizing a Trainium2 kernel written in Bass

```python
import math
import jax, jax.numpy as jnp
from jax import lax
import numpy as np

D_MODEL = 2048
BATCH = 4
SEQ = 4096
DEPTH = 2

CHUNK = 64
Q_BLOCK = 128
MIX_WIDTH = D_MODEL
DIFF_WIDTH = MIX_WIDTH // 2
MLA_WIDTH = MIX_WIDTH - DIFF_WIDTH
DIFF_HEAD_DIM = 64
DIFF_HEADS = DIFF_WIDTH // (2 * DIFF_HEAD_DIM)
MLA_V_DIM = 128
MLA_HEADS = MLA_WIDTH // MLA_V_DIM
MLA_NOPE = 128
MLA_ROPE = 64
MLA_Q_LORA = 512
MLA_KV_LORA = 256
ROPE_BASE = 10000.0
REL_BUCKETS = 32
REL_MAX_DIST = 128
N_BIAS_HEADS = 2 * DIFF_HEADS
EPS = 1e-6
NEG = -1e30
SPLIT_SIZES = (
    DIFF_HEADS * 2 * DIFF_HEAD_DIM,
    DIFF_HEADS * 2 * DIFF_HEAD_DIM,
    DIFF_HEADS * 2 * DIFF_HEAD_DIM,
    MLA_Q_LORA,
    MLA_KV_LORA,
    MLA_ROPE,
    MIX_WIDTH,
)
IN_WIDTH = sum(SPLIT_SIZES)

kernel_name = 'hybrid_diffattn_mla_chunk_causal'


def _rmsnorm(x, g):
    xf = x.astype(jnp.float32)
    y = xf * lax.rsqrt(jnp.mean(xf * xf, axis=-1, keepdims=True) + EPS)
    return (y * g.astype(jnp.float32)).astype(x.dtype)


def _rel_bucket(rel):
    nb = REL_BUCKETS // 2
    max_exact = nb // 2
    ret = (rel > 0).astype(jnp.int32) * nb
    n = jnp.abs(rel)
    nf = jnp.maximum(n, 1).astype(jnp.float32)
    large = max_exact + (jnp.log(nf / max_exact) / math.log(REL_MAX_DIST / max_exact)
                         * (nb - max_exact)).astype(jnp.int32)
    large = jnp.minimum(large, nb - 1)
    return ret + jnp.where(n < max_exact, n, large)


def _chunk_mask(qs, ke):
    qpos = jnp.arange(qs, qs + Q_BLOCK)
    kpos = jnp.arange(ke)
    allowed = (kpos[None, :] // CHUNK) <= (qpos[:, None] // CHUNK)
    return qpos, kpos, allowed


def _rope(x, cos, sin):
    half = x.shape[-1] // 2
    x1 = x[..., :half].astype(jnp.float32)
    x2 = x[..., half:].astype(jnp.float32)
    return jnp.concatenate([x1 * cos - x2 * sin, x1 * sin + x2 * cos], axis=-1).astype(x.dtype)


def _diff_attention(q1, q2, k1, k2, v, lam, rel_bias):
    S = q1.shape[2]
    scale = DIFF_HEAD_DIM ** -0.5
    outs = []
    for qs in range(0, S, Q_BLOCK):
        ke = qs + Q_BLOCK
        qpos, kpos, allowed = _chunk_mask(qs, ke)
        bias = rel_bias[_rel_bucket(kpos[None, :] - qpos[:, None])].astype(jnp.float32)
        bias = bias.transpose(2, 0, 1).reshape(DIFF_HEADS, 2, Q_BLOCK, ke)

        def probs(qm, km, m):
            s = jnp.einsum('bhqd,bhkd->bhqk', qm[:, :, qs:ke], km[:, :, :ke]).astype(jnp.float32)
            s = s * scale + bias[:, m]
            return jax.nn.softmax(jnp.where(allowed, s, NEG), axis=-1)

        p = probs(q1, k1, 0) - lam * probs(q2, k2, 1)
        outs.append(jnp.einsum('bhqk,bhkd->bhqd', p.astype(v.dtype), v[:, :, :ke]))
    return jnp.concatenate(outs, axis=2)


def _mla_attention(qn, qr, kn, kr, v):
    S = qn.shape[2]
    scale = (MLA_NOPE + MLA_ROPE) ** -0.5
    outs = []
    for qs in range(0, S, Q_BLOCK):
        ke = qs + Q_BLOCK
        _, _, allowed = _chunk_mask(qs, ke)
        s = (jnp.einsum('bhqd,bhkd->bhqk', qn[:, :, qs:ke], kn[:, :, :ke])
             + jnp.einsum('bhqr,bkr->bhqk', qr[:, :, qs:ke], kr[:, :ke])).astype(jnp.float32) * scale
        p = jax.nn.softmax(jnp.where(allowed, s, NEG), axis=-1)
        outs.append(jnp.einsum('bhqk,bhkd->bhqd', p.astype(v.dtype), v[:, :, :ke]))
    return jnp.concatenate(outs, axis=2)


def setup_inputs(seed: int = 0) -> dict:
    key = jax.random.key(seed)
    ks = jax.random.split(key, 12)

    def nrm(k, shape, s):
        return jax.random.normal(k, shape, jnp.float32) * s

    return {
        'x': nrm(ks[0], (BATCH, SEQ, D_MODEL), 1.0),
        'norm_g': 1.0 + nrm(ks[1], (DEPTH, D_MODEL), 0.02),
        'w_in': nrm(ks[2], (DEPTH, D_MODEL, IN_WIDTH), D_MODEL ** -0.5),
        'diff_lambda': nrm(ks[3], (DEPTH, 4, DIFF_HEAD_DIM), 0.1),
        'diff_subln_g': 1.0 + nrm(ks[4], (DEPTH, 2 * DIFF_HEAD_DIM), 0.02),
        'mla_q_norm_g': 1.0 + nrm(ks[5], (DEPTH, MLA_Q_LORA), 0.02),
        'mla_w_q_b': nrm(ks[6], (DEPTH, MLA_Q_LORA, MLA_HEADS * (MLA_NOPE + MLA_ROPE)), MLA_Q_LORA ** -0.5),
        'mla_kv_norm_g': 1.0 + nrm(ks[7], (DEPTH, MLA_KV_LORA), 0.02),
        'mla_w_kv_b': nrm(ks[8], (DEPTH, MLA_KV_LORA, MLA_HEADS * (MLA_NOPE + MLA_V_DIM)), MLA_KV_LORA ** -0.5),
        'w_out': nrm(ks[9], (DEPTH, MIX_WIDTH, D_MODEL), MIX_WIDTH ** -0.5),
        'rel_bias': nrm(ks[10], (REL_BUCKETS, N_BIAS_HEADS), 0.2),
        'final_norm_g': 1.0 + nrm(ks[11], (D_MODEL,), 0.02),
    }


def reference(x, norm_g, w_in, diff_lambda, diff_subln_g, mla_q_norm_g, mla_w_q_b,
              mla_kv_norm_g, mla_w_kv_b, w_out, rel_bias, final_norm_g):
    B, S, _ = x.shape
    offs = [sum(SPLIT_SIZES[:i + 1]) for i in range(len(SPLIT_SIZES) - 1)]

    pos = jnp.arange(S, dtype=jnp.float32)
    inv_freq = ROPE_BASE ** (-jnp.arange(0, MLA_ROPE, 2, dtype=jnp.float32) / MLA_ROPE)
    ang = pos[:, None] * inv_freq[None, :]
    cos, sin = jnp.cos(ang), jnp.sin(ang)

    for l in range(DEPTH):
        h = _rmsnorm(x, norm_g[l])
        proj = jnp.einsum('bsd,de->bse', h, w_in[l])
        dq, dk, dv, cq, ckv, kr, gate = jnp.split(proj, offs, axis=-1)

        dq = dq.reshape(B, S, DIFF_HEADS, 2, DIFF_HEAD_DIM).transpose(0, 2, 3, 1, 4)
        dk = dk.reshape(B, S, DIFF_HEADS, 2, DIFF_HEAD_DIM).transpose(0, 2, 3, 1, 4)
        dv = dv.reshape(B, S, DIFF_HEADS, 2 * DIFF_HEAD_DIM).transpose(0, 2, 1, 3)
        lam_init = 0.8 - 0.6 * math.exp(-0.3 * l)
        lp = diff_lambda[l].astype(jnp.float32)
        lam = jnp.exp(jnp.sum(lp[0] * lp[1])) - jnp.exp(jnp.sum(lp[2] * lp[3])) + lam_init
        o_a = _diff_attention(dq[:, :, 0], dq[:, :, 1], dk[:, :, 0], dk[:, :, 1], dv, lam, rel_bias)
        o_a = _rmsnorm(o_a, diff_subln_g[l]) * (1.0 - lam_init)
        o_a = o_a.transpose(0, 2, 1, 3).reshape(B, S, DIFF_WIDTH)

        q = jnp.einsum('bsr,re->bse', _rmsnorm(cq, mla_q_norm_g[l]), mla_w_q_b[l])
        q = q.reshape(B, S, MLA_HEADS, MLA_NOPE + MLA_ROPE)
        q_nope = q[..., :MLA_NOPE].transpose(0, 2, 1, 3)
        q_rope = _rope(q[..., MLA_NOPE:], cos[None, :, None, :], sin[None, :, None, :]).transpose(0, 2, 1, 3)
        kv = jnp.einsum('bsr,re->bse', _rmsnorm(ckv, mla_kv_norm_g[l]), mla_w_kv_b[l])
        kv = kv.reshape(B, S, MLA_HEADS, MLA_NOPE + MLA_V_DIM)
        k_nope = kv[..., :MLA_NOPE].transpose(0, 2, 1, 3)
        v_b = kv[..., MLA_NOPE:].transpose(0, 2, 1, 3)
        k_rope = _rope(kr, cos[None], sin[None])
        o_b = _mla_attention(q_nope, q_rope, k_nope, k_rope, v_b)
        o_b = o_b.transpose(0, 2, 1, 3).reshape(B, S, MLA_WIDTH)

        y = jnp.concatenate([o_a, o_b], axis=-1) * jax.nn.silu(gate)
        x = x + jnp.einsum('bse,ed->bsd', y, w_out[l])

    return _rmsnorm(x, final_norm_g)
```

```python
import math
from contextlib import ExitStack

import numpy as np
import ml_dtypes

import concourse.bass as bass
import concourse.mybir as mybir
from concourse.bass_utils import run_bass_kernel_spmd

F32 = mybir.dt.float32
BF = mybir.dt.bfloat16
AF = mybir.ActivationFunctionType
ALU = mybir.AluOpType
AX = mybir.AxisListType

D = 2048
S = 4096
NB = S // 128
DEPTH = 2
H = 8
EPS = 1e-6
NFT = 39
HALF = 2048
SC_DIFF = 64 ** -0.5
SC_MLA = 192 ** -0.5
SAME_ENG_SYNC = True


class R:
    __slots__ = ("w", "r", "name")

    def __init__(self, name=""):
        self.w = None
        self.r = {}
        self.name = name


class Prog:
    NDS = 24

    def __init__(self, nc, es):
        self.nc = nc
        self.E = {"pe": nc.tensor, "act": nc.scalar, "dve": nc.vector, "pool": nc.gpsimd, "sp": nc.sync}
        self.csem = {e: es.enter_context(nc.semaphore("c_" + e)) for e in ("pe", "act", "dve", "pool")}
        self.ccnt = {e: 0 for e in self.csem}
        self.dsem = [es.enter_context(nc.semaphore("d%d" % i)) for i in range(self.NDS)]
        self.dcnt = [0] * self.NDS
        self.dnext = 0
        self.waited = {}
        self.ninst = 0

    def _sem(self, tag):
        return self.csem[tag[1]] if tag[0] == "c" else self.dsem[tag[1]]

    def _wait(self, eng, tag):
        key = (eng, tag[0], tag[1])
        if self.waited.get(key, 0) >= tag[2]:
            return
        self.E[eng].wait_ge(self._sem(tag), tag[2])
        self.waited[key] = tag[2]

    def _deps(self, eng, reads, writes):
        deps = {}
        for r in reads:
            if r.w is not None:
                k = (r.w[0], r.w[1])
                deps[k] = max(deps.get(k, 0), r.w[2])
        for w in writes:
            if w.w is not None:
                k = (w.w[0], w.w[1])
                deps[k] = max(deps.get(k, 0), w.w[2])
            for k, v in w.r.items():
                deps[k] = max(deps.get(k, 0), v)
        for k in sorted(deps, key=str):
            if k[0] == "c" and k[1] == eng and (eng == "pe" or not SAME_ENG_SYNC):
                continue
            self._wait(eng, (k[0], k[1], deps[k]))

    def _mark(self, tag, reads, writes):
        k = (tag[0], tag[1])
        for r in reads:
            r.r[k] = max(r.r.get(k, 0), tag[2])
        for w in writes:
            w.w = tag
            w.r = {}

    def op(self, eng, fn, reads=(), writes=(), ms=True):
        self._deps(eng, reads, writes)
        inst = fn()
        self.ninst += 1
        if ms:
            self.ccnt[eng] += 1
            tag = ("c", eng, self.ccnt[eng])
            inst.then_inc(self.csem[eng], 1)
        else:
            tag = ("c", eng, self.ccnt[eng] + 1)
        self._mark(tag, reads, writes)
        return inst

    def dma(self, q, out, in_, reads=(), writes=()):
        i = self.dnext
        self.dnext = (self.dnext + 1) % self.NDS
        if self.dcnt[i] > 0:
            self._wait(q, ("d", i, self.dcnt[i]))
        self._deps(q, reads, writes)
        self.dcnt[i] += 16
        tag = ("d", i, self.dcnt[i])
        self.E[q].dma_start(out=out, in_=in_).then_inc(self.dsem[i], 16)
        self.ninst += 1
        self._mark(tag, reads, writes)

    def barrier(self, engines=("pe", "act", "dve", "pool", "sp")):
        tags = [("c", e, self.ccnt[e]) for e in self.csem if self.ccnt[e] > 0]
        tags += [("d", i, self.dcnt[i]) for i in range(self.NDS) if self.dcnt[i] > 0]
        for e in engines:
            for t in tags:
                if t[0] == "c" and t[1] == e:
                    continue
                self._wait(e, t)


def build_program(dbg=None):
    dbg = dbg or {}
    stop_after = dbg.get("stop", None)
    expose = dbg.get("expose", ())
    nc = bass.Bass("TRN2", target_bir_lowering=False)

    def din(name, shape, dt=F32):
        return nc.dram_tensor(name, list(shape), dt, kind="ExternalInput")

    def dscr(name, shape, dt=BF):
        if name in expose:
            return nc.dram_tensor(name, list(shape), dt, kind="ExternalOutput")
        return nc.dram_tensor(name, list(shape), dt)

    x_in = din("x", [S, D])
    wF = din("wF", [DEPTH, D, NFT * 128])
    wT = din("wT", [DEPTH, D, 1024])
    wq = din("wq", [DEPTH, 512, 2048])
    wkvk = din("wkvk", [DEPTH, 256, 1024])
    wkvv = din("wkvv", [DEPTH, 256, 1024])
    wo = din("wo", [DEPTH, D, D])
    g_in = din("g_in", [128, DEPTH * 16])
    gq_in = din("gq", [128, DEPTH * 4])
    gkv_in = din("gkv", [128, DEPTH * 2])
    gsub_in = din("gsub", [128, DEPTH * 128])
    gfin_in = din("gfin", [128, D])
    dlam_in = din("dlam", [128, DEPTH * 256])
    cfar_in = din("cfar", [128, 16])
    bt_in = din("bt", [128, 16 * 2 * 128])
    t1_in = din("t1", [128, S])
    identf_in = din("identf", [128, 128])
    identb_in = din("identb", [128, 128], BF)
    dfold_in = din("dfold", [128, 128], BF)
    out_d = nc.dram_tensor("out", [S, D], F32, kind="ExternalOutput")

    qdT = dscr("qdT", [H, 128, S])
    kdT = dscr("kdT", [H, 128, S])
    qnT = dscr("qnT", [H, 128, S])
    qrT = dscr("qrT", [H, 128, S])
    knT = dscr("knT", [H, 128, S])
    zkT = dscr("zkT", [128, S])
    gT = dscr("gT", [16, 128, S])
    yT = dscr("yT", [16, 128, S])
    vd = dscr("vd", [H, 128, NB, 128])
    vb = dscr("vb", [H, 128, NB, 128])
    x1 = dscr("x1", [S, D], F32)

    with ExitStack() as es:
        p = Prog(nc, es)
        E = p.E

        uid = [0]

        def sb(name, shape, dt, stack=es):
            uid[0] += 1
            return stack.enter_context(nc.sbuf_tensor("s%d_%s" % (uid[0], name), list(shape), dt))

        def ps(name, shape, dt, stack):
            uid[0] += 1
            return stack.enter_context(nc.psum_tensor("p%d_%s" % (uid[0], name), list(shape), dt))

        identf = sb("identf", [128, 128], F32); r_const = R("const")
        identb = sb("identb", [128, 128], BF)
        dfold = sb("dfold", [128, 128], BF)
        onesb = sb("onesb", [128, 128], BF)
        onesf = sb("onesf", [128, 128], F32)
        t1 = sb("t1", [128, S], F32)
        eb = sb("eb", [128, 16, 2, 128], F32)
        gin = sb("gin", [128, DEPTH * 16], F32)
        gq = sb("gq", [128, DEPTH * 4], F32)
        gkv = sb("gkv", [128, DEPTH * 2], F32)
        gsub = sb("gsub", [128, DEPTH * 128], F32)
        dlam = sb("dlam", [128, DEPTH * 256], F32)
        cfar = sb("cfar", [128, 16], F32)
        epsb = sb("epsb", [128, 1], F32)
        lam = sb("lam", [128, 8], F32)
        lamtmp = sb("lamtmp", [128, DEPTH * 128], F32)
        for dst, src in ((identf, identf_in), (identb, identb_in), (dfold, dfold_in), (t1, t1_in),
                         (gin, g_in), (gq, gq_in), (gkv, gkv_in), (gsub, gsub_in), (dlam, dlam_in),
                         (cfar, cfar_in)):
            p.dma("sp", dst[:, :], src[:, :], writes=[r_const])
        p.dma("sp", eb[:, :, :, :].rearrange("p a b c -> p (a b c)"), bt_in[:, :], writes=[r_const])
        p.op("dve", lambda: E["dve"].memset(onesb[:, :], 1.0), writes=[r_const])
        p.op("dve", lambda: E["dve"].memset(onesf[:, :], 1.0), writes=[r_const])
        p.op("dve", lambda: E["dve"].memset(epsb[:, :], EPS), writes=[r_const])
        p.op("dve", lambda: E["dve"].tensor_scalar(cfar[:, :], cfar[:, :], -1.0, None, ALU.mult),
             reads=[r_const], writes=[r_const])
        for hm in range(16):
            p.op("act", lambda hm=hm: E["act"].activation(
                eb[:, hm, :, :], eb[:, hm, :, :], AF.Exp, bias=cfar[:, hm:hm + 1], scale=1.0),
                reads=[r_const], writes=[r_const])
        p.op("dve", lambda: E["dve"].memset(eb[64:128, :, 0, 0:64], 0.0), reads=[r_const], writes=[r_const])
        for l in range(DEPTH):
            lam_init = 0.8 - 0.6 * math.exp(-0.3 * l)
            dl = dlam[:, l * 256:(l + 1) * 256].rearrange("p (a b c) -> p a b c", a=2, b=2)
            pr = lamtmp[:, l * 128:(l + 1) * 128].rearrange("p (a c) -> p a c", a=2)
            p.op("dve", lambda dl=dl, pr=pr: E["dve"].tensor_tensor(pr, dl[:, :, 0, :], dl[:, :, 1, :], ALU.mult),
                 reads=[r_const], writes=[r_const])
            p.op("dve", lambda pr=pr, l=l: E["dve"].tensor_reduce(lam[:, 4 * l + 2:4 * l + 4], pr, AX.X, ALU.add),
                 reads=[r_const], writes=[r_const])
            p.op("act", lambda l=l: E["act"].activation(lam[:, 4 * l + 2:4 * l + 4], lam[:, 4 * l + 2:4 * l + 4], AF.Exp),
                 reads=[r_const], writes=[r_const])
            p.op("dve", lambda l=l: E["dve"].tensor_tensor(lam[:, 4 * l:4 * l + 1], lam[:, 4 * l + 2:4 * l + 3],
                                                           lam[:, 4 * l + 3:4 * l + 4], ALU.subtract),
                 reads=[r_const], writes=[r_const])
            p.op("dve", lambda l=l, li=lam_init: E["dve"].tensor_scalar(
                lam[:, 4 * l:4 * l + 1], lam[:, 4 * l:4 * l + 1], li, None, ALU.add),
                reads=[r_const], writes=[r_const])
            p.op("dve", lambda l=l: E["dve"].tensor_scalar(
                lam[:, 4 * l + 1:4 * l + 2], lam[:, 4 * l:4 * l + 1], -1.0, None, ALU.mult),
                reads=[r_const], writes=[r_const])
            p.op("dve", lambda l=l, li=lam_init: E["dve"].tensor_scalar(
                gsub[:, l * 128:(l + 1) * 128], gsub[:, l * 128:(l + 1) * 128], 1.0 - li, None, ALU.mult),
                reads=[r_const], writes=[r_const])
        p.barrier()

        r_scr = {n: R(n) for n in ("qdT", "kdT", "qnT", "qrT", "knT", "zkT", "gT", "yT", "vd", "vb", "x1")}

        def phase_A(l, half, xsrc):
            t0 = half * HALF
            NT = HALF // 128
            NCK = HALF // 512
            with ExitStack() as sa:
                xT = sb("xT", [128, 16, HALF], BF, sa); r_xT = R()
                wbuf = [sb("wbuf%d" % i, [128, 8192], BF, sa) for i in range(2)]
                r_wb = [R(), R()]
                rstd_bc = sb("rstd_bc", [128, HALF], F32, sa); r_rbc = R()
                rcol = sb("rcol", [128, 3 * NT], F32, sa); r_rcol = [R() for _ in range(NT)]
                cqT = sb("cqT", [128, 4, HALF], BF, sa); r_cq = [R() for _ in range(NCK)]
                ckvT = sb("ckvT", [128, 2, HALF], BF, sa); r_ckv = [R() for _ in range(NCK)]
                bank = [ps("pa%d" % i, [128, 512], F32, sa) for i in range(8)]
                r_bank = [R() for _ in range(8)]
                bi = [0]

                def nb():
                    i = bi[0]
                    bi[0] = (i + 1) % 8
                    return bank[i], r_bank[i]

                with ExitStack() as s0:
                    xs = [sb("xs%d" % i, [128, D], F32, s0) for i in range(2)]
                    r_xs = [R(), R()]
                    junk = sb("junk", [128, D], BF, s0); r_junk = R()
                    rb = sb("rb", [128, 128], F32, s0); r_rb = R()
                    for tt in range(NT):
                        b = tt % 2
                        p.dma("sp", xs[b][:, :], xsrc[t0 + tt * 128:t0 + (tt + 1) * 128, :], writes=[r_xs[b]])
                        c0 = 3 * tt
                        p.op("act", lambda b=b, c0=c0: E["act"].activation(
                            junk[:, :], xs[b][:, :], AF.Square, accum_out=rcol[:, c0:c0 + 1]),
                            reads=[r_xs[b]], writes=[r_junk, r_rcol[tt]])
                        p.op("act", lambda c0=c0: E["act"].activation(
                            rcol[:, c0 + 1:c0 + 2], rcol[:, c0:c0 + 1], AF.Sqrt, bias=epsb[:, 0:1], scale=1.0 / D),
                            reads=[r_rcol[tt]], writes=[r_rcol[tt]])
                        p.op("dve", lambda c0=c0: E["dve"].reciprocal(rcol[:, c0 + 2:c0 + 3], rcol[:, c0 + 1:c0 + 2]),
                             reads=[r_rcol[tt]], writes=[r_rcol[tt]])
                        p.op("dve", lambda c0=c0: E["dve"].tensor_scalar(
                            rb[:, :], onesf[:, :], rcol[:, c0 + 2:c0 + 3], None, ALU.mult),
                            reads=[r_rcol[tt]], writes=[r_rb])
                        bk, rbk = nb()
                        p.op("pe", lambda bk=bk: E["pe"].transpose(bk[:, 0:128], rb[:, :], identf[:, :]),
                             reads=[r_rb], writes=[rbk])
                        p.op("act", lambda bk=bk, tt=tt: E["act"].copy(rstd_bc[:, tt * 128:(tt + 1) * 128], bk[:, 0:128]),
                             reads=[rbk], writes=[r_rbc])
                        for c4 in range(4):
                            bk, rbk = nb()
                            for j in range(4):
                                c = c4 * 4 + j
                                p.op("pe", lambda bk=bk, j=j, c=c, b=b: E["pe"].transpose(
                                    bk[:, j * 128:(j + 1) * 128], xs[b][:, c * 128:(c + 1) * 128], identf[:, :]),
                                    reads=[r_xs[b]], writes=[rbk], ms=(j == 3))
                            for j in range(4):
                                c = c4 * 4 + j
                                eng = "dve" if j % 2 == 0 else "act"
                                if eng == "dve":
                                    p.op("dve", lambda bk=bk, j=j, c=c, tt=tt: E["dve"].tensor_scalar(
                                        xT[:, c, tt * 128:(tt + 1) * 128], bk[:, j * 128:(j + 1) * 128],
                                        gin[:, l * 16 + c:l * 16 + c + 1], None, ALU.mult),
                                        reads=[rbk], writes=[r_xT])
                                else:
                                    p.op("act", lambda bk=bk, j=j, c=c, tt=tt: E["act"].activation(
                                        xT[:, c, tt * 128:(tt + 1) * 128], bk[:, j * 128:(j + 1) * 128],
                                        AF.Copy, scale=gin[:, l * 16 + c:l * 16 + c + 1]),
                                        reads=[rbk], writes=[r_xT])
                    p.barrier()

                with ExitStack() as s1:
                    t32 = [sb("t32_%d" % i, [128, 512], F32, s1) for i in range(3)]
                    r_t32 = [R() for _ in range(3)]
                    ost = [sb("ost%d" % i, [128, 512], BF, s1) for i in range(4)]
                    r_ost = [R() for _ in range(4)]
                    sqb = [sb("sqb%d" % i, [128, 4, 512], BF, s1) for i in range(2)]
                    r_sqb = [R(), R()]
                    rq_bc = sb("rq_bc", [128, HALF], F32, s1); r_rq = [R() for _ in range(NCK)]
                    rkv_bc = sb("rkv_bc", [128, HALF], F32, s1); r_rkv = [R() for _ in range(NCK)]
                    rkvc = sb("rkvc", [128, 3 * NT], F32, s1); r_rkvc = [R() for _ in range(NT)]
                    sqkv = [sb("sqkv%d" % i, [128, 2, 512], BF, s1) for i in range(2)]
                    r_sqkv = [R(), R()]
                    ctr = {"t": 0, "o": 0, "s": 0, "w": 0}

                    def nxt(k, n):
                        i = ctr[k]
                        ctr[k] = (i + 1) % n
                        return i

                    def load_w(src_ap, view_shape):
                        i = nxt("w", 2)
                        n = 1
                        for d_ in view_shape[1:]:
                            n *= d_
                        flat = wbuf[i][:, 0:n]
                        if len(view_shape) == 3:
                            v = flat.rearrange("p (a b) -> p a b", a=view_shape[1])
                        else:
                            v = flat
                        p.dma("pool", v, src_ap, writes=[r_wb[i]])
                        return v, r_wb[i]

                    def store(dst_ap, src_ap, rsrc, rdst):
                        p.dma("sp", dst_ap, src_ap, reads=[rsrc], writes=[rdst])

                    groups = [list(range(g, min(g + 4, NFT))) for g in range(0, NFT, 4)]
                    for grp in groups:
                        c0 = grp[0] * 128
                        ncol = len(grp) * 128
                        wv, rw = load_w(
                            wF[l, :, c0:c0 + ncol].rearrange("(c p) n -> p c n", p=128), [128, 16, ncol])
                        for n in range(NCK):
                            tsl = slice(n * 512, (n + 1) * 512)
                            gsl = slice(t0 + n * 512, t0 + (n + 1) * 512)
                            for jj, ft in enumerate(grp):
                                bk, rbk = nb()
                                for c in range(16):
                                    p.op("pe", lambda bk=bk, c=c, jj=jj, tsl=tsl: E["pe"].matmul(
                                        bk[:, :], wv[:, c, jj * 128:(jj + 1) * 128], xT[:, c, tsl],
                                        start=(c == 0), stop=(c == 15)),
                                        reads=[rw, r_xT], writes=[rbk], ms=(c == 15))
                                if ft < 16 or ft == 22:
                                    io = nxt("o", 4)
                                    if ft == 22:
                                        it = nxt("t", 3)
                                        p.op("dve", lambda bk=bk, it=it, tsl=tsl: E["dve"].tensor_tensor(
                                            t32[it][:, :], bk[:, :], rstd_bc[:, tsl], ALU.mult),
                                            reads=[rbk, r_rbc], writes=[r_t32[it]])
                                        p.op("pool", lambda it=it, io=io, gsl=gsl: E["pool"].tensor_tensor(
                                            ost[io][:, :], t32[it][:, :], t1[:, gsl], ALU.mult),
                                            reads=[r_t32[it]], writes=[r_ost[io]])
                                        store(zkT[:, gsl], ost[io][:, :], r_ost[io], r_scr["zkT"])
                                    else:
                                        p.op("dve", lambda bk=bk, io=io, tsl=tsl: E["dve"].tensor_tensor(
                                            ost[io][:, :], bk[:, :], rstd_bc[:, tsl], ALU.mult),
                                            reads=[rbk, r_rbc], writes=[r_ost[io]])
                                        if ft < 8:
                                            store(qdT[ft, :, gsl], ost[io][:, :], r_ost[io], r_scr["qdT"])
                                        else:
                                            store(kdT[ft - 8, :, gsl], ost[io][:, :], r_ost[io], r_scr["kdT"])
                                elif ft >= 23:
                                    it = nxt("t", 3)
                                    io = nxt("o", 4)
                                    p.op("dve", lambda bk=bk, it=it, tsl=tsl: E["dve"].tensor_tensor(
                                        t32[it][:, :], bk[:, :], rstd_bc[:, tsl], ALU.mult),
                                        reads=[rbk, r_rbc], writes=[r_t32[it]])
                                    p.op("act", lambda it=it, io=io: E["act"].activation(
                                        ost[io][:, :], t32[it][:, :], AF.Silu),
                                        reads=[r_t32[it]], writes=[r_ost[io]])
                                    store(gT[ft - 23, :, gsl], ost[io][:, :], r_ost[io], r_scr["gT"])
                                else:
                                    it = nxt("t", 3)
                                    p.op("dve", lambda bk=bk, it=it, tsl=tsl: E["dve"].tensor_tensor(
                                        t32[it][:, :], bk[:, :], rstd_bc[:, tsl], ALU.mult),
                                        reads=[rbk, r_rbc], writes=[r_t32[it]])
                                    if ft < 20:
                                        j = ft - 16
                                        isq = n % 2
                                        p.op("act", lambda it=it, isq=isq, j=j: E["act"].activation(
                                            sqb[isq][:, j, :], t32[it][:, :], AF.Square),
                                            reads=[r_t32[it]], writes=[r_sqb[isq]])
                                        p.op("pool", lambda it=it, j=j, tsl=tsl: E["pool"].tensor_scalar(
                                            cqT[:, j, tsl], t32[it][:, :], gq[:, l * 4 + j:l * 4 + j + 1], None, ALU.mult),
                                            reads=[r_t32[it]], writes=[r_cq[n]])
                                    else:
                                        j = ft - 20
                                        p.op("act", lambda it=it, j=j, n=n: E["act"].activation(
                                            sqkv[n % 2][:, j, :], t32[it][:, :], AF.Square),
                                            reads=[r_t32[it]], writes=[r_sqkv[n % 2]])
                                        p.op("pool", lambda it=it, j=j, tsl=tsl: E["pool"].tensor_scalar(
                                            ckvT[:, j, tsl], t32[it][:, :], gkv[:, l * 2 + j:l * 2 + j + 1], None, ALU.mult),
                                            reads=[r_t32[it]], writes=[r_ckv[n]])
                            if grp[0] == 16:
                                isq = n % 2
                                bk, rbk = nb()
                                for j in range(4):
                                    p.op("pe", lambda bk=bk, j=j, isq=isq: E["pe"].matmul(
                                        bk[:, :], onesb[:, :], sqb[isq][:, j, :], start=(j == 0), stop=(j == 3)),
                                        reads=[r_sqb[isq]], writes=[rbk], ms=(j == 3))
                                p.op("act", lambda bk=bk, tsl=tsl: E["act"].activation(
                                    rq_bc[:, tsl], bk[:, :], AF.Sqrt, bias=epsb[:, 0:1], scale=1.0 / 512),
                                    reads=[rbk], writes=[r_rq[n]])
                                p.op("dve", lambda tsl=tsl: E["dve"].reciprocal(rq_bc[:, tsl], rq_bc[:, tsl]),
                                     reads=[r_rq[n]], writes=[r_rq[n]])
                            if grp[0] == 20:
                                isq = n % 2
                                bk, rbk = nb()
                                for j in range(2):
                                    p.op("pe", lambda bk=bk, j=j, isq=isq: E["pe"].matmul(
                                        bk[:, :], onesb[:, :], sqkv[isq][:, j, :], start=(j == 0), stop=(j == 1)),
                                        reads=[r_sqkv[isq]], writes=[rbk], ms=(j == 1))
                                p.op("act", lambda bk=bk, tsl=tsl: E["act"].activation(
                                    rkv_bc[:, tsl], bk[:, :], AF.Sqrt, bias=epsb[:, 0:1], scale=1.0 / 256),
                                    reads=[rbk], writes=[r_rkv[n]])
                                p.op("dve", lambda tsl=tsl: E["dve"].reciprocal(rkv_bc[:, tsl], rkv_bc[:, tsl]),
                                     reads=[r_rkv[n]], writes=[r_rkv[n]])
                                for q4 in range(4):
                                    tt = n * 4 + q4
                                    bk, rbk = nb()
                                    for j in range(2):
                                        p.op("pe", lambda bk=bk, j=j, q4=q4, isq=isq: E["pe"].matmul(
                                            bk[:, 0:1], sqkv[isq][:, j, q4 * 128:(q4 + 1) * 128], onesb[:, 0:1],
                                            start=(j == 0), stop=(j == 1)),
                                            reads=[r_sqkv[isq]], writes=[rbk], ms=(j == 1))
                                    c0 = 3 * tt
                                    p.op("act", lambda bk=bk, c0=c0: E["act"].activation(
                                        rkvc[:, c0:c0 + 1], bk[:, 0:1], AF.Sqrt, bias=epsb[:, 0:1], scale=1.0 / 256),
                                        reads=[rbk], writes=[r_rkvc[tt]])
                                    p.op("dve", lambda c0=c0: E["dve"].reciprocal(rkvc[:, c0 + 1:c0 + 2], rkvc[:, c0:c0 + 1]),
                                         reads=[r_rkvc[tt]], writes=[r_rkvc[tt]])

                    for cg in range(2):
                        wv, rw = load_w(
                            wT[l, :, cg * 512:(cg + 1) * 512].rearrange("(c p) n -> p c n", p=128), [128, 16, 512])
                        for tt in range(NT):
                            bk, rbk = nb()
                            for c in range(16):
                                p.op("pe", lambda bk=bk, c=c, tt=tt: E["pe"].matmul(
                                    bk[:, :], xT[:, c, tt * 128:(tt + 1) * 128], wv[:, c, :],
                                    start=(c == 0), stop=(c == 15)),
                                    reads=[rw, r_xT], writes=[rbk], ms=(c == 15))
                            io = nxt("o", 4)
                            p.op("act", lambda bk=bk, io=io, tt=tt: E["act"].activation(
                                ost[io][:, :], bk[:, :], AF.Copy, scale=rcol[:, 3 * tt + 2:3 * tt + 3]),
                                reads=[rbk, r_rcol[tt]], writes=[r_ost[io]])
                            gb = half * NT + tt
                            store(vd[cg * 4:(cg + 1) * 4, :, gb, :].rearrange("h p d -> p h d"),
                                  ost[io][:, :].rearrange("p (h d) -> p h d", h=4), r_ost[io], r_scr["vd"])

                    wqv, rwq = load_w(wq[l, :, :].rearrange("(c p) n -> p c n", p=128), [128, 4, 2048])
                    for h in range(H):
                        for n in range(NCK):
                            tsl = slice(n * 512, (n + 1) * 512)
                            gsl = slice(t0 + n * 512, t0 + (n + 1) * 512)
                            for part in range(2):
                                cs = h * 256 + part * 128
                                bk, rbk = nb()
                                for c in range(4):
                                    p.op("pe", lambda bk=bk, c=c, cs=cs, tsl=tsl: E["pe"].matmul(
                                        bk[:, :], wqv[:, c, cs:cs + 128], cqT[:, c, tsl], start=(c == 0), stop=(c == 3)),
                                        reads=[rwq, r_cq[n]], writes=[rbk], ms=(c == 3))
                                io = nxt("o", 4)
                                if part == 0:
                                    p.op("dve", lambda bk=bk, io=io, tsl=tsl: E["dve"].tensor_tensor(
                                        ost[io][:, :], bk[:, :], rq_bc[:, tsl], ALU.mult),
                                        reads=[rbk, r_rq[n]], writes=[r_ost[io]])
                                    store(qnT[h, :, gsl], ost[io][:, :], r_ost[io], r_scr["qnT"])
                                else:
                                    it = nxt("t", 3)
                                    p.op("dve", lambda bk=bk, it=it, tsl=tsl: E["dve"].tensor_tensor(
                                        t32[it][:, :], bk[:, :], rq_bc[:, tsl], ALU.mult),
                                        reads=[rbk, r_rq[n]], writes=[r_t32[it]])
                                    p.op("pool", lambda it=it, io=io, gsl=gsl: E["pool"].tensor_tensor(
                                        ost[io][:, :], t32[it][:, :], t1[:, gsl], ALU.mult),
                                        reads=[r_t32[it]], writes=[r_ost[io]])
                                    bk2, rbk2 = nb()
                                    p.op("pe", lambda bk2=bk2, io=io: E["pe"].matmul(
                                        bk2[:, :], dfold[:, :], ost[io][:, :], start=True, stop=True),
                                        reads=[r_ost[io]], writes=[rbk2])
                                    io2 = nxt("o", 4)
                                    p.op("act", lambda bk2=bk2, io2=io2: E["act"].copy(ost[io2][:, :], bk2[:, :]),
                                         reads=[rbk2], writes=[r_ost[io2]])
                                    store(qrT[h, :, gsl], ost[io2][:, :], r_ost[io2], r_scr["qrT"])

                    i = nxt("w", 2)
                    wkk = wbuf[i][:, 0:2048].rearrange("p (a b) -> p a b", a=2)
                    wkv_ = wbuf[i][:, 2048:4096].rearrange("p (a b) -> p a b", a=2)
                    rwk = r_wb[i]
                    p.dma("pool", wkk, wkvk[l, :, :].rearrange("(c p) n -> p c n", p=128), writes=[rwk])
                    p.dma("pool", wkv_, wkvv[l, :, :].rearrange("(c p) n -> p c n", p=128), writes=[rwk])
                    for h in range(H):
                        for n in range(NCK):
                            tsl = slice(n * 512, (n + 1) * 512)
                            gsl = slice(t0 + n * 512, t0 + (n + 1) * 512)
                            bk, rbk = nb()
                            for c in range(2):
                                p.op("pe", lambda bk=bk, c=c, h=h, tsl=tsl: E["pe"].matmul(
                                    bk[:, :], wkk[:, c, h * 128:(h + 1) * 128], ckvT[:, c, tsl],
                                    start=(c == 0), stop=(c == 1)),
                                    reads=[rwk, r_ckv[n]], writes=[rbk], ms=(c == 1))
                            io = nxt("o", 4)
                            p.op("dve", lambda bk=bk, io=io, tsl=tsl: E["dve"].tensor_tensor(
                                ost[io][:, :], bk[:, :], rkv_bc[:, tsl], ALU.mult),
                                reads=[rbk, r_rkv[n]], writes=[r_ost[io]])
                            store(knT[h, :, gsl], ost[io][:, :], r_ost[io], r_scr["knT"])
                    for cg in range(2):
                        for tt in range(NT):
                            n = tt // 4
                            bk, rbk = nb()
                            for c in range(2):
                                p.op("pe", lambda bk=bk, c=c, tt=tt, cg=cg: E["pe"].matmul(
                                    bk[:, :], ckvT[:, c, tt * 128:(tt + 1) * 128], wkv_[:, c, cg * 512:(cg + 1) * 512],
                                    start=(c == 0), stop=(c == 1)),
                                    reads=[rwk, r_ckv[n]], writes=[rbk], ms=(c == 1))
                            io = nxt("o", 4)
                            p.op("act", lambda bk=bk, io=io, tt=tt: E["act"].activation(
                                ost[io][:, :], bk[:, :], AF.Copy, scale=rkvc[:, 3 * tt + 1:3 * tt + 2]),
                                reads=[rbk, r_rkvc[tt]], writes=[r_ost[io]])
                            gb = half * NT + tt
                            store(vb[cg * 4:(cg + 1) * 4, :, gb, :].rearrange("h p d -> p h d"),
                                  ost[io][:, :].rearrange("p (h d) -> p h d", h=4), r_ost[io], r_scr["vb"])
                    p.barrier()

        def phase_C(l):
            with ExitStack() as sc:
                KT = [sb("KT%d" % i, [128, S], BF, sc) for i in range(2)]
                QT = [sb("QT%d" % i, [128, S], BF, sc) for i in range(2)]
                QR = [sb("QR%d" % i, [128, S], BF, sc) for i in range(2)]
                QB = [sb("QB%d" % i, [128, S], BF, sc) for i in range(2)]
                GT = [sb("GT%d" % i, [128, S], BF, sc) for i in range(2)]
                VX = [sb("VX%d" % i, [128, NB, 129], BF, sc) for i in range(2)]
                YS = [sb("YS%d" % i, [128, S], BF, sc) for i in range(2)]
                ZK = sb("ZK", [128, S], BF, sc)
                r_in = [R(), R()]
                r_ys = [R(), R()]
                r_zk = R()
                PT = [sb("PT%d" % i, [128, 2, 256], BF, sc) for i in range(3)]
                r_pt = [R() for _ in range(3)]
                fa = [sb("fa%d" % i, [128, 128], F32, sc) for i in range(2)]
                fo = [sb("fo%d" % i, [128, 128], F32, sc) for i in range(2)]
                fj = sb("fj", [128, 128], BF, sc)
                fn_ = [sb("fn%d" % i, [128, 128], BF, sc) for i in range(2)]
                fs = [sb("fs%d" % i, [128, 8], F32, sc) for i in range(2)]
                r_f = [R(), R()]
                r_fj = R()
                SB_ = [ps("pS%d" % i, [128, 512], F32, sc) for i in range(2)]
                r_S = [R(), R()]
                OB = [ps("pO%d" % i, [128, 512], F32, sc) for i in range(4)]
                r_O = [R() for _ in range(4)]
                TP = [ps("pT%d" % i, [128, 1024], BF, sc) for i in range(2)]
                r_T = [R(), R()]
                ctr = {"s": 0, "p": 0, "o": 0, "f": 0, "t": 0}

                def nxt(k, n):
                    i = ctr[k]
                    ctr[k] = (i + 1) % n
                    return i

                for i in range(2):
                    p.op("pool", lambda i=i: E["pool"].memset(VX[i][:, :, 128:129], 1.0), writes=[r_in[i]])
                    p.op("pool", lambda i=i: E["pool"].memset(QT[i][64:128, :], 0.0), writes=[r_in[i]])
                    p.op("pool", lambda i=i: E["pool"].memset(QB[i][0:64, :], 0.0), writes=[r_in[i]])
                    if "c_ng" in dbg:
                        p.op("pool", lambda i=i: E["pool"].memset(YS[i][:, :], 0.0), writes=[r_ys[i]])
                p.dma("sp", ZK[:, :], zkT[:, :], reads=[r_scr["zkT"]], writes=[r_zk])

                heads = [("d", h) for h in range(H)] + [("m", h) for h in range(H)]
                if "c_heads" in dbg:
                    heads = [heads[i] for i in dbg["c_heads"]]
                n_groups = dbg.get("c_ng", NB // 2)

                def load_head(hi):
                    kind, h = heads[hi]
                    b = hi % 2
                    rr = r_in[b]
                    if kind == "d":
                        p.dma("sp", KT[b][:, :], kdT[h, :, :], reads=[r_scr["kdT"]], writes=[rr])
                        p.dma("sp", QT[b][0:64, :], qdT[h, 0:64, :], reads=[r_scr["qdT"]], writes=[rr])
                        p.dma("sp", QB[b][64:128, :], qdT[h, 64:128, :], reads=[r_scr["qdT"]], writes=[rr])
                        p.dma("sp", VX[b][:, :, 0:128], vd[h, :, :, :], reads=[r_scr["vd"]], writes=[rr])
                        p.dma("sp", GT[b][:, :], gT[h, :, :], reads=[r_scr["gT"]], writes=[rr])
                    else:
                        p.dma("sp", KT[b][:, :], knT[h, :, :], reads=[r_scr["knT"]], writes=[rr])
                        p.dma("sp", QT[b][:, :], qnT[h, :, :], reads=[r_scr["qnT"]], writes=[rr])
                        p.dma("sp", QR[b][:, :], qrT[h, :, :], reads=[r_scr["qrT"]], writes=[rr])
                        p.dma("sp", VX[b][:, :, 0:128], vb[h, :, :, :], reads=[r_scr["vb"]], writes=[rr])
                        p.dma("sp", GT[b][:, :], gT[8 + h, :, :], reads=[r_scr["gT"]], writes=[rr])

                def finalize(hi, lb, ob, rob):
                    kind, h = heads[hi]
                    b = hi % 2
                    f = nxt("f", 2)
                    rf = r_f[f]
                    qs = slice(lb * 128, (lb + 1) * 128)
                    if kind == "d":
                        p.op("dve", lambda: E["dve"].reciprocal(fs[f][:, 0:1], ob[:, 128:129]),
                             reads=[rob], writes=[rf])
                        p.op("dve", lambda: E["dve"].reciprocal(fs[f][:, 1:2], ob[:, 384:385]),
                             reads=[rob], writes=[rf])
                        p.op("dve", lambda: E["dve"].tensor_tensor(
                            fs[f][:, 2:3], fs[f][:, 1:2], lam[:, 4 * l + 1:4 * l + 2], ALU.mult),
                            reads=[rf], writes=[rf])
                        p.op("dve", lambda: E["dve"].tensor_scalar(
                            fa[f][:, :], ob[:, 0:128], fs[f][:, 0:1], None, ALU.mult),
                            reads=[rob, rf], writes=[rf])
                        p.op("dve", lambda: E["dve"].tensor_scalar(
                            fo[f][:, :], ob[:, 256:384], fs[f][:, 2:3], None, ALU.mult),
                            reads=[rob, rf], writes=[rf])
                        p.op("dve", lambda: E["dve"].tensor_tensor(
                            fo[f][:, :], fo[f][:, :], fa[f][:, :], ALU.add),
                            reads=[rf], writes=[rf])
                        p.op("act", lambda: E["act"].activation(
                            fj[:, :], fo[f][:, :], AF.Square, accum_out=fs[f][:, 3:4]),
                            reads=[rf], writes=[rf, r_fj])
                        p.op("act", lambda: E["act"].activation(
                            fs[f][:, 4:5], fs[f][:, 3:4], AF.Sqrt, bias=epsb[:, 0:1], scale=1.0 / 128),
                            reads=[rf], writes=[rf])
                        p.op("dve", lambda: E["dve"].reciprocal(fs[f][:, 5:6], fs[f][:, 4:5]),
                             reads=[rf], writes=[rf])
                        p.op("dve", lambda: E["dve"].tensor_scalar(
                            fa[f][:, :], fo[f][:, :], fs[f][:, 5:6], None, ALU.mult),
                            reads=[rf], writes=[rf])
                        p.op("dve", lambda: E["dve"].tensor_tensor(
                            fn_[f][:, :], fa[f][:, :], gsub[:, l * 128:(l + 1) * 128], ALU.mult),
                            reads=[rf], writes=[rf])
                    else:
                        p.op("dve", lambda: E["dve"].reciprocal(fs[f][:, 0:1], ob[:, 128:129]),
                             reads=[rob], writes=[rf])
                        p.op("dve", lambda: E["dve"].tensor_scalar(
                            fn_[f][:, :], ob[:, 0:128], fs[f][:, 0:1], None, ALU.mult),
                            reads=[rob, rf], writes=[rf])
                    ti = nxt("t", 16)
                    tb, ts_ = ti // 8, ti % 8
                    p.op("pe", lambda: E["pe"].transpose(TP[tb][:, ts_ * 128:(ts_ + 1) * 128], fn_[f][:, :], identb[:, :]),
                         reads=[rf], writes=[r_T[tb]])
                    p.op("dve", lambda: E["dve"].tensor_tensor(
                        YS[b][:, qs], TP[tb][:, ts_ * 128:(ts_ + 1) * 128], GT[b][:, qs], ALU.mult),
                        reads=[r_T[tb], r_in[b]], writes=[r_ys[b]])

                load_head(0)
                for hi in range(len(heads)):
                    kind, h = heads[hi]
                    b = hi % 2
                    if hi + 1 < len(heads):
                        load_head(hi + 1)
                    nm = 2 if kind == "d" else 1
                    scale = SC_DIFF if kind == "d" else SC_MLA
                    for G in range(n_groups):
                        lbs = (2 * G, 2 * G + 1)
                        obk = []
                        for lb in lbs:
                            io = nxt("o", 4)
                            obk.append((OB[io], r_O[io]))
                        for t in range(lbs[1] + 1):
                            act_l = [i for i, lb in enumerate(lbs) if lb >= t]
                            q0 = lbs[act_l[0]] * 128
                            N = len(act_l) * 128
                            ks = slice(t * 128, (t + 1) * 128)
                            isb = nxt("s", 2)
                            sbk, rsb = SB_[isb], r_S[isb]
                            if kind == "d":
                                for m in range(2):
                                    qsrc = QT[b] if m == 0 else QB[b]
                                    p.op("pe", lambda m=m, qsrc=qsrc, sbk=sbk, ks=ks, q0=q0, N=N, b=b: E["pe"].matmul(
                                        sbk[:, m * 256:m * 256 + N], KT[b][:, ks],
                                        qsrc[:, q0:q0 + N], start=True, stop=True),
                                        reads=[r_in[b]], writes=[rsb], ms=(m == 1))
                            else:
                                p.op("pe", lambda sbk=sbk, ks=ks, q0=q0, N=N, b=b: E["pe"].matmul(
                                    sbk[:, 0:N], KT[b][:, ks], QT[b][:, q0:q0 + N], start=True, stop=False),
                                    reads=[r_in[b]], writes=[rsb], ms=False)
                                p.op("pe", lambda sbk=sbk, ks=ks, q0=q0, N=N, b=b: E["pe"].matmul(
                                    sbk[:, 0:N], ZK[:, ks], QR[b][:, q0:q0 + N], start=False, stop=True),
                                    reads=[r_in[b], r_zk], writes=[rsb])
                            ip = nxt("p", 3)
                            pt, rpt = PT[ip], r_pt[ip]
                            sv = sbk[:, :].rearrange("p (m n) -> p m n", m=2)
                            p.op("act", lambda pt=pt, sv=sv, nm=nm, N=N, scale=scale: E["act"].activation(
                                pt[:, 0:nm, 0:N], sv[:, 0:nm, 0:N], AF.Exp, scale=scale),
                                reads=[rsb], writes=[rpt])
                            for i in act_l:
                                lb = lbs[i]
                                co = (i - act_l[0]) * 128
                                if kind == "d" and (t == lb or t == lb - 1) and not dbg.get("x_nospecial"):
                                    kd = 0 if t == lb else 1
                                    p.op("dve", lambda pt=pt, co=co, kd=kd, h=h: E["dve"].tensor_tensor(
                                        pt[:, :, co:co + 128], pt[:, :, co:co + 128],
                                        eb[:, 2 * h:2 * h + 2, kd, :], ALU.mult),
                                        reads=[rpt], writes=[rpt])
                                elif kind == "m" and t == lb:
                                    p.op("pool", lambda pt=pt, co=co: E["pool"].memset(pt[64:128, 0, co:co + 64], 0.0),
                                         reads=[rpt], writes=[rpt])
                            for i in act_l:
                                lb = lbs[i]
                                co = (i - act_l[0]) * 128
                                ob, rob = obk[i]
                                for m in range(nm):
                                    p.op("pe", lambda pt=pt, co=co, ob=ob, m=m, t=t, lb=lb, b=b: E["pe"].matmul(
                                        ob[:, m * 256:m * 256 + 129], pt[:, m, co:co + 128], VX[b][:, t, :],
                                        start=(t == 0 and m == 0), stop=(t == lb), skip_group_check=True),
                                        reads=[rpt, r_in[b]], writes=[rob], ms=(m == nm - 1))
                                if t == lb and not (kind == "d" and dbg.get("x_nofin")):
                                    finalize(hi, lb, ob, rob)
                    tile_idx = h if kind == "d" else 8 + h
                    p.dma("sp", yT[tile_idx, :, :], YS[b][:, :], reads=[r_ys[b]], writes=[r_scr["yT"]])
                p.barrier()

        def phase_D(l, xsrc, last):
            with ExitStack() as sd:
                wo_sb = sb("wo_sb", [128, 16, D], BF, sd); r_wo = R()
                ys = [sb("ysD%d" % i, [128, 16, 512], BF, sd) for i in range(2)]
                r_y = [R(), R()]
                xs = [sb("xsD%d" % i, [128, D], F32, sd) for i in range(3)]
                r_x = [R() for _ in range(3)]
                junk = sb("junkD", [128, D], BF, sd); r_junk = R()
                gfin = sb("gfin", [128, D], F32, sd); r_g = R()
                fsd = sb("fsd", [128, 3 * NB], F32, sd); r_fs = [R() for _ in range(NB)]
                bank = [ps("pd%d" % i, [128, 512], F32, sd) for i in range(8)]
                r_bank = [R() for _ in range(8)]
                bi = [0]
                for cg in range(4):
                    p.dma("pool", wo_sb[:, :, cg * 512:(cg + 1) * 512],
                          wo[l, :, cg * 512:(cg + 1) * 512].rearrange("(c p) n -> p c n", p=128), writes=[r_wo])
                if last:
                    p.dma("sp", gfin[:, :], gfin_in[:, :], writes=[r_g])
                for tt in range(NB):
                    n = tt // 4
                    yb = n % 2
                    if tt % 4 == 0:
                        p.dma("sp", ys[yb][:, :, :], yT[:, :, n * 512:(n + 1) * 512].rearrange("c p n -> p c n"),
                              reads=[r_scr["yT"]], writes=[r_y[yb]])
                    xb = tt % 3
                    p.dma("sp", xs[xb][:, :], xsrc[tt * 128:(tt + 1) * 128, :],
                          reads=([r_scr["x1"]] if l > 0 else []), writes=[r_x[xb]])
                    for dg in range(4):
                        i = bi[0]
                        bi[0] = (i + 1) % 8
                        bk, rbk = bank[i], r_bank[i]
                        for c in range(16):
                            p.op("pe", lambda bk=bk, c=c, yb=yb, tt=tt, dg=dg: E["pe"].matmul(
                                bk[:, :], ys[yb][:, c, (tt % 4) * 128:(tt % 4 + 1) * 128],
                                wo_sb[:, c, dg * 512:(dg + 1) * 512], start=(c == 0), stop=(c == 15)),
                                reads=[r_y[yb], r_wo], writes=[rbk], ms=(c == 15))
                        p.op("dve", lambda bk=bk, xb=xb, dg=dg: E["dve"].tensor_tensor(
                            xs[xb][:, dg * 512:(dg + 1) * 512], bk[:, :], xs[xb][:, dg * 512:(dg + 1) * 512], ALU.add),
                            reads=[rbk, r_x[xb]], writes=[r_x[xb]])
                    if not last:
                        p.dma("sp", x1[tt * 128:(tt + 1) * 128, :], xs[xb][:, :], reads=[r_x[xb]], writes=[r_scr["x1"]])
                    else:
                        c0 = 3 * tt
                        p.op("act", lambda xb=xb, c0=c0: E["act"].activation(
                            junk[:, :], xs[xb][:, :], AF.Square, accum_out=fsd[:, c0:c0 + 1]),
                            reads=[r_x[xb]], writes=[r_junk, r_fs[tt]])
                        p.op("act", lambda c0=c0: E["act"].activation(
                            fsd[:, c0 + 1:c0 + 2], fsd[:, c0:c0 + 1], AF.Sqrt, bias=epsb[:, 0:1], scale=1.0 / D),
                            reads=[r_fs[tt]], writes=[r_fs[tt]])
                        p.op("dve", lambda c0=c0: E["dve"].reciprocal(fsd[:, c0 + 2:c0 + 3], fsd[:, c0 + 1:c0 + 2]),
                             reads=[r_fs[tt]], writes=[r_fs[tt]])
                        p.op("act", lambda xb=xb, c0=c0: E["act"].activation(
                            xs[xb][:, :], xs[xb][:, :], AF.Copy, scale=fsd[:, c0 + 2:c0 + 3]),
                            reads=[r_x[xb], r_fs[tt]], writes=[r_x[xb]])
                        p.op("dve", lambda xb=xb: E["dve"].tensor_tensor(
                            xs[xb][:, :], xs[xb][:, :], gfin[:, :], ALU.mult),
                            reads=[r_x[xb], r_g], writes=[r_x[xb]])
                        p.dma("sp", out_d[tt * 128:(tt + 1) * 128, :], xs[xb][:, :], reads=[r_x[xb]], writes=[r_scr["x1"]])
                p.barrier()

        done = False
        for l in range(DEPTH):
            xsrc = x_in if l == 0 else x1
            for half in range(S // HALF):
                if not dbg.get("skipA"):
                    phase_A(l, half, xsrc)
            if stop_after == ("A", l):
                done = True
                break
            phase_C(l)
            if stop_after == ("C", l):
                done = True
                break
            phase_D(l, xsrc, last=(l == DEPTH - 1))
            if stop_after == ("D", l):
                done = True
                break
        p.barrier()
        build_program.ninst = p.ninst
    return nc


def _rel_bucket_np(rel):
    nb = 16
    max_exact = 8
    ret = (rel > 0).astype(np.int32) * nb
    n = np.abs(rel)
    nf = np.maximum(n, 1).astype(np.float32)
    large = max_exact + (np.log(nf / max_exact) / math.log(128 / max_exact) * (nb - max_exact)).astype(np.int32)
    large = np.minimum(large, nb - 1)
    return ret + np.where(n < max_exact, n, large)


def prepare_inputs(x, norm_g, w_in, diff_lambda, diff_subln_g, mla_q_norm_g, mla_w_q_b,
                   mla_kv_norm_g, mla_w_kv_b, w_out, rel_bias, final_norm_g):
    f = np.float32
    x = np.asarray(x, f)
    w_in = np.asarray(w_in, f)
    kr0 = 3840
    swap = np.concatenate([np.arange(32, 64), np.arange(0, 32)])
    colsF = np.concatenate([np.arange(0, 2048), np.arange(3072, 3840), kr0 + np.arange(64), kr0 + swap,
                            np.arange(3904, 5952)])
    assert colsF.size == NFT * 128
    wF = np.ascontiguousarray(w_in[:, :, colsF])
    wT = np.ascontiguousarray(w_in[:, :, 2048:3072])
    wqb = np.asarray(mla_w_q_b, f)
    cq_cols = []
    for h in range(H):
        b0 = h * 192
        cq_cols += [b0 + np.arange(128), b0 + 128 + np.arange(64), b0 + 128 + swap]
    wq = np.ascontiguousarray(wqb[:, :, np.concatenate(cq_cols)])
    wkvb = np.asarray(mla_w_kv_b, f)
    kc = np.concatenate([h * 256 + np.arange(128) for h in range(H)])
    vc = np.concatenate([h * 256 + 128 + np.arange(128) for h in range(H)])
    wkvk = np.ascontiguousarray(wkvb[:, :, kc])
    wkvv = np.ascontiguousarray(wkvb[:, :, vc])
    wo = np.ascontiguousarray(np.asarray(w_out, f))

    def colmajor(v, nchunk):
        v = np.asarray(v, f).reshape(DEPTH, nchunk, 128)
        return np.ascontiguousarray(v.transpose(2, 0, 1).reshape(128, DEPTH * nchunk))

    def bcast(v):
        v = np.asarray(v, f).reshape(1, -1)
        return np.ascontiguousarray(np.broadcast_to(v, (128, v.shape[1])))

    rb = np.asarray(rel_bias, f)
    k = np.arange(128)[:, None]
    q = np.arange(128)[None, :]
    idx = np.stack([_rel_bucket_np(k - q), _rel_bucket_np(k - 128 - q)], 0)
    bt = rb[idx]
    bt = np.ascontiguousarray(bt.transpose(1, 3, 0, 2).reshape(128, 16 * 2 * 128))
    cfar = bcast(rb[15, :])
    pos = np.arange(S, dtype=np.float32)
    inv_freq = (10000.0 ** (-np.arange(0, 64, 2, dtype=np.float32) / 64)).astype(np.float32)
    ang = pos[:, None] * inv_freq[None, :]
    cos, sin = np.cos(ang).astype(f).T, np.sin(ang).astype(f).T
    t1 = np.ascontiguousarray(np.concatenate([cos, cos, -sin, sin], 0))
    identf = np.eye(128, dtype=f)
    identb = np.eye(128, dtype=f).astype(ml_dtypes.bfloat16)
    dfold = (np.arange(128)[:, None] % 64 == np.arange(128)[None, :] % 64).astype(f).astype(ml_dtypes.bfloat16)
    shared = dict(
        wF=wF, wT=wT, wq=wq, wkvk=wkvk, wkvv=wkvv, wo=wo,
        g_in=colmajor(norm_g, 16), gq=colmajor(mla_q_norm_g, 4), gkv=colmajor(mla_kv_norm_g, 2),
        gsub=bcast(np.asarray(diff_subln_g, f).reshape(-1)), gfin=bcast(final_norm_g),
        dlam=bcast(np.asarray(diff_lambda, f).reshape(-1)), cfar=cfar, bt=bt, t1=t1,
        identf=identf, identb=identb, dfold=dfold)
    in_maps = []
    for c in range(8):
        m = dict(shared)
        m["x"] = np.ascontiguousarray(x[c % 4])
        in_maps.append(m)
    return in_maps


_NC_CACHE = {}


def kernel(**inputs):
    in_maps = prepare_inputs(**inputs)
    if "nc" not in _NC_CACHE:
        _NC_CACHE["nc"] = build_program()
    res = run_bass_kernel_spmd(_NC_CACHE["nc"], in_maps, core_ids=list(range(8)))
    out = np.stack([np.asarray(res.results[c]["out"], dtype=np.float32) for c in range(4)], 0)
    return out
```

```python
import math
from contextlib import ExitStack

import numpy as np
import ml_dtypes

import concourse.bass as bass
import concourse.mybir as mybir
from concourse.bass_utils import run_bass_kernel_spmd

F32 = mybir.dt.float32
BF = mybir.dt.bfloat16
AF = mybir.ActivationFunctionType
ALU = mybir.AluOpType
AX = mybir.AxisListType

D = 2048
S = 4096
NB = S // 128
DEPTH = 2
H = 8
EPS = 1e-6
NFT = 39
HALF = 2048
SC_DIFF = 64 ** -0.5
SC_MLA = 192 ** -0.5
SAME_ENG_SYNC = True


class R:
    __slots__ = ("w", "r", "name")

    def __init__(self, name=""):
        self.w = None
        self.r = {}
        self.name = name


class Prog:
    NDS = 24

    def __init__(self, nc, es):
        self.nc = nc
        self.E = {"pe": nc.tensor, "act": nc.scalar, "dve": nc.vector, "pool": nc.gpsimd, "sp": nc.sync}
        self.csem = {e: es.enter_context(nc.semaphore("c_" + e)) for e in ("pe", "act", "dve", "pool")}
        self.ccnt = {e: 0 for e in self.csem}
        self.dsem = [es.enter_context(nc.semaphore("d%d" % i)) for i in range(self.NDS)]
        self.dcnt = [0] * self.NDS
        self.dnext = 0
        self.waited = {}
        self.ninst = 0

    def _sem(self, tag):
        return self.csem[tag[1]] if tag[0] == "c" else self.dsem[tag[1]]

    def _wait(self, eng, tag):
        key = (eng, tag[0], tag[1])
        if self.waited.get(key, 0) >= tag[2]:
            return
        self.E[eng].wait_ge(self._sem(tag), tag[2])
        self.waited[key] = tag[2]

    def _deps(self, eng, reads, writes):
        deps = {}
        for r in reads:
            if r.w is not None:
                k = (r.w[0], r.w[1])
                deps[k] = max(deps.get(k, 0), r.w[2])
        for w in writes:
            if w.w is not None:
                k = (w.w[0], w.w[1])
                deps[k] = max(deps.get(k, 0), w.w[2])
            for k, v in w.r.items():
                deps[k] = max(deps.get(k, 0), v)
        for k in sorted(deps, key=str):
            if k[0] == "c" and k[1] == eng and (eng == "pe" or not SAME_ENG_SYNC):
                continue
            self._wait(eng, (k[0], k[1], deps[k]))

    def _mark(self, tag, reads, writes):
        k = (tag[0], tag[1])
        for r in reads:
            r.r[k] = max(r.r.get(k, 0), tag[2])
        for w in writes:
            w.w = tag
            w.r = {}

    def op(self, eng, fn, reads=(), writes=(), ms=True):
        self._deps(eng, reads, writes)
        inst = fn()
        self.ninst += 1
        if ms:
            self.ccnt[eng] += 1
            tag = ("c", eng, self.ccnt[eng])
            inst.then_inc(self.csem[eng], 1)
        else:
            tag = ("c", eng, self.ccnt[eng] + 1)
        self._mark(tag, reads, writes)
        return inst

    def dma(self, q, out, in_, reads=(), writes=()):
        i = self.dnext
        self.dnext = (self.dnext + 1) % self.NDS
        if self.dcnt[i] > 0:
            self._wait(q, ("d", i, self.dcnt[i]))
        self._deps(q, reads, writes)
        self.dcnt[i] += 16
        tag = ("d", i, self.dcnt[i])
        self.E[q].dma_start(out=out, in_=in_).then_inc(self.dsem[i], 16)
        self.ninst += 1
        self._mark(tag, reads, writes)

    def barrier(self, engines=("pe", "act", "dve", "pool", "sp")):
        tags = [("c", e, self.ccnt[e]) for e in self.csem if self.ccnt[e] > 0]
        tags += [("d", i, self.dcnt[i]) for i in range(self.NDS) if self.dcnt[i] > 0]
        for e in engines:
            for t in tags:
                if t[0] == "c" and t[1] == e:
                    continue
                self._wait(e, t)


def build_program(dbg=None):
    dbg = dbg or {}
    stop_after = dbg.get("stop", None)
    expose = dbg.get("expose", ())
    nc = bass.Bass("TRN2", target_bir_lowering=False)

    def din(name, shape, dt=F32):
        return nc.dram_tensor(name, list(shape), dt, kind="ExternalInput")

    def dscr(name, shape, dt=BF):
        if name in expose:
            return nc.dram_tensor(name, list(shape), dt, kind="ExternalOutput")
        return nc.dram_tensor(name, list(shape), dt)

    x_in = din("x", [S, D])
    wF = din("wF", [DEPTH, D, NFT * 128])
    wT = din("wT", [DEPTH, D, 1024])
    wq = din("wq", [DEPTH, 512, 2048])
    wkvk = din("wkvk", [DEPTH, 256, 1024])
    wkvv = din("wkvv", [DEPTH, 256, 1024])
    wo = din("wo", [DEPTH, D, D])
    g_in = din("g_in", [128, DEPTH * 16])
    gq_in = din("gq", [128, DEPTH * 4])
    gkv_in = din("gkv", [128, DEPTH * 2])
    gsub_in = din("gsub", [128, DEPTH * 128])
    gfin_in = din("gfin", [128, D])
    dlam_in = din("dlam", [128, DEPTH * 256])
    cfar_in = din("cfar", [128, 16])
    bt_in = din("bt", [128, 16 * 2 * 128])
    t1_in = din("t1", [128, S])
    identf_in = din("identf", [128, 128])
    identb_in = din("identb", [128, 128], BF)
    dfold_in = din("dfold", [128, 128], BF)
    out_d = nc.dram_tensor("out", [S, D], F32, kind="ExternalOutput")

    qdT = dscr("qdT", [H, 128, S])
    kdT = dscr("kdT", [H, 128, S])
    qnT = dscr("qnT", [H, 128, S])
    qrT = dscr("qrT", [H, 128, S])
    knT = dscr("knT", [H, 128, S])
    zkT = dscr("zkT", [128, S])
    gT = dscr("gT", [16, 128, S])
    yT = dscr("yT", [16, 128, S])
    vd = dscr("vd", [H, 128, NB, 128])
    vb = dscr("vb", [H, 128, NB, 128])
    x1 = dscr("x1", [S, D], F32)

    with ExitStack() as es:
        p = Prog(nc, es)
        E = p.E

        uid = [0]

        def sb(name, shape, dt, stack=es):
            uid[0] += 1
            return stack.enter_context(nc.sbuf_tensor("s%d_%s" % (uid[0], name), list(shape), dt))

        def ps(name, shape, dt, stack):
            uid[0] += 1
            return stack.enter_context(nc.psum_tensor("p%d_%s" % (uid[0], name), list(shape), dt))

        identf = sb("identf", [128, 128], F32); r_const = R("const")
        identb = sb("identb", [128, 128], BF)
        dfold = sb("dfold", [128, 128], BF)
        onesb = sb("onesb", [128, 128], BF)
        onesf = sb("onesf", [128, 128], F32)
        t1 = sb("t1", [128, S], F32)
        eb = sb("eb", [128, 16, 2, 128], F32)
        gin = sb("gin", [128, DEPTH * 16], F32)
        gq = sb("gq", [128, DEPTH * 4], F32)
        gkv = sb("gkv", [128, DEPTH * 2], F32)
        gsub = sb("gsub", [128, DEPTH * 128], F32)
        dlam = sb("dlam", [128, DEPTH * 256], F32)
        cfar = sb("cfar", [128, 16], F32)
        epsb = sb("epsb", [128, 1], F32)
        lam = sb("lam", [128, 8], F32)
        lamtmp = sb("lamtmp", [128, DEPTH * 128], F32)
        for dst, src in ((identf, identf_in), (identb, identb_in), (dfold, dfold_in), (t1, t1_in),
                         (gin, g_in), (gq, gq_in), (gkv, gkv_in), (gsub, gsub_in), (dlam, dlam_in),
                         (cfar, cfar_in)):
            p.dma("sp", dst[:, :], src[:, :], writes=[r_const])
        p.dma("sp", eb[:, :, :, :].rearrange("p a b c -> p (a b c)"), bt_in[:, :], writes=[r_const])
        p.op("dve", lambda: E["dve"].memset(onesb[:, :], 1.0), writes=[r_const])
        p.op("dve", lambda: E["dve"].memset(onesf[:, :], 1.0), writes=[r_const])
        p.op("dve", lambda: E["dve"].memset(epsb[:, :], EPS), writes=[r_const])
        p.op("dve", lambda: E["dve"].tensor_scalar(cfar[:, :], cfar[:, :], -1.0, None, ALU.mult),
             reads=[r_const], writes=[r_const])
        for hm in range(16):
            p.op("act", lambda hm=hm: E["act"].activation(
                eb[:, hm, :, :], eb[:, hm, :, :], AF.Exp, bias=cfar[:, hm:hm + 1], scale=1.0),
                reads=[r_const], writes=[r_const])
        p.op("dve", lambda: E["dve"].memset(eb[64:128, :, 0, 0:64], 0.0), reads=[r_const], writes=[r_const])
        for l in range(DEPTH):
            lam_init = 0.8 - 0.6 * math.exp(-0.3 * l)
            dl = dlam[:, l * 256:(l + 1) * 256].rearrange("p (a b c) -> p a b c", a=2, b=2)
            pr = lamtmp[:, l * 128:(l + 1) * 128].rearrange("p (a c) -> p a c", a=2)
            p.op("dve", lambda dl=dl, pr=pr: E["dve"].tensor_tensor(pr, dl[:, :, 0, :], dl[:, :, 1, :], ALU.mult),
                 reads=[r_const], writes=[r_const])
            p.op("dve", lambda pr=pr, l=l: E["dve"].tensor_reduce(lam[:, 4 * l + 2:4 * l + 4], pr, AX.X, ALU.add),
                 reads=[r_const], writes=[r_const])
            p.op("act", lambda l=l: E["act"].activation(lam[:, 4 * l + 2:4 * l + 4], lam[:, 4 * l + 2:4 * l + 4], AF.Exp),
                 reads=[r_const], writes=[r_const])
            p.op("dve", lambda l=l: E["dve"].tensor_tensor(lam[:, 4 * l:4 * l + 1], lam[:, 4 * l + 2:4 * l + 3],
                                                           lam[:, 4 * l + 3:4 * l + 4], ALU.subtract),
                 reads=[r_const], writes=[r_const])
            p.op("dve", lambda l=l, li=lam_init: E["dve"].tensor_scalar(
                lam[:, 4 * l:4 * l + 1], lam[:, 4 * l:4 * l + 1], li, None, ALU.add),
                reads=[r_const], writes=[r_const])
            p.op("dve", lambda l=l: E["dve"].tensor_scalar(
                lam[:, 4 * l + 1:4 * l + 2], lam[:, 4 * l:4 * l + 1], -1.0, None, ALU.mult),
                reads=[r_const], writes=[r_const])
            p.op("dve", lambda l=l, li=lam_init: E["dve"].tensor_scalar(
                gsub[:, l * 128:(l + 1) * 128], gsub[:, l * 128:(l + 1) * 128], 1.0 - li, None, ALU.mult),
                reads=[r_const], writes=[r_const])
        p.barrier()

        r_scr = {n: R(n) for n in ("qdT", "kdT", "qnT", "qrT", "knT", "zkT", "gT", "yT", "vd", "vb", "x1")}

        def phase_A(l, half, xsrc):
            t0 = half * HALF
            NT = HALF // 128
            NCK = HALF // 512
            with ExitStack() as sa:
                xT = sb("xT", [128, 16, HALF], BF, sa); r_xT = R()
                wbuf = [sb("wbuf%d" % i, [128, 8192], BF, sa) for i in range(2)]
                r_wb = [R(), R()]
                rstd_bc = sb("rstd_bc", [128, HALF], F32, sa); r_rbc = R()
                rcol = sb("rcol", [128, 3 * NT], F32, sa); r_rcol = [R() for _ in range(NT)]
                cqT = sb("cqT", [128, 4, HALF], BF, sa); r_cq = [R() for _ in range(NCK)]
                ckvT = sb("ckvT", [128, 2, HALF], BF, sa); r_ckv = [R() for _ in range(NCK)]
                bank = [ps("pa%d" % i, [128, 512], F32, sa) for i in range(8)]
                r_bank = [R() for _ in range(8)]
                bi = [0]

                def nb():
                    i = bi[0]
                    bi[0] = (i + 1) % 8
                    return bank[i], r_bank[i]

                with ExitStack() as s0:
                    xs = [sb("xs%d" % i, [128, D], F32, s0) for i in range(2)]
                    r_xs = [R(), R()]
                    junk = sb("junk", [128, D], BF, s0); r_junk = R()
                    rb = sb("rb", [128, 128], F32, s0); r_rb = R()
                    for tt in range(NT):
                        b = tt % 2
                        p.dma("sp", xs[b][:, :], xsrc[t0 + tt * 128:t0 + (tt + 1) * 128, :], writes=[r_xs[b]])
                        c0 = 3 * tt
                        p.op("act", lambda b=b, c0=c0: E["act"].activation(
                            junk[:, :], xs[b][:, :], AF.Square, accum_out=rcol[:, c0:c0 + 1]),
                            reads=[r_xs[b]], writes=[r_junk, r_rcol[tt]])
                        p.op("act", lambda c0=c0: E["act"].activation(
                            rcol[:, c0 + 1:c0 + 2], rcol[:, c0:c0 + 1], AF.Sqrt, bias=epsb[:, 0:1], scale=1.0 / D),
                            reads=[r_rcol[tt]], writes=[r_rcol[tt]])
                        p.op("dve", lambda c0=c0: E["dve"].reciprocal(rcol[:, c0 + 2:c0 + 3], rcol[:, c0 + 1:c0 + 2]),
                             reads=[r_rcol[tt]], writes=[r_rcol[tt]])
                        p.op("dve", lambda c0=c0: E["dve"].tensor_scalar(
                            rb[:, :], onesf[:, :], rcol[:, c0 + 2:c0 + 3], None, ALU.mult),
                            reads=[r_rcol[tt]], writes=[r_rb])
                        bk, rbk = nb()
                        p.op("pe", lambda bk=bk: E["pe"].transpose(bk[:, 0:128], rb[:, :], identf[:, :]),
                             reads=[r_rb], writes=[rbk])
                        p.op("act", lambda bk=bk, tt=tt: E["act"].copy(rstd_bc[:, tt * 128:(tt + 1) * 128], bk[:, 0:128]),
                             reads=[rbk], writes=[r_rbc])
                        for c4 in range(4):
                            bk, rbk = nb()
                            for j in range(4):
                                c = c4 * 4 + j
                                p.op("pe", lambda bk=bk, j=j, c=c, b=b: E["pe"].transpose(
                                    bk[:, j * 128:(j + 1) * 128], xs[b][:, c * 128:(c + 1) * 128], identf[:, :]),
                                    reads=[r_xs[b]], writes=[rbk], ms=(j == 3))
                            for j in range(4):
                                c = c4 * 4 + j
                                eng = "dve" if j % 2 == 0 else "act"
                                if eng == "dve":
                                    p.op("dve", lambda bk=bk, j=j, c=c, tt=tt: E["dve"].tensor_scalar(
                                        xT[:, c, tt * 128:(tt + 1) * 128], bk[:, j * 128:(j + 1) * 128],
                                        gin[:, l * 16 + c:l * 16 + c + 1], None, ALU.mult),
                                        reads=[rbk], writes=[r_xT])
                                else:
                                    p.op("act", lambda bk=bk, j=j, c=c, tt=tt: E["act"].activation(
                                        xT[:, c, tt * 128:(tt + 1) * 128], bk[:, j * 128:(j + 1) * 128],
                                        AF.Copy, scale=gin[:, l * 16 + c:l * 16 + c + 1]),
                                        reads=[rbk], writes=[r_xT])
                    p.barrier()

                with ExitStack() as s1:
                    t32 = [sb("t32_%d" % i, [128, 512], F32, s1) for i in range(3)]
                    r_t32 = [R() for _ in range(3)]
                    ost = [sb("ost%d" % i, [128, 512], BF, s1) for i in range(4)]
                    r_ost = [R() for _ in range(4)]
                    sqb = [sb("sqb%d" % i, [128, 4, 512], BF, s1) for i in range(2)]
                    r_sqb = [R(), R()]
                    rq_bc = sb("rq_bc", [128, HALF], F32, s1); r_rq = [R() for _ in range(NCK)]
                    rkv_bc = sb("rkv_bc", [128, HALF], F32, s1); r_rkv = [R() for _ in range(NCK)]
                    rkvc = sb("rkvc", [128, 3 * NT], F32, s1); r_rkvc = [R() for _ in range(NT)]
                    sqkv = [sb("sqkv%d" % i, [128, 2, 512], BF, s1) for i in range(2)]
                    r_sqkv = [R(), R()]
                    ctr = {"t": 0, "o": 0, "s": 0, "w": 0}

                    def nxt(k, n):
                        i = ctr[k]
                        ctr[k] = (i + 1) % n
                        return i

                    def load_w(src_ap, view_shape):
                        i = nxt("w", 2)
                        n = 1
                        for d_ in view_shape[1:]:
                            n *= d_
                        flat = wbuf[i][:, 0:n]
                        if len(view_shape) == 3:
                            v = flat.rearrange("p (a b) -> p a b", a=view_shape[1])
                        else:
                            v = flat
                        p.dma("pool", v, src_ap, writes=[r_wb[i]])
                        return v, r_wb[i]

                    def store(dst_ap, src_ap, rsrc, rdst):
                        p.dma("sp", dst_ap, src_ap, reads=[rsrc], writes=[rdst])

                    groups = [list(range(g, min(g + 4, NFT))) for g in range(0, NFT, 4)]
                    for grp in groups:
                        c0 = grp[0] * 128
                        ncol = len(grp) * 128
                        wv, rw = load_w(
                            wF[l, :, c0:c0 + ncol].rearrange("(c p) n -> p c n", p=128), [128, 16, ncol])
                        for n in range(NCK):
                            tsl = slice(n * 512, (n + 1) * 512)
                            gsl = slice(t0 + n * 512, t0 + (n + 1) * 512)
                            for jj, ft in enumerate(grp):
                                bk, rbk = nb()
                                for c in range(16):
                                    p.op("pe", lambda bk=bk, c=c, jj=jj, tsl=tsl: E["pe"].matmul(
                                        bk[:, :], wv[:, c, jj * 128:(jj + 1) * 128], xT[:, c, tsl],
                                        start=(c == 0), stop=(c == 15)),
                                        reads=[rw, r_xT], writes=[rbk], ms=(c == 15))
                                if ft < 16 or ft == 22:
                                    io = nxt("o", 4)
                                    if ft == 22:
                                        it = nxt("t", 3)
                                        p.op("dve", lambda bk=bk, it=it, tsl=tsl: E["dve"].tensor_tensor(
                                            t32[it][:, :], bk[:, :], rstd_bc[:, tsl], ALU.mult),
                                            reads=[rbk, r_rbc], writes=[r_t32[it]])
                                        p.op("pool", lambda it=it, io=io, gsl=gsl: E["pool"].tensor_tensor(
                                            ost[io][:, :], t32[it][:, :], t1[:, gsl], ALU.mult),
                                            reads=[r_t32[it]], writes=[r_ost[io]])
                                        store(zkT[:, gsl], ost[io][:, :], r_ost[io], r_scr["zkT"])
                                    else:
                                        p.op("dve", lambda bk=bk, io=io, tsl=tsl: E["dve"].tensor_tensor(
                                            ost[io][:, :], bk[:, :], rstd_bc[:, tsl], ALU.mult),
                                            reads=[rbk, r_rbc], writes=[r_ost[io]])
                                        if ft < 8:
                                            store(qdT[ft, :, gsl], ost[io][:, :], r_ost[io], r_scr["qdT"])
                                        else:
                                            store(kdT[ft - 8, :, gsl], ost[io][:, :], r_ost[io], r_scr["kdT"])
                                elif ft >= 23:
                                    it = nxt("t", 3)
                                    io = nxt("o", 4)
                                    p.op("dve", lambda bk=bk, it=it, tsl=tsl: E["dve"].tensor_tensor(
                                        t32[it][:, :], bk[:, :], rstd_bc[:, tsl], ALU.mult),
                                        reads=[rbk, r_rbc], writes=[r_t32[it]])
                                    p.op("act", lambda it=it, io=io: E["act"].activation(
                                        ost[io][:, :], t32[it][:, :], AF.Silu),
                                        reads=[r_t32[it]], writes=[r_ost[io]])
                                    store(gT[ft - 23, :, gsl], ost[io][:, :], r_ost[io], r_scr["gT"])
                                else:
                                    it = nxt("t", 3)
                                    p.op("dve", lambda bk=bk, it=it, tsl=tsl: E["dve"].tensor_tensor(
                                        t32[it][:, :], bk[:, :], rstd_bc[:, tsl], ALU.mult),
                                        reads=[rbk, r_rbc], writes=[r_t32[it]])
                                    if ft < 20:
                                        j = ft - 16
                                        isq = n % 2
                                        p.op("act", lambda it=it, isq=isq, j=j: E["act"].activation(
                                            sqb[isq][:, j, :], t32[it][:, :], AF.Square),
                                            reads=[r_t32[it]], writes=[r_sqb[isq]])
                                        p.op("pool", lambda it=it, j=j, tsl=tsl: E["pool"].tensor_scalar(
                                            cqT[:, j, tsl], t32[it][:, :], gq[:, l * 4 + j:l * 4 + j + 1], None, ALU.mult),
                                            reads=[r_t32[it]], writes=[r_cq[n]])
                                    else:
                                        j = ft - 20
                                        p.op("act", lambda it=it, j=j, n=n: E["act"].activation(
                                            sqkv[n % 2][:, j, :], t32[it][:, :], AF.Square),
                                            reads=[r_t32[it]], writes=[r_sqkv[n % 2]])
                                        p.op("pool", lambda it=it, j=j, tsl=tsl: E["pool"].tensor_scalar(
                                            ckvT[:, j, tsl], t32[it][:, :], gkv[:, l * 2 + j:l * 2 + j + 1], None, ALU.mult),
                                            reads=[r_t32[it]], writes=[r_ckv[n]])
                            if grp[0] == 16:
                                isq = n % 2
                                bk, rbk = nb()
                                for j in range(4):
                                    p.op("pe", lambda bk=bk, j=j, isq=isq: E["pe"].matmul(
                                        bk[:, :], onesb[:, :], sqb[isq][:, j, :], start=(j == 0), stop=(j == 3)),
                                        reads=[r_sqb[isq]], writes=[rbk], ms=(j == 3))
                                p.op("act", lambda bk=bk, tsl=tsl: E["act"].activation(
                                    rq_bc[:, tsl], bk[:, :], AF.Sqrt, bias=epsb[:, 0:1], scale=1.0 / 512),
                                    reads=[rbk], writes=[r_rq[n]])
                                p.op("dve", lambda tsl=tsl: E["dve"].reciprocal(rq_bc[:, tsl], rq_bc[:, tsl]),
                                     reads=[r_rq[n]], writes=[r_rq[n]])
                            if grp[0] == 20:
                                isq = n % 2
                                bk, rbk = nb()
                                for j in range(2):
                                    p.op("pe", lambda bk=bk, j=j, isq=isq: E["pe"].matmul(
                                        bk[:, :], onesb[:, :], sqkv[isq][:, j, :], start=(j == 0), stop=(j == 1)),
                                        reads=[r_sqkv[isq]], writes=[rbk], ms=(j == 1))
                                p.op("act", lambda bk=bk, tsl=tsl: E["act"].activation(
                                    rkv_bc[:, tsl], bk[:, :], AF.Sqrt, bias=epsb[:, 0:1], scale=1.0 / 256),
                                    reads=[rbk], writes=[r_rkv[n]])
                                p.op("dve", lambda tsl=tsl: E["dve"].reciprocal(rkv_bc[:, tsl], rkv_bc[:, tsl]),
                                     reads=[r_rkv[n]], writes=[r_rkv[n]])
                                for q4 in range(4):
                                    tt = n * 4 + q4
                                    bk, rbk = nb()
                                    for j in range(2):
                                        p.op("pe", lambda bk=bk, j=j, q4=q4, isq=isq: E["pe"].matmul(
                                            bk[:, 0:1], sqkv[isq][:, j, q4 * 128:(q4 + 1) * 128], onesb[:, 0:1],
                                            start=(j == 0), stop=(j == 1)),
                                            reads=[r_sqkv[isq]], writes=[rbk], ms=(j == 1))
                                    c0 = 3 * tt
                                    p.op("act", lambda bk=bk, c0=c0: E["act"].activation(
                                        rkvc[:, c0:c0 + 1], bk[:, 0:1], AF.Sqrt, bias=epsb[:, 0:1], scale=1.0 / 256),
                                        reads=[rbk], writes=[r_rkvc[tt]])
                                    p.op("dve", lambda c0=c0: E["dve"].reciprocal(rkvc[:, c0 + 1:c0 + 2], rkvc[:, c0:c0 + 1]),
                                         reads=[r_rkvc[tt]], writes=[r_rkvc[tt]])

                    for cg in range(2):
                        wv, rw = load_w(
                            wT[l, :, cg * 512:(cg + 1) * 512].rearrange("(c p) n -> p c n", p=128), [128, 16, 512])
                        for tt in range(NT):
                            bk, rbk = nb()
                            for c in range(16):
                                p.op("pe", lambda bk=bk, c=c, tt=tt: E["pe"].matmul(
                                    bk[:, :], xT[:, c, tt * 128:(tt + 1) * 128], wv[:, c, :],
                                    start=(c == 0), stop=(c == 15)),
                                    reads=[rw, r_xT], writes=[rbk], ms=(c == 15))
                            io = nxt("o", 4)
                            p.op("act", lambda bk=bk, io=io, tt=tt: E["act"].activation(
                                ost[io][:, :], bk[:, :], AF.Copy, scale=rcol[:, 3 * tt + 2:3 * tt + 3]),
                                reads=[rbk, r_rcol[tt]], writes=[r_ost[io]])
                            gb = half * NT + tt
                            store(vd[cg * 4:(cg + 1) * 4, :, gb, :].rearrange("h p d -> p h d"),
                                  ost[io][:, :].rearrange("p (h d) -> p h d", h=4), r_ost[io], r_scr["vd"])

                    wqv, rwq = load_w(wq[l, :, :].rearrange("(c p) n -> p c n", p=128), [128, 4, 2048])
                    for h in range(H):
                        for n in range(NCK):
                            tsl = slice(n * 512, (n + 1) * 512)
                            gsl = slice(t0 + n * 512, t0 + (n + 1) * 512)
                            for part in range(2):
                                cs = h * 256 + part * 128
                                bk, rbk = nb()
                                for c in range(4):
                                    p.op("pe", lambda bk=bk, c=c, cs=cs, tsl=tsl: E["pe"].matmul(
                                        bk[:, :], wqv[:, c, cs:cs + 128], cqT[:, c, tsl], start=(c == 0), stop=(c == 3)),
                                        reads=[rwq, r_cq[n]], writes=[rbk], ms=(c == 3))
                                io = nxt("o", 4)
                                if part == 0:
                                    p.op("dve", lambda bk=bk, io=io, tsl=tsl: E["dve"].tensor_tensor(
                                        ost[io][:, :], bk[:, :], rq_bc[:, tsl], ALU.mult),
                                        reads=[rbk, r_rq[n]], writes=[r_ost[io]])
                                    store(qnT[h, :, gsl], ost[io][:, :], r_ost[io], r_scr["qnT"])
                                else:
                                    it = nxt("t", 3)
                                    p.op("dve", lambda bk=bk, it=it, tsl=tsl: E["dve"].tensor_tensor(
                                        t32[it][:, :], bk[:, :], rq_bc[:, tsl], ALU.mult),
                                        reads=[rbk, r_rq[n]], writes=[r_t32[it]])
                                    p.op("pool", lambda it=it, io=io, gsl=gsl: E["pool"].tensor_tensor(
                                        ost[io][:, :], t32[it][:, :], t1[:, gsl], ALU.mult),
                                        reads=[r_t32[it]], writes=[r_ost[io]])
                                    bk2, rbk2 = nb()
                                    p.op("pe", lambda bk2=bk2, io=io: E["pe"].matmul(
                                        bk2[:, :], dfold[:, :], ost[io][:, :], start=True, stop=True),
                                        reads=[r_ost[io]], writes=[rbk2])
                                    io2 = nxt("o", 4)
                                    p.op("act", lambda bk2=bk2, io2=io2: E["act"].copy(ost[io2][:, :], bk2[:, :]),
                                         reads=[rbk2], writes=[r_ost[io2]])
                                    store(qrT[h, :, gsl], ost[io2][:, :], r_ost[io2], r_scr["qrT"])

                    i = nxt("w", 2)
                    wkk = wbuf[i][:, 0:2048].rearrange("p (a b) -> p a b", a=2)
                    wkv_ = wbuf[i][:, 2048:4096].rearrange("p (a b) -> p a b", a=2)
                    rwk = r_wb[i]
                    p.dma("pool", wkk, wkvk[l, :, :].rearrange("(c p) n -> p c n", p=128), writes=[rwk])
                    p.dma("pool", wkv_, wkvv[l, :, :].rearrange("(c p) n -> p c n", p=128), writes=[rwk])
                    for h in range(H):
                        for n in range(NCK):
                            tsl = slice(n * 512, (n + 1) * 512)
                            gsl = slice(t0 + n * 512, t0 + (n + 1) * 512)
                            bk, rbk = nb()
                            for c in range(2):
                                p.op("pe", lambda bk=bk, c=c, h=h, tsl=tsl: E["pe"].matmul(
                                    bk[:, :], wkk[:, c, h * 128:(h + 1) * 128], ckvT[:, c, tsl],
                                    start=(c == 0), stop=(c == 1)),
                                    reads=[rwk, r_ckv[n]], writes=[rbk], ms=(c == 1))
                            io = nxt("o", 4)
                            p.op("dve", lambda bk=bk, io=io, tsl=tsl: E["dve"].tensor_tensor(
                                ost[io][:, :], bk[:, :], rkv_bc[:, tsl], ALU.mult),
                                reads=[rbk, r_rkv[n]], writes=[r_ost[io]])
                            store(knT[h, :, gsl], ost[io][:, :], r_ost[io], r_scr["knT"])
                    for cg in range(2):
                        for tt in range(NT):
                            n = tt // 4
                            bk, rbk = nb()
                            for c in range(2):
                                p.op("pe", lambda bk=bk, c=c, tt=tt, cg=cg: E["pe"].matmul(
                                    bk[:, :], ckvT[:, c, tt * 128:(tt + 1) * 128], wkv_[:, c, cg * 512:(cg + 1) * 512],
                                    start=(c == 0), stop=(c == 1)),
                                    reads=[rwk, r_ckv[n]], writes=[rbk], ms=(c == 1))
                            io = nxt("o", 4)
                            p.op("act", lambda bk=bk, io=io, tt=tt: E["act"].activation(
                                ost[io][:, :], bk[:, :], AF.Copy, scale=rkvc[:, 3 * tt + 1:3 * tt + 2]),
                                reads=[rbk, r_rkvc[tt]], writes=[r_ost[io]])
                            gb = half * NT + tt
                            store(vb[cg * 4:(cg + 1) * 4, :, gb, :].rearrange("h p d -> p h d"),
                                  ost[io][:, :].rearrange("p (h d) -> p h d", h=4), r_ost[io], r_scr["vb"])
                    p.barrier()

        def phase_C(l):
            with ExitStack() as sc:
                KT = [sb("KT%d" % i, [128, S], BF, sc) for i in range(2)]
                QT = [sb("QT%d" % i, [128, S], BF, sc) for i in range(2)]
                QR = [sb("QR%d" % i, [128, S], BF, sc) for i in range(2)]
                QB = [sb("QB%d" % i, [128, S], BF, sc) for i in range(2)]
                GT = [sb("GT%d" % i, [128, S], BF, sc) for i in range(2)]
                VX = [sb("VX%d" % i, [128, NB, 129], BF, sc) for i in range(2)]
                YS = [sb("YS%d" % i, [128, S], BF, sc) for i in range(2)]
                ZK = sb("ZK", [128, S], BF, sc)
                r_in = [R(), R()]
                r_ys = [R(), R()]
                r_zk = R()
                NPT = 4
                PT = [sb("PT%d" % i, [128, 2, 256], BF, sc) for i in range(NPT)]
                r_pt = [R() for _ in range(NPT)]
                NF = 6
                fa = [sb("fa%d" % i, [128, 128], F32, sc) for i in range(NF)]
                fo = [sb("fo%d" % i, [128, 128], F32, sc) for i in range(NF)]
                fn_ = [sb("fn%d" % i, [128, 128], BF, sc) for i in range(NF)]
                fs = [sb("fs%d" % i, [128, 8], F32, sc) for i in range(NF)]
                r_f = [R() for _ in range(NF)]
                NS = 3
                SB_ = [ps("pS%d" % i, [128, 512], F32, sc) for i in range(NS)]
                r_S = [R() for _ in range(NS)]
                OB = [ps("pO%d" % i, [128, 512], F32, sc) for i in range(4)]
                r_O = [R() for _ in range(4)]
                TP = ps("pT", [128, 1024], BF, sc)
                r_T = R()
                ctr = {"s": 0, "p": 0, "o": 0, "f": 0, "t": 0}
                LOOK = 2

                def nxt(k, n):
                    i = ctr[k]
                    ctr[k] = (i + 1) % n
                    return i

                for i in range(2):
                    p.op("pool", lambda i=i: E["pool"].memset(VX[i][:, :, 128:129], 1.0), writes=[r_in[i]])
                    p.op("pool", lambda i=i: E["pool"].memset(QT[i][64:128, :], 0.0), writes=[r_in[i]])
                    p.op("pool", lambda i=i: E["pool"].memset(QB[i][0:64, :], 0.0), writes=[r_in[i]])
                    if "c_ng" in dbg:
                        p.op("pool", lambda i=i: E["pool"].memset(YS[i][:, :], 0.0), writes=[r_ys[i]])
                p.dma("sp", ZK[:, :], zkT[:, :], reads=[r_scr["zkT"]], writes=[r_zk])

                heads = [("d", h) for h in range(H)] + [("m", h) for h in range(H)]
                if "c_heads" in dbg:
                    heads = [heads[i] for i in dbg["c_heads"]]
                n_groups = dbg.get("c_ng", NB // 2)

                def load_head(hi):
                    kind, h = heads[hi]
                    b = hi % 2
                    rr = r_in[b]
                    if kind == "d":
                        p.dma("sp", KT[b][:, :], kdT[h, :, :], reads=[r_scr["kdT"]], writes=[rr])
                        p.dma("sp", QT[b][0:64, :], qdT[h, 0:64, :], reads=[r_scr["qdT"]], writes=[rr])
                        p.dma("sp", QB[b][64:128, :], qdT[h, 64:128, :], reads=[r_scr["qdT"]], writes=[rr])
                        p.dma("sp", VX[b][:, :, 0:128], vd[h, :, :, :], reads=[r_scr["vd"]], writes=[rr])
                        p.dma("sp", GT[b][:, :], gT[h, :, :], reads=[r_scr["gT"]], writes=[rr])
                    else:
                        p.dma("sp", KT[b][:, :], knT[h, :, :], reads=[r_scr["knT"]], writes=[rr])
                        p.dma("sp", QT[b][:, :], qnT[h, :, :], reads=[r_scr["qnT"]], writes=[rr])
                        p.dma("sp", QR[b][:, :], qrT[h, :, :], reads=[r_scr["qrT"]], writes=[rr])
                        p.dma("sp", VX[b][:, :, 0:128], vb[h, :, :, :], reads=[r_scr["vb"]], writes=[rr])
                        p.dma("sp", GT[b][:, :], gT[8 + h, :, :], reads=[r_scr["gT"]], writes=[rr])

                def finalize_gen(hi, lb, ob, rob):
                    kind, h = heads[hi]
                    b = hi % 2
                    f = nxt("f", NF)
                    rf = r_f[f]
                    qs = slice(lb * 128, (lb + 1) * 128)
                    if kind == "d":
                        p.op("dve", lambda: E["dve"].reciprocal(fs[f][:, 0:1], ob[:, 128:129]),
                             reads=[rob], writes=[rf])
                        p.op("dve", lambda: E["dve"].reciprocal(fs[f][:, 1:2], ob[:, 384:385]),
                             reads=[rob], writes=[rf])
                        p.op("dve", lambda: E["dve"].tensor_tensor(
                            fs[f][:, 2:3], fs[f][:, 1:2], lam[:, 4 * l + 1:4 * l + 2], ALU.mult),
                            reads=[rf], writes=[rf])
                        p.op("dve", lambda: E["dve"].tensor_scalar(
                            fa[f][:, :], ob[:, 0:128], fs[f][:, 0:1], None, ALU.mult),
                            reads=[rob, rf], writes=[rf])
                        p.op("dve", lambda: E["dve"].tensor_scalar(
                            fo[f][:, :], ob[:, 256:384], fs[f][:, 2:3], None, ALU.mult),
                            reads=[rob, rf], writes=[rf])
                        p.op("dve", lambda: E["dve"].tensor_tensor(
                            fo[f][:, :], fo[f][:, :], fa[f][:, :], ALU.add),
                            reads=[rf], writes=[rf])
                        p.op("dve", lambda: E["dve"].tensor_tensor(
                            fa[f][:, :], fo[f][:, :], fo[f][:, :], ALU.mult),
                            reads=[rf], writes=[rf])
                        p.op("dve", lambda: E["dve"].tensor_reduce(fs[f][:, 3:4], fa[f][:, :], AX.X, ALU.add),
                             reads=[rf], writes=[rf])
                        yield
                        yield
                        p.op("act", lambda: E["act"].activation(
                            fs[f][:, 4:5], fs[f][:, 3:4], AF.Sqrt, bias=epsb[:, 0:1], scale=1.0 / 128),
                            reads=[rf], writes=[rf])
                        yield
                        yield
                        p.op("dve", lambda: E["dve"].reciprocal(fs[f][:, 5:6], fs[f][:, 4:5]),
                             reads=[rf], writes=[rf])
                        p.op("dve", lambda: E["dve"].tensor_scalar(
                            fa[f][:, :], fo[f][:, :], fs[f][:, 5:6], None, ALU.mult),
                            reads=[rf], writes=[rf])
                        p.op("dve", lambda: E["dve"].tensor_tensor(
                            fn_[f][:, :], fa[f][:, :], gsub[:, l * 128:(l + 1) * 128], ALU.mult),
                            reads=[rf], writes=[rf])
                    else:
                        p.op("dve", lambda: E["dve"].reciprocal(fs[f][:, 0:1], ob[:, 128:129]),
                             reads=[rob], writes=[rf])
                        p.op("dve", lambda: E["dve"].tensor_scalar(
                            fn_[f][:, :], ob[:, 0:128], fs[f][:, 0:1], None, ALU.mult),
                            reads=[rob, rf], writes=[rf])
                    yield
                    yield
                    ts_ = nxt("t", 8)
                    p.op("pe", lambda: E["pe"].transpose(TP[:, ts_ * 128:(ts_ + 1) * 128], fn_[f][:, :], identb[:, :]),
                         reads=[rf], writes=[r_T])
                    yield
                    yield
                    p.op("dve", lambda: E["dve"].tensor_tensor(
                        YS[b][:, qs], TP[:, ts_ * 128:(ts_ + 1) * 128], GT[b][:, qs], ALU.mult),
                        reads=[r_T, r_in[b]], writes=[r_ys[b]])

                pending = []

                def advance(flush=False):
                    while True:
                        for g in list(pending):
                            try:
                                next(g)
                            except StopIteration:
                                pending.remove(g)
                        if not flush or not pending:
                            break

                steps = []
                for hi in range(len(heads)):
                    for G in range(n_groups):
                        lbs = (2 * G, 2 * G + 1)
                        grp = {}
                        for t in range(lbs[1] + 1):
                            steps.append(dict(hi=hi, G=G, t=t, lbs=lbs, grp=grp,
                                              first=(G == 0 and t == 0),
                                              last=(G == n_groups - 1 and t == lbs[1])))

                def emit_S(st):
                    hi, t, lbs = st["hi"], st["t"], st["lbs"]
                    kind, h = heads[hi]
                    b = hi % 2
                    act_l = [i for i, lb in enumerate(lbs) if lb >= t]
                    q0 = lbs[act_l[0]] * 128
                    N = len(act_l) * 128
                    ks = slice(t * 128, (t + 1) * 128)
                    isb = nxt("s", NS)
                    sbk, rsb = SB_[isb], r_S[isb]
                    st.update(act_l=act_l, N=N, sbk=sbk, rsb=rsb)
                    if kind == "d":
                        for m in range(2):
                            qsrc = QT[b] if m == 0 else QB[b]
                            p.op("pe", lambda m=m, qsrc=qsrc: E["pe"].matmul(
                                sbk[:, m * 256:m * 256 + N], KT[b][:, ks], qsrc[:, q0:q0 + N], start=True, stop=True),
                                reads=[r_in[b]], writes=[rsb], ms=(m == 1))
                    else:
                        p.op("pe", lambda: E["pe"].matmul(
                            sbk[:, 0:N], KT[b][:, ks], QT[b][:, q0:q0 + N], start=True, stop=False),
                            reads=[r_in[b]], writes=[rsb], ms=False)
                        p.op("pe", lambda: E["pe"].matmul(
                            sbk[:, 0:N], ZK[:, ks], QR[b][:, q0:q0 + N], start=False, stop=True),
                            reads=[r_in[b], r_zk], writes=[rsb])

                def emit_rest(st):
                    hi, t, lbs, grp = st["hi"], st["t"], st["lbs"], st["grp"]
                    kind, h = heads[hi]
                    b = hi % 2
                    nm = 2 if kind == "d" else 1
                    scale = SC_DIFF if kind == "d" else SC_MLA
                    act_l, N, sbk, rsb = st["act_l"], st["N"], st["sbk"], st["rsb"]
                    if t == 0:
                        grp["ob"] = []
                        for lb in lbs:
                            io = nxt("o", 4)
                            grp["ob"].append((OB[io], r_O[io]))
                    ip = nxt("p", NPT)
                    pt, rpt = PT[ip], r_pt[ip]
                    sv = sbk[:, :].rearrange("p (m n) -> p m n", m=2)
                    p.op("act", lambda: E["act"].activation(
                        pt[:, 0:nm, 0:N], sv[:, 0:nm, 0:N], AF.Exp, scale=scale),
                        reads=[rsb], writes=[rpt])
                    for i in act_l:
                        lb = lbs[i]
                        co = (i - act_l[0]) * 128
                        if kind == "d" and (t == lb or t == lb - 1):
                            kd = 0 if t == lb else 1
                            p.op("dve", lambda co=co, kd=kd: E["dve"].tensor_tensor(
                                pt[:, :, co:co + 128], pt[:, :, co:co + 128],
                                eb[:, 2 * h:2 * h + 2, kd, :], ALU.mult),
                                reads=[rpt], writes=[rpt])
                        elif kind == "m" and t == lb:
                            p.op("pool", lambda co=co: E["pool"].memset(pt[64:128, 0, co:co + 64], 0.0),
                                 reads=[rpt], writes=[rpt])
                    advance()
                    for i in act_l:
                        lb = lbs[i]
                        co = (i - act_l[0]) * 128
                        ob, rob = grp["ob"][i]
                        for m in range(nm):
                            p.op("pe", lambda co=co, ob=ob, m=m, lb=lb: E["pe"].matmul(
                                ob[:, m * 256:m * 256 + 129], pt[:, m, co:co + 128], VX[b][:, t, :],
                                start=(t == 0 and m == 0), stop=(t == lb), skip_group_check=True),
                                reads=[rpt, r_in[b]], writes=[rob], ms=(m == nm - 1))
                        if t == lb:
                            pending.append(finalize_gen(hi, lb, ob, rob))

                load_head(0)
                for i in range(min(LOOK, len(steps))):
                    emit_S(steps[i])
                for i, st in enumerate(steps):
                    if st["first"] and st["hi"] + 1 < len(heads):
                        load_head(st["hi"] + 1)
                    if i + LOOK < len(steps):
                        emit_S(steps[i + LOOK])
                    emit_rest(st)
                    if st["last"]:
                        advance(flush=True)
                        kind, h = heads[st["hi"]]
                        b = st["hi"] % 2
                        tile_idx = h if kind == "d" else 8 + h
                        p.dma("sp", yT[tile_idx, :, :], YS[b][:, :], reads=[r_ys[b]], writes=[r_scr["yT"]])
                p.barrier()

        def phase_D(l, xsrc, last):
            with ExitStack() as sd:
                wo_sb = sb("wo_sb", [128, 16, D], BF, sd); r_wo = R()
                ys = [sb("ysD%d" % i, [128, 16, 512], BF, sd) for i in range(2)]
                r_y = [R(), R()]
                xs = [sb("xsD%d" % i, [128, D], F32, sd) for i in range(3)]
                r_x = [R() for _ in range(3)]
                junk = sb("junkD", [128, D], BF, sd); r_junk = R()
                gfin = sb("gfin", [128, D], F32, sd); r_g = R()
                fsd = sb("fsd", [128, 3 * NB], F32, sd); r_fs = [R() for _ in range(NB)]
                bank = [ps("pd%d" % i, [128, 512], F32, sd) for i in range(8)]
                r_bank = [R() for _ in range(8)]
                bi = [0]
                for cg in range(4):
                    p.dma("pool", wo_sb[:, :, cg * 512:(cg + 1) * 512],
                          wo[l, :, cg * 512:(cg + 1) * 512].rearrange("(c p) n -> p c n", p=128), writes=[r_wo])
                if last:
                    p.dma("sp", gfin[:, :], gfin_in[:, :], writes=[r_g])
                for tt in range(NB):
                    n = tt // 4
                    yb = n % 2
                    if tt % 4 == 0:
                        p.dma("sp", ys[yb][:, :, :], yT[:, :, n * 512:(n + 1) * 512].rearrange("c p n -> p c n"),
                              reads=[r_scr["yT"]], writes=[r_y[yb]])
                    xb = tt % 3
                    p.dma("sp", xs[xb][:, :], xsrc[tt * 128:(tt + 1) * 128, :],
                          reads=([r_scr["x1"]] if l > 0 else []), writes=[r_x[xb]])
                    for dg in range(4):
                        i = bi[0]
                        bi[0] = (i + 1) % 8
                        bk, rbk = bank[i], r_bank[i]
                        for c in range(16):
                            p.op("pe", lambda bk=bk, c=c, yb=yb, tt=tt, dg=dg: E["pe"].matmul(
                                bk[:, :], ys[yb][:, c, (tt % 4) * 128:(tt % 4 + 1) * 128],
                                wo_sb[:, c, dg * 512:(dg + 1) * 512], start=(c == 0), stop=(c == 15)),
                                reads=[r_y[yb], r_wo], writes=[rbk], ms=(c == 15))
                        p.op("dve", lambda bk=bk, xb=xb, dg=dg: E["dve"].tensor_tensor(
                            xs[xb][:, dg * 512:(dg + 1) * 512], bk[:, :], xs[xb][:, dg * 512:(dg + 1) * 512], ALU.add),
                            reads=[rbk, r_x[xb]], writes=[r_x[xb]])
                    if not last:
                        p.dma("sp", x1[tt * 128:(tt + 1) * 128, :], xs[xb][:, :], reads=[r_x[xb]], writes=[r_scr["x1"]])
                    else:
                        c0 = 3 * tt
                        p.op("act", lambda xb=xb, c0=c0: E["act"].activation(
                            junk[:, :], xs[xb][:, :], AF.Square, accum_out=fsd[:, c0:c0 + 1]),
                            reads=[r_x[xb]], writes=[r_junk, r_fs[tt]])
                        p.op("act", lambda c0=c0: E["act"].activation(
                            fsd[:, c0 + 1:c0 + 2], fsd[:, c0:c0 + 1], AF.Sqrt, bias=epsb[:, 0:1], scale=1.0 / D),
                            reads=[r_fs[tt]], writes=[r_fs[tt]])
                        p.op("dve", lambda c0=c0: E["dve"].reciprocal(fsd[:, c0 + 2:c0 + 3], fsd[:, c0 + 1:c0 + 2]),
                             reads=[r_fs[tt]], writes=[r_fs[tt]])
                        p.op("act", lambda xb=xb, c0=c0: E["act"].activation(
                            xs[xb][:, :], xs[xb][:, :], AF.Copy, scale=fsd[:, c0 + 2:c0 + 3]),
                            reads=[r_x[xb], r_fs[tt]], writes=[r_x[xb]])
                        p.op("dve", lambda xb=xb: E["dve"].tensor_tensor(
                            xs[xb][:, :], xs[xb][:, :], gfin[:, :], ALU.mult),
                            reads=[r_x[xb], r_g], writes=[r_x[xb]])
                        p.dma("sp", out_d[tt * 128:(tt + 1) * 128, :], xs[xb][:, :], reads=[r_x[xb]], writes=[r_scr["x1"]])
                p.barrier()

        done = False
        for l in range(DEPTH):
            xsrc = x_in if l == 0 else x1
            for half in range(S // HALF):
                if not dbg.get("skipA"):
                    phase_A(l, half, xsrc)
            if stop_after == ("A", l):
                done = True
                break
            phase_C(l)
            if stop_after == ("C", l):
                done = True
                break
            phase_D(l, xsrc, last=(l == DEPTH - 1))
            if stop_after == ("D", l):
                done = True
                break
        p.barrier()
        build_program.ninst = p.ninst
    return nc


def _rel_bucket_np(rel):
    nb = 16
    max_exact = 8
    ret = (rel > 0).astype(np.int32) * nb
    n = np.abs(rel)
    nf = np.maximum(n, 1).astype(np.float32)
    large = max_exact + (np.log(nf / max_exact) / math.log(128 / max_exact) * (nb - max_exact)).astype(np.int32)
    large = np.minimum(large, nb - 1)
    return ret + np.where(n < max_exact, n, large)


def prepare_inputs(x, norm_g, w_in, diff_lambda, diff_subln_g, mla_q_norm_g, mla_w_q_b,
                   mla_kv_norm_g, mla_w_kv_b, w_out, rel_bias, final_norm_g):
    f = np.float32
    x = np.asarray(x, f)
    w_in = np.asarray(w_in, f)
    kr0 = 3840
    swap = np.concatenate([np.arange(32, 64), np.arange(0, 32)])
    colsF = np.concatenate([np.arange(0, 2048), np.arange(3072, 3840), kr0 + np.arange(64), kr0 + swap,
                            np.arange(3904, 5952)])
    assert colsF.size == NFT * 128
    wF = np.ascontiguousarray(w_in[:, :, colsF])
    wT = np.ascontiguousarray(w_in[:, :, 2048:3072])
    wqb = np.asarray(mla_w_q_b, f)
    cq_cols = []
    for h in range(H):
        b0 = h * 192
        cq_cols += [b0 + np.arange(128), b0 + 128 + np.arange(64), b0 + 128 + swap]
    wq = np.ascontiguousarray(wqb[:, :, np.concatenate(cq_cols)])
    wkvb = np.asarray(mla_w_kv_b, f)
    kc = np.concatenate([h * 256 + np.arange(128) for h in range(H)])
    vc = np.concatenate([h * 256 + 128 + np.arange(128) for h in range(H)])
    wkvk = np.ascontiguousarray(wkvb[:, :, kc])
    wkvv = np.ascontiguousarray(wkvb[:, :, vc])
    wo = np.ascontiguousarray(np.asarray(w_out, f))

    def colmajor(v, nchunk):
        v = np.asarray(v, f).reshape(DEPTH, nchunk, 128)
        return np.ascontiguousarray(v.transpose(2, 0, 1).reshape(128, DEPTH * nchunk))

    def bcast(v):
        v = np.asarray(v, f).reshape(1, -1)
        return np.ascontiguousarray(np.broadcast_to(v, (128, v.shape[1])))

    rb = np.asarray(rel_bias, f)
    k = np.arange(128)[:, None]
    q = np.arange(128)[None, :]
    idx = np.stack([_rel_bucket_np(k - q), _rel_bucket_np(k - 128 - q)], 0)
    bt = rb[idx]
    bt = np.ascontiguousarray(bt.transpose(1, 3, 0, 2).reshape(128, 16 * 2 * 128))
    cfar = bcast(rb[15, :])
    pos = np.arange(S, dtype=np.float32)
    inv_freq = (10000.0 ** (-np.arange(0, 64, 2, dtype=np.float32) / 64)).astype(np.float32)
    ang = pos[:, None] * inv_freq[None, :]
    cos, sin = np.cos(ang).astype(f).T, np.sin(ang).astype(f).T
    t1 = np.ascontiguousarray(np.concatenate([cos, cos, -sin, sin], 0))
    identf = np.eye(128, dtype=f)
    identb = np.eye(128, dtype=f).astype(ml_dtypes.bfloat16)
    dfold = (np.arange(128)[:, None] % 64 == np.arange(128)[None, :] % 64).astype(f).astype(ml_dtypes.bfloat16)
    shared = dict(
        wF=wF, wT=wT, wq=wq, wkvk=wkvk, wkvv=wkvv, wo=wo,
        g_in=colmajor(norm_g, 16), gq=colmajor(mla_q_norm_g, 4), gkv=colmajor(mla_kv_norm_g, 2),
        gsub=bcast(np.asarray(diff_subln_g, f).reshape(-1)), gfin=bcast(final_norm_g),
        dlam=bcast(np.asarray(diff_lambda, f).reshape(-1)), cfar=cfar, bt=bt, t1=t1,
        identf=identf, identb=identb, dfold=dfold)
    in_maps = []
    for c in range(8):
        m = dict(shared)
        m["x"] = np.ascontiguousarray(x[c % 4])
        in_maps.append(m)
    return in_maps


_NC_CACHE = {}


def kernel(**inputs):
    in_maps = prepare_inputs(**inputs)
    if "nc" not in _NC_CACHE:
        _NC_CACHE["nc"] = build_program()
    res = run_bass_kernel_spmd(_NC_CACHE["nc"], in_maps, core_ids=list(range(8)))
    out = np.stack([np.asarray(res.results[c]["out"], dtype=np.float32) for c in range(4)], 0)
    return out
```

```python
import math
from contextlib import ExitStack

import numpy as np
import ml_dtypes

import concourse.bass as bass
import concourse.mybir as mybir
from concourse.bass_utils import run_bass_kernel_spmd

F32 = mybir.dt.float32
BF = mybir.dt.bfloat16
AF = mybir.ActivationFunctionType
ALU = mybir.AluOpType
AX = mybir.AxisListType

D = 2048
S = 4096
NB = S // 128
DEPTH = 2
H = 8
EPS = 1e-6
NFT = 39
HALF = 2048
SC_DIFF = 64 ** -0.5
SC_MLA = 192 ** -0.5
SAME_ENG_SYNC = True


class R:
    __slots__ = ("w", "r", "name")

    def __init__(self, name=""):
        self.w = None
        self.r = {}
        self.name = name


class Prog:
    NDS = 24

    def __init__(self, nc, es):
        self.nc = nc
        self.E = {"pe": nc.tensor, "act": nc.scalar, "dve": nc.vector, "pool": nc.gpsimd, "sp": nc.sync}
        self.csem = {e: es.enter_context(nc.semaphore("c_" + e)) for e in ("pe", "act", "dve", "pool")}
        self.ccnt = {e: 0 for e in self.csem}
        self.dsem = [es.enter_context(nc.semaphore("d%d" % i)) for i in range(self.NDS)]
        self.dcnt = [0] * self.NDS
        self.dnext = 0
        self.waited = {}
        self.ninst = 0

    def _sem(self, tag):
        return self.csem[tag[1]] if tag[0] == "c" else self.dsem[tag[1]]

    def _wait(self, eng, tag):
        key = (eng, tag[0], tag[1])
        if self.waited.get(key, 0) >= tag[2]:
            return
        self.E[eng].wait_ge(self._sem(tag), tag[2])
        self.waited[key] = tag[2]

    def _deps(self, eng, reads, writes):
        deps = {}
        for r in reads:
            if r.w is not None:
                k = (r.w[0], r.w[1])
                deps[k] = max(deps.get(k, 0), r.w[2])
        for w in writes:
            if w.w is not None:
                k = (w.w[0], w.w[1])
                deps[k] = max(deps.get(k, 0), w.w[2])
            for k, v in w.r.items():
                deps[k] = max(deps.get(k, 0), v)
        for k in sorted(deps, key=str):
            if k[0] == "c" and k[1] == eng and (eng == "pe" or not SAME_ENG_SYNC):
                continue
            self._wait(eng, (k[0], k[1], deps[k]))

    def _mark(self, tag, reads, writes):
        k = (tag[0], tag[1])
        for r in reads:
            r.r[k] = max(r.r.get(k, 0), tag[2])
        for w in writes:
            w.w = tag
            w.r = {}

    def op(self, eng, fn, reads=(), writes=(), ms=True):
        self._deps(eng, reads, writes)
        inst = fn()
        self.ninst += 1
        if ms:
            self.ccnt[eng] += 1
            tag = ("c", eng, self.ccnt[eng])
            inst.then_inc(self.csem[eng], 1)
        else:
            tag = ("c", eng, self.ccnt[eng] + 1)
        self._mark(tag, reads, writes)
        return inst

    def dma(self, q, out, in_, reads=(), writes=()):
        i = self.dnext
        self.dnext = (self.dnext + 1) % self.NDS
        if self.dcnt[i] > 0:
            self._wait(q, ("d", i, self.dcnt[i]))
        self._deps(q, reads, writes)
        self.dcnt[i] += 16
        tag = ("d", i, self.dcnt[i])
        self.E[q].dma_start(out=out, in_=in_).then_inc(self.dsem[i], 16)
        self.ninst += 1
        self._mark(tag, reads, writes)

    def barrier(self, engines=("pe", "act", "dve", "pool", "sp")):
        tags = [("c", e, self.ccnt[e]) for e in self.csem if self.ccnt[e] > 0]
        tags += [("d", i, self.dcnt[i]) for i in range(self.NDS) if self.dcnt[i] > 0]
        for e in engines:
            for t in tags:
                if t[0] == "c" and t[1] == e:
                    continue
                self._wait(e, t)


def build_program(dbg=None):
    dbg = dbg or {}
    stop_after = dbg.get("stop", None)
    expose = dbg.get("expose", ())
    nc = bass.Bass("TRN2", target_bir_lowering=False)

    def din(name, shape, dt=F32):
        return nc.dram_tensor(name, list(shape), dt, kind="ExternalInput")

    def dscr(name, shape, dt=BF):
        if name in expose:
            return nc.dram_tensor(name, list(shape), dt, kind="ExternalOutput")
        return nc.dram_tensor(name, list(shape), dt)

    x_in = din("x", [S, D])
    wF = din("wF", [DEPTH, D, NFT * 128])
    wT = din("wT", [DEPTH, D, 1024])
    wq = din("wq", [DEPTH, 512, 2048])
    wkvk = din("wkvk", [DEPTH, 256, 1024])
    wkvv = din("wkvv", [DEPTH, 256, 1024])
    wo = din("wo", [DEPTH, D, D])
    g_in = din("g_in", [128, DEPTH * 16])
    gbc_in = din("gbc", [128, DEPTH * D])
    gq_in = din("gq", [128, DEPTH * 4])
    gkv_in = din("gkv", [128, DEPTH * 2])
    gsub_in = din("gsub", [128, DEPTH * 128])
    gfin_in = din("gfin", [128, D])
    dlam_in = din("dlam", [128, DEPTH * 256])
    cfar_in = din("cfar", [128, 16])
    bt_in = din("bt", [128, 16 * 2 * 128])
    t1_in = din("t1", [128, S])
    identf_in = din("identf", [128, 128])
    identb_in = din("identb", [128, 128], BF)
    dfold_in = din("dfold", [128, 128], BF)
    out_d = nc.dram_tensor("out", [S, D], F32, kind="ExternalOutput")

    qdT = dscr("qdT", [H, 128, S])
    kdT = dscr("kdT", [H, 128, S])
    qnT = dscr("qnT", [H, 128, S])
    qrT = dscr("qrT", [H, 128, S])
    knT = dscr("knT", [H, 128, S])
    zkT = dscr("zkT", [128, S])
    gT = dscr("gT", [16, 128, S])
    yT = dscr("yT", [16, 128, S])
    vd = dscr("vd", [H, 128, NB, 128])
    vb = dscr("vb", [H, 128, NB, 128])
    x1 = dscr("x1", [S, D], F32)

    with ExitStack() as es:
        p = Prog(nc, es)
        E = p.E

        uid = [0]

        def sb(name, shape, dt, stack=es):
            uid[0] += 1
            return stack.enter_context(nc.sbuf_tensor("s%d_%s" % (uid[0], name), list(shape), dt))

        def ps(name, shape, dt, stack):
            uid[0] += 1
            return stack.enter_context(nc.psum_tensor("p%d_%s" % (uid[0], name), list(shape), dt))

        identf = sb("identf", [128, 128], F32); r_const = R("const")
        identb = sb("identb", [128, 128], BF)
        dfold = sb("dfold", [128, 128], BF)
        onesb = sb("onesb", [128, 128], BF)
        onesf = sb("onesf", [128, 128], F32)
        t1 = sb("t1", [128, S], F32)
        eb = sb("eb", [128, 16, 2, 128], F32)
        gin = sb("gin", [128, DEPTH * 16], F32)
        gq = sb("gq", [128, DEPTH * 4], F32)
        gkv = sb("gkv", [128, DEPTH * 2], F32)
        gsub = sb("gsub", [128, DEPTH * 128], F32)
        dlam = sb("dlam", [128, DEPTH * 256], F32)
        cfar = sb("cfar", [128, 16], F32)
        epsb = sb("epsb", [128, 1], F32)
        lam = sb("lam", [128, 8], F32)
        lamtmp = sb("lamtmp", [128, DEPTH * 128], F32)
        for dst, src in ((identf, identf_in), (identb, identb_in), (dfold, dfold_in), (t1, t1_in),
                         (gin, g_in), (gq, gq_in), (gkv, gkv_in), (gsub, gsub_in), (dlam, dlam_in),
                         (cfar, cfar_in)):
            p.dma("sp", dst[:, :], src[:, :], writes=[r_const])
        p.dma("sp", eb[:, :, :, :].rearrange("p a b c -> p (a b c)"), bt_in[:, :], writes=[r_const])
        p.op("dve", lambda: E["dve"].memset(onesb[:, :], 1.0), writes=[r_const])
        p.op("dve", lambda: E["dve"].memset(onesf[:, :], 1.0), writes=[r_const])
        p.op("dve", lambda: E["dve"].memset(epsb[:, :], EPS), writes=[r_const])
        p.op("dve", lambda: E["dve"].tensor_scalar(cfar[:, :], cfar[:, :], -1.0, None, ALU.mult),
             reads=[r_const], writes=[r_const])
        for hm in range(16):
            p.op("act", lambda hm=hm: E["act"].activation(
                eb[:, hm, :, :], eb[:, hm, :, :], AF.Exp, bias=cfar[:, hm:hm + 1], scale=1.0),
                reads=[r_const], writes=[r_const])
        p.op("dve", lambda: E["dve"].memset(eb[64:128, :, 0, 0:64], 0.0), reads=[r_const], writes=[r_const])
        for l in range(DEPTH):
            lam_init = 0.8 - 0.6 * math.exp(-0.3 * l)
            dl = dlam[:, l * 256:(l + 1) * 256].rearrange("p (a b c) -> p a b c", a=2, b=2)
            pr = lamtmp[:, l * 128:(l + 1) * 128].rearrange("p (a c) -> p a c", a=2)
            p.op("dve", lambda dl=dl, pr=pr: E["dve"].tensor_tensor(pr, dl[:, :, 0, :], dl[:, :, 1, :], ALU.mult),
                 reads=[r_const], writes=[r_const])
            p.op("dve", lambda pr=pr, l=l: E["dve"].tensor_reduce(lam[:, 4 * l + 2:4 * l + 4], pr, AX.X, ALU.add),
                 reads=[r_const], writes=[r_const])
            p.op("act", lambda l=l: E["act"].activation(lam[:, 4 * l + 2:4 * l + 4], lam[:, 4 * l + 2:4 * l + 4], AF.Exp),
                 reads=[r_const], writes=[r_const])
            p.op("dve", lambda l=l: E["dve"].tensor_tensor(lam[:, 4 * l:4 * l + 1], lam[:, 4 * l + 2:4 * l + 3],
                                                           lam[:, 4 * l + 3:4 * l + 4], ALU.subtract),
                 reads=[r_const], writes=[r_const])
            p.op("dve", lambda l=l, li=lam_init: E["dve"].tensor_scalar(
                lam[:, 4 * l:4 * l + 1], lam[:, 4 * l:4 * l + 1], li, None, ALU.add),
                reads=[r_const], writes=[r_const])
            p.op("dve", lambda l=l: E["dve"].tensor_scalar(
                lam[:, 4 * l + 1:4 * l + 2], lam[:, 4 * l:4 * l + 1], -1.0, None, ALU.mult),
                reads=[r_const], writes=[r_const])
            p.op("dve", lambda l=l, li=lam_init: E["dve"].tensor_scalar(
                gsub[:, l * 128:(l + 1) * 128], gsub[:, l * 128:(l + 1) * 128], 1.0 - li, None, ALU.mult),
                reads=[r_const], writes=[r_const])
        p.barrier()

        r_scr = {n: R(n) for n in ("qdT", "kdT", "qnT", "qrT", "knT", "zkT", "gT", "yT", "vd", "vb", "x1")}

        def phase_A(l, half, xsrc):
            t0 = half * HALF
            NT = HALF // 128
            NCK = HALF // 512
            with ExitStack() as sa:
                xT = sb("xT", [128, 16, HALF], BF, sa); r_xT = R()
                wbuf = [sb("wbuf%d" % i, [128, 8192], BF, sa) for i in range(2)]
                r_wb = [R(), R()]
                rstd_bc = sb("rstd_bc", [128, HALF], F32, sa); r_rbc = R()
                rcol = sb("rcol", [128, 3 * NT], F32, sa); r_rcol = [R() for _ in range(NT)]
                cqT = sb("cqT", [128, 4, HALF], BF, sa); r_cq = [R() for _ in range(NCK)]
                ckvT = sb("ckvT", [128, 2, HALF], BF, sa); r_ckv = [R() for _ in range(NCK)]
                bank = [ps("pa%d" % i, [128, 512], F32, sa) for i in range(6)]
                r_bank = [R() for _ in range(6)]
                bi = [0]

                def nb():
                    i = bi[0]
                    bi[0] = (i + 1) % 6
                    return bank[i], r_bank[i]

                with ExitStack() as s0:
                    xs = [sb("xs%d" % i, [128, D], F32, s0) for i in range(2)]
                    r_xs = [R(), R()]
                    junk = sb("junk", [128, D], BF, s0); r_junk = R()
                    rb = sb("rb", [128, 128], F32, s0); r_rb = R()
                    g_bc = sb("g_bc", [128, D], F32, s0); r_gbc = R()
                    xg = [sb("xg%d" % i, [128, D], BF, s0) for i in range(2)]
                    r_xg = [R(), R()]
                    tpa = [ps("tpa%d" % i, [128, 1024], BF, s0) for i in range(2)]
                    r_tpa = [R(), R()]
                    tpc = [0]

                    def nxt_tp():
                        i = tpc[0]
                        tpc[0] = (i + 1) % 2
                        return i

                    p.dma("sp", g_bc[:, :], gbc_in[:, l * D:(l + 1) * D], writes=[r_gbc])
                    for tt in range(NT):
                        b = tt % 2
                        p.dma("sp", xs[b][:, :], xsrc[t0 + tt * 128:t0 + (tt + 1) * 128, :], writes=[r_xs[b]])
                        c0 = 3 * tt
                        p.op("act", lambda b=b, c0=c0: E["act"].activation(
                            junk[:, :], xs[b][:, :], AF.Square, accum_out=rcol[:, c0:c0 + 1]),
                            reads=[r_xs[b]], writes=[r_junk, r_rcol[tt]])
                        p.op("act", lambda c0=c0: E["act"].activation(
                            rcol[:, c0 + 1:c0 + 2], rcol[:, c0:c0 + 1], AF.Sqrt, bias=epsb[:, 0:1], scale=1.0 / D),
                            reads=[r_rcol[tt]], writes=[r_rcol[tt]])
                        p.op("dve", lambda c0=c0: E["dve"].reciprocal(rcol[:, c0 + 2:c0 + 3], rcol[:, c0 + 1:c0 + 2]),
                             reads=[r_rcol[tt]], writes=[r_rcol[tt]])
                        p.op("dve", lambda c0=c0: E["dve"].tensor_scalar(
                            rb[:, :], onesf[:, :], rcol[:, c0 + 2:c0 + 3], None, ALU.mult),
                            reads=[r_rcol[tt]], writes=[r_rb])
                        bk, rbk = nb()
                        p.op("pe", lambda bk=bk: E["pe"].transpose(bk[:, 0:128], rb[:, :], identf[:, :]),
                             reads=[r_rb], writes=[rbk])
                        p.op("act", lambda bk=bk, tt=tt: E["act"].copy(rstd_bc[:, tt * 128:(tt + 1) * 128], bk[:, 0:128]),
                             reads=[rbk], writes=[r_rbc])
                        xb_ = tt % 2
                        p.op("dve", lambda b=b, xb_=xb_: E["dve"].tensor_tensor(
                            xg[xb_][:, :], xs[b][:, :], g_bc[:, :], ALU.mult),
                            reads=[r_xs[b], r_gbc], writes=[r_xg[xb_]])
                        for hb in range(2):
                            tk = nxt_tp()
                            for j in range(8):
                                c = hb * 8 + j
                                p.op("pe", lambda tk=tk, j=j, c=c, xb_=xb_: E["pe"].transpose(
                                    tpa[tk][:, j * 128:(j + 1) * 128], xg[xb_][:, c * 128:(c + 1) * 128], identb[:, :]),
                                    reads=[r_xg[xb_]], writes=[r_tpa[tk]], ms=(j == 7))
                            src = tpa[tk][:, :].rearrange("p (c n) -> p c n", c=8)
                            dst = xT[:, hb * 8:(hb + 1) * 8, tt * 128:(tt + 1) * 128]
                            if hb == 0:
                                p.op("act", lambda src=src, dst=dst: E["act"].copy(dst, src),
                                     reads=[r_tpa[tk]], writes=[R()])
                            else:
                                p.op("dve", lambda src=src, dst=dst: E["dve"].tensor_copy(dst, src),
                                     reads=[r_tpa[tk]], writes=[R()])
                    p.barrier()

                with ExitStack() as s1:
                    t32 = [sb("t32_%d" % i, [128, 512], F32, s1) for i in range(3)]
                    r_t32 = [R() for _ in range(3)]
                    ost = [sb("ost%d" % i, [128, 512], BF, s1) for i in range(4)]
                    r_ost = [R() for _ in range(4)]
                    sqb = [sb("sqb%d" % i, [128, 4, 512], BF, s1) for i in range(2)]
                    r_sqb = [R(), R()]
                    rq_bc = sb("rq_bc", [128, HALF], F32, s1); r_rq = [R() for _ in range(NCK)]
                    rkv_bc = sb("rkv_bc", [128, HALF], F32, s1); r_rkv = [R() for _ in range(NCK)]
                    rkvc = sb("rkvc", [128, 3 * NT], F32, s1); r_rkvc = [R() for _ in range(NT)]
                    sqkv = [sb("sqkv%d" % i, [128, 2, 512], BF, s1) for i in range(2)]
                    r_sqkv = [R(), R()]
                    ctr = {"t": 0, "o": 0, "s": 0, "w": 0}

                    def nxt(k, n):
                        i = ctr[k]
                        ctr[k] = (i + 1) % n
                        return i

                    def load_w(src_ap, view_shape):
                        i = nxt("w", 2)
                        n = 1
                        for d_ in view_shape[1:]:
                            n *= d_
                        flat = wbuf[i][:, 0:n]
                        if len(view_shape) == 3:
                            v = flat.rearrange("p (a b) -> p a b", a=view_shape[1])
                        else:
                            v = flat
                        p.dma("pool", v, src_ap, writes=[r_wb[i]])
                        return v, r_wb[i]

                    def store(dst_ap, src_ap, rsrc, rdst):
                        p.dma("sp", dst_ap, src_ap, reads=[rsrc], writes=[rdst])

                    groups = [list(range(g, min(g + 4, NFT))) for g in range(0, NFT, 4)]
                    for grp in groups:
                        c0 = grp[0] * 128
                        ncol = len(grp) * 128
                        wv, rw = load_w(
                            wF[l, :, c0:c0 + ncol].rearrange("(c p) n -> p c n", p=128), [128, 16, ncol])
                        for n in range(NCK):
                            tsl = slice(n * 512, (n + 1) * 512)
                            gsl = slice(t0 + n * 512, t0 + (n + 1) * 512)
                            for jj, ft in enumerate(grp):
                                bk, rbk = nb()
                                for c in range(16):
                                    p.op("pe", lambda bk=bk, c=c, jj=jj, tsl=tsl: E["pe"].matmul(
                                        bk[:, :], wv[:, c, jj * 128:(jj + 1) * 128], xT[:, c, tsl],
                                        start=(c == 0), stop=(c == 15)),
                                        reads=[rw, r_xT], writes=[rbk], ms=(c == 15))
                                if ft < 16 or ft == 22:
                                    io = nxt("o", 4)
                                    if ft == 22:
                                        it = nxt("t", 3)
                                        p.op("dve", lambda bk=bk, it=it, tsl=tsl: E["dve"].tensor_tensor(
                                            t32[it][:, :], bk[:, :], rstd_bc[:, tsl], ALU.mult),
                                            reads=[rbk, r_rbc], writes=[r_t32[it]])
                                        p.op("dve", lambda it=it, io=io, gsl=gsl: E["dve"].tensor_tensor(
                                            ost[io][:, :], t32[it][:, :], t1[:, gsl], ALU.mult),
                                            reads=[r_t32[it]], writes=[r_ost[io]])
                                        store(zkT[:, gsl], ost[io][:, :], r_ost[io], r_scr["zkT"])
                                    else:
                                        p.op("dve", lambda bk=bk, io=io, tsl=tsl: E["dve"].tensor_tensor(
                                            ost[io][:, :], bk[:, :], rstd_bc[:, tsl], ALU.mult),
                                            reads=[rbk, r_rbc], writes=[r_ost[io]])
                                        if ft < 8:
                                            store(qdT[ft, :, gsl], ost[io][:, :], r_ost[io], r_scr["qdT"])
                                        else:
                                            store(kdT[ft - 8, :, gsl], ost[io][:, :], r_ost[io], r_scr["kdT"])
                                elif ft >= 23:
                                    it = nxt("t", 3)
                                    io = nxt("o", 4)
                                    p.op("dve", lambda bk=bk, it=it, tsl=tsl: E["dve"].tensor_tensor(
                                        t32[it][:, :], bk[:, :], rstd_bc[:, tsl], ALU.mult),
                                        reads=[rbk, r_rbc], writes=[r_t32[it]])
                                    p.op("act", lambda it=it, io=io: E["act"].activation(
                                        ost[io][:, :], t32[it][:, :], AF.Silu),
                                        reads=[r_t32[it]], writes=[r_ost[io]])
                                    store(gT[ft - 23, :, gsl], ost[io][:, :], r_ost[io], r_scr["gT"])
                                else:
                                    it = nxt("t", 3)
                                    p.op("dve", lambda bk=bk, it=it, tsl=tsl: E["dve"].tensor_tensor(
                                        t32[it][:, :], bk[:, :], rstd_bc[:, tsl], ALU.mult),
                                        reads=[rbk, r_rbc], writes=[r_t32[it]])
                                    if ft < 20:
                                        j = ft - 16
                                        isq = n % 2
                                        p.op("act", lambda it=it, isq=isq, j=j: E["act"].activation(
                                            sqb[isq][:, j, :], t32[it][:, :], AF.Square),
                                            reads=[r_t32[it]], writes=[r_sqb[isq]])
                                        p.op("act", lambda it=it, j=j, tsl=tsl: E["act"].activation(
                                            cqT[:, j, tsl], t32[it][:, :], AF.Copy, scale=gq[:, l * 4 + j:l * 4 + j + 1]),
                                            reads=[r_t32[it]], writes=[r_cq[n]])
                                    else:
                                        j = ft - 20
                                        p.op("act", lambda it=it, j=j, n=n: E["act"].activation(
                                            sqkv[n % 2][:, j, :], t32[it][:, :], AF.Square),
                                            reads=[r_t32[it]], writes=[r_sqkv[n % 2]])
                                        p.op("act", lambda it=it, j=j, tsl=tsl: E["act"].activation(
                                            ckvT[:, j, tsl], t32[it][:, :], AF.Copy, scale=gkv[:, l * 2 + j:l * 2 + j + 1]),
                                            reads=[r_t32[it]], writes=[r_ckv[n]])
                            if grp[0] == 16:
                                isq = n % 2
                                bk, rbk = nb()
                                for j in range(4):
                                    p.op("pe", lambda bk=bk, j=j, isq=isq: E["pe"].matmul(
                                        bk[:, :], onesb[:, :], sqb[isq][:, j, :], start=(j == 0), stop=(j == 3)),
                                        reads=[r_sqb[isq]], writes=[rbk], ms=(j == 3))
                                p.op("act", lambda bk=bk, tsl=tsl: E["act"].activation(
                                    rq_bc[:, tsl], bk[:, :], AF.Sqrt, bias=epsb[:, 0:1], scale=1.0 / 512),
                                    reads=[rbk], writes=[r_rq[n]])
                                p.op("dve", lambda tsl=tsl: E["dve"].reciprocal(rq_bc[:, tsl], rq_bc[:, tsl]),
                                     reads=[r_rq[n]], writes=[r_rq[n]])
                            if grp[0] == 20:
                                isq = n % 2
                                bk, rbk = nb()
                                for j in range(2):
                                    p.op("pe", lambda bk=bk, j=j, isq=isq: E["pe"].matmul(
                                        bk[:, :], onesb[:, :], sqkv[isq][:, j, :], start=(j == 0), stop=(j == 1)),
                                        reads=[r_sqkv[isq]], writes=[rbk], ms=(j == 1))
                                p.op("act", lambda bk=bk, tsl=tsl: E["act"].activation(
                                    rkv_bc[:, tsl], bk[:, :], AF.Sqrt, bias=epsb[:, 0:1], scale=1.0 / 256),
                                    reads=[rbk], writes=[r_rkv[n]])
                                p.op("dve", lambda tsl=tsl: E["dve"].reciprocal(rkv_bc[:, tsl], rkv_bc[:, tsl]),
                                     reads=[r_rkv[n]], writes=[r_rkv[n]])
                                for q4 in range(4):
                                    tt = n * 4 + q4
                                    bk, rbk = nb()
                                    for j in range(2):
                                        p.op("pe", lambda bk=bk, j=j, q4=q4, isq=isq: E["pe"].matmul(
                                            bk[:, 0:1], sqkv[isq][:, j, q4 * 128:(q4 + 1) * 128], onesb[:, 0:1],
                                            start=(j == 0), stop=(j == 1)),
                                            reads=[r_sqkv[isq]], writes=[rbk], ms=(j == 1))
                                    c0 = 3 * tt
                                    p.op("act", lambda bk=bk, c0=c0: E["act"].activation(
                                        rkvc[:, c0:c0 + 1], bk[:, 0:1], AF.Sqrt, bias=epsb[:, 0:1], scale=1.0 / 256),
                                        reads=[rbk], writes=[r_rkvc[tt]])
                                    p.op("dve", lambda c0=c0: E["dve"].reciprocal(rkvc[:, c0 + 1:c0 + 2], rkvc[:, c0:c0 + 1]),
                                         reads=[r_rkvc[tt]], writes=[r_rkvc[tt]])

                    for cg in range(2):
                        wv, rw = load_w(
                            wT[l, :, cg * 512:(cg + 1) * 512].rearrange("(c p) n -> p c n", p=128), [128, 16, 512])
                        for tt in range(NT):
                            bk, rbk = nb()
                            for c in range(16):
                                p.op("pe", lambda bk=bk, c=c, tt=tt: E["pe"].matmul(
                                    bk[:, :], xT[:, c, tt * 128:(tt + 1) * 128], wv[:, c, :],
                                    start=(c == 0), stop=(c == 15)),
                                    reads=[rw, r_xT], writes=[rbk], ms=(c == 15))
                            io = nxt("o", 4)
                            p.op("act", lambda bk=bk, io=io, tt=tt: E["act"].activation(
                                ost[io][:, :], bk[:, :], AF.Copy, scale=rcol[:, 3 * tt + 2:3 * tt + 3]),
                                reads=[rbk, r_rcol[tt]], writes=[r_ost[io]])
                            gb = half * NT + tt
                            store(vd[cg * 4:(cg + 1) * 4, :, gb, :].rearrange("h p d -> p h d"),
                                  ost[io][:, :].rearrange("p (h d) -> p h d", h=4), r_ost[io], r_scr["vd"])

                    wqv, rwq = load_w(wq[l, :, :].rearrange("(c p) n -> p c n", p=128), [128, 4, 2048])
                    for h in range(H):
                        for n in range(NCK):
                            tsl = slice(n * 512, (n + 1) * 512)
                            gsl = slice(t0 + n * 512, t0 + (n + 1) * 512)
                            for part in range(2):
                                cs = h * 256 + part * 128
                                bk, rbk = nb()
                                for c in range(4):
                                    p.op("pe", lambda bk=bk, c=c, cs=cs, tsl=tsl: E["pe"].matmul(
                                        bk[:, :], wqv[:, c, cs:cs + 128], cqT[:, c, tsl], start=(c == 0), stop=(c == 3)),
                                        reads=[rwq, r_cq[n]], writes=[rbk], ms=(c == 3))
                                io = nxt("o", 4)
                                if part == 0:
                                    p.op("dve", lambda bk=bk, io=io, tsl=tsl: E["dve"].tensor_tensor(
                                        ost[io][:, :], bk[:, :], rq_bc[:, tsl], ALU.mult),
                                        reads=[rbk, r_rq[n]], writes=[r_ost[io]])
                                    store(qnT[h, :, gsl], ost[io][:, :], r_ost[io], r_scr["qnT"])
                                else:
                                    it = nxt("t", 3)
                                    p.op("dve", lambda bk=bk, it=it, tsl=tsl: E["dve"].tensor_tensor(
                                        t32[it][:, :], bk[:, :], rq_bc[:, tsl], ALU.mult),
                                        reads=[rbk, r_rq[n]], writes=[r_t32[it]])
                                    p.op("dve", lambda it=it, io=io, gsl=gsl: E["dve"].tensor_tensor(
                                        ost[io][:, :], t32[it][:, :], t1[:, gsl], ALU.mult),
                                        reads=[r_t32[it]], writes=[r_ost[io]])
                                    bk2, rbk2 = nb()
                                    p.op("pe", lambda bk2=bk2, io=io: E["pe"].matmul(
                                        bk2[:, :], dfold[:, :], ost[io][:, :], start=True, stop=True),
                                        reads=[r_ost[io]], writes=[rbk2])
                                    io2 = nxt("o", 4)
                                    p.op("act", lambda bk2=bk2, io2=io2: E["act"].copy(ost[io2][:, :], bk2[:, :]),
                                         reads=[rbk2], writes=[r_ost[io2]])
                                    store(qrT[h, :, gsl], ost[io2][:, :], r_ost[io2], r_scr["qrT"])

                    i = nxt("w", 2)
                    wkk = wbuf[i][:, 0:2048].rearrange("p (a b) -> p a b", a=2)
                    wkv_ = wbuf[i][:, 2048:4096].rearrange("p (a b) -> p a b", a=2)
                    rwk = r_wb[i]
                    p.dma("pool", wkk, wkvk[l, :, :].rearrange("(c p) n -> p c n", p=128), writes=[rwk])
                    p.dma("pool", wkv_, wkvv[l, :, :].rearrange("(c p) n -> p c n", p=128), writes=[rwk])
                    for h in range(H):
                        for n in range(NCK):
                            tsl = slice(n * 512, (n + 1) * 512)
                            gsl = slice(t0 + n * 512, t0 + (n + 1) * 512)
                            bk, rbk = nb()
                            for c in range(2):
                                p.op("pe", lambda bk=bk, c=c, h=h, tsl=tsl: E["pe"].matmul(
                                    bk[:, :], wkk[:, c, h * 128:(h + 1) * 128], ckvT[:, c, tsl],
                                    start=(c == 0), stop=(c == 1)),
                                    reads=[rwk, r_ckv[n]], writes=[rbk], ms=(c == 1))
                            io = nxt("o", 4)
                            p.op("dve", lambda bk=bk, io=io, tsl=tsl: E["dve"].tensor_tensor(
                                ost[io][:, :], bk[:, :], rkv_bc[:, tsl], ALU.mult),
                                reads=[rbk, r_rkv[n]], writes=[r_ost[io]])
                            store(knT[h, :, gsl], ost[io][:, :], r_ost[io], r_scr["knT"])
                    for cg in range(2):
                        for tt in range(NT):
                            n = tt // 4
                            bk, rbk = nb()
                            for c in range(2):
                                p.op("pe", lambda bk=bk, c=c, tt=tt, cg=cg: E["pe"].matmul(
                                    bk[:, :], ckvT[:, c, tt * 128:(tt + 1) * 128], wkv_[:, c, cg * 512:(cg + 1) * 512],
                                    start=(c == 0), stop=(c == 1)),
                                    reads=[rwk, r_ckv[n]], writes=[rbk], ms=(c == 1))
                            io = nxt("o", 4)
                            p.op("act", lambda bk=bk, io=io, tt=tt: E["act"].activation(
                                ost[io][:, :], bk[:, :], AF.Copy, scale=rkvc[:, 3 * tt + 1:3 * tt + 2]),
                                reads=[rbk, r_rkvc[tt]], writes=[r_ost[io]])
                            gb = half * NT + tt
                            store(vb[cg * 4:(cg + 1) * 4, :, gb, :].rearrange("h p d -> p h d"),
                                  ost[io][:, :].rearrange("p (h d) -> p h d", h=4), r_ost[io], r_scr["vb"])
                    p.barrier()

        def phase_C(l):
            with ExitStack() as sc:
                KT = [sb("KT%d" % i, [128, S], BF, sc) for i in range(2)]
                QT = [sb("QT%d" % i, [128, S], BF, sc) for i in range(2)]
                QR = [sb("QR%d" % i, [128, S], BF, sc) for i in range(2)]
                QB = [sb("QB%d" % i, [128, S], BF, sc) for i in range(2)]
                GT = [sb("GT%d" % i, [128, S], BF, sc) for i in range(2)]
                VX = [sb("VX%d" % i, [128, NB, 129], BF, sc) for i in range(2)]
                YS = [sb("YS%d" % i, [128, S], BF, sc) for i in range(2)]
                ZK = sb("ZK", [128, S], BF, sc)
                r_in = [R(), R()]
                r_ys = [R(), R()]
                r_zk = R()
                NPT = 4
                PT = [sb("PT%d" % i, [128, 512], BF, sc) for i in range(NPT)]
                r_pt = [R() for _ in range(NPT)]
                NF = 6
                fa = [sb("fa%d" % i, [128, 128], F32, sc) for i in range(NF)]
                fo = [sb("fo%d" % i, [128, 128], F32, sc) for i in range(NF)]
                fn_ = [sb("fn%d" % i, [128, 128], BF, sc) for i in range(NF)]
                fs = [sb("fs%d" % i, [128, 8], F32, sc) for i in range(NF)]
                r_f = [R() for _ in range(NF)]
                NS = 3
                SB_ = [ps("pS%d" % i, [128, 512], F32, sc) for i in range(NS)]
                r_S = [R() for _ in range(NS)]
                OB = [ps("pO%d" % i, [128, 512], F32, sc) for i in range(4)]
                r_O = [R() for _ in range(4)]
                TP = ps("pT", [128, 1024], BF, sc)
                r_T = R()
                ctr = {"s": 0, "p": 0, "o": 0, "f": 0, "t": 0}
                LOOK = 2

                def nxt(k, n):
                    i = ctr[k]
                    ctr[k] = (i + 1) % n
                    return i

                for i in range(2):
                    p.op("pool", lambda i=i: E["pool"].memset(VX[i][:, :, 128:129], 1.0), writes=[r_in[i]])
                    p.op("pool", lambda i=i: E["pool"].memset(QT[i][64:128, :], 0.0), writes=[r_in[i]])
                    p.op("pool", lambda i=i: E["pool"].memset(QB[i][0:64, :], 0.0), writes=[r_in[i]])
                    if "c_ng" in dbg:
                        p.op("pool", lambda i=i: E["pool"].memset(YS[i][:, :], 0.0), writes=[r_ys[i]])
                p.dma("sp", ZK[:, :], zkT[:, :], reads=[r_scr["zkT"]], writes=[r_zk])

                heads = [("d", h) for h in range(H)] + [("m", h) for h in range(H)]
                if "c_heads" in dbg:
                    heads = [heads[i] for i in dbg["c_heads"]]
                n_groups = dbg.get("c_ng", NB // 2)

                def load_head(hi):
                    kind, h = heads[hi]
                    b = hi % 2
                    rr = r_in[b]
                    if kind == "d":
                        p.dma("sp", KT[b][:, :], kdT[h, :, :], reads=[r_scr["kdT"]], writes=[rr])
                        p.dma("sp", QT[b][0:64, :], qdT[h, 0:64, :], reads=[r_scr["qdT"]], writes=[rr])
                        p.dma("sp", QB[b][64:128, :], qdT[h, 64:128, :], reads=[r_scr["qdT"]], writes=[rr])
                        p.dma("sp", VX[b][:, :, 0:128], vd[h, :, :, :], reads=[r_scr["vd"]], writes=[rr])
                        p.dma("sp", GT[b][:, :], gT[h, :, :], reads=[r_scr["gT"]], writes=[rr])
                    else:
                        p.dma("sp", KT[b][:, :], knT[h, :, :], reads=[r_scr["knT"]], writes=[rr])
                        p.dma("sp", QT[b][:, :], qnT[h, :, :], reads=[r_scr["qnT"]], writes=[rr])
                        p.dma("sp", QR[b][:, :], qrT[h, :, :], reads=[r_scr["qrT"]], writes=[rr])
                        p.dma("sp", VX[b][:, :, 0:128], vb[h, :, :, :], reads=[r_scr["vb"]], writes=[rr])
                        p.dma("sp", GT[b][:, :], gT[8 + h, :, :], reads=[r_scr["gT"]], writes=[rr])

                def finalize_gen(hi, lb, ob, rob):
                    kind, h = heads[hi]
                    b = hi % 2
                    f = nxt("f", NF)
                    rf = r_f[f]
                    qs = slice(lb * 128, (lb + 1) * 128)
                    if kind == "d":
                        p.op("dve", lambda: E["dve"].reciprocal(fs[f][:, 0:1], ob[:, 128:129]),
                             reads=[rob], writes=[rf])
                        p.op("dve", lambda: E["dve"].reciprocal(fs[f][:, 1:2], ob[:, 384:385]),
                             reads=[rob], writes=[rf])
                        p.op("dve", lambda: E["dve"].tensor_tensor(
                            fs[f][:, 2:3], fs[f][:, 1:2], lam[:, 4 * l + 1:4 * l + 2], ALU.mult),
                            reads=[rf], writes=[rf])
                        p.op("dve", lambda: E["dve"].tensor_scalar(
                            fa[f][:, :], ob[:, 0:128], fs[f][:, 0:1], None, ALU.mult),
                            reads=[rob, rf], writes=[rf])
                        p.op("dve", lambda: E["dve"].tensor_scalar(
                            fo[f][:, :], ob[:, 256:384], fs[f][:, 2:3], None, ALU.mult),
                            reads=[rob, rf], writes=[rf])
                        p.op("dve", lambda: E["dve"].tensor_tensor(
                            fo[f][:, :], fo[f][:, :], fa[f][:, :], ALU.add),
                            reads=[rf], writes=[rf])
                        p.op("dve", lambda: E["dve"].tensor_tensor(
                            fa[f][:, :], fo[f][:, :], fo[f][:, :], ALU.mult),
                            reads=[rf], writes=[rf])
                        p.op("dve", lambda: E["dve"].tensor_reduce(fs[f][:, 3:4], fa[f][:, :], AX.X, ALU.add),
                             reads=[rf], writes=[rf])
                        yield
                        yield
                        p.op("act", lambda: E["act"].activation(
                            fs[f][:, 4:5], fs[f][:, 3:4], AF.Sqrt, bias=epsb[:, 0:1], scale=1.0 / 128),
                            reads=[rf], writes=[rf])
                        yield
                        yield
                        p.op("dve", lambda: E["dve"].reciprocal(fs[f][:, 5:6], fs[f][:, 4:5]),
                             reads=[rf], writes=[rf])
                        p.op("dve", lambda: E["dve"].tensor_scalar(
                            fa[f][:, :], fo[f][:, :], fs[f][:, 5:6], None, ALU.mult),
                            reads=[rf], writes=[rf])
                        p.op("dve", lambda: E["dve"].tensor_tensor(
                            fn_[f][:, :], fa[f][:, :], gsub[:, l * 128:(l + 1) * 128], ALU.mult),
                            reads=[rf], writes=[rf])
                    else:
                        p.op("dve", lambda: E["dve"].reciprocal(fs[f][:, 0:1], ob[:, 128:129]),
                             reads=[rob], writes=[rf])
                        p.op("dve", lambda: E["dve"].tensor_scalar(
                            fn_[f][:, :], ob[:, 0:128], fs[f][:, 0:1], None, ALU.mult),
                            reads=[rob, rf], writes=[rf])
                    yield
                    yield
                    ts_ = nxt("t", 8)
                    p.op("pe", lambda: E["pe"].transpose(TP[:, ts_ * 128:(ts_ + 1) * 128], fn_[f][:, :], identb[:, :]),
                         reads=[rf], writes=[r_T])
                    yield
                    yield
                    p.op("dve", lambda: E["dve"].tensor_tensor(
                        YS[b][:, qs], TP[:, ts_ * 128:(ts_ + 1) * 128], GT[b][:, qs], ALU.mult),
                        reads=[r_T, r_in[b]], writes=[r_ys[b]])

                pending = []

                def advance(flush=False):
                    while True:
                        for g in list(pending):
                            try:
                                next(g)
                            except StopIteration:
                                pending.remove(g)
                        if not flush or not pending:
                            break

                steps = []
                for hi in range(len(heads)):
                    gs = 2 if heads[hi][0] == "d" else 4
                    ngr = (n_groups * 2) // gs
                    for G in range(ngr):
                        lbs = tuple(range(gs * G, gs * G + gs))
                        grp = {}
                        for t in range(lbs[-1] + 1):
                            steps.append(dict(hi=hi, G=G, t=t, lbs=lbs, grp=grp,
                                              first=(G == 0 and t == 0),
                                              last=(G == ngr - 1 and t == lbs[-1])))

                def emit_S(st):
                    hi, t, lbs = st["hi"], st["t"], st["lbs"]
                    kind, h = heads[hi]
                    b = hi % 2
                    act_l = [i for i, lb in enumerate(lbs) if lb >= t]
                    q0 = lbs[act_l[0]] * 128
                    N = len(act_l) * 128
                    ks = slice(t * 128, (t + 1) * 128)
                    isb = nxt("s", NS)
                    sbk, rsb = SB_[isb], r_S[isb]
                    st.update(act_l=act_l, N=N, sbk=sbk, rsb=rsb)
                    if kind == "d":
                        for m in range(2):
                            qsrc = QT[b] if m == 0 else QB[b]
                            p.op("pe", lambda m=m, qsrc=qsrc: E["pe"].matmul(
                                sbk[:, m * 256:m * 256 + N], KT[b][:, ks], qsrc[:, q0:q0 + N], start=True, stop=True),
                                reads=[r_in[b]], writes=[rsb], ms=(m == 1))
                    else:
                        p.op("pe", lambda: E["pe"].matmul(
                            sbk[:, 0:N], KT[b][:, ks], QT[b][:, q0:q0 + N], start=True, stop=False),
                            reads=[r_in[b]], writes=[rsb], ms=False)
                        p.op("pe", lambda: E["pe"].matmul(
                            sbk[:, 0:N], ZK[:, ks], QR[b][:, q0:q0 + N], start=False, stop=True),
                            reads=[r_in[b], r_zk], writes=[rsb])

                def emit_rest(st):
                    hi, t, lbs, grp = st["hi"], st["t"], st["lbs"], st["grp"]
                    kind, h = heads[hi]
                    b = hi % 2
                    nm = 2 if kind == "d" else 1
                    scale = SC_DIFF if kind == "d" else SC_MLA
                    act_l, N, sbk, rsb = st["act_l"], st["N"], st["sbk"], st["rsb"]
                    if t == 0:
                        grp["ob"] = []
                        for lb in lbs:
                            io = nxt("o", 4)
                            grp["ob"].append((OB[io], r_O[io]))
                    ip = nxt("p", NPT)
                    pt, rpt = PT[ip], r_pt[ip]
                    if kind == "d":
                        sv = sbk[:, :].rearrange("p (m n) -> p m n", m=2)[:, :, 0:N]
                        ptv = pt[:, :].rearrange("p (m n) -> p m n", m=2)
                        pv_ = ptv[:, :, 0:N]
                    else:
                        sv = sbk[:, 0:N]
                        pv_ = pt[:, 0:N]
                    p.op("act", lambda: E["act"].activation(pv_, sv, AF.Exp, scale=scale),
                         reads=[rsb], writes=[rpt])
                    for i in act_l:
                        lb = lbs[i]
                        co = (i - act_l[0]) * 128
                        if kind == "d" and (t == lb or t == lb - 1):
                            kd = 0 if t == lb else 1
                            p.op("dve", lambda co=co, kd=kd: E["dve"].tensor_tensor(
                                ptv[:, :, co:co + 128], ptv[:, :, co:co + 128],
                                eb[:, 2 * h:2 * h + 2, kd, :], ALU.mult),
                                reads=[rpt], writes=[rpt])
                        elif kind == "m" and t == lb:
                            p.op("pool", lambda co=co: E["pool"].memset(pt[64:128, co:co + 64], 0.0),
                                 reads=[rpt], writes=[rpt])
                    advance()
                    for i in act_l:
                        lb = lbs[i]
                        co = (i - act_l[0]) * 128
                        ob, rob = grp["ob"][i]
                        for m in range(nm):
                            lhs = ptv[:, m, co:co + 128] if kind == "d" else pt[:, co:co + 128]
                            p.op("pe", lambda lhs=lhs, ob=ob, m=m, lb=lb: E["pe"].matmul(
                                ob[:, m * 256:m * 256 + 129], lhs, VX[b][:, t, :],
                                start=(t == 0 and m == 0), stop=(t == lb), skip_group_check=True),
                                reads=[rpt, r_in[b]], writes=[rob], ms=(m == nm - 1))
                        if t == lb:
                            pending.append(finalize_gen(hi, lb, ob, rob))

                load_head(0)
                for i in range(min(LOOK, len(steps))):
                    emit_S(steps[i])
                for i, st in enumerate(steps):
                    if st["first"] and st["hi"] + 1 < len(heads):
                        load_head(st["hi"] + 1)
                    if i + LOOK < len(steps):
                        emit_S(steps[i + LOOK])
                    emit_rest(st)
                    if st["last"]:
                        advance(flush=True)
                        kind, h = heads[st["hi"]]
                        b = st["hi"] % 2
                        tile_idx = h if kind == "d" else 8 + h
                        p.dma("sp", yT[tile_idx, :, :], YS[b][:, :], reads=[r_ys[b]], writes=[r_scr["yT"]])
                p.barrier()

        def phase_D(l, xsrc, last):
            with ExitStack() as sd:
                wo_sb = sb("wo_sb", [128, 16, D], BF, sd); r_wo = R()
                ys = [sb("ysD%d" % i, [128, 16, 512], BF, sd) for i in range(2)]
                r_y = [R(), R()]
                xs = [sb("xsD%d" % i, [128, D], F32, sd) for i in range(3)]
                r_x = [R() for _ in range(3)]
                junk = sb("junkD", [128, D], BF, sd); r_junk = R()
                gfin = sb("gfin", [128, D], F32, sd); r_g = R()
                fsd = sb("fsd", [128, 3 * NB], F32, sd); r_fs = [R() for _ in range(NB)]
                bank = [ps("pd%d" % i, [128, 512], F32, sd) for i in range(8)]
                r_bank = [R() for _ in range(8)]
                bi = [0]
                for cg in range(4):
                    p.dma("pool", wo_sb[:, :, cg * 512:(cg + 1) * 512],
                          wo[l, :, cg * 512:(cg + 1) * 512].rearrange("(c p) n -> p c n", p=128), writes=[r_wo])
                if last:
                    p.dma("sp", gfin[:, :], gfin_in[:, :], writes=[r_g])
                for tt in range(NB):
                    n = tt // 4
                    yb = n % 2
                    if tt % 4 == 0:
                        p.dma("sp", ys[yb][:, :, :], yT[:, :, n * 512:(n + 1) * 512].rearrange("c p n -> p c n"),
                              reads=[r_scr["yT"]], writes=[r_y[yb]])
                    xb = tt % 3
                    p.dma("sp", xs[xb][:, :], xsrc[tt * 128:(tt + 1) * 128, :],
                          reads=([r_scr["x1"]] if l > 0 else []), writes=[r_x[xb]])
                    for dg in range(4):
                        i = bi[0]
                        bi[0] = (i + 1) % 8
                        bk, rbk = bank[i], r_bank[i]
                        for c in range(16):
                            p.op("pe", lambda bk=bk, c=c, yb=yb, tt=tt, dg=dg: E["pe"].matmul(
                                bk[:, :], ys[yb][:, c, (tt % 4) * 128:(tt % 4 + 1) * 128],
                                wo_sb[:, c, dg * 512:(dg + 1) * 512], start=(c == 0), stop=(c == 15)),
                                reads=[r_y[yb], r_wo], writes=[rbk], ms=(c == 15))
                        p.op("dve", lambda bk=bk, xb=xb, dg=dg: E["dve"].tensor_tensor(
                            xs[xb][:, dg * 512:(dg + 1) * 512], bk[:, :], xs[xb][:, dg * 512:(dg + 1) * 512], ALU.add),
                            reads=[rbk, r_x[xb]], writes=[r_x[xb]])
                    if not last:
                        p.dma("sp", x1[tt * 128:(tt + 1) * 128, :], xs[xb][:, :], reads=[r_x[xb]], writes=[r_scr["x1"]])
                    else:
                        c0 = 3 * tt
                        p.op("act", lambda xb=xb, c0=c0: E["act"].activation(
                            junk[:, :], xs[xb][:, :], AF.Square, accum_out=fsd[:, c0:c0 + 1]),
                            reads=[r_x[xb]], writes=[r_junk, r_fs[tt]])
                        p.op("act", lambda c0=c0: E["act"].activation(
                            fsd[:, c0 + 1:c0 + 2], fsd[:, c0:c0 + 1], AF.Sqrt, bias=epsb[:, 0:1], scale=1.0 / D),
                            reads=[r_fs[tt]], writes=[r_fs[tt]])
                        p.op("dve", lambda c0=c0: E["dve"].reciprocal(fsd[:, c0 + 2:c0 + 3], fsd[:, c0 + 1:c0 + 2]),
                             reads=[r_fs[tt]], writes=[r_fs[tt]])
                        p.op("act", lambda xb=xb, c0=c0: E["act"].activation(
                            xs[xb][:, :], xs[xb][:, :], AF.Copy, scale=fsd[:, c0 + 2:c0 + 3]),
                            reads=[r_x[xb], r_fs[tt]], writes=[r_x[xb]])
                        p.op("dve", lambda xb=xb: E["dve"].tensor_tensor(
                            xs[xb][:, :], xs[xb][:, :], gfin[:, :], ALU.mult),
                            reads=[r_x[xb], r_g], writes=[r_x[xb]])
                        p.dma("sp", out_d[tt * 128:(tt + 1) * 128, :], xs[xb][:, :], reads=[r_x[xb]], writes=[r_scr["x1"]])
                p.barrier()

        done = False
        for l in range(DEPTH):
            xsrc = x_in if l == 0 else x1
            for half in range(S // HALF):
                if not dbg.get("skipA"):
                    phase_A(l, half, xsrc)
            if stop_after == ("A", l):
                done = True
                break
            phase_C(l)
            if stop_after == ("C", l):
                done = True
                break
            phase_D(l, xsrc, last=(l == DEPTH - 1))
            if stop_after == ("D", l):
                done = True
                break
        p.barrier()
        build_program.ninst = p.ninst
    return nc


def _rel_bucket_np(rel):
    nb = 16
    max_exact = 8
    ret = (rel > 0).astype(np.int32) * nb
    n = np.abs(rel)
    nf = np.maximum(n, 1).astype(np.float32)
    large = max_exact + (np.log(nf / max_exact) / math.log(128 / max_exact) * (nb - max_exact)).astype(np.int32)
    large = np.minimum(large, nb - 1)
    return ret + np.where(n < max_exact, n, large)


def prepare_inputs(x, norm_g, w_in, diff_lambda, diff_subln_g, mla_q_norm_g, mla_w_q_b,
                   mla_kv_norm_g, mla_w_kv_b, w_out, rel_bias, final_norm_g):
    f = np.float32
    x = np.asarray(x, f)
    w_in = np.asarray(w_in, f)
    kr0 = 3840
    swap = np.concatenate([np.arange(32, 64), np.arange(0, 32)])
    colsF = np.concatenate([np.arange(0, 2048), np.arange(3072, 3840), kr0 + np.arange(64), kr0 + swap,
                            np.arange(3904, 5952)])
    assert colsF.size == NFT * 128
    wF = np.ascontiguousarray(w_in[:, :, colsF])
    wT = np.ascontiguousarray(w_in[:, :, 2048:3072])
    wqb = np.asarray(mla_w_q_b, f)
    cq_cols = []
    for h in range(H):
        b0 = h * 192
        cq_cols += [b0 + np.arange(128), b0 + 128 + np.arange(64), b0 + 128 + swap]
    wq = np.ascontiguousarray(wqb[:, :, np.concatenate(cq_cols)])
    wkvb = np.asarray(mla_w_kv_b, f)
    kc = np.concatenate([h * 256 + np.arange(128) for h in range(H)])
    vc = np.concatenate([h * 256 + 128 + np.arange(128) for h in range(H)])
    wkvk = np.ascontiguousarray(wkvb[:, :, kc])
    wkvv = np.ascontiguousarray(wkvb[:, :, vc])
    wo = np.ascontiguousarray(np.asarray(w_out, f))

    def colmajor(v, nchunk):
        v = np.asarray(v, f).reshape(DEPTH, nchunk, 128)
        return np.ascontiguousarray(v.transpose(2, 0, 1).reshape(128, DEPTH * nchunk))

    def bcast(v):
        v = np.asarray(v, f).reshape(1, -1)
        return np.ascontiguousarray(np.broadcast_to(v, (128, v.shape[1])))

    rb = np.asarray(rel_bias, f)
    k = np.arange(128)[:, None]
    q = np.arange(128)[None, :]
    idx = np.stack([_rel_bucket_np(k - q), _rel_bucket_np(k - 128 - q)], 0)
    bt = rb[idx]
    bt = np.ascontiguousarray(bt.transpose(1, 3, 0, 2).reshape(128, 16 * 2 * 128))
    cfar = bcast(rb[15, :])
    pos = np.arange(S, dtype=np.float32)
    inv_freq = (10000.0 ** (-np.arange(0, 64, 2, dtype=np.float32) / 64)).astype(np.float32)
    ang = pos[:, None] * inv_freq[None, :]
    cos, sin = np.cos(ang).astype(f).T, np.sin(ang).astype(f).T
    t1 = np.ascontiguousarray(np.concatenate([cos, cos, -sin, sin], 0))
    identf = np.eye(128, dtype=f)
    identb = np.eye(128, dtype=f).astype(ml_dtypes.bfloat16)
    dfold = (np.arange(128)[:, None] % 64 == np.arange(128)[None, :] % 64).astype(f).astype(ml_dtypes.bfloat16)
    shared = dict(
        wF=wF, wT=wT, wq=wq, wkvk=wkvk, wkvv=wkvv, wo=wo,
        g_in=colmajor(norm_g, 16), gbc=bcast(np.asarray(norm_g, f).reshape(-1)), gq=colmajor(mla_q_norm_g, 4), gkv=colmajor(mla_kv_norm_g, 2),
        gsub=bcast(np.asarray(diff_subln_g, f).reshape(-1)), gfin=bcast(final_norm_g),
        dlam=bcast(np.asarray(diff_lambda, f).reshape(-1)), cfar=cfar, bt=bt, t1=t1,
        identf=identf, identb=identb, dfold=dfold)
    in_maps = []
    for c in range(8):
        m = dict(shared)
        m["x"] = np.ascontiguousarray(x[c % 4])
        in_maps.append(m)
    return in_maps


_NC_CACHE = {}


def kernel(**inputs):
    in_maps = prepare_inputs(**inputs)
    if "nc" not in _NC_CACHE:
        _NC_CACHE["nc"] = build_program()
    res = run_bass_kernel_spmd(_NC_CACHE["nc"], in_maps, core_ids=list(range(8)))
    out = np.stack([np.asarray(res.results[c]["out"], dtype=np.float32) for c in range(4)], 0)
    return out
```

```python
import math
from contextlib import ExitStack

import numpy as np
import ml_dtypes

import concourse.bass as bass
import concourse.mybir as mybir
from concourse.bass_utils import run_bass_kernel_spmd

F32 = mybir.dt.float32
BF = mybir.dt.bfloat16
AF = mybir.ActivationFunctionType
ALU = mybir.AluOpType
AX = mybir.AxisListType

D = 2048
S = 4096
NB = S // 128
DEPTH = 2
H = 8
EPS = 1e-6
NFT = 39
HALF = 2048
SC_DIFF = 64 ** -0.5
SC_MLA = 192 ** -0.5
SAME_ENG_SYNC = True


class R:
    __slots__ = ("w", "r", "name")

    def __init__(self, name=""):
        self.w = None
        self.r = {}
        self.name = name


class Prog:
    NDS = 24

    def __init__(self, nc, es):
        self.nc = nc
        self.E = {"pe": nc.tensor, "act": nc.scalar, "dve": nc.vector, "pool": nc.gpsimd, "sp": nc.sync}
        self.csem = {e: es.enter_context(nc.semaphore("c_" + e)) for e in ("pe", "act", "dve", "pool")}
        self.ccnt = {e: 0 for e in self.csem}
        self.dsem = [es.enter_context(nc.semaphore("d%d" % i)) for i in range(self.NDS)]
        self.dcnt = [0] * self.NDS
        self.dnext = 0
        self.waited = {}
        self.ninst = 0

    def _sem(self, tag):
        return self.csem[tag[1]] if tag[0] == "c" else self.dsem[tag[1]]

    def _wait(self, eng, tag):
        key = (eng, tag[0], tag[1])
        if self.waited.get(key, 0) >= tag[2]:
            return
        self.E[eng].wait_ge(self._sem(tag), tag[2])
        self.waited[key] = tag[2]

    def _deps(self, eng, reads, writes):
        deps = {}
        for r in reads:
            if r.w is not None:
                k = (r.w[0], r.w[1])
                deps[k] = max(deps.get(k, 0), r.w[2])
        for w in writes:
            if w.w is not None:
                k = (w.w[0], w.w[1])
                deps[k] = max(deps.get(k, 0), w.w[2])
            for k, v in w.r.items():
                deps[k] = max(deps.get(k, 0), v)
        for k in sorted(deps, key=str):
            if k[0] == "c" and k[1] == eng and (eng == "pe" or not SAME_ENG_SYNC):
                continue
            self._wait(eng, (k[0], k[1], deps[k]))

    def _mark(self, tag, reads, writes):
        k = (tag[0], tag[1])
        for r in reads:
            r.r[k] = max(r.r.get(k, 0), tag[2])
        for w in writes:
            w.w = tag
            w.r = {}

    def op(self, eng, fn, reads=(), writes=(), ms=True):
        self._deps(eng, reads, writes)
        inst = fn()
        self.ninst += 1
        if ms:
            self.ccnt[eng] += 1
            tag = ("c", eng, self.ccnt[eng])
            inst.then_inc(self.csem[eng], 1)
        else:
            tag = ("c", eng, self.ccnt[eng] + 1)
        self._mark(tag, reads, writes)
        return inst

    def dma(self, q, out, in_, reads=(), writes=()):
        i = self.dnext
        self.dnext = (self.dnext + 1) % self.NDS
        if self.dcnt[i] > 0:
            self._wait(q, ("d", i, self.dcnt[i]))
        self._deps(q, reads, writes)
        self.dcnt[i] += 16
        tag = ("d", i, self.dcnt[i])
        self.E[q].dma_start(out=out, in_=in_).then_inc(self.dsem[i], 16)
        self.ninst += 1
        self._mark(tag, reads, writes)

    def barrier(self, engines=("pe", "act", "dve", "pool", "sp")):
        tags = [("c", e, self.ccnt[e]) for e in self.csem if self.ccnt[e] > 0]
        tags += [("d", i, self.dcnt[i]) for i in range(self.NDS) if self.dcnt[i] > 0]
        for e in engines:
            for t in tags:
                if t[0] == "c" and t[1] == e:
                    continue
                self._wait(e, t)


def build_program(dbg=None):
    dbg = dbg or {}
    stop_after = dbg.get("stop", None)
    expose = dbg.get("expose", ())
    nc = bass.Bass("TRN2", target_bir_lowering=False)

    def din(name, shape, dt=F32):
        return nc.dram_tensor(name, list(shape), dt, kind="ExternalInput")

    def dscr(name, shape, dt=BF):
        if name in expose:
            return nc.dram_tensor(name, list(shape), dt, kind="ExternalOutput")
        return nc.dram_tensor(name, list(shape), dt)

    x_in = din("x", [S, D])
    wF = din("wF", [DEPTH, D, NFT * 128])
    wT = din("wT", [DEPTH, D, 1024])
    wq = din("wq", [DEPTH, 512, 2048])
    wkvk = din("wkvk", [DEPTH, 256, 1024])
    wkvv = din("wkvv", [DEPTH, 256, 1024])
    wo = din("wo", [DEPTH, D, D])
    g_in = din("g_in", [128, DEPTH * 16])
    gbc_in = din("gbc", [128, DEPTH * D])
    gq_in = din("gq", [128, DEPTH * 4])
    gkv_in = din("gkv", [128, DEPTH * 2])
    gsub_in = din("gsub", [128, DEPTH * 128])
    gfin_in = din("gfin", [128, D])
    dlam_in = din("dlam", [128, DEPTH * 256])
    cfar_in = din("cfar", [128, 16])
    bt_in = din("bt", [128, 16 * 2 * 128])
    t1_in = din("t1", [128, S])
    identf_in = din("identf", [128, 128])
    identb_in = din("identb", [128, 128], BF)
    dfold_in = din("dfold", [128, 128], BF)
    out_d = nc.dram_tensor("out", [S, D], F32, kind="ExternalOutput")

    qdT = dscr("qdT", [H, 128, S])
    kdT = dscr("kdT", [H, 128, S])
    qnT = dscr("qnT", [H, 128, S])
    qrT = dscr("qrT", [H, 128, S])
    knT = dscr("knT", [H, 128, S])
    zkT = dscr("zkT", [128, S])
    gT = dscr("gT", [16, 128, S])
    yT = dscr("yT", [16, 128, S])
    vd = dscr("vd", [S, 1024])
    vb = dscr("vb", [S, 1024])
    x1 = dscr("x1", [S, D], F32)

    with ExitStack() as es:
        p = Prog(nc, es)
        E = p.E

        uid = [0]

        def sb(name, shape, dt, stack=es):
            uid[0] += 1
            return stack.enter_context(nc.sbuf_tensor("s%d_%s" % (uid[0], name), list(shape), dt))

        def ps(name, shape, dt, stack):
            uid[0] += 1
            return stack.enter_context(nc.psum_tensor("p%d_%s" % (uid[0], name), list(shape), dt))

        identf = sb("identf", [128, 128], F32); r_const = R("const")
        identb = sb("identb", [128, 128], BF)
        dfold = sb("dfold", [128, 128], BF)
        onesb = sb("onesb", [128, 128], BF)
        onesf = sb("onesf", [128, 128], F32)
        t1 = sb("t1", [128, S], F32)
        eb = sb("eb", [128, 16, 2, 128], F32)
        gin = sb("gin", [128, DEPTH * 16], F32)
        gq = sb("gq", [128, DEPTH * 4], F32)
        gkv = sb("gkv", [128, DEPTH * 2], F32)
        gsub = sb("gsub", [128, DEPTH * 128], F32)
        dlam = sb("dlam", [128, DEPTH * 256], F32)
        cfar = sb("cfar", [128, 16], F32)
        epsb = sb("epsb", [128, 1], F32)
        lam = sb("lam", [128, 8], F32)
        lamtmp = sb("lamtmp", [128, DEPTH * 128], F32)
        for dst, src in ((identf, identf_in), (identb, identb_in), (dfold, dfold_in), (t1, t1_in),
                         (gin, g_in), (gq, gq_in), (gkv, gkv_in), (gsub, gsub_in), (dlam, dlam_in),
                         (cfar, cfar_in)):
            p.dma("sp", dst[:, :], src[:, :], writes=[r_const])
        p.dma("sp", eb[:, :, :, :].rearrange("p a b c -> p (a b c)"), bt_in[:, :], writes=[r_const])
        p.op("dve", lambda: E["dve"].memset(onesb[:, :], 1.0), writes=[r_const])
        p.op("dve", lambda: E["dve"].memset(onesf[:, :], 1.0), writes=[r_const])
        p.op("dve", lambda: E["dve"].memset(epsb[:, :], EPS), writes=[r_const])
        p.op("dve", lambda: E["dve"].tensor_scalar(cfar[:, :], cfar[:, :], -1.0, None, ALU.mult),
             reads=[r_const], writes=[r_const])
        for hm in range(16):
            p.op("act", lambda hm=hm: E["act"].activation(
                eb[:, hm, :, :], eb[:, hm, :, :], AF.Exp, bias=cfar[:, hm:hm + 1], scale=1.0),
                reads=[r_const], writes=[r_const])
        p.op("dve", lambda: E["dve"].memset(eb[64:128, :, 0, 0:64], 0.0), reads=[r_const], writes=[r_const])
        for l in range(DEPTH):
            lam_init = 0.8 - 0.6 * math.exp(-0.3 * l)
            dl = dlam[:, l * 256:(l + 1) * 256].rearrange("p (a b c) -> p a b c", a=2, b=2)
            pr = lamtmp[:, l * 128:(l + 1) * 128].rearrange("p (a c) -> p a c", a=2)
            p.op("dve", lambda dl=dl, pr=pr: E["dve"].tensor_tensor(pr, dl[:, :, 0, :], dl[:, :, 1, :], ALU.mult),
                 reads=[r_const], writes=[r_const])
            p.op("dve", lambda pr=pr, l=l: E["dve"].tensor_reduce(lam[:, 4 * l + 2:4 * l + 4], pr, AX.X, ALU.add),
                 reads=[r_const], writes=[r_const])
            p.op("act", lambda l=l: E["act"].activation(lam[:, 4 * l + 2:4 * l + 4], lam[:, 4 * l + 2:4 * l + 4], AF.Exp),
                 reads=[r_const], writes=[r_const])
            p.op("dve", lambda l=l: E["dve"].tensor_tensor(lam[:, 4 * l:4 * l + 1], lam[:, 4 * l + 2:4 * l + 3],
                                                           lam[:, 4 * l + 3:4 * l + 4], ALU.subtract),
                 reads=[r_const], writes=[r_const])
            p.op("dve", lambda l=l, li=lam_init: E["dve"].tensor_scalar(
                lam[:, 4 * l:4 * l + 1], lam[:, 4 * l:4 * l + 1], li, None, ALU.add),
                reads=[r_const], writes=[r_const])
            p.op("dve", lambda l=l: E["dve"].tensor_scalar(
                lam[:, 4 * l + 1:4 * l + 2], lam[:, 4 * l:4 * l + 1], -1.0, None, ALU.mult),
                reads=[r_const], writes=[r_const])
            p.op("dve", lambda l=l, li=lam_init: E["dve"].tensor_scalar(
                gsub[:, l * 128:(l + 1) * 128], gsub[:, l * 128:(l + 1) * 128], 1.0 - li, None, ALU.mult),
                reads=[r_const], writes=[r_const])
        p.barrier()

        r_scr = {n: R(n) for n in ("qdT", "kdT", "qnT", "qrT", "knT", "zkT", "gT", "yT", "vd", "vb", "x1")}

        def phase_A(l, half, xsrc):
            t0 = half * HALF
            NT = HALF // 128
            NCK = HALF // 512
            with ExitStack() as sa:
                xT = sb("xT", [128, 16, HALF], BF, sa); r_xT = R()
                wbuf = [sb("wbuf%d" % i, [128, 8192], BF, sa) for i in range(2)]
                r_wb = [R(), R()]
                rstd_bc = sb("rstd_bc", [128, HALF], F32, sa); r_rbc = R()
                rcol = sb("rcol", [128, 3 * NT], F32, sa); r_rcol = [R() for _ in range(NT)]
                bank = [ps("pa%d" % i, [128, 512], F32, sa) for i in range(6)]
                r_bank = [R() for _ in range(6)]
                bi = [0]

                def nb():
                    i = bi[0]
                    bi[0] = (i + 1) % 6
                    return bank[i], r_bank[i]

                with ExitStack() as s0:
                    xs = [sb("xs%d" % i, [128, D], F32, s0) for i in range(4)]
                    r_xs = [R() for _ in range(4)]
                    junk = sb("junk", [128, D], BF, s0); r_junk = R()
                    rb = sb("rb", [128, 128], F32, s0); r_rb = R()
                    g_bc = sb("g_bc", [128, D], F32, s0); r_gbc = R()
                    xg = [sb("xg%d" % i, [128, D], BF, s0) for i in range(2)]
                    r_xg = [R(), R()]
                    tpa = [ps("tpa%d" % i, [128, 1024], BF, s0) for i in range(2)]
                    r_tpa = [R(), R()]
                    tpc = [0]

                    def nxt_tp():
                        i = tpc[0]
                        tpc[0] = (i + 1) % 2
                        return i

                    p.dma("sp", g_bc[:, :], gbc_in[:, l * D:(l + 1) * D], writes=[r_gbc])
                    for tt in range(3):
                        p.dma("sp", xs[tt][:, :], xsrc[t0 + tt * 128:t0 + (tt + 1) * 128, :], writes=[r_xs[tt]])
                    for tt in range(NT):
                        b = tt % 4
                        if tt + 3 < NT:
                            p.dma("sp", xs[(tt + 3) % 4][:, :], xsrc[t0 + (tt + 3) * 128:t0 + (tt + 4) * 128, :],
                                  writes=[r_xs[(tt + 3) % 4]])
                        c0 = 3 * tt
                        p.op("act", lambda b=b, c0=c0: E["act"].activation(
                            junk[:, :], xs[b][:, :], AF.Square, accum_out=rcol[:, c0:c0 + 1]),
                            reads=[r_xs[b]], writes=[r_junk, r_rcol[tt]])
                        p.op("act", lambda c0=c0: E["act"].activation(
                            rcol[:, c0 + 1:c0 + 2], rcol[:, c0:c0 + 1], AF.Sqrt, bias=epsb[:, 0:1], scale=1.0 / D),
                            reads=[r_rcol[tt]], writes=[r_rcol[tt]])
                        p.op("dve", lambda c0=c0: E["dve"].reciprocal(rcol[:, c0 + 2:c0 + 3], rcol[:, c0 + 1:c0 + 2]),
                             reads=[r_rcol[tt]], writes=[r_rcol[tt]])
                        p.op("dve", lambda c0=c0: E["dve"].tensor_scalar(
                            rb[:, :], onesf[:, :], rcol[:, c0 + 2:c0 + 3], None, ALU.mult),
                            reads=[r_rcol[tt]], writes=[r_rb])
                        bk, rbk = nb()
                        p.op("pe", lambda bk=bk: E["pe"].transpose(bk[:, 0:128], rb[:, :], identf[:, :]),
                             reads=[r_rb], writes=[rbk])
                        p.op("act", lambda bk=bk, tt=tt: E["act"].copy(rstd_bc[:, tt * 128:(tt + 1) * 128], bk[:, 0:128]),
                             reads=[rbk], writes=[r_rbc])
                        xb_ = tt % 2
                        p.op("dve", lambda b=b, xb_=xb_: E["dve"].tensor_tensor(
                            xg[xb_][:, :], xs[b][:, :], g_bc[:, :], ALU.mult),
                            reads=[r_xs[b], r_gbc], writes=[r_xg[xb_]])
                        for hb in range(2):
                            tk = nxt_tp()
                            for j in range(8):
                                c = hb * 8 + j
                                p.op("pe", lambda tk=tk, j=j, c=c, xb_=xb_: E["pe"].transpose(
                                    tpa[tk][:, j * 128:(j + 1) * 128], xg[xb_][:, c * 128:(c + 1) * 128], identb[:, :]),
                                    reads=[r_xg[xb_]], writes=[r_tpa[tk]], ms=(j == 7))
                            src = tpa[tk][:, :].rearrange("p (c n) -> p c n", c=8)
                            dst = xT[:, hb * 8:(hb + 1) * 8, tt * 128:(tt + 1) * 128]
                            if hb == 0:
                                p.op("act", lambda src=src, dst=dst: E["act"].copy(dst, src),
                                     reads=[r_tpa[tk]], writes=[R()])
                            else:
                                p.op("dve", lambda src=src, dst=dst: E["dve"].tensor_copy(dst, src),
                                     reads=[r_tpa[tk]], writes=[R()])
                    p.barrier()

                with ExitStack() as s1:
                    cqT = sb("cqT", [128, 4, HALF], BF, s1); r_cq = [R() for _ in range(NCK)]
                    ckvT = sb("ckvT", [128, 2, HALF], BF, s1); r_ckv = [R() for _ in range(NCK)]
                    t32 = [sb("t32_%d" % i, [128, 512], F32, s1) for i in range(3)]
                    r_t32 = [R() for _ in range(3)]
                    ost = [sb("ost%d" % i, [128, 512], BF, s1) for i in range(4)]
                    r_ost = [R() for _ in range(4)]
                    sqb = [sb("sqb%d" % i, [128, 4, 512], BF, s1) for i in range(2)]
                    r_sqb = [R(), R()]
                    rq_bc = sb("rq_bc", [128, HALF], F32, s1); r_rq = [R() for _ in range(NCK)]
                    rkv_bc = sb("rkv_bc", [128, HALF], F32, s1); r_rkv = [R() for _ in range(NCK)]
                    rkvc = sb("rkvc", [128, 3 * NT], F32, s1); r_rkvc = [R() for _ in range(NT)]
                    sqkv = [sb("sqkv%d" % i, [128, 2, 512], BF, s1) for i in range(2)]
                    r_sqkv = [R(), R()]
                    ctr = {"t": 0, "o": 0, "s": 0, "w": 0}

                    def nxt(k, n):
                        i = ctr[k]
                        ctr[k] = (i + 1) % n
                        return i

                    def load_w(src_ap, view_shape):
                        i = nxt("w", 2)
                        n = 1
                        for d_ in view_shape[1:]:
                            n *= d_
                        flat = wbuf[i][:, 0:n]
                        if len(view_shape) == 3:
                            v = flat.rearrange("p (a b) -> p a b", a=view_shape[1])
                        else:
                            v = flat
                        p.dma("pool", v, src_ap, writes=[r_wb[i]])
                        return v, r_wb[i]

                    def store(dst_ap, src_ap, rsrc, rdst):
                        p.dma("sp", dst_ap, src_ap, reads=[rsrc], writes=[rdst])

                    groups = [list(range(g, min(g + 4, NFT))) for g in range(0, NFT, 4)]
                    for grp in groups:
                        c0 = grp[0] * 128
                        ncol = len(grp) * 128
                        wv, rw = load_w(
                            wF[l, :, c0:c0 + ncol].rearrange("(c p) n -> p c n", p=128), [128, 16, ncol])
                        for n in range(NCK):
                            tsl = slice(n * 512, (n + 1) * 512)
                            gsl = slice(t0 + n * 512, t0 + (n + 1) * 512)
                            for jj, ft in enumerate(grp):
                                bk, rbk = nb()
                                for c in range(16):
                                    p.op("pe", lambda bk=bk, c=c, jj=jj, tsl=tsl: E["pe"].matmul(
                                        bk[:, :], wv[:, c, jj * 128:(jj + 1) * 128], xT[:, c, tsl],
                                        start=(c == 0), stop=(c == 15)),
                                        reads=[rw, r_xT], writes=[rbk], ms=(c == 15))
                                if ft < 16 or ft == 22:
                                    io = nxt("o", 4)
                                    if ft == 22:
                                        it = nxt("t", 3)
                                        p.op("dve", lambda bk=bk, it=it, tsl=tsl: E["dve"].tensor_tensor(
                                            t32[it][:, :], bk[:, :], rstd_bc[:, tsl], ALU.mult),
                                            reads=[rbk, r_rbc], writes=[r_t32[it]])
                                        p.op("dve", lambda it=it, io=io, gsl=gsl: E["dve"].tensor_tensor(
                                            ost[io][:, :], t32[it][:, :], t1[:, gsl], ALU.mult),
                                            reads=[r_t32[it]], writes=[r_ost[io]])
                                        store(zkT[:, gsl], ost[io][:, :], r_ost[io], r_scr["zkT"])
                                    else:
                                        p.op("dve", lambda bk=bk, io=io, tsl=tsl: E["dve"].tensor_tensor(
                                            ost[io][:, :], bk[:, :], rstd_bc[:, tsl], ALU.mult),
                                            reads=[rbk, r_rbc], writes=[r_ost[io]])
                                        if ft < 8:
                                            store(qdT[ft, :, gsl], ost[io][:, :], r_ost[io], r_scr["qdT"])
                                        else:
                                            store(kdT[ft - 8, :, gsl], ost[io][:, :], r_ost[io], r_scr["kdT"])
                                elif ft >= 23:
                                    it = nxt("t", 3)
                                    io = nxt("o", 4)
                                    p.op("dve", lambda bk=bk, it=it, tsl=tsl: E["dve"].tensor_tensor(
                                        t32[it][:, :], bk[:, :], rstd_bc[:, tsl], ALU.mult),
                                        reads=[rbk, r_rbc], writes=[r_t32[it]])
                                    p.op("act", lambda it=it, io=io: E["act"].activation(
                                        ost[io][:, :], t32[it][:, :], AF.Silu),
                                        reads=[r_t32[it]], writes=[r_ost[io]])
                                    store(gT[ft - 23, :, gsl], ost[io][:, :], r_ost[io], r_scr["gT"])
                                else:
                                    it = nxt("t", 3)
                                    p.op("dve", lambda bk=bk, it=it, tsl=tsl: E["dve"].tensor_tensor(
                                        t32[it][:, :], bk[:, :], rstd_bc[:, tsl], ALU.mult),
                                        reads=[rbk, r_rbc], writes=[r_t32[it]])
                                    if ft < 20:
                                        j = ft - 16
                                        isq = n % 2
                                        p.op("act", lambda it=it, isq=isq, j=j: E["act"].activation(
                                            sqb[isq][:, j, :], t32[it][:, :], AF.Square),
                                            reads=[r_t32[it]], writes=[r_sqb[isq]])
                                        p.op("act", lambda it=it, j=j, tsl=tsl: E["act"].activation(
                                            cqT[:, j, tsl], t32[it][:, :], AF.Copy, scale=gq[:, l * 4 + j:l * 4 + j + 1]),
                                            reads=[r_t32[it]], writes=[r_cq[n]])
                                    else:
                                        j = ft - 20
                                        p.op("act", lambda it=it, j=j, n=n: E["act"].activation(
                                            sqkv[n % 2][:, j, :], t32[it][:, :], AF.Square),
                                            reads=[r_t32[it]], writes=[r_sqkv[n % 2]])
                                        p.op("act", lambda it=it, j=j, tsl=tsl: E["act"].activation(
                                            ckvT[:, j, tsl], t32[it][:, :], AF.Copy, scale=gkv[:, l * 2 + j:l * 2 + j + 1]),
                                            reads=[r_t32[it]], writes=[r_ckv[n]])
                            if grp[0] == 16:
                                isq = n % 2
                                bk, rbk = nb()
                                for j in range(4):
                                    p.op("pe", lambda bk=bk, j=j, isq=isq: E["pe"].matmul(
                                        bk[:, :], onesb[:, :], sqb[isq][:, j, :], start=(j == 0), stop=(j == 3)),
                                        reads=[r_sqb[isq]], writes=[rbk], ms=(j == 3))
                                p.op("act", lambda bk=bk, tsl=tsl: E["act"].activation(
                                    rq_bc[:, tsl], bk[:, :], AF.Sqrt, bias=epsb[:, 0:1], scale=1.0 / 512),
                                    reads=[rbk], writes=[r_rq[n]])
                                p.op("dve", lambda tsl=tsl: E["dve"].reciprocal(rq_bc[:, tsl], rq_bc[:, tsl]),
                                     reads=[r_rq[n]], writes=[r_rq[n]])
                            if grp[0] == 20:
                                isq = n % 2
                                bk, rbk = nb()
                                for j in range(2):
                                    p.op("pe", lambda bk=bk, j=j, isq=isq: E["pe"].matmul(
                                        bk[:, :], onesb[:, :], sqkv[isq][:, j, :], start=(j == 0), stop=(j == 1)),
                                        reads=[r_sqkv[isq]], writes=[rbk], ms=(j == 1))
                                p.op("act", lambda bk=bk, tsl=tsl: E["act"].activation(
                                    rkv_bc[:, tsl], bk[:, :], AF.Sqrt, bias=epsb[:, 0:1], scale=1.0 / 256),
                                    reads=[rbk], writes=[r_rkv[n]])
                                p.op("dve", lambda tsl=tsl: E["dve"].reciprocal(rkv_bc[:, tsl], rkv_bc[:, tsl]),
                                     reads=[r_rkv[n]], writes=[r_rkv[n]])
                                for q4 in range(4):
                                    tt = n * 4 + q4
                                    bk, rbk = nb()
                                    for j in range(2):
                                        p.op("pe", lambda bk=bk, j=j, q4=q4, isq=isq: E["pe"].matmul(
                                            bk[:, 0:1], sqkv[isq][:, j, q4 * 128:(q4 + 1) * 128], onesb[:, 0:1],
                                            start=(j == 0), stop=(j == 1)),
                                            reads=[r_sqkv[isq]], writes=[rbk], ms=(j == 1))
                                    c0 = 3 * tt
                                    p.op("act", lambda bk=bk, c0=c0: E["act"].activation(
                                        rkvc[:, c0:c0 + 1], bk[:, 0:1], AF.Sqrt, bias=epsb[:, 0:1], scale=1.0 / 256),
                                        reads=[rbk], writes=[r_rkvc[tt]])
                                    p.op("dve", lambda c0=c0: E["dve"].reciprocal(rkvc[:, c0 + 1:c0 + 2], rkvc[:, c0:c0 + 1]),
                                         reads=[r_rkvc[tt]], writes=[r_rkvc[tt]])

                    for cg in range(2):
                        wv, rw = load_w(
                            wT[l, :, cg * 512:(cg + 1) * 512].rearrange("(c p) n -> p c n", p=128), [128, 16, 512])
                        for tt in range(NT):
                            bk, rbk = nb()
                            for c in range(16):
                                p.op("pe", lambda bk=bk, c=c, tt=tt: E["pe"].matmul(
                                    bk[:, :], xT[:, c, tt * 128:(tt + 1) * 128], wv[:, c, :],
                                    start=(c == 0), stop=(c == 15)),
                                    reads=[rw, r_xT], writes=[rbk], ms=(c == 15))
                            io = nxt("o", 4)
                            p.op("act", lambda bk=bk, io=io, tt=tt: E["act"].activation(
                                ost[io][:, :], bk[:, :], AF.Copy, scale=rcol[:, 3 * tt + 2:3 * tt + 3]),
                                reads=[rbk, r_rcol[tt]], writes=[r_ost[io]])
                            store(vd[t0 + tt * 128:t0 + (tt + 1) * 128, cg * 512:(cg + 1) * 512],
                                  ost[io][:, :], r_ost[io], r_scr["vd"])

                    wqv, rwq = load_w(wq[l, :, :].rearrange("(c p) n -> p c n", p=128), [128, 4, 2048])
                    for h in range(H):
                        for n in range(NCK):
                            tsl = slice(n * 512, (n + 1) * 512)
                            gsl = slice(t0 + n * 512, t0 + (n + 1) * 512)
                            for part in range(2):
                                cs = h * 256 + part * 128
                                bk, rbk = nb()
                                for c in range(4):
                                    p.op("pe", lambda bk=bk, c=c, cs=cs, tsl=tsl: E["pe"].matmul(
                                        bk[:, :], wqv[:, c, cs:cs + 128], cqT[:, c, tsl], start=(c == 0), stop=(c == 3)),
                                        reads=[rwq, r_cq[n]], writes=[rbk], ms=(c == 3))
                                io = nxt("o", 4)
                                if part == 0:
                                    p.op("dve", lambda bk=bk, io=io, tsl=tsl: E["dve"].tensor_tensor(
                                        ost[io][:, :], bk[:, :], rq_bc[:, tsl], ALU.mult),
                                        reads=[rbk, r_rq[n]], writes=[r_ost[io]])
                                    store(qnT[h, :, gsl], ost[io][:, :], r_ost[io], r_scr["qnT"])
                                else:
                                    it = nxt("t", 3)
                                    p.op("dve", lambda bk=bk, it=it, tsl=tsl: E["dve"].tensor_tensor(
                                        t32[it][:, :], bk[:, :], rq_bc[:, tsl], ALU.mult),
                                        reads=[rbk, r_rq[n]], writes=[r_t32[it]])
                                    p.op("dve", lambda it=it, io=io, gsl=gsl: E["dve"].tensor_tensor(
                                        ost[io][:, :], t32[it][:, :], t1[:, gsl], ALU.mult),
                                        reads=[r_t32[it]], writes=[r_ost[io]])
                                    bk2, rbk2 = nb()
                                    p.op("pe", lambda bk2=bk2, io=io: E["pe"].matmul(
                                        bk2[:, :], dfold[:, :], ost[io][:, :], start=True, stop=True),
                                        reads=[r_ost[io]], writes=[rbk2])
                                    io2 = nxt("o", 4)
                                    p.op("act", lambda bk2=bk2, io2=io2: E["act"].copy(ost[io2][:, :], bk2[:, :]),
                                         reads=[rbk2], writes=[r_ost[io2]])
                                    store(qrT[h, :, gsl], ost[io2][:, :], r_ost[io2], r_scr["qrT"])

                    i = nxt("w", 2)
                    wkk = wbuf[i][:, 0:2048].rearrange("p (a b) -> p a b", a=2)
                    wkv_ = wbuf[i][:, 2048:4096].rearrange("p (a b) -> p a b", a=2)
                    rwk = r_wb[i]
                    p.dma("pool", wkk, wkvk[l, :, :].rearrange("(c p) n -> p c n", p=128), writes=[rwk])
                    p.dma("pool", wkv_, wkvv[l, :, :].rearrange("(c p) n -> p c n", p=128), writes=[rwk])
                    for h in range(H):
                        for n in range(NCK):
                            tsl = slice(n * 512, (n + 1) * 512)
                            gsl = slice(t0 + n * 512, t0 + (n + 1) * 512)
                            bk, rbk = nb()
                            for c in range(2):
                                p.op("pe", lambda bk=bk, c=c, h=h, tsl=tsl: E["pe"].matmul(
                                    bk[:, :], wkk[:, c, h * 128:(h + 1) * 128], ckvT[:, c, tsl],
                                    start=(c == 0), stop=(c == 1)),
                                    reads=[rwk, r_ckv[n]], writes=[rbk], ms=(c == 1))
                            io = nxt("o", 4)
                            p.op("dve", lambda bk=bk, io=io, tsl=tsl: E["dve"].tensor_tensor(
                                ost[io][:, :], bk[:, :], rkv_bc[:, tsl], ALU.mult),
                                reads=[rbk, r_rkv[n]], writes=[r_ost[io]])
                            store(knT[h, :, gsl], ost[io][:, :], r_ost[io], r_scr["knT"])
                    for cg in range(2):
                        for tt in range(NT):
                            n = tt // 4
                            bk, rbk = nb()
                            for c in range(2):
                                p.op("pe", lambda bk=bk, c=c, tt=tt, cg=cg: E["pe"].matmul(
                                    bk[:, :], ckvT[:, c, tt * 128:(tt + 1) * 128], wkv_[:, c, cg * 512:(cg + 1) * 512],
                                    start=(c == 0), stop=(c == 1)),
                                    reads=[rwk, r_ckv[n]], writes=[rbk], ms=(c == 1))
                            io = nxt("o", 4)
                            p.op("act", lambda bk=bk, io=io, tt=tt: E["act"].activation(
                                ost[io][:, :], bk[:, :], AF.Copy, scale=rkvc[:, 3 * tt + 1:3 * tt + 2]),
                                reads=[rbk, r_rkvc[tt]], writes=[r_ost[io]])
                            store(vb[t0 + tt * 128:t0 + (tt + 1) * 128, cg * 512:(cg + 1) * 512],
                                  ost[io][:, :], r_ost[io], r_scr["vb"])
                    p.barrier()

        def phase_C(l):
            with ExitStack() as sc:
                KT = [sb("KT%d" % i, [128, S], BF, sc) for i in range(2)]
                QT = [sb("QT%d" % i, [128, S], BF, sc) for i in range(2)]
                QR = [sb("QR%d" % i, [128, S], BF, sc) for i in range(2)]
                QB = [sb("QB%d" % i, [128, S], BF, sc) for i in range(2)]
                GT = [sb("GT%d" % i, [128, S], BF, sc) for i in range(2)]
                VX = [sb("VX%d" % i, [128, NB, 129], BF, sc) for i in range(2)]
                YS = [sb("YS%d" % i, [128, S], BF, sc) for i in range(2)]
                ZK = sb("ZK", [128, S], BF, sc)
                r_in = [R(), R()]
                r_ys = [R(), R()]
                r_zk = R()
                NPT = 4
                PT = [sb("PT%d" % i, [128, 512], BF, sc) for i in range(NPT)]
                r_pt = [R() for _ in range(NPT)]
                NF = 6
                fa = [sb("fa%d" % i, [128, 128], F32, sc) for i in range(NF)]
                fo = [sb("fo%d" % i, [128, 128], F32, sc) for i in range(NF)]
                fn_ = [sb("fn%d" % i, [128, 128], BF, sc) for i in range(NF)]
                fs = [sb("fs%d" % i, [128, 8], F32, sc) for i in range(NF)]
                r_f = [R() for _ in range(NF)]
                NS = 3
                SB_ = [ps("pS%d" % i, [128, 512], F32, sc) for i in range(NS)]
                r_S = [R() for _ in range(NS)]
                OB = [ps("pO%d" % i, [128, 512], F32, sc) for i in range(4)]
                r_O = [R() for _ in range(4)]
                TP = ps("pT", [128, 1024], BF, sc)
                r_T = R()
                ctr = {"s": 0, "p": 0, "o": 0, "f": 0, "t": 0}
                LOOK = 2

                def nxt(k, n):
                    i = ctr[k]
                    ctr[k] = (i + 1) % n
                    return i

                for i in range(2):
                    p.op("pool", lambda i=i: E["pool"].memset(VX[i][:, :, 128:129], 1.0), writes=[r_in[i]])
                    p.op("pool", lambda i=i: E["pool"].memset(QT[i][64:128, :], 0.0), writes=[r_in[i]])
                    p.op("pool", lambda i=i: E["pool"].memset(QB[i][0:64, :], 0.0), writes=[r_in[i]])
                    if "c_ng" in dbg:
                        p.op("pool", lambda i=i: E["pool"].memset(YS[i][:, :], 0.0), writes=[r_ys[i]])
                p.dma("sp", ZK[:, :], zkT[:, :], reads=[r_scr["zkT"]], writes=[r_zk])

                heads = [("d", h) for h in range(H)] + [("m", h) for h in range(H)]
                if "c_heads" in dbg:
                    heads = [heads[i] for i in dbg["c_heads"]]
                n_groups = dbg.get("c_ng", NB // 2)

                def load_head(hi):
                    kind, h = heads[hi]
                    b = hi % 2
                    rr = r_in[b]
                    if kind == "d":
                        p.dma("sp", KT[b][:, :], kdT[h, :, :], reads=[r_scr["kdT"]], writes=[rr])
                        p.dma("sp", QT[b][0:64, :], qdT[h, 0:64, :], reads=[r_scr["qdT"]], writes=[rr])
                        p.dma("sp", QB[b][64:128, :], qdT[h, 64:128, :], reads=[r_scr["qdT"]], writes=[rr])
                        p.dma("sp", VX[b][:, :, 0:128], vd[:, h * 128:(h + 1) * 128].rearrange("(b p) d -> p b d", p=128),
                              reads=[r_scr["vd"]], writes=[rr])
                        p.dma("sp", GT[b][:, :], gT[h, :, :], reads=[r_scr["gT"]], writes=[rr])
                    else:
                        p.dma("sp", KT[b][:, :], knT[h, :, :], reads=[r_scr["knT"]], writes=[rr])
                        p.dma("sp", QT[b][:, :], qnT[h, :, :], reads=[r_scr["qnT"]], writes=[rr])
                        p.dma("sp", QR[b][:, :], qrT[h, :, :], reads=[r_scr["qrT"]], writes=[rr])
                        p.dma("sp", VX[b][:, :, 0:128], vb[:, h * 128:(h + 1) * 128].rearrange("(b p) d -> p b d", p=128),
                              reads=[r_scr["vb"]], writes=[rr])
                        p.dma("sp", GT[b][:, :], gT[8 + h, :, :], reads=[r_scr["gT"]], writes=[rr])

                def finalize_gen(hi, lb, ob, rob):
                    kind, h = heads[hi]
                    b = hi % 2
                    f = nxt("f", NF)
                    rf = r_f[f]
                    qs = slice(lb * 128, (lb + 1) * 128)
                    if kind == "d":
                        p.op("dve", lambda: E["dve"].reciprocal(fs[f][:, 0:1], ob[:, 128:129]),
                             reads=[rob], writes=[rf])
                        p.op("dve", lambda: E["dve"].reciprocal(fs[f][:, 1:2], ob[:, 384:385]),
                             reads=[rob], writes=[rf])
                        p.op("dve", lambda: E["dve"].tensor_tensor(
                            fs[f][:, 2:3], fs[f][:, 1:2], lam[:, 4 * l + 1:4 * l + 2], ALU.mult),
                            reads=[rf], writes=[rf])
                        p.op("dve", lambda: E["dve"].tensor_scalar(
                            fa[f][:, :], ob[:, 0:128], fs[f][:, 0:1], None, ALU.mult),
                            reads=[rob, rf], writes=[rf])
                        p.op("dve", lambda: E["dve"].tensor_scalar(
                            fo[f][:, :], ob[:, 256:384], fs[f][:, 2:3], None, ALU.mult),
                            reads=[rob, rf], writes=[rf])
                        p.op("dve", lambda: E["dve"].tensor_tensor(
                            fo[f][:, :], fo[f][:, :], fa[f][:, :], ALU.add),
                            reads=[rf], writes=[rf])
                        p.op("dve", lambda: E["dve"].tensor_tensor(
                            fa[f][:, :], fo[f][:, :], fo[f][:, :], ALU.mult),
                            reads=[rf], writes=[rf])
                        p.op("dve", lambda: E["dve"].tensor_reduce(fs[f][:, 3:4], fa[f][:, :], AX.X, ALU.add),
                             reads=[rf], writes=[rf])
                        yield
                        yield
                        p.op("act", lambda: E["act"].activation(
                            fs[f][:, 4:5], fs[f][:, 3:4], AF.Sqrt, bias=epsb[:, 0:1], scale=1.0 / 128),
                            reads=[rf], writes=[rf])
                        yield
                        yield
                        p.op("dve", lambda: E["dve"].reciprocal(fs[f][:, 5:6], fs[f][:, 4:5]),
                             reads=[rf], writes=[rf])
                        p.op("dve", lambda: E["dve"].tensor_scalar(
                            fa[f][:, :], fo[f][:, :], fs[f][:, 5:6], None, ALU.mult),
                            reads=[rf], writes=[rf])
                        p.op("dve", lambda: E["dve"].tensor_tensor(
                            fn_[f][:, :], fa[f][:, :], gsub[:, l * 128:(l + 1) * 128], ALU.mult),
                            reads=[rf], writes=[rf])
                    else:
                        p.op("dve", lambda: E["dve"].reciprocal(fs[f][:, 0:1], ob[:, 128:129]),
                             reads=[rob], writes=[rf])
                        p.op("dve", lambda: E["dve"].tensor_scalar(
                            fn_[f][:, :], ob[:, 0:128], fs[f][:, 0:1], None, ALU.mult),
                            reads=[rob, rf], writes=[rf])
                    yield
                    yield
                    ts_ = nxt("t", 8)
                    p.op("pe", lambda: E["pe"].transpose(TP[:, ts_ * 128:(ts_ + 1) * 128], fn_[f][:, :], identb[:, :]),
                         reads=[rf], writes=[r_T])
                    yield
                    yield
                    p.op("dve", lambda: E["dve"].tensor_tensor(
                        YS[b][:, qs], TP[:, ts_ * 128:(ts_ + 1) * 128], GT[b][:, qs], ALU.mult),
                        reads=[r_T, r_in[b]], writes=[r_ys[b]])

                pending = []

                def advance(flush=False):
                    while True:
                        for g in list(pending):
                            try:
                                next(g)
                            except StopIteration:
                                pending.remove(g)
                        if not flush or not pending:
                            break

                steps = []
                for hi in range(len(heads)):
                    gs = 2 if heads[hi][0] == "d" else 4
                    ngr = (n_groups * 2) // gs
                    for G in range(ngr):
                        lbs = tuple(range(gs * G, gs * G + gs))
                        grp = {}
                        for t in range(lbs[-1] + 1):
                            steps.append(dict(hi=hi, G=G, t=t, lbs=lbs, grp=grp,
                                              first=(G == 0 and t == 0),
                                              last=(G == ngr - 1 and t == lbs[-1])))

                def emit_S(st):
                    hi, t, lbs = st["hi"], st["t"], st["lbs"]
                    kind, h = heads[hi]
                    b = hi % 2
                    act_l = [i for i, lb in enumerate(lbs) if lb >= t]
                    q0 = lbs[act_l[0]] * 128
                    N = len(act_l) * 128
                    ks = slice(t * 128, (t + 1) * 128)
                    isb = nxt("s", NS)
                    sbk, rsb = SB_[isb], r_S[isb]
                    st.update(act_l=act_l, N=N, sbk=sbk, rsb=rsb)
                    if kind == "d":
                        for m in range(2):
                            qsrc = QT[b] if m == 0 else QB[b]
                            p.op("pe", lambda m=m, qsrc=qsrc: E["pe"].matmul(
                                sbk[:, m * 256:m * 256 + N], KT[b][:, ks], qsrc[:, q0:q0 + N], start=True, stop=True),
                                reads=[r_in[b]], writes=[rsb], ms=(m == 1))
                    else:
                        p.op("pe", lambda: E["pe"].matmul(
                            sbk[:, 0:N], KT[b][:, ks], QT[b][:, q0:q0 + N], start=True, stop=False),
                            reads=[r_in[b]], writes=[rsb], ms=False)
                        p.op("pe", lambda: E["pe"].matmul(
                            sbk[:, 0:N], ZK[:, ks], QR[b][:, q0:q0 + N], start=False, stop=True),
                            reads=[r_in[b], r_zk], writes=[rsb])

                def emit_rest(st):
                    hi, t, lbs, grp = st["hi"], st["t"], st["lbs"], st["grp"]
                    kind, h = heads[hi]
                    b = hi % 2
                    nm = 2 if kind == "d" else 1
                    scale = SC_DIFF if kind == "d" else SC_MLA
                    act_l, N, sbk, rsb = st["act_l"], st["N"], st["sbk"], st["rsb"]
                    if t == 0:
                        grp["ob"] = []
                        for lb in lbs:
                            io = nxt("o", 4)
                            grp["ob"].append((OB[io], r_O[io]))
                    ip = nxt("p", NPT)
                    pt, rpt = PT[ip], r_pt[ip]
                    if kind == "d":
                        sv = sbk[:, :].rearrange("p (m n) -> p m n", m=2)[:, :, 0:N]
                        ptv = pt[:, :].rearrange("p (m n) -> p m n", m=2)
                        pv_ = ptv[:, :, 0:N]
                    else:
                        sv = sbk[:, 0:N]
                        pv_ = pt[:, 0:N]
                    p.op("act", lambda: E["act"].activation(pv_, sv, AF.Exp, scale=scale),
                         reads=[rsb], writes=[rpt])
                    for i in act_l:
                        lb = lbs[i]
                        co = (i - act_l[0]) * 128
                        if kind == "d" and (t == lb or t == lb - 1):
                            kd = 0 if t == lb else 1
                            p.op("dve", lambda co=co, kd=kd: E["dve"].tensor_tensor(
                                ptv[:, :, co:co + 128], ptv[:, :, co:co + 128],
                                eb[:, 2 * h:2 * h + 2, kd, :], ALU.mult),
                                reads=[rpt], writes=[rpt])
                        elif kind == "m" and t == lb:
                            p.op("pool", lambda co=co: E["pool"].memset(pt[64:128, co:co + 64], 0.0),
                                 reads=[rpt], writes=[rpt])
                    advance()
                    for i in act_l:
                        lb = lbs[i]
                        co = (i - act_l[0]) * 128
                        ob, rob = grp["ob"][i]
                        for m in range(nm):
                            lhs = ptv[:, m, co:co + 128] if kind == "d" else pt[:, co:co + 128]
                            p.op("pe", lambda lhs=lhs, ob=ob, m=m, lb=lb: E["pe"].matmul(
                                ob[:, m * 256:m * 256 + 129], lhs, VX[b][:, t, :],
                                start=(t == 0 and m == 0), stop=(t == lb), skip_group_check=True),
                                reads=[rpt, r_in[b]], writes=[rob], ms=(m == nm - 1))
                        if t == lb:
                            pending.append(finalize_gen(hi, lb, ob, rob))

                load_head(0)
                for i in range(min(LOOK, len(steps))):
                    emit_S(steps[i])
                for i, st in enumerate(steps):
                    if st["first"] and st["hi"] + 1 < len(heads):
                        load_head(st["hi"] + 1)
                    if i + LOOK < len(steps):
                        emit_S(steps[i + LOOK])
                    emit_rest(st)
                    if st["last"]:
                        advance(flush=True)
                        kind, h = heads[st["hi"]]
                        b = st["hi"] % 2
                        tile_idx = h if kind == "d" else 8 + h
                        p.dma("sp", yT[tile_idx, :, :], YS[b][:, :], reads=[r_ys[b]], writes=[r_scr["yT"]])
                p.barrier()

        def phase_D(l, xsrc, last):
            with ExitStack() as sd:
                wo_sb = sb("wo_sb", [128, 16, D], BF, sd); r_wo = R()
                ys = [sb("ysD%d" % i, [128, 16, 512], BF, sd) for i in range(2)]
                r_y = [R(), R()]
                xs = [sb("xsD%d" % i, [128, D], F32, sd) for i in range(3)]
                r_x = [R() for _ in range(3)]
                junk = sb("junkD", [128, D], BF, sd); r_junk = R()
                gfin = sb("gfin", [128, D], F32, sd); r_g = R()
                fsd = sb("fsd", [128, 3 * NB], F32, sd); r_fs = [R() for _ in range(NB)]
                bank = [ps("pd%d" % i, [128, 512], F32, sd) for i in range(8)]
                r_bank = [R() for _ in range(8)]
                bi = [0]
                for cg in range(4):
                    p.dma("pool", wo_sb[:, :, cg * 512:(cg + 1) * 512],
                          wo[l, :, cg * 512:(cg + 1) * 512].rearrange("(c p) n -> p c n", p=128), writes=[r_wo])
                if last:
                    p.dma("sp", gfin[:, :], gfin_in[:, :], writes=[r_g])
                for tt in range(NB):
                    n = tt // 4
                    yb = n % 2
                    if tt % 4 == 0:
                        p.dma("sp", ys[yb][:, :, :], yT[:, :, n * 512:(n + 1) * 512].rearrange("c p n -> p c n"),
                              reads=[r_scr["yT"]], writes=[r_y[yb]])
                    xb = tt % 3
                    p.dma("sp", xs[xb][:, :], xsrc[tt * 128:(tt + 1) * 128, :],
                          reads=([r_scr["x1"]] if l > 0 else []), writes=[r_x[xb]])
                    for dg in range(4):
                        i = bi[0]
                        bi[0] = (i + 1) % 8
                        bk, rbk = bank[i], r_bank[i]
                        for c in range(16):
                            p.op("pe", lambda bk=bk, c=c, yb=yb, tt=tt, dg=dg: E["pe"].matmul(
                                bk[:, :], ys[yb][:, c, (tt % 4) * 128:(tt % 4 + 1) * 128],
                                wo_sb[:, c, dg * 512:(dg + 1) * 512], start=(c == 0), stop=(c == 15)),
                                reads=[r_y[yb], r_wo], writes=[rbk], ms=(c == 15))
                        p.op("dve", lambda bk=bk, xb=xb, dg=dg: E["dve"].tensor_tensor(
                            xs[xb][:, dg * 512:(dg + 1) * 512], bk[:, :], xs[xb][:, dg * 512:(dg + 1) * 512], ALU.add),
                            reads=[rbk, r_x[xb]], writes=[r_x[xb]])
                    if not last:
                        p.dma("sp", x1[tt * 128:(tt + 1) * 128, :], xs[xb][:, :], reads=[r_x[xb]], writes=[r_scr["x1"]])
                    else:
                        c0 = 3 * tt
                        p.op("act", lambda xb=xb, c0=c0: E["act"].activation(
                            junk[:, :], xs[xb][:, :], AF.Square, accum_out=fsd[:, c0:c0 + 1]),
                            reads=[r_x[xb]], writes=[r_junk, r_fs[tt]])
                        p.op("act", lambda c0=c0: E["act"].activation(
                            fsd[:, c0 + 1:c0 + 2], fsd[:, c0:c0 + 1], AF.Sqrt, bias=epsb[:, 0:1], scale=1.0 / D),
                            reads=[r_fs[tt]], writes=[r_fs[tt]])
                        p.op("dve", lambda c0=c0: E["dve"].reciprocal(fsd[:, c0 + 2:c0 + 3], fsd[:, c0 + 1:c0 + 2]),
                             reads=[r_fs[tt]], writes=[r_fs[tt]])
                        p.op("act", lambda xb=xb, c0=c0: E["act"].activation(
                            xs[xb][:, :], xs[xb][:, :], AF.Copy, scale=fsd[:, c0 + 2:c0 + 3]),
                            reads=[r_x[xb], r_fs[tt]], writes=[r_x[xb]])
                        p.op("dve", lambda xb=xb: E["dve"].tensor_tensor(
                            xs[xb][:, :], xs[xb][:, :], gfin[:, :], ALU.mult),
                            reads=[r_x[xb], r_g], writes=[r_x[xb]])
                        p.dma("sp", out_d[tt * 128:(tt + 1) * 128, :], xs[xb][:, :], reads=[r_x[xb]], writes=[r_scr["x1"]])
                p.barrier()

        done = False
        for l in range(DEPTH):
            xsrc = x_in if l == 0 else x1
            for half in range(S // HALF):
                if not dbg.get("skipA"):
                    phase_A(l, half, xsrc)
            if stop_after == ("A", l):
                done = True
                break
            phase_C(l)
            if stop_after == ("C", l):
                done = True
                break
            phase_D(l, xsrc, last=(l == DEPTH - 1))
            if stop_after == ("D", l):
                done = True
                break
        p.barrier()
        build_program.ninst = p.ninst
    return nc


def _rel_bucket_np(rel):
    nb = 16
    max_exact = 8
    ret = (rel > 0).astype(np.int32) * nb
    n = np.abs(rel)
    nf = np.maximum(n, 1).astype(np.float32)
    large = max_exact + (np.log(nf / max_exact) / math.log(128 / max_exact) * (nb - max_exact)).astype(np.int32)
    large = np.minimum(large, nb - 1)
    return ret + np.where(n < max_exact, n, large)


def prepare_inputs(x, norm_g, w_in, diff_lambda, diff_subln_g, mla_q_norm_g, mla_w_q_b,
                   mla_kv_norm_g, mla_w_kv_b, w_out, rel_bias, final_norm_g):
    f = np.float32
    x = np.asarray(x, f)
    w_in = np.asarray(w_in, f)
    kr0 = 3840
    swap = np.concatenate([np.arange(32, 64), np.arange(0, 32)])
    colsF = np.concatenate([np.arange(0, 2048), np.arange(3072, 3840), kr0 + np.arange(64), kr0 + swap,
                            np.arange(3904, 5952)])
    assert colsF.size == NFT * 128
    wF = np.ascontiguousarray(w_in[:, :, colsF])
    wT = np.ascontiguousarray(w_in[:, :, 2048:3072])
    wqb = np.asarray(mla_w_q_b, f)
    cq_cols = []
    for h in range(H):
        b0 = h * 192
        cq_cols += [b0 + np.arange(128), b0 + 128 + np.arange(64), b0 + 128 + swap]
    wq = np.ascontiguousarray(wqb[:, :, np.concatenate(cq_cols)])
    wkvb = np.asarray(mla_w_kv_b, f)
    kc = np.concatenate([h * 256 + np.arange(128) for h in range(H)])
    vc = np.concatenate([h * 256 + 128 + np.arange(128) for h in range(H)])
    wkvk = np.ascontiguousarray(wkvb[:, :, kc])
    wkvv = np.ascontiguousarray(wkvb[:, :, vc])
    wo = np.ascontiguousarray(np.asarray(w_out, f))

    def colmajor(v, nchunk):
        v = np.asarray(v, f).reshape(DEPTH, nchunk, 128)
        return np.ascontiguousarray(v.transpose(2, 0, 1).reshape(128, DEPTH * nchunk))

    def bcast(v):
        v = np.asarray(v, f).reshape(1, -1)
        return np.ascontiguousarray(np.broadcast_to(v, (128, v.shape[1])))

    rb = np.asarray(rel_bias, f)
    k = np.arange(128)[:, None]
    q = np.arange(128)[None, :]
    idx = np.stack([_rel_bucket_np(k - q), _rel_bucket_np(k - 128 - q)], 0)
    bt = rb[idx]
    bt = np.ascontiguousarray(bt.transpose(1, 3, 0, 2).reshape(128, 16 * 2 * 128))
    cfar = bcast(rb[15, :])
    pos = np.arange(S, dtype=np.float32)
    inv_freq = (10000.0 ** (-np.arange(0, 64, 2, dtype=np.float32) / 64)).astype(np.float32)
    ang = pos[:, None] * inv_freq[None, :]
    cos, sin = np.cos(ang).astype(f).T, np.sin(ang).astype(f).T
    t1 = np.ascontiguousarray(np.concatenate([cos, cos, -sin, sin], 0))
    identf = np.eye(128, dtype=f)
    identb = np.eye(128, dtype=f).astype(ml_dtypes.bfloat16)
    dfold = (np.arange(128)[:, None] % 64 == np.arange(128)[None, :] % 64).astype(f).astype(ml_dtypes.bfloat16)
    shared = dict(
        wF=wF, wT=wT, wq=wq, wkvk=wkvk, wkvv=wkvv, wo=wo,
        g_in=colmajor(norm_g, 16), gbc=bcast(np.asarray(norm_g, f).reshape(-1)), gq=colmajor(mla_q_norm_g, 4), gkv=colmajor(mla_kv_norm_g, 2),
        gsub=bcast(np.asarray(diff_subln_g, f).reshape(-1)), gfin=bcast(final_norm_g),
        dlam=bcast(np.asarray(diff_lambda, f).reshape(-1)), cfar=cfar, bt=bt, t1=t1,
        identf=identf, identb=identb, dfold=dfold)
    in_maps = []
    for c in range(8):
        m = dict(shared)
        m["x"] = np.ascontiguousarray(x[c % 4])
        in_maps.append(m)
    return in_maps


_NC_CACHE = {}


def kernel(**inputs):
    in_maps = prepare_inputs(**inputs)
    if "nc" not in _NC_CACHE:
        _NC_CACHE["nc"] = build_program()
    res = run_bass_kernel_spmd(_NC_CACHE["nc"], in_maps, core_ids=list(range(8)))
    out = np.stack([np.asarray(res.results[c]["out"], dtype=np.float32) for c in range(4)], 0)
    return out
```

```python
import math
from contextlib import ExitStack

import numpy as np
import ml_dtypes

import concourse.bass as bass
import concourse.mybir as mybir
from concourse.bass_utils import run_bass_kernel_spmd

F32 = mybir.dt.float32
BF = mybir.dt.bfloat16
AF = mybir.ActivationFunctionType
ALU = mybir.AluOpType
AX = mybir.AxisListType

D = 2048
S = 4096
NB = S // 128
DEPTH = 2
H = 8
EPS = 1e-6
NFT = 39
HALF = 2048
SC_DIFF = 64 ** -0.5
SC_MLA = 192 ** -0.5
SAME_ENG_SYNC = True


class R:
    __slots__ = ("w", "r", "name")

    def __init__(self, name=""):
        self.w = None
        self.r = {}
        self.name = name


class Prog:
    NDS = 24

    def __init__(self, nc, es):
        self.nc = nc
        self.E = {"pe": nc.tensor, "act": nc.scalar, "dve": nc.vector, "pool": nc.gpsimd, "sp": nc.sync}
        self.csem = {e: es.enter_context(nc.semaphore("c_" + e)) for e in ("pe", "act", "dve", "pool")}
        self.ccnt = {e: 0 for e in self.csem}
        self.dsem = [es.enter_context(nc.semaphore("d%d" % i)) for i in range(self.NDS)]
        self.dcnt = [0] * self.NDS
        self.dnext = 0
        self.waited = {}
        self.ninst = 0

    def _sem(self, tag):
        return self.csem[tag[1]] if tag[0] == "c" else self.dsem[tag[1]]

    def _wait(self, eng, tag):
        key = (eng, tag[0], tag[1])
        if self.waited.get(key, 0) >= tag[2]:
            return
        self.E[eng].wait_ge(self._sem(tag), tag[2])
        self.waited[key] = tag[2]

    def _deps(self, eng, reads, writes):
        deps = {}
        for r in reads:
            if r.w is not None:
                k = (r.w[0], r.w[1])
                deps[k] = max(deps.get(k, 0), r.w[2])
        for w in writes:
            if w.w is not None:
                k = (w.w[0], w.w[1])
                deps[k] = max(deps.get(k, 0), w.w[2])
            for k, v in w.r.items():
                deps[k] = max(deps.get(k, 0), v)
        for k in sorted(deps, key=str):
            if k[0] == "c" and k[1] == eng and (eng == "pe" or not SAME_ENG_SYNC):
                continue
            self._wait(eng, (k[0], k[1], deps[k]))

    def _mark(self, tag, reads, writes):
        k = (tag[0], tag[1])
        for r in reads:
            r.r[k] = max(r.r.get(k, 0), tag[2])
        for w in writes:
            w.w = tag
            w.r = {}

    def op(self, eng, fn, reads=(), writes=(), ms=True):
        self._deps(eng, reads, writes)
        inst = fn()
        self.ninst += 1
        if ms:
            self.ccnt[eng] += 1
            tag = ("c", eng, self.ccnt[eng])
            inst.then_inc(self.csem[eng], 1)
        else:
            tag = ("c", eng, self.ccnt[eng] + 1)
        self._mark(tag, reads, writes)
        return inst

    def dma(self, q, out, in_, reads=(), writes=()):
        i = self.dnext
        self.dnext = (self.dnext + 1) % self.NDS
        if self.dcnt[i] > 0:
            self._wait(q, ("d", i, self.dcnt[i]))
        self._deps(q, reads, writes)
        self.dcnt[i] += 16
        tag = ("d", i, self.dcnt[i])
        self.E[q].dma_start(out=out, in_=in_).then_inc(self.dsem[i], 16)
        self.ninst += 1
        self._mark(tag, reads, writes)

    def barrier(self, engines=("pe", "act", "dve", "pool", "sp")):
        tags = [("c", e, self.ccnt[e]) for e in self.csem if self.ccnt[e] > 0]
        tags += [("d", i, self.dcnt[i]) for i in range(self.NDS) if self.dcnt[i] > 0]
        for e in engines:
            for t in tags:
                if t[0] == "c" and t[1] == e:
                    continue
                self._wait(e, t)


def build_program(dbg=None):
    dbg = dbg or {}
    stop_after = dbg.get("stop", None)
    expose = dbg.get("expose", ())
    nc = bass.Bass("TRN2", target_bir_lowering=False)

    def din(name, shape, dt=F32):
        return nc.dram_tensor(name, list(shape), dt, kind="ExternalInput")

    def dscr(name, shape, dt=BF):
        if name in expose:
            return nc.dram_tensor(name, list(shape), dt, kind="ExternalOutput")
        return nc.dram_tensor(name, list(shape), dt)

    x_in = din("x", [S, D])
    wF = din("wF", [DEPTH, D, NFT * 128])
    wT = din("wT", [DEPTH, D, 1024])
    wq = din("wq", [DEPTH, 512, 2048])
    wkvk = din("wkvk", [DEPTH, 256, 1024])
    wkvv = din("wkvv", [DEPTH, 256, 1024])
    wo = din("wo", [DEPTH, D, D])
    g_in = din("g_in", [128, DEPTH * 16])
    gbc_in = din("gbc", [128, DEPTH * D])
    gq_in = din("gq", [128, DEPTH * 4])
    gkv_in = din("gkv", [128, DEPTH * 2])
    gsub_in = din("gsub", [128, DEPTH * 128])
    gfin_in = din("gfin", [128, D])
    dlam_in = din("dlam", [128, DEPTH * 256])
    cfar_in = din("cfar", [128, 16])
    bt_in = din("bt", [128, 16 * 2 * 128])
    t1_in = din("t1", [128, S])
    identf_in = din("identf", [128, 128])
    identb_in = din("identb", [128, 128], BF)
    dfold_in = din("dfold", [128, 128], BF)
    out_d = nc.dram_tensor("out", [S, D], F32, kind="ExternalOutput")

    qdT = dscr("qdT", [H, 128, S])
    kdT = dscr("kdT", [H, 128, S])
    qnT = dscr("qnT", [H, 128, S])
    qrT = dscr("qrT", [H, 128, S])
    knT = dscr("knT", [H, 128, S])
    zkT = dscr("zkT", [128, S])
    gT = dscr("gT", [16, 128, S])
    yT = dscr("yT", [16, 128, S])
    vd = dscr("vd", [S, 1024])
    vb = dscr("vb", [S, 1024])
    x1 = dscr("x1", [S, D], F32)

    with ExitStack() as es:
        p = Prog(nc, es)
        E = p.E

        uid = [0]

        def sb(name, shape, dt, stack=es):
            uid[0] += 1
            return stack.enter_context(nc.sbuf_tensor("s%d_%s" % (uid[0], name), list(shape), dt))

        def ps(name, shape, dt, stack):
            uid[0] += 1
            return stack.enter_context(nc.psum_tensor("p%d_%s" % (uid[0], name), list(shape), dt))

        identf = sb("identf", [128, 128], F32); r_const = R("const")
        identb = sb("identb", [128, 128], BF)
        dfold = sb("dfold", [128, 128], BF)
        onesb = sb("onesb", [128, 128], BF)
        onesf = sb("onesf", [128, 128], F32)
        t1 = sb("t1", [128, S], F32)
        eb = sb("eb", [128, 16, 2, 128], F32)
        gin = sb("gin", [128, DEPTH * 16], F32)
        gq = sb("gq", [128, DEPTH * 4], F32)
        gkv = sb("gkv", [128, DEPTH * 2], F32)
        gsub = sb("gsub", [128, DEPTH * 128], F32)
        dlam = sb("dlam", [128, DEPTH * 256], F32)
        cfar = sb("cfar", [128, 16], F32)
        epsb = sb("epsb", [128, 1], F32)
        lam = sb("lam", [128, 8], F32)
        lamtmp = sb("lamtmp", [128, DEPTH * 128], F32)
        for dst, src in ((identf, identf_in), (identb, identb_in), (dfold, dfold_in), (t1, t1_in),
                         (gin, g_in), (gq, gq_in), (gkv, gkv_in), (gsub, gsub_in), (dlam, dlam_in),
                         (cfar, cfar_in)):
            p.dma("sp", dst[:, :], src[:, :], writes=[r_const])
        p.dma("sp", eb[:, :, :, :].rearrange("p a b c -> p (a b c)"), bt_in[:, :], writes=[r_const])
        p.op("dve", lambda: E["dve"].memset(onesb[:, :], 1.0), writes=[r_const])
        p.op("dve", lambda: E["dve"].memset(onesf[:, :], 1.0), writes=[r_const])
        p.op("dve", lambda: E["dve"].memset(epsb[:, :], EPS), writes=[r_const])
        p.op("dve", lambda: E["dve"].tensor_scalar(cfar[:, :], cfar[:, :], -1.0, None, ALU.mult),
             reads=[r_const], writes=[r_const])
        for hm in range(16):
            p.op("act", lambda hm=hm: E["act"].activation(
                eb[:, hm, :, :], eb[:, hm, :, :], AF.Exp, bias=cfar[:, hm:hm + 1], scale=1.0),
                reads=[r_const], writes=[r_const])
        p.op("dve", lambda: E["dve"].memset(eb[64:128, :, 0, 0:64], 0.0), reads=[r_const], writes=[r_const])
        for l in range(DEPTH):
            lam_init = 0.8 - 0.6 * math.exp(-0.3 * l)
            dl = dlam[:, l * 256:(l + 1) * 256].rearrange("p (a b c) -> p a b c", a=2, b=2)
            pr = lamtmp[:, l * 128:(l + 1) * 128].rearrange("p (a c) -> p a c", a=2)
            p.op("dve", lambda dl=dl, pr=pr: E["dve"].tensor_tensor(pr, dl[:, :, 0, :], dl[:, :, 1, :], ALU.mult),
                 reads=[r_const], writes=[r_const])
            p.op("dve", lambda pr=pr, l=l: E["dve"].tensor_reduce(lam[:, 4 * l + 2:4 * l + 4], pr, AX.X, ALU.add),
                 reads=[r_const], writes=[r_const])
            p.op("act", lambda l=l: E["act"].activation(lam[:, 4 * l + 2:4 * l + 4], lam[:, 4 * l + 2:4 * l + 4], AF.Exp),
                 reads=[r_const], writes=[r_const])
            p.op("dve", lambda l=l: E["dve"].tensor_tensor(lam[:, 4 * l:4 * l + 1], lam[:, 4 * l + 2:4 * l + 3],
                                                           lam[:, 4 * l + 3:4 * l + 4], ALU.subtract),
                 reads=[r_const], writes=[r_const])
            p.op("dve", lambda l=l, li=lam_init: E["dve"].tensor_scalar(
                lam[:, 4 * l:4 * l + 1], lam[:, 4 * l:4 * l + 1], li, None, ALU.add),
                reads=[r_const], writes=[r_const])
            p.op("dve", lambda l=l: E["dve"].tensor_scalar(
                lam[:, 4 * l + 1:4 * l + 2], lam[:, 4 * l:4 * l + 1], -1.0, None, ALU.mult),
                reads=[r_const], writes=[r_const])
            p.op("dve", lambda l=l, li=lam_init: E["dve"].tensor_scalar(
                gsub[:, l * 128:(l + 1) * 128], gsub[:, l * 128:(l + 1) * 128], 1.0 - li, None, ALU.mult),
                reads=[r_const], writes=[r_const])
        p.barrier()

        r_scr = {n: R(n) for n in ("qdT", "kdT", "qnT", "qrT", "knT", "zkT", "gT", "yT", "vd", "vb", "x1")}

        def phase_A(l, half, xsrc):
            t0 = half * HALF
            NT = HALF // 128
            NCK = HALF // 512
            with ExitStack() as sa:
                xT = sb("xT", [128, 16, HALF], BF, sa); r_xT = R()
                wbuf = [sb("wbuf%d" % i, [128, 8192], BF, sa) for i in range(2)]
                r_wb = [R(), R()]
                rstd_bc = sb("rstd_bc", [128, HALF], F32, sa); r_rbc = R()
                rcol = sb("rcol", [128, 3 * NT], F32, sa); r_rcol = [R() for _ in range(NT)]
                bank = [ps("pa%d" % i, [128, 512], F32, sa) for i in range(6)]
                r_bank = [R() for _ in range(6)]
                bi = [0]

                def nb():
                    i = bi[0]
                    bi[0] = (i + 1) % 6
                    return bank[i], r_bank[i]

                with ExitStack() as s0:
                    xs = [sb("xs%d" % i, [128, D], F32, s0) for i in range(4)]
                    r_xs = [R() for _ in range(4)]
                    junk = sb("junk", [128, D], BF, s0); r_junk = R()
                    rb = sb("rb", [128, 128], F32, s0); r_rb = R()
                    g_bc = sb("g_bc", [128, D], F32, s0); r_gbc = R()
                    xg = [sb("xg%d" % i, [128, D], BF, s0) for i in range(2)]
                    r_xg = [R(), R()]
                    tpa = [ps("tpa%d" % i, [128, 1024], BF, s0) for i in range(2)]
                    r_tpa = [R(), R()]
                    tpc = [0]

                    def nxt_tp():
                        i = tpc[0]
                        tpc[0] = (i + 1) % 2
                        return i

                    p.dma("sp", g_bc[:, :], gbc_in[:, l * D:(l + 1) * D], writes=[r_gbc])
                    for tt in range(3):
                        p.dma("sp", xs[tt][:, :], xsrc[t0 + tt * 128:t0 + (tt + 1) * 128, :], writes=[r_xs[tt]])
                    rb2 = [rb, sb("rb2", [128, 128], F32, s0)]
                    r_rb2 = [r_rb, R()]
                    prev = None
                    for tt in range(NT):
                        b = tt % 4
                        if tt + 3 < NT:
                            p.dma("sp", xs[(tt + 3) % 4][:, :], xsrc[t0 + (tt + 3) * 128:t0 + (tt + 4) * 128, :],
                                  writes=[r_xs[(tt + 3) % 4]])
                        c0 = 3 * tt
                        p.op("act", lambda b=b, c0=c0: E["act"].activation(
                            junk[:, :], xs[b][:, :], AF.Square, accum_out=rcol[:, c0:c0 + 1]),
                            reads=[r_xs[b]], writes=[r_junk, r_rcol[tt]])
                        p.op("act", lambda c0=c0: E["act"].activation(
                            rcol[:, c0 + 1:c0 + 2], rcol[:, c0:c0 + 1], AF.Sqrt, bias=epsb[:, 0:1], scale=1.0 / D),
                            reads=[r_rcol[tt]], writes=[r_rcol[tt]])
                        xb_ = tt % 2
                        p.op("dve", lambda b=b, xb_=xb_: E["dve"].tensor_tensor(
                            xg[xb_][:, :], xs[b][:, :], g_bc[:, :], ALU.mult),
                            reads=[r_xs[b], r_gbc], writes=[r_xg[xb_]])
                        if prev is not None:
                            pbk, prbk, ptt = prev
                            p.op("act", lambda pbk=pbk, ptt=ptt: E["act"].copy(
                                rstd_bc[:, ptt * 128:(ptt + 1) * 128], pbk[:, 0:128]),
                                reads=[prbk], writes=[r_rbc])
                        for hb in range(2):
                            tk = nxt_tp()
                            for j in range(8):
                                c = hb * 8 + j
                                p.op("pe", lambda tk=tk, j=j, c=c, xb_=xb_: E["pe"].transpose(
                                    tpa[tk][:, j * 128:(j + 1) * 128], xg[xb_][:, c * 128:(c + 1) * 128], identb[:, :]),
                                    reads=[r_xg[xb_]], writes=[r_tpa[tk]], ms=(j == 7))
                            src = tpa[tk][:, :].rearrange("p (c n) -> p c n", c=8)
                            dst = xT[:, hb * 8:(hb + 1) * 8, tt * 128:(tt + 1) * 128]
                            if hb == 0:
                                p.op("act", lambda src=src, dst=dst: E["act"].copy(dst, src),
                                     reads=[r_tpa[tk]], writes=[R()])
                            else:
                                p.op("dve", lambda src=src, dst=dst: E["dve"].tensor_copy(dst, src),
                                     reads=[r_tpa[tk]], writes=[R()])
                        ri = tt % 2
                        p.op("dve", lambda c0=c0: E["dve"].reciprocal(rcol[:, c0 + 2:c0 + 3], rcol[:, c0 + 1:c0 + 2]),
                             reads=[r_rcol[tt]], writes=[r_rcol[tt]])
                        p.op("dve", lambda c0=c0, ri=ri: E["dve"].tensor_scalar(
                            rb2[ri][:, :], onesf[:, :], rcol[:, c0 + 2:c0 + 3], None, ALU.mult),
                            reads=[r_rcol[tt]], writes=[r_rb2[ri]])
                        bk, rbk = nb()
                        p.op("pe", lambda bk=bk, ri=ri: E["pe"].transpose(bk[:, 0:128], rb2[ri][:, :], identf[:, :]),
                             reads=[r_rb2[ri]], writes=[rbk])
                        prev = (bk, rbk, tt)
                    pbk, prbk, ptt = prev
                    p.op("act", lambda: E["act"].copy(rstd_bc[:, ptt * 128:(ptt + 1) * 128], pbk[:, 0:128]),
                         reads=[prbk], writes=[r_rbc])
                    p.barrier()

                with ExitStack() as s1:
                    cqT = sb("cqT", [128, 4, HALF], BF, s1); r_cq = [R() for _ in range(NCK)]
                    ckvT = sb("ckvT", [128, 2, HALF], BF, s1); r_ckv = [R() for _ in range(NCK)]
                    t32 = [sb("t32_%d" % i, [128, 512], F32, s1) for i in range(3)]
                    r_t32 = [R() for _ in range(3)]
                    ost = [sb("ost%d" % i, [128, 512], BF, s1) for i in range(4)]
                    r_ost = [R() for _ in range(4)]
                    sqb = [sb("sqb%d" % i, [128, 4, 512], BF, s1) for i in range(2)]
                    r_sqb = [R(), R()]
                    rq_bc = sb("rq_bc", [128, HALF], F32, s1); r_rq = [R() for _ in range(NCK)]
                    rkv_bc = sb("rkv_bc", [128, HALF], F32, s1); r_rkv = [R() for _ in range(NCK)]
                    rkvc = sb("rkvc", [128, 3 * NT], F32, s1); r_rkvc = [R() for _ in range(NT)]
                    sqkv = [sb("sqkv%d" % i, [128, 2, 512], BF, s1) for i in range(2)]
                    r_sqkv = [R(), R()]
                    ctr = {"t": 0, "o": 0, "s": 0, "w": 0}

                    def nxt(k, n):
                        i = ctr[k]
                        ctr[k] = (i + 1) % n
                        return i

                    def load_w(src_ap, view_shape):
                        i = nxt("w", 2)
                        n = 1
                        for d_ in view_shape[1:]:
                            n *= d_
                        flat = wbuf[i][:, 0:n]
                        if len(view_shape) == 3:
                            v = flat.rearrange("p (a b) -> p a b", a=view_shape[1])
                        else:
                            v = flat
                        p.dma("pool", v, src_ap, writes=[r_wb[i]])
                        return v, r_wb[i]

                    def store(dst_ap, src_ap, rsrc, rdst):
                        p.dma("sp", dst_ap, src_ap, reads=[rsrc], writes=[rdst])

                    groups = [list(range(g, min(g + 4, NFT))) for g in range(0, NFT, 4)]
                    for grp in groups:
                        c0 = grp[0] * 128
                        ncol = len(grp) * 128
                        wv, rw = load_w(
                            wF[l, :, c0:c0 + ncol].rearrange("(c p) n -> p c n", p=128), [128, 16, ncol])
                        for n in range(NCK):
                            tsl = slice(n * 512, (n + 1) * 512)
                            gsl = slice(t0 + n * 512, t0 + (n + 1) * 512)
                            for jj, ft in enumerate(grp):
                                bk, rbk = nb()
                                for c in range(16):
                                    p.op("pe", lambda bk=bk, c=c, jj=jj, tsl=tsl: E["pe"].matmul(
                                        bk[:, :], wv[:, c, jj * 128:(jj + 1) * 128], xT[:, c, tsl],
                                        start=(c == 0), stop=(c == 15)),
                                        reads=[rw, r_xT], writes=[rbk], ms=(c == 15))
                                if ft < 16 or ft == 22:
                                    io = nxt("o", 4)
                                    if ft == 22:
                                        it = nxt("t", 3)
                                        p.op("dve", lambda bk=bk, it=it, tsl=tsl: E["dve"].tensor_tensor(
                                            t32[it][:, :], bk[:, :], rstd_bc[:, tsl], ALU.mult),
                                            reads=[rbk, r_rbc], writes=[r_t32[it]])
                                        p.op("dve", lambda it=it, io=io, gsl=gsl: E["dve"].tensor_tensor(
                                            ost[io][:, :], t32[it][:, :], t1[:, gsl], ALU.mult),
                                            reads=[r_t32[it]], writes=[r_ost[io]])
                                        store(zkT[:, gsl], ost[io][:, :], r_ost[io], r_scr["zkT"])
                                    else:
                                        p.op("dve", lambda bk=bk, io=io, tsl=tsl: E["dve"].tensor_tensor(
                                            ost[io][:, :], bk[:, :], rstd_bc[:, tsl], ALU.mult),
                                            reads=[rbk, r_rbc], writes=[r_ost[io]])
                                        if ft < 8:
                                            store(qdT[ft, :, gsl], ost[io][:, :], r_ost[io], r_scr["qdT"])
                                        else:
                                            store(kdT[ft - 8, :, gsl], ost[io][:, :], r_ost[io], r_scr["kdT"])
                                elif ft >= 23:
                                    it = nxt("t", 3)
                                    io = nxt("o", 4)
                                    p.op("dve", lambda bk=bk, it=it, tsl=tsl: E["dve"].tensor_tensor(
                                        t32[it][:, :], bk[:, :], rstd_bc[:, tsl], ALU.mult),
                                        reads=[rbk, r_rbc], writes=[r_t32[it]])
                                    p.op("act", lambda it=it, io=io: E["act"].activation(
                                        ost[io][:, :], t32[it][:, :], AF.Silu),
                                        reads=[r_t32[it]], writes=[r_ost[io]])
                                    store(gT[ft - 23, :, gsl], ost[io][:, :], r_ost[io], r_scr["gT"])
                                else:
                                    it = nxt("t", 3)
                                    p.op("dve", lambda bk=bk, it=it, tsl=tsl: E["dve"].tensor_tensor(
                                        t32[it][:, :], bk[:, :], rstd_bc[:, tsl], ALU.mult),
                                        reads=[rbk, r_rbc], writes=[r_t32[it]])
                                    if ft < 20:
                                        j = ft - 16
                                        isq = n % 2
                                        p.op("act", lambda it=it, isq=isq, j=j: E["act"].activation(
                                            sqb[isq][:, j, :], t32[it][:, :], AF.Square),
                                            reads=[r_t32[it]], writes=[r_sqb[isq]])
                                        p.op("act", lambda it=it, j=j, tsl=tsl: E["act"].activation(
                                            cqT[:, j, tsl], t32[it][:, :], AF.Copy, scale=gq[:, l * 4 + j:l * 4 + j + 1]),
                                            reads=[r_t32[it]], writes=[r_cq[n]])
                                    else:
                                        j = ft - 20
                                        p.op("act", lambda it=it, j=j, n=n: E["act"].activation(
                                            sqkv[n % 2][:, j, :], t32[it][:, :], AF.Square),
                                            reads=[r_t32[it]], writes=[r_sqkv[n % 2]])
                                        p.op("act", lambda it=it, j=j, tsl=tsl: E["act"].activation(
                                            ckvT[:, j, tsl], t32[it][:, :], AF.Copy, scale=gkv[:, l * 2 + j:l * 2 + j + 1]),
                                            reads=[r_t32[it]], writes=[r_ckv[n]])
                            if grp[0] == 16:
                                isq = n % 2
                                bk, rbk = nb()
                                for j in range(4):
                                    p.op("pe", lambda bk=bk, j=j, isq=isq: E["pe"].matmul(
                                        bk[:, :], onesb[:, :], sqb[isq][:, j, :], start=(j == 0), stop=(j == 3)),
                                        reads=[r_sqb[isq]], writes=[rbk], ms=(j == 3))
                                p.op("act", lambda bk=bk, tsl=tsl: E["act"].activation(
                                    rq_bc[:, tsl], bk[:, :], AF.Sqrt, bias=epsb[:, 0:1], scale=1.0 / 512),
                                    reads=[rbk], writes=[r_rq[n]])
                                p.op("dve", lambda tsl=tsl: E["dve"].reciprocal(rq_bc[:, tsl], rq_bc[:, tsl]),
                                     reads=[r_rq[n]], writes=[r_rq[n]])
                            if grp[0] == 20:
                                isq = n % 2
                                bk, rbk = nb()
                                for j in range(2):
                                    p.op("pe", lambda bk=bk, j=j, isq=isq: E["pe"].matmul(
                                        bk[:, :], onesb[:, :], sqkv[isq][:, j, :], start=(j == 0), stop=(j == 1)),
                                        reads=[r_sqkv[isq]], writes=[rbk], ms=(j == 1))
                                p.op("act", lambda bk=bk, tsl=tsl: E["act"].activation(
                                    rkv_bc[:, tsl], bk[:, :], AF.Sqrt, bias=epsb[:, 0:1], scale=1.0 / 256),
                                    reads=[rbk], writes=[r_rkv[n]])
                                p.op("dve", lambda tsl=tsl: E["dve"].reciprocal(rkv_bc[:, tsl], rkv_bc[:, tsl]),
                                     reads=[r_rkv[n]], writes=[r_rkv[n]])
                                for q4 in range(4):
                                    tt = n * 4 + q4
                                    bk, rbk = nb()
                                    for j in range(2):
                                        p.op("pe", lambda bk=bk, j=j, q4=q4, isq=isq: E["pe"].matmul(
                                            bk[:, 0:1], sqkv[isq][:, j, q4 * 128:(q4 + 1) * 128], onesb[:, 0:1],
                                            start=(j == 0), stop=(j == 1)),
                                            reads=[r_sqkv[isq]], writes=[rbk], ms=(j == 1))
                                    c0 = 3 * tt
                                    p.op("act", lambda bk=bk, c0=c0: E["act"].activation(
                                        rkvc[:, c0:c0 + 1], bk[:, 0:1], AF.Sqrt, bias=epsb[:, 0:1], scale=1.0 / 256),
                                        reads=[rbk], writes=[r_rkvc[tt]])
                                    p.op("dve", lambda c0=c0: E["dve"].reciprocal(rkvc[:, c0 + 1:c0 + 2], rkvc[:, c0:c0 + 1]),
                                         reads=[r_rkvc[tt]], writes=[r_rkvc[tt]])

                    for cg in range(2):
                        wv, rw = load_w(
                            wT[l, :, cg * 512:(cg + 1) * 512].rearrange("(c p) n -> p c n", p=128), [128, 16, 512])
                        for tt in range(NT):
                            bk, rbk = nb()
                            for c in range(16):
                                p.op("pe", lambda bk=bk, c=c, tt=tt: E["pe"].matmul(
                                    bk[:, :], xT[:, c, tt * 128:(tt + 1) * 128], wv[:, c, :],
                                    start=(c == 0), stop=(c == 15)),
                                    reads=[rw, r_xT], writes=[rbk], ms=(c == 15))
                            io = nxt("o", 4)
                            p.op("act", lambda bk=bk, io=io, tt=tt: E["act"].activation(
                                ost[io][:, :], bk[:, :], AF.Copy, scale=rcol[:, 3 * tt + 2:3 * tt + 3]),
                                reads=[rbk, r_rcol[tt]], writes=[r_ost[io]])
                            store(vd[t0 + tt * 128:t0 + (tt + 1) * 128, cg * 512:(cg + 1) * 512],
                                  ost[io][:, :], r_ost[io], r_scr["vd"])

                    wqv, rwq = load_w(wq[l, :, :].rearrange("(c p) n -> p c n", p=128), [128, 4, 2048])
                    for h in range(H):
                        for n in range(NCK):
                            tsl = slice(n * 512, (n + 1) * 512)
                            gsl = slice(t0 + n * 512, t0 + (n + 1) * 512)
                            for part in range(2):
                                cs = h * 256 + part * 128
                                bk, rbk = nb()
                                for c in range(4):
                                    p.op("pe", lambda bk=bk, c=c, cs=cs, tsl=tsl: E["pe"].matmul(
                                        bk[:, :], wqv[:, c, cs:cs + 128], cqT[:, c, tsl], start=(c == 0), stop=(c == 3)),
                                        reads=[rwq, r_cq[n]], writes=[rbk], ms=(c == 3))
                                io = nxt("o", 4)
                                if part == 0:
                                    p.op("dve", lambda bk=bk, io=io, tsl=tsl: E["dve"].tensor_tensor(
                                        ost[io][:, :], bk[:, :], rq_bc[:, tsl], ALU.mult),
                                        reads=[rbk, r_rq[n]], writes=[r_ost[io]])
                                    store(qnT[h, :, gsl], ost[io][:, :], r_ost[io], r_scr["qnT"])
                                else:
                                    it = nxt("t", 3)
                                    p.op("dve", lambda bk=bk, it=it, tsl=tsl: E["dve"].tensor_tensor(
                                        t32[it][:, :], bk[:, :], rq_bc[:, tsl], ALU.mult),
                                        reads=[rbk, r_rq[n]], writes=[r_t32[it]])
                                    p.op("dve", lambda it=it, io=io, gsl=gsl: E["dve"].tensor_tensor(
                                        ost[io][:, :], t32[it][:, :], t1[:, gsl], ALU.mult),
                                        reads=[r_t32[it]], writes=[r_ost[io]])
                                    bk2, rbk2 = nb()
                                    p.op("pe", lambda bk2=bk2, io=io: E["pe"].matmul(
                                        bk2[:, :], dfold[:, :], ost[io][:, :], start=True, stop=True),
                                        reads=[r_ost[io]], writes=[rbk2])
                                    io2 = nxt("o", 4)
                                    p.op("act", lambda bk2=bk2, io2=io2: E["act"].copy(ost[io2][:, :], bk2[:, :]),
                                         reads=[rbk2], writes=[r_ost[io2]])
                                    store(qrT[h, :, gsl], ost[io2][:, :], r_ost[io2], r_scr["qrT"])

                    i = nxt("w", 2)
                    wkk = wbuf[i][:, 0:2048].rearrange("p (a b) -> p a b", a=2)
                    wkv_ = wbuf[i][:, 2048:4096].rearrange("p (a b) -> p a b", a=2)
                    rwk = r_wb[i]
                    p.dma("pool", wkk, wkvk[l, :, :].rearrange("(c p) n -> p c n", p=128), writes=[rwk])
                    p.dma("pool", wkv_, wkvv[l, :, :].rearrange("(c p) n -> p c n", p=128), writes=[rwk])
                    for h in range(H):
                        for n in range(NCK):
                            tsl = slice(n * 512, (n + 1) * 512)
                            gsl = slice(t0 + n * 512, t0 + (n + 1) * 512)
                            bk, rbk = nb()
                            for c in range(2):
                                p.op("pe", lambda bk=bk, c=c, h=h, tsl=tsl: E["pe"].matmul(
                                    bk[:, :], wkk[:, c, h * 128:(h + 1) * 128], ckvT[:, c, tsl],
                                    start=(c == 0), stop=(c == 1)),
                                    reads=[rwk, r_ckv[n]], writes=[rbk], ms=(c == 1))
                            io = nxt("o", 4)
                            p.op("dve", lambda bk=bk, io=io, tsl=tsl: E["dve"].tensor_tensor(
                                ost[io][:, :], bk[:, :], rkv_bc[:, tsl], ALU.mult),
                                reads=[rbk, r_rkv[n]], writes=[r_ost[io]])
                            store(knT[h, :, gsl], ost[io][:, :], r_ost[io], r_scr["knT"])
                    for cg in range(2):
                        for tt in range(NT):
                            n = tt // 4
                            bk, rbk = nb()
                            for c in range(2):
                                p.op("pe", lambda bk=bk, c=c, tt=tt, cg=cg: E["pe"].matmul(
                                    bk[:, :], ckvT[:, c, tt * 128:(tt + 1) * 128], wkv_[:, c, cg * 512:(cg + 1) * 512],
                                    start=(c == 0), stop=(c == 1)),
                                    reads=[rwk, r_ckv[n]], writes=[rbk], ms=(c == 1))
                            io = nxt("o", 4)
                            p.op("act", lambda bk=bk, io=io, tt=tt: E["act"].activation(
                                ost[io][:, :], bk[:, :], AF.Copy, scale=rkvc[:, 3 * tt + 1:3 * tt + 2]),
                                reads=[rbk, r_rkvc[tt]], writes=[r_ost[io]])
                            store(vb[t0 + tt * 128:t0 + (tt + 1) * 128, cg * 512:(cg + 1) * 512],
                                  ost[io][:, :], r_ost[io], r_scr["vb"])
                    p.barrier()

        def phase_C(l):
            with ExitStack() as sc:
                KT = [sb("KT%d" % i, [128, S], BF, sc) for i in range(2)]
                QT = [sb("QT%d" % i, [128, S], BF, sc) for i in range(2)]
                QR = [sb("QR%d" % i, [128, S], BF, sc) for i in range(2)]
                QB = [sb("QB%d" % i, [128, S], BF, sc) for i in range(2)]
                GT = [sb("GT%d" % i, [128, S], BF, sc) for i in range(2)]
                VX = [sb("VX%d" % i, [128, NB, 129], BF, sc) for i in range(2)]
                YS = [sb("YS%d" % i, [128, S], BF, sc) for i in range(2)]
                ZK = sb("ZK", [128, S], BF, sc)
                r_in = [R(), R()]
                r_ys = [R(), R()]
                r_zk = R()
                NPT = 4
                PT = [sb("PT%d" % i, [128, 512], BF, sc) for i in range(NPT)]
                r_pt = [R() for _ in range(NPT)]
                NF = 6
                fa = [sb("fa%d" % i, [128, 128], F32, sc) for i in range(NF)]
                fo = [sb("fo%d" % i, [128, 128], F32, sc) for i in range(NF)]
                fn_ = [sb("fn%d" % i, [128, 128], BF, sc) for i in range(NF)]
                fs = [sb("fs%d" % i, [128, 8], F32, sc) for i in range(NF)]
                r_f = [R() for _ in range(NF)]
                NS = 3
                SB_ = [ps("pS%d" % i, [128, 512], F32, sc) for i in range(NS)]
                r_S = [R() for _ in range(NS)]
                OB = [ps("pO%d" % i, [128, 512], F32, sc) for i in range(4)]
                r_O = [R() for _ in range(4)]
                TP = ps("pT", [128, 1024], BF, sc)
                r_T = R()
                ctr = {"s": 0, "p": 0, "o": 0, "f": 0, "t": 0}
                LOOK = 2

                def nxt(k, n):
                    i = ctr[k]
                    ctr[k] = (i + 1) % n
                    return i

                for i in range(2):
                    p.op("pool", lambda i=i: E["pool"].memset(VX[i][:, :, 128:129], 1.0), writes=[r_in[i]])
                    p.op("pool", lambda i=i: E["pool"].memset(QT[i][64:128, :], 0.0), writes=[r_in[i]])
                    p.op("pool", lambda i=i: E["pool"].memset(QB[i][0:64, :], 0.0), writes=[r_in[i]])
                    if "c_ng" in dbg:
                        p.op("pool", lambda i=i: E["pool"].memset(YS[i][:, :], 0.0), writes=[r_ys[i]])
                p.dma("sp", ZK[:, :], zkT[:, :], reads=[r_scr["zkT"]], writes=[r_zk])

                heads = [("d", h) for h in range(H)] + [("m", h) for h in range(H)]
                if "c_heads" in dbg:
                    heads = [heads[i] for i in dbg["c_heads"]]
                n_groups = dbg.get("c_ng", NB // 2)

                def load_head(hi):
                    kind, h = heads[hi]
                    b = hi % 2
                    rr = r_in[b]
                    if kind == "d":
                        p.dma("sp", KT[b][:, :], kdT[h, :, :], reads=[r_scr["kdT"]], writes=[rr])
                        p.dma("sp", QT[b][0:64, :], qdT[h, 0:64, :], reads=[r_scr["qdT"]], writes=[rr])
                        p.dma("sp", QB[b][64:128, :], qdT[h, 64:128, :], reads=[r_scr["qdT"]], writes=[rr])
                        p.dma("sp", VX[b][:, :, 0:128], vd[:, h * 128:(h + 1) * 128].rearrange("(b p) d -> p b d", p=128),
                              reads=[r_scr["vd"]], writes=[rr])
                        p.dma("sp", GT[b][:, :], gT[h, :, :], reads=[r_scr["gT"]], writes=[rr])
                    else:
                        p.dma("sp", KT[b][:, :], knT[h, :, :], reads=[r_scr["knT"]], writes=[rr])
                        p.dma("sp", QT[b][:, :], qnT[h, :, :], reads=[r_scr["qnT"]], writes=[rr])
                        p.dma("sp", QR[b][:, :], qrT[h, :, :], reads=[r_scr["qrT"]], writes=[rr])
                        p.dma("sp", VX[b][:, :, 0:128], vb[:, h * 128:(h + 1) * 128].rearrange("(b p) d -> p b d", p=128),
                              reads=[r_scr["vb"]], writes=[rr])
                        p.dma("sp", GT[b][:, :], gT[8 + h, :, :], reads=[r_scr["gT"]], writes=[rr])

                def finalize_gen(hi, lb, ob, rob):
                    kind, h = heads[hi]
                    b = hi % 2
                    f = nxt("f", NF)
                    rf = r_f[f]
                    qs = slice(lb * 128, (lb + 1) * 128)
                    if kind == "d":
                        p.op("dve", lambda: E["dve"].reciprocal(fs[f][:, 0:1], ob[:, 128:129]),
                             reads=[rob], writes=[rf])
                        p.op("dve", lambda: E["dve"].reciprocal(fs[f][:, 1:2], ob[:, 384:385]),
                             reads=[rob], writes=[rf])
                        p.op("dve", lambda: E["dve"].tensor_tensor(
                            fs[f][:, 2:3], fs[f][:, 1:2], lam[:, 4 * l + 1:4 * l + 2], ALU.mult),
                            reads=[rf], writes=[rf])
                        p.op("dve", lambda: E["dve"].tensor_scalar(
                            fa[f][:, :], ob[:, 0:128], fs[f][:, 0:1], None, ALU.mult),
                            reads=[rob, rf], writes=[rf])
                        p.op("dve", lambda: E["dve"].tensor_scalar(
                            fo[f][:, :], ob[:, 256:384], fs[f][:, 2:3], None, ALU.mult),
                            reads=[rob, rf], writes=[rf])
                        p.op("dve", lambda: E["dve"].tensor_tensor(
                            fo[f][:, :], fo[f][:, :], fa[f][:, :], ALU.add),
                            reads=[rf], writes=[rf])
                        p.op("dve", lambda: E["dve"].tensor_tensor(
                            fa[f][:, :], fo[f][:, :], fo[f][:, :], ALU.mult),
                            reads=[rf], writes=[rf])
                        p.op("dve", lambda: E["dve"].tensor_reduce(fs[f][:, 3:4], fa[f][:, :], AX.X, ALU.add),
                             reads=[rf], writes=[rf])
                        yield
                        yield
                        p.op("act", lambda: E["act"].activation(
                            fs[f][:, 4:5], fs[f][:, 3:4], AF.Ln, bias=epsb[:, 0:1], scale=1.0 / 128),
                            reads=[rf], writes=[rf])
                        yield
                        p.op("act", lambda: E["act"].activation(
                            fs[f][:, 5:6], fs[f][:, 4:5], AF.Exp, scale=-0.5),
                            reads=[rf], writes=[rf])
                        yield
                        yield
                        p.op("dve", lambda: E["dve"].tensor_scalar(
                            fa[f][:, :], fo[f][:, :], fs[f][:, 5:6], None, ALU.mult),
                            reads=[rf], writes=[rf])
                        p.op("dve", lambda: E["dve"].tensor_tensor(
                            fn_[f][:, :], fa[f][:, :], gsub[:, l * 128:(l + 1) * 128], ALU.mult),
                            reads=[rf], writes=[rf])
                    else:
                        p.op("dve", lambda: E["dve"].reciprocal(fs[f][:, 0:1], ob[:, 128:129]),
                             reads=[rob], writes=[rf])
                        p.op("dve", lambda: E["dve"].tensor_scalar(
                            fn_[f][:, :], ob[:, 0:128], fs[f][:, 0:1], None, ALU.mult),
                            reads=[rob, rf], writes=[rf])
                    yield
                    yield
                    ts_ = nxt("t", 8)
                    p.op("pe", lambda: E["pe"].transpose(TP[:, ts_ * 128:(ts_ + 1) * 128], fn_[f][:, :], identb[:, :]),
                         reads=[rf], writes=[r_T])
                    yield
                    yield
                    p.op("dve", lambda: E["dve"].tensor_tensor(
                        YS[b][:, qs], TP[:, ts_ * 128:(ts_ + 1) * 128], GT[b][:, qs], ALU.mult),
                        reads=[r_T, r_in[b]], writes=[r_ys[b]])

                pending = []

                def advance(flush=False):
                    while True:
                        for g in list(pending):
                            try:
                                next(g)
                            except StopIteration:
                                pending.remove(g)
                        if not flush or not pending:
                            break

                steps = []
                for hi in range(len(heads)):
                    gs = 2 if heads[hi][0] == "d" else 4
                    ngr = (n_groups * 2) // gs
                    for G in range(ngr):
                        lbs = tuple(range(gs * G, gs * G + gs))
                        grp = {}
                        for t in range(lbs[-1] + 1):
                            steps.append(dict(hi=hi, G=G, t=t, lbs=lbs, grp=grp,
                                              first=(G == 0 and t == 0),
                                              last=(G == ngr - 1 and t == lbs[-1])))

                def emit_S(st):
                    hi, t, lbs = st["hi"], st["t"], st["lbs"]
                    kind, h = heads[hi]
                    b = hi % 2
                    act_l = [i for i, lb in enumerate(lbs) if lb >= t]
                    q0 = lbs[act_l[0]] * 128
                    N = len(act_l) * 128
                    ks = slice(t * 128, (t + 1) * 128)
                    isb = nxt("s", NS)
                    sbk, rsb = SB_[isb], r_S[isb]
                    st.update(act_l=act_l, N=N, sbk=sbk, rsb=rsb)
                    if kind == "d":
                        for m in range(2):
                            qsrc = QT[b] if m == 0 else QB[b]
                            p.op("pe", lambda m=m, qsrc=qsrc: E["pe"].matmul(
                                sbk[:, m * 256:m * 256 + N], KT[b][:, ks], qsrc[:, q0:q0 + N], start=True, stop=True),
                                reads=[r_in[b]], writes=[rsb], ms=(m == 1))
                    else:
                        p.op("pe", lambda: E["pe"].matmul(
                            sbk[:, 0:N], KT[b][:, ks], QT[b][:, q0:q0 + N], start=True, stop=False),
                            reads=[r_in[b]], writes=[rsb], ms=False)
                        p.op("pe", lambda: E["pe"].matmul(
                            sbk[:, 0:N], ZK[:, ks], QR[b][:, q0:q0 + N], start=False, stop=True),
                            reads=[r_in[b], r_zk], writes=[rsb])

                def emit_rest(st):
                    hi, t, lbs, grp = st["hi"], st["t"], st["lbs"], st["grp"]
                    kind, h = heads[hi]
                    b = hi % 2
                    nm = 2 if kind == "d" else 1
                    scale = SC_DIFF if kind == "d" else SC_MLA
                    act_l, N, sbk, rsb = st["act_l"], st["N"], st["sbk"], st["rsb"]
                    if t == 0:
                        grp["ob"] = []
                        for lb in lbs:
                            io = nxt("o", 4)
                            grp["ob"].append((OB[io], r_O[io]))
                    ip = nxt("p", NPT)
                    pt, rpt = PT[ip], r_pt[ip]
                    if kind == "d":
                        sv = sbk[:, :].rearrange("p (m n) -> p m n", m=2)[:, :, 0:N]
                        ptv = pt[:, :].rearrange("p (m n) -> p m n", m=2)
                        pv_ = ptv[:, :, 0:N]
                    else:
                        sv = sbk[:, 0:N]
                        pv_ = pt[:, 0:N]
                    p.op("act", lambda: E["act"].activation(pv_, sv, AF.Exp, scale=scale),
                         reads=[rsb], writes=[rpt])
                    for i in act_l:
                        lb = lbs[i]
                        co = (i - act_l[0]) * 128
                        if kind == "d" and (t == lb or t == lb - 1):
                            kd = 0 if t == lb else 1
                            p.op("dve", lambda co=co, kd=kd: E["dve"].tensor_tensor(
                                ptv[:, :, co:co + 128], ptv[:, :, co:co + 128],
                                eb[:, 2 * h:2 * h + 2, kd, :], ALU.mult),
                                reads=[rpt], writes=[rpt])
                        elif kind == "m" and t == lb:
                            p.op("pool", lambda co=co: E["pool"].memset(pt[64:128, co:co + 64], 0.0),
                                 reads=[rpt], writes=[rpt])
                    advance()
                    for i in act_l:
                        lb = lbs[i]
                        co = (i - act_l[0]) * 128
                        ob, rob = grp["ob"][i]
                        for m in range(nm):
                            lhs = ptv[:, m, co:co + 128] if kind == "d" else pt[:, co:co + 128]
                            p.op("pe", lambda lhs=lhs, ob=ob, m=m, lb=lb: E["pe"].matmul(
                                ob[:, m * 256:m * 256 + 129], lhs, VX[b][:, t, :],
                                start=(t == 0 and m == 0), stop=(t == lb), skip_group_check=True),
                                reads=[rpt, r_in[b]], writes=[rob], ms=(m == nm - 1))
                        if t == lb:
                            pending.append(finalize_gen(hi, lb, ob, rob))

                load_head(0)
                for i in range(min(LOOK, len(steps))):
                    emit_S(steps[i])
                for i, st in enumerate(steps):
                    if st["first"] and st["hi"] + 1 < len(heads):
                        load_head(st["hi"] + 1)
                    if i + LOOK < len(steps):
                        emit_S(steps[i + LOOK])
                    emit_rest(st)
                    if st["last"]:
                        advance(flush=True)
                        kind, h = heads[st["hi"]]
                        b = st["hi"] % 2
                        tile_idx = h if kind == "d" else 8 + h
                        p.dma("sp", yT[tile_idx, :, :], YS[b][:, :], reads=[r_ys[b]], writes=[r_scr["yT"]])
                p.barrier()

        def phase_D(l, xsrc, last):
            with ExitStack() as sd:
                wo_sb = sb("wo_sb", [128, 16, D], BF, sd); r_wo = R()
                ys = [sb("ysD%d" % i, [128, 16, 512], BF, sd) for i in range(2)]
                r_y = [R(), R()]
                xs = [sb("xsD%d" % i, [128, D], F32, sd) for i in range(3)]
                r_x = [R() for _ in range(3)]
                junk = sb("junkD", [128, D], BF, sd); r_junk = R()
                gfin = sb("gfin", [128, D], F32, sd); r_g = R()
                fsd = sb("fsd", [128, 3 * NB], F32, sd); r_fs = [R() for _ in range(NB)]
                bank = [ps("pd%d" % i, [128, 512], F32, sd) for i in range(8)]
                r_bank = [R() for _ in range(8)]
                bi = [0]
                for cg in range(4):
                    p.dma("pool", wo_sb[:, :, cg * 512:(cg + 1) * 512],
                          wo[l, :, cg * 512:(cg + 1) * 512].rearrange("(c p) n -> p c n", p=128), writes=[r_wo])
                if last:
                    p.dma("sp", gfin[:, :], gfin_in[:, :], writes=[r_g])
                for tt in range(NB):
                    n = tt // 4
                    yb = n % 2
                    if tt % 4 == 0:
                        p.dma("sp", ys[yb][:, :, :], yT[:, :, n * 512:(n + 1) * 512].rearrange("c p n -> p c n"),
                              reads=[r_scr["yT"]], writes=[r_y[yb]])
                    xb = tt % 3
                    p.dma("sp", xs[xb][:, :], xsrc[tt * 128:(tt + 1) * 128, :],
                          reads=([r_scr["x1"]] if l > 0 else []), writes=[r_x[xb]])
                    for dg in range(4):
                        i = bi[0]
                        bi[0] = (i + 1) % 8
                        bk, rbk = bank[i], r_bank[i]
                        for c in range(16):
                            p.op("pe", lambda bk=bk, c=c, yb=yb, tt=tt, dg=dg: E["pe"].matmul(
                                bk[:, :], ys[yb][:, c, (tt % 4) * 128:(tt % 4 + 1) * 128],
                                wo_sb[:, c, dg * 512:(dg + 1) * 512], start=(c == 0), stop=(c == 15)),
                                reads=[r_y[yb], r_wo], writes=[rbk], ms=(c == 15))
                        p.op("dve", lambda bk=bk, xb=xb, dg=dg: E["dve"].tensor_tensor(
                            xs[xb][:, dg * 512:(dg + 1) * 512], bk[:, :], xs[xb][:, dg * 512:(dg + 1) * 512], ALU.add),
                            reads=[rbk, r_x[xb]], writes=[r_x[xb]])
                    if not last:
                        p.dma("sp", x1[tt * 128:(tt + 1) * 128, :], xs[xb][:, :], reads=[r_x[xb]], writes=[r_scr["x1"]])
                    else:
                        c0 = 3 * tt
                        p.op("act", lambda xb=xb, c0=c0: E["act"].activation(
                            junk[:, :], xs[xb][:, :], AF.Square, accum_out=fsd[:, c0:c0 + 1]),
                            reads=[r_x[xb]], writes=[r_junk, r_fs[tt]])
                        p.op("act", lambda c0=c0: E["act"].activation(
                            fsd[:, c0 + 1:c0 + 2], fsd[:, c0:c0 + 1], AF.Sqrt, bias=epsb[:, 0:1], scale=1.0 / D),
                            reads=[r_fs[tt]], writes=[r_fs[tt]])
                        p.op("dve", lambda c0=c0: E["dve"].reciprocal(fsd[:, c0 + 2:c0 + 3], fsd[:, c0 + 1:c0 + 2]),
                             reads=[r_fs[tt]], writes=[r_fs[tt]])
                        p.op("act", lambda xb=xb, c0=c0: E["act"].activation(
                            xs[xb][:, :], xs[xb][:, :], AF.Copy, scale=fsd[:, c0 + 2:c0 + 3]),
                            reads=[r_x[xb], r_fs[tt]], writes=[r_x[xb]])
                        p.op("dve", lambda xb=xb: E["dve"].tensor_tensor(
                            xs[xb][:, :], xs[xb][:, :], gfin[:, :], ALU.mult),
                            reads=[r_x[xb], r_g], writes=[r_x[xb]])
                        p.dma("sp", out_d[tt * 128:(tt + 1) * 128, :], xs[xb][:, :], reads=[r_x[xb]], writes=[r_scr["x1"]])
                p.barrier()

        done = False
        for l in range(DEPTH):
            xsrc = x_in if l == 0 else x1
            for half in range(S // HALF):
                if not dbg.get("skipA"):
                    phase_A(l, half, xsrc)
            if stop_after == ("A", l):
                done = True
                break
            phase_C(l)
            if stop_after == ("C", l):
                done = True
                break
            phase_D(l, xsrc, last=(l == DEPTH - 1))
            if stop_after == ("D", l):
                done = True
                break
        p.barrier()
        build_program.ninst = p.ninst
    return nc


def _rel_bucket_np(rel):
    nb = 16
    max_exact = 8
    ret = (rel > 0).astype(np.int32) * nb
    n = np.abs(rel)
    nf = np.maximum(n, 1).astype(np.float32)
    large = max_exact + (np.log(nf / max_exact) / math.log(128 / max_exact) * (nb - max_exact)).astype(np.int32)
    large = np.minimum(large, nb - 1)
    return ret + np.where(n < max_exact, n, large)


def prepare_inputs(x, norm_g, w_in, diff_lambda, diff_subln_g, mla_q_norm_g, mla_w_q_b,
                   mla_kv_norm_g, mla_w_kv_b, w_out, rel_bias, final_norm_g):
    f = np.float32
    x = np.asarray(x, f)
    w_in = np.asarray(w_in, f)
    kr0 = 3840
    swap = np.concatenate([np.arange(32, 64), np.arange(0, 32)])
    colsF = np.concatenate([np.arange(0, 2048), np.arange(3072, 3840), kr0 + np.arange(64), kr0 + swap,
                            np.arange(3904, 5952)])
    assert colsF.size == NFT * 128
    wF = np.ascontiguousarray(w_in[:, :, colsF])
    wT = np.ascontiguousarray(w_in[:, :, 2048:3072])
    wqb = np.asarray(mla_w_q_b, f)
    cq_cols = []
    for h in range(H):
        b0 = h * 192
        cq_cols += [b0 + np.arange(128), b0 + 128 + np.arange(64), b0 + 128 + swap]
    wq = np.ascontiguousarray(wqb[:, :, np.concatenate(cq_cols)])
    wkvb = np.asarray(mla_w_kv_b, f)
    kc = np.concatenate([h * 256 + np.arange(128) for h in range(H)])
    vc = np.concatenate([h * 256 + 128 + np.arange(128) for h in range(H)])
    wkvk = np.ascontiguousarray(wkvb[:, :, kc])
    wkvv = np.ascontiguousarray(wkvb[:, :, vc])
    wo = np.ascontiguousarray(np.asarray(w_out, f))

    def colmajor(v, nchunk):
        v = np.asarray(v, f).reshape(DEPTH, nchunk, 128)
        return np.ascontiguousarray(v.transpose(2, 0, 1).reshape(128, DEPTH * nchunk))

    def bcast(v):
        v = np.asarray(v, f).reshape(1, -1)
        return np.ascontiguousarray(np.broadcast_to(v, (128, v.shape[1])))

    rb = np.asarray(rel_bias, f)
    k = np.arange(128)[:, None]
    q = np.arange(128)[None, :]
    idx = np.stack([_rel_bucket_np(k - q), _rel_bucket_np(k - 128 - q)], 0)
    bt = rb[idx]
    bt = np.ascontiguousarray(bt.transpose(1, 3, 0, 2).reshape(128, 16 * 2 * 128))
    cfar = bcast(rb[15, :])
    pos = np.arange(S, dtype=np.float32)
    inv_freq = (10000.0 ** (-np.arange(0, 64, 2, dtype=np.float32) / 64)).astype(np.float32)
    ang = pos[:, None] * inv_freq[None, :]
    cos, sin = np.cos(ang).astype(f).T, np.sin(ang).astype(f).T
    t1 = np.ascontiguousarray(np.concatenate([cos, cos, -sin, sin], 0))
    identf = np.eye(128, dtype=f)
    identb = np.eye(128, dtype=f).astype(ml_dtypes.bfloat16)
    dfold = (np.arange(128)[:, None] % 64 == np.arange(128)[None, :] % 64).astype(f).astype(ml_dtypes.bfloat16)
    shared = dict(
        wF=wF, wT=wT, wq=wq, wkvk=wkvk, wkvv=wkvv, wo=wo,
        g_in=colmajor(norm_g, 16), gbc=bcast(np.asarray(norm_g, f).reshape(-1)), gq=colmajor(mla_q_norm_g, 4), gkv=colmajor(mla_kv_norm_g, 2),
        gsub=bcast(np.asarray(diff_subln_g, f).reshape(-1)), gfin=bcast(final_norm_g),
        dlam=bcast(np.asarray(diff_lambda, f).reshape(-1)), cfar=cfar, bt=bt, t1=t1,
        identf=identf, identb=identb, dfold=dfold)
    in_maps = []
    for c in range(8):
        m = dict(shared)
        m["x"] = np.ascontiguousarray(x[c % 4])
        in_maps.append(m)
    return in_maps


_NC_CACHE = {}


def kernel(**inputs):
    in_maps = prepare_inputs(**inputs)
    if "nc" not in _NC_CACHE:
        _NC_CACHE["nc"] = build_program()
    res = run_bass_kernel_spmd(_NC_CACHE["nc"], in_maps, core_ids=list(range(8)))
    out = np.stack([np.asarray(res.results[c]["out"], dtype=np.float32) for c in range(4)], 0)
    return out
```

```python
import math
from contextlib import ExitStack

import numpy as np
import ml_dtypes

import concourse.bass as bass
import concourse.mybir as mybir
from concourse.bass_utils import run_bass_kernel_spmd

F32 = mybir.dt.float32
BF = mybir.dt.bfloat16
AF = mybir.ActivationFunctionType
ALU = mybir.AluOpType
AX = mybir.AxisListType

D = 2048
S = 4096
NB = S // 128
DEPTH = 2
H = 8
EPS = 1e-6
NFT = 39
HALF = 2048
SC_DIFF = 64 ** -0.5
SC_MLA = 192 ** -0.5
SAME_ENG_SYNC = True


class R:
    __slots__ = ("w", "r", "name")

    def __init__(self, name=""):
        self.w = None
        self.r = {}
        self.name = name


class Prog:
    NDS = 24

    def __init__(self, nc, es):
        self.nc = nc
        self.E = {"pe": nc.tensor, "act": nc.scalar, "dve": nc.vector, "pool": nc.gpsimd, "sp": nc.sync}
        self.csem = {e: es.enter_context(nc.semaphore("c_" + e)) for e in ("pe", "act", "dve", "pool")}
        self.ccnt = {e: 0 for e in self.csem}
        self.dsem = [es.enter_context(nc.semaphore("d%d" % i)) for i in range(self.NDS)]
        self.dcnt = [0] * self.NDS
        self.dnext = 0
        self.waited = {}
        self.ninst = 0
        self.es = es
        self.xsem = []

    def _sem(self, tag):
        if tag[0] == "c":
            return self.csem[tag[1]]
        if tag[0] == "x":
            return self.xsem[tag[1]]
        return self.dsem[tag[1]]

    def _wait(self, eng, tag):
        key = (eng, tag[0], tag[1])
        if self.waited.get(key, 0) >= tag[2]:
            return
        self.E[eng].wait_ge(self._sem(tag), tag[2])
        self.waited[key] = tag[2]

    def _deps(self, eng, reads, writes):
        deps = {}
        for r in reads:
            if r.w is not None:
                k = (r.w[0], r.w[1])
                deps[k] = max(deps.get(k, 0), r.w[2])
        for w in writes:
            if w.w is not None:
                k = (w.w[0], w.w[1])
                deps[k] = max(deps.get(k, 0), w.w[2])
            for k, v in w.r.items():
                deps[k] = max(deps.get(k, 0), v)
        for k in sorted(deps, key=str):
            if k[0] == "c" and k[1] == eng and (eng == "pe" or not SAME_ENG_SYNC):
                continue
            self._wait(eng, (k[0], k[1], deps[k]))

    def _mark(self, tag, reads, writes):
        k = (tag[0], tag[1])
        for r in reads:
            r.r[k] = max(r.r.get(k, 0), tag[2])
        for w in writes:
            w.w = tag
            w.r = {}

    def op(self, eng, fn, reads=(), writes=(), ms=True):
        self._deps(eng, reads, writes)
        inst = fn()
        self.ninst += 1
        if ms:
            self.ccnt[eng] += 1
            tag = ("c", eng, self.ccnt[eng])
            inst.then_inc(self.csem[eng], 1)
        else:
            tag = ("c", eng, self.ccnt[eng] + 1)
        self._mark(tag, reads, writes)
        return inst

    def dma(self, q, out, in_, reads=(), writes=()):
        if q == "pool":
            self._deps(q, reads, writes)
            sem = self.es.enter_context(self.nc.semaphore("x%d" % len(self.xsem)))
            self.xsem.append(sem)
            tag = ("x", len(self.xsem) - 1, 16)
            self.E[q].dma_start(out=out, in_=in_).then_inc(sem, 16)
            self.ninst += 1
            self._mark(tag, reads, writes)
            return
        i = self.dnext
        self.dnext = (self.dnext + 1) % self.NDS
        if self.dcnt[i] > 0:
            self._wait(q, ("d", i, self.dcnt[i]))
        self._deps(q, reads, writes)
        self.dcnt[i] += 16
        tag = ("d", i, self.dcnt[i])
        self.E[q].dma_start(out=out, in_=in_).then_inc(self.dsem[i], 16)
        self.ninst += 1
        self._mark(tag, reads, writes)

    def barrier(self, engines=("pe", "act", "dve", "pool", "sp")):
        tags = [("c", e, self.ccnt[e]) for e in self.csem if self.ccnt[e] > 0]
        tags += [("d", i, self.dcnt[i]) for i in range(self.NDS) if self.dcnt[i] > 0]
        tags += [("x", i, 16) for i in range(len(self.xsem))]
        for e in engines:
            for t in tags:
                if t[0] == "c" and t[1] == e:
                    continue
                self._wait(e, t)


def build_program(dbg=None):
    dbg = dbg or {}
    stop_after = dbg.get("stop", None)
    expose = dbg.get("expose", ())
    nc = bass.Bass("TRN2", target_bir_lowering=False)

    def din(name, shape, dt=F32):
        return nc.dram_tensor(name, list(shape), dt, kind="ExternalInput")

    def dscr(name, shape, dt=BF):
        if name in expose:
            return nc.dram_tensor(name, list(shape), dt, kind="ExternalOutput")
        return nc.dram_tensor(name, list(shape), dt)

    x_in = din("x", [S, D])
    wF = din("wF", [DEPTH, D, NFT * 128])
    wT = din("wT", [DEPTH, D, 1024])
    wq = din("wq", [DEPTH, 512, 2048])
    wkvk = din("wkvk", [DEPTH, 256, 1024])
    wkvv = din("wkvv", [DEPTH, 256, 1024])
    wo = din("wo", [DEPTH, D, D])
    g_in = din("g_in", [128, DEPTH * 16])
    gbc_in = din("gbc", [128, DEPTH * D])
    gq_in = din("gq", [128, DEPTH * 4])
    gkv_in = din("gkv", [128, DEPTH * 2])
    gsub_in = din("gsub", [128, DEPTH * 128])
    gfin_in = din("gfin", [128, D])
    dlam_in = din("dlam", [128, DEPTH * 256])
    cfar_in = din("cfar", [128, 16])
    bt_in = din("bt", [128, 16 * 2 * 128])
    t1_in = din("t1", [128, S])
    identf_in = din("identf", [128, 128])
    identb_in = din("identb", [128, 128], BF)
    dfold_in = din("dfold", [128, 128], BF)
    out_d = nc.dram_tensor("out", [S, D], F32, kind="ExternalOutput")

    qdT = dscr("qdT", [H, 128, S])
    kdT = dscr("kdT", [H, 128, S])
    qnT = dscr("qnT", [H, 128, S])
    qrT = dscr("qrT", [H, 128, S])
    knT = dscr("knT", [H, 128, S])
    zkT = dscr("zkT", [128, S])
    gT = dscr("gT", [16, 128, S])
    yT = dscr("yT", [16, 128, S])
    vd = dscr("vd", [S, 1024])
    vb = dscr("vb", [S, 1024])
    x1 = dscr("x1", [S, D], F32)

    with ExitStack() as es:
        p = Prog(nc, es)
        E = p.E

        uid = [0]

        def sb(name, shape, dt, stack=es):
            uid[0] += 1
            return stack.enter_context(nc.sbuf_tensor("s%d_%s" % (uid[0], name), list(shape), dt))

        def ps(name, shape, dt, stack):
            uid[0] += 1
            return stack.enter_context(nc.psum_tensor("p%d_%s" % (uid[0], name), list(shape), dt))

        identf = sb("identf", [128, 128], F32); r_const = R("const")
        identb = sb("identb", [128, 128], BF)
        dfold = sb("dfold", [128, 128], BF)
        onesb = sb("onesb", [128, 128], BF)
        onesf = sb("onesf", [128, 128], F32)
        t1 = sb("t1", [128, S], F32)
        eb = sb("eb", [128, 16, 2, 128], F32)
        gin = sb("gin", [128, DEPTH * 16], F32)
        gq = sb("gq", [128, DEPTH * 4], F32)
        gkv = sb("gkv", [128, DEPTH * 2], F32)
        gsub = sb("gsub", [128, DEPTH * 128], F32)
        dlam = sb("dlam", [128, DEPTH * 256], F32)
        cfar = sb("cfar", [128, 16], F32)
        epsb = sb("epsb", [128, 1], F32)
        lam = sb("lam", [128, 8], F32)
        lamtmp = sb("lamtmp", [128, DEPTH * 128], F32)
        for dst, src in ((identf, identf_in), (identb, identb_in), (dfold, dfold_in), (t1, t1_in),
                         (gin, g_in), (gq, gq_in), (gkv, gkv_in), (gsub, gsub_in), (dlam, dlam_in),
                         (cfar, cfar_in)):
            p.dma("sp", dst[:, :], src[:, :], writes=[r_const])
        p.dma("sp", eb[:, :, :, :].rearrange("p a b c -> p (a b c)"), bt_in[:, :], writes=[r_const])
        p.op("dve", lambda: E["dve"].memset(onesb[:, :], 1.0), writes=[r_const])
        p.op("dve", lambda: E["dve"].memset(onesf[:, :], 1.0), writes=[r_const])
        p.op("dve", lambda: E["dve"].memset(epsb[:, :], EPS), writes=[r_const])
        p.op("dve", lambda: E["dve"].tensor_scalar(cfar[:, :], cfar[:, :], -1.0, None, ALU.mult),
             reads=[r_const], writes=[r_const])
        for hm in range(16):
            p.op("act", lambda hm=hm: E["act"].activation(
                eb[:, hm, :, :], eb[:, hm, :, :], AF.Exp, bias=cfar[:, hm:hm + 1], scale=1.0),
                reads=[r_const], writes=[r_const])
        p.op("dve", lambda: E["dve"].memset(eb[64:128, :, 0, 0:64], 0.0), reads=[r_const], writes=[r_const])
        for l in range(DEPTH):
            lam_init = 0.8 - 0.6 * math.exp(-0.3 * l)
            dl = dlam[:, l * 256:(l + 1) * 256].rearrange("p (a b c) -> p a b c", a=2, b=2)
            pr = lamtmp[:, l * 128:(l + 1) * 128].rearrange("p (a c) -> p a c", a=2)
            p.op("dve", lambda dl=dl, pr=pr: E["dve"].tensor_tensor(pr, dl[:, :, 0, :], dl[:, :, 1, :], ALU.mult),
                 reads=[r_const], writes=[r_const])
            p.op("dve", lambda pr=pr, l=l: E["dve"].tensor_reduce(lam[:, 4 * l + 2:4 * l + 4], pr, AX.X, ALU.add),
                 reads=[r_const], writes=[r_const])
            p.op("act", lambda l=l: E["act"].activation(lam[:, 4 * l + 2:4 * l + 4], lam[:, 4 * l + 2:4 * l + 4], AF.Exp),
                 reads=[r_const], writes=[r_const])
            p.op("dve", lambda l=l: E["dve"].tensor_tensor(lam[:, 4 * l:4 * l + 1], lam[:, 4 * l + 2:4 * l + 3],
                                                           lam[:, 4 * l + 3:4 * l + 4], ALU.subtract),
                 reads=[r_const], writes=[r_const])
            p.op("dve", lambda l=l, li=lam_init: E["dve"].tensor_scalar(
                lam[:, 4 * l:4 * l + 1], lam[:, 4 * l:4 * l + 1], li, None, ALU.add),
                reads=[r_const], writes=[r_const])
            p.op("dve", lambda l=l: E["dve"].tensor_scalar(
                lam[:, 4 * l + 1:4 * l + 2], lam[:, 4 * l:4 * l + 1], -1.0, None, ALU.mult),
                reads=[r_const], writes=[r_const])
            p.op("dve", lambda l=l, li=lam_init: E["dve"].tensor_scalar(
                gsub[:, l * 128:(l + 1) * 128], gsub[:, l * 128:(l + 1) * 128], 1.0 - li, None, ALU.mult),
                reads=[r_const], writes=[r_const])
        p.barrier()

        r_scr = {n: R(n) for n in ("qdT", "kdT", "qnT", "qrT", "knT", "zkT", "gT", "yT", "vd", "vb", "x1")}

        def phase_A(l, half, xsrc):
            t0 = half * HALF
            NT = HALF // 128
            NCK = HALF // 512
            with ExitStack() as sa:
                xT = sb("xT", [128, 16, HALF], BF, sa); r_xT = R()
                wbuf = [sb("wbuf%d" % i, [128, 8192], BF, sa) for i in range(2)]
                r_wb = [R(), R()]
                rstd_bc = sb("rstd_bc", [128, HALF], F32, sa); r_rbc = R()
                rcol = sb("rcol", [128, 3 * NT], F32, sa); r_rcol = [R() for _ in range(NT)]
                bank = [ps("pa%d" % i, [128, 512], F32, sa) for i in range(6)]
                r_bank = [R() for _ in range(6)]
                bi = [0]

                def nb():
                    i = bi[0]
                    bi[0] = (i + 1) % 6
                    return bank[i], r_bank[i]

                with ExitStack() as s0:
                    xs = [sb("xs%d" % i, [128, D], F32, s0) for i in range(4)]
                    r_xs = [R() for _ in range(4)]
                    junk = sb("junk", [128, D], BF, s0); r_junk = R()
                    rb = sb("rb", [128, 128], F32, s0); r_rb = R()
                    g_bc = sb("g_bc", [128, D], F32, s0); r_gbc = R()
                    xg = [sb("xg%d" % i, [128, D], BF, s0) for i in range(2)]
                    r_xg = [R(), R()]
                    tpa = [ps("tpa%d" % i, [128, 1024], BF, s0) for i in range(2)]
                    r_tpa = [R(), R()]
                    tpc = [0]

                    def nxt_tp():
                        i = tpc[0]
                        tpc[0] = (i + 1) % 2
                        return i

                    p.dma("sp", g_bc[:, :], gbc_in[:, l * D:(l + 1) * D], writes=[r_gbc])
                    for tt in range(3):
                        p.dma("sp", xs[tt][:, :], xsrc[t0 + tt * 128:t0 + (tt + 1) * 128, :], writes=[r_xs[tt]])
                    rb2 = [rb, sb("rb2", [128, 128], F32, s0)]
                    r_rb2 = [r_rb, R()]
                    prev = None
                    for tt in range(NT):
                        b = tt % 4
                        if tt + 3 < NT:
                            p.dma("sp", xs[(tt + 3) % 4][:, :], xsrc[t0 + (tt + 3) * 128:t0 + (tt + 4) * 128, :],
                                  writes=[r_xs[(tt + 3) % 4]])
                        c0 = 3 * tt
                        p.op("act", lambda b=b, c0=c0: E["act"].activation(
                            junk[:, :], xs[b][:, :], AF.Square, accum_out=rcol[:, c0:c0 + 1]),
                            reads=[r_xs[b]], writes=[r_junk, r_rcol[tt]])
                        p.op("act", lambda c0=c0: E["act"].activation(
                            rcol[:, c0 + 1:c0 + 2], rcol[:, c0:c0 + 1], AF.Sqrt, bias=epsb[:, 0:1], scale=1.0 / D),
                            reads=[r_rcol[tt]], writes=[r_rcol[tt]])
                        xb_ = tt % 2
                        p.op("dve", lambda b=b, xb_=xb_: E["dve"].tensor_tensor(
                            xg[xb_][:, :], xs[b][:, :], g_bc[:, :], ALU.mult),
                            reads=[r_xs[b], r_gbc], writes=[r_xg[xb_]])
                        if prev is not None:
                            pbk, prbk, ptt = prev
                            p.op("act", lambda pbk=pbk, ptt=ptt: E["act"].copy(
                                rstd_bc[:, ptt * 128:(ptt + 1) * 128], pbk[:, 0:128]),
                                reads=[prbk], writes=[r_rbc])
                        for hb in range(2):
                            tk = nxt_tp()
                            for j in range(8):
                                c = hb * 8 + j
                                p.op("pe", lambda tk=tk, j=j, c=c, xb_=xb_: E["pe"].transpose(
                                    tpa[tk][:, j * 128:(j + 1) * 128], xg[xb_][:, c * 128:(c + 1) * 128], identb[:, :]),
                                    reads=[r_xg[xb_]], writes=[r_tpa[tk]], ms=(j == 7))
                            src = tpa[tk][:, :].rearrange("p (c n) -> p c n", c=8)
                            dst = xT[:, hb * 8:(hb + 1) * 8, tt * 128:(tt + 1) * 128]
                            if hb == 0:
                                p.op("act", lambda src=src, dst=dst: E["act"].copy(dst, src),
                                     reads=[r_tpa[tk]], writes=[R()])
                            else:
                                p.op("dve", lambda src=src, dst=dst: E["dve"].tensor_copy(dst, src),
                                     reads=[r_tpa[tk]], writes=[R()])
                        ri = tt % 2
                        p.op("dve", lambda c0=c0: E["dve"].reciprocal(rcol[:, c0 + 2:c0 + 3], rcol[:, c0 + 1:c0 + 2]),
                             reads=[r_rcol[tt]], writes=[r_rcol[tt]])
                        p.op("dve", lambda c0=c0, ri=ri: E["dve"].tensor_scalar(
                            rb2[ri][:, :], onesf[:, :], rcol[:, c0 + 2:c0 + 3], None, ALU.mult),
                            reads=[r_rcol[tt]], writes=[r_rb2[ri]])
                        bk, rbk = nb()
                        p.op("pe", lambda bk=bk, ri=ri: E["pe"].transpose(bk[:, 0:128], rb2[ri][:, :], identf[:, :]),
                             reads=[r_rb2[ri]], writes=[rbk])
                        prev = (bk, rbk, tt)
                    pbk, prbk, ptt = prev
                    p.op("act", lambda: E["act"].copy(rstd_bc[:, ptt * 128:(ptt + 1) * 128], pbk[:, 0:128]),
                         reads=[prbk], writes=[r_rbc])
                    p.barrier()

                with ExitStack() as s1:
                    cqT = sb("cqT", [128, 4, HALF], BF, s1); r_cq = [R() for _ in range(NCK)]
                    ckvT = sb("ckvT", [128, 2, HALF], BF, s1); r_ckv = [R() for _ in range(NCK)]
                    t32 = [sb("t32_%d" % i, [128, 512], F32, s1) for i in range(3)]
                    r_t32 = [R() for _ in range(3)]
                    ost = [sb("ost%d" % i, [128, 512], BF, s1) for i in range(4)]
                    r_ost = [R() for _ in range(4)]
                    sqb = [sb("sqb%d" % i, [128, 4, 512], BF, s1) for i in range(2)]
                    r_sqb = [R(), R()]
                    rq_bc = sb("rq_bc", [128, HALF], F32, s1); r_rq = [R() for _ in range(NCK)]
                    rkv_bc = sb("rkv_bc", [128, HALF], F32, s1); r_rkv = [R() for _ in range(NCK)]
                    rkvc = sb("rkvc", [128, 3 * NT], F32, s1); r_rkvc = [R() for _ in range(NT)]
                    sqkv = [sb("sqkv%d" % i, [128, 2, 512], BF, s1) for i in range(2)]
                    r_sqkv = [R(), R()]
                    ctr = {"t": 0, "o": 0, "s": 0, "w": 0}

                    def nxt(k, n):
                        i = ctr[k]
                        ctr[k] = (i + 1) % n
                        return i

                    def load_w(src_ap, view_shape):
                        i = nxt("w", 2)
                        n = 1
                        for d_ in view_shape[1:]:
                            n *= d_
                        flat = wbuf[i][:, 0:n]
                        if len(view_shape) == 3:
                            v = flat.rearrange("p (a b) -> p a b", a=view_shape[1])
                        else:
                            v = flat
                        p.dma("pool", v, src_ap, writes=[r_wb[i]])
                        return v, r_wb[i]

                    def store(dst_ap, src_ap, rsrc, rdst):
                        p.dma("sp", dst_ap, src_ap, reads=[rsrc], writes=[rdst])

                    groups = [list(range(g, min(g + 4, NFT))) for g in range(0, NFT, 4)]
                    for grp in groups:
                        c0 = grp[0] * 128
                        ncol = len(grp) * 128
                        wv, rw = load_w(
                            wF[l, :, c0:c0 + ncol].rearrange("(c p) n -> p c n", p=128), [128, 16, ncol])
                        for n in range(NCK):
                            tsl = slice(n * 512, (n + 1) * 512)
                            gsl = slice(t0 + n * 512, t0 + (n + 1) * 512)
                            for jj, ft in enumerate(grp):
                                bk, rbk = nb()
                                for c in range(16):
                                    p.op("pe", lambda bk=bk, c=c, jj=jj, tsl=tsl: E["pe"].matmul(
                                        bk[:, :], wv[:, c, jj * 128:(jj + 1) * 128], xT[:, c, tsl],
                                        start=(c == 0), stop=(c == 15)),
                                        reads=[rw, r_xT], writes=[rbk], ms=(c == 15))
                                if ft < 16 or ft == 22:
                                    io = nxt("o", 4)
                                    if ft == 22:
                                        it = nxt("t", 3)
                                        p.op("dve", lambda bk=bk, it=it, tsl=tsl: E["dve"].tensor_tensor(
                                            t32[it][:, :], bk[:, :], rstd_bc[:, tsl], ALU.mult),
                                            reads=[rbk, r_rbc], writes=[r_t32[it]])
                                        p.op("dve", lambda it=it, io=io, gsl=gsl: E["dve"].tensor_tensor(
                                            ost[io][:, :], t32[it][:, :], t1[:, gsl], ALU.mult),
                                            reads=[r_t32[it]], writes=[r_ost[io]])
                                        store(zkT[:, gsl], ost[io][:, :], r_ost[io], r_scr["zkT"])
                                    else:
                                        p.op("dve", lambda bk=bk, io=io, tsl=tsl: E["dve"].tensor_tensor(
                                            ost[io][:, :], bk[:, :], rstd_bc[:, tsl], ALU.mult),
                                            reads=[rbk, r_rbc], writes=[r_ost[io]])
                                        if ft < 8:
                                            store(qdT[ft, :, gsl], ost[io][:, :], r_ost[io], r_scr["qdT"])
                                        else:
                                            store(kdT[ft - 8, :, gsl], ost[io][:, :], r_ost[io], r_scr["kdT"])
                                elif ft >= 23:
                                    it = nxt("t", 3)
                                    io = nxt("o", 4)
                                    p.op("dve", lambda bk=bk, it=it, tsl=tsl: E["dve"].tensor_tensor(
                                        t32[it][:, :], bk[:, :], rstd_bc[:, tsl], ALU.mult),
                                        reads=[rbk, r_rbc], writes=[r_t32[it]])
                                    p.op("act", lambda it=it, io=io: E["act"].activation(
                                        ost[io][:, :], t32[it][:, :], AF.Silu),
                                        reads=[r_t32[it]], writes=[r_ost[io]])
                                    store(gT[ft - 23, :, gsl], ost[io][:, :], r_ost[io], r_scr["gT"])
                                else:
                                    it = nxt("t", 3)
                                    p.op("dve", lambda bk=bk, it=it, tsl=tsl: E["dve"].tensor_tensor(
                                        t32[it][:, :], bk[:, :], rstd_bc[:, tsl], ALU.mult),
                                        reads=[rbk, r_rbc], writes=[r_t32[it]])
                                    if ft < 20:
                                        j = ft - 16
                                        isq = n % 2
                                        p.op("act", lambda it=it, isq=isq, j=j: E["act"].activation(
                                            sqb[isq][:, j, :], t32[it][:, :], AF.Square),
                                            reads=[r_t32[it]], writes=[r_sqb[isq]])
                                        p.op("act", lambda it=it, j=j, tsl=tsl: E["act"].activation(
                                            cqT[:, j, tsl], t32[it][:, :], AF.Copy, scale=gq[:, l * 4 + j:l * 4 + j + 1]),
                                            reads=[r_t32[it]], writes=[r_cq[n]])
                                    else:
                                        j = ft - 20
                                        p.op("act", lambda it=it, j=j, n=n: E["act"].activation(
                                            sqkv[n % 2][:, j, :], t32[it][:, :], AF.Square),
                                            reads=[r_t32[it]], writes=[r_sqkv[n % 2]])
                                        p.op("act", lambda it=it, j=j, tsl=tsl: E["act"].activation(
                                            ckvT[:, j, tsl], t32[it][:, :], AF.Copy, scale=gkv[:, l * 2 + j:l * 2 + j + 1]),
                                            reads=[r_t32[it]], writes=[r_ckv[n]])
                            if grp[0] == 16:
                                isq = n % 2
                                bk, rbk = nb()
                                for j in range(4):
                                    p.op("pe", lambda bk=bk, j=j, isq=isq: E["pe"].matmul(
                                        bk[:, :], onesb[:, :], sqb[isq][:, j, :], start=(j == 0), stop=(j == 3)),
                                        reads=[r_sqb[isq]], writes=[rbk], ms=(j == 3))
                                p.op("act", lambda bk=bk, tsl=tsl: E["act"].activation(
                                    rq_bc[:, tsl], bk[:, :], AF.Sqrt, bias=epsb[:, 0:1], scale=1.0 / 512),
                                    reads=[rbk], writes=[r_rq[n]])
                                p.op("dve", lambda tsl=tsl: E["dve"].reciprocal(rq_bc[:, tsl], rq_bc[:, tsl]),
                                     reads=[r_rq[n]], writes=[r_rq[n]])
                            if grp[0] == 20:
                                isq = n % 2
                                bk, rbk = nb()
                                for j in range(2):
                                    p.op("pe", lambda bk=bk, j=j, isq=isq: E["pe"].matmul(
                                        bk[:, :], onesb[:, :], sqkv[isq][:, j, :], start=(j == 0), stop=(j == 1)),
                                        reads=[r_sqkv[isq]], writes=[rbk], ms=(j == 1))
                                p.op("act", lambda bk=bk, tsl=tsl: E["act"].activation(
                                    rkv_bc[:, tsl], bk[:, :], AF.Sqrt, bias=epsb[:, 0:1], scale=1.0 / 256),
                                    reads=[rbk], writes=[r_rkv[n]])
                                p.op("dve", lambda tsl=tsl: E["dve"].reciprocal(rkv_bc[:, tsl], rkv_bc[:, tsl]),
                                     reads=[r_rkv[n]], writes=[r_rkv[n]])
                                for q4 in range(4):
                                    tt = n * 4 + q4
                                    bk, rbk = nb()
                                    for j in range(2):
                                        p.op("pe", lambda bk=bk, j=j, q4=q4, isq=isq: E["pe"].matmul(
                                            bk[:, 0:1], sqkv[isq][:, j, q4 * 128:(q4 + 1) * 128], onesb[:, 0:1],
                                            start=(j == 0), stop=(j == 1)),
                                            reads=[r_sqkv[isq]], writes=[rbk], ms=(j == 1))
                                    c0 = 3 * tt
                                    p.op("act", lambda bk=bk, c0=c0: E["act"].activation(
                                        rkvc[:, c0:c0 + 1], bk[:, 0:1], AF.Sqrt, bias=epsb[:, 0:1], scale=1.0 / 256),
                                        reads=[rbk], writes=[r_rkvc[tt]])
                                    p.op("dve", lambda c0=c0: E["dve"].reciprocal(rkvc[:, c0 + 1:c0 + 2], rkvc[:, c0:c0 + 1]),
                                         reads=[r_rkvc[tt]], writes=[r_rkvc[tt]])

                    for cg in range(2):
                        wv, rw = load_w(
                            wT[l, :, cg * 512:(cg + 1) * 512].rearrange("(c p) n -> p c n", p=128), [128, 16, 512])
                        for tt in range(NT):
                            bk, rbk = nb()
                            for c in range(16):
                                p.op("pe", lambda bk=bk, c=c, tt=tt: E["pe"].matmul(
                                    bk[:, :], xT[:, c, tt * 128:(tt + 1) * 128], wv[:, c, :],
                                    start=(c == 0), stop=(c == 15)),
                                    reads=[rw, r_xT], writes=[rbk], ms=(c == 15))
                            io = nxt("o", 4)
                            p.op("act", lambda bk=bk, io=io, tt=tt: E["act"].activation(
                                ost[io][:, :], bk[:, :], AF.Copy, scale=rcol[:, 3 * tt + 2:3 * tt + 3]),
                                reads=[rbk, r_rcol[tt]], writes=[r_ost[io]])
                            store(vd[t0 + tt * 128:t0 + (tt + 1) * 128, cg * 512:(cg + 1) * 512],
                                  ost[io][:, :], r_ost[io], r_scr["vd"])

                    wqv, rwq = load_w(wq[l, :, :].rearrange("(c p) n -> p c n", p=128), [128, 4, 2048])
                    for h in range(H):
                        for n in range(NCK):
                            tsl = slice(n * 512, (n + 1) * 512)
                            gsl = slice(t0 + n * 512, t0 + (n + 1) * 512)
                            for part in range(2):
                                cs = h * 256 + part * 128
                                bk, rbk = nb()
                                for c in range(4):
                                    p.op("pe", lambda bk=bk, c=c, cs=cs, tsl=tsl: E["pe"].matmul(
                                        bk[:, :], wqv[:, c, cs:cs + 128], cqT[:, c, tsl], start=(c == 0), stop=(c == 3)),
                                        reads=[rwq, r_cq[n]], writes=[rbk], ms=(c == 3))
                                io = nxt("o", 4)
                                if part == 0:
                                    p.op("dve", lambda bk=bk, io=io, tsl=tsl: E["dve"].tensor_tensor(
                                        ost[io][:, :], bk[:, :], rq_bc[:, tsl], ALU.mult),
                                        reads=[rbk, r_rq[n]], writes=[r_ost[io]])
                                    store(qnT[h, :, gsl], ost[io][:, :], r_ost[io], r_scr["qnT"])
                                else:
                                    it = nxt("t", 3)
                                    p.op("dve", lambda bk=bk, it=it, tsl=tsl: E["dve"].tensor_tensor(
                                        t32[it][:, :], bk[:, :], rq_bc[:, tsl], ALU.mult),
                                        reads=[rbk, r_rq[n]], writes=[r_t32[it]])
                                    p.op("dve", lambda it=it, io=io, gsl=gsl: E["dve"].tensor_tensor(
                                        ost[io][:, :], t32[it][:, :], t1[:, gsl], ALU.mult),
                                        reads=[r_t32[it]], writes=[r_ost[io]])
                                    bk2, rbk2 = nb()
                                    p.op("pe", lambda bk2=bk2, io=io: E["pe"].matmul(
                                        bk2[:, :], dfold[:, :], ost[io][:, :], start=True, stop=True),
                                        reads=[r_ost[io]], writes=[rbk2])
                                    io2 = nxt("o", 4)
                                    p.op("act", lambda bk2=bk2, io2=io2: E["act"].copy(ost[io2][:, :], bk2[:, :]),
                                         reads=[rbk2], writes=[r_ost[io2]])
                                    store(qrT[h, :, gsl], ost[io2][:, :], r_ost[io2], r_scr["qrT"])

                    i = nxt("w", 2)
                    wkk = wbuf[i][:, 0:2048].rearrange("p (a b) -> p a b", a=2)
                    wkv_ = wbuf[i][:, 2048:4096].rearrange("p (a b) -> p a b", a=2)
                    rwk = r_wb[i]
                    p.dma("pool", wkk, wkvk[l, :, :].rearrange("(c p) n -> p c n", p=128), writes=[rwk])
                    p.dma("pool", wkv_, wkvv[l, :, :].rearrange("(c p) n -> p c n", p=128), writes=[rwk])
                    for h in range(H):
                        for n in range(NCK):
                            tsl = slice(n * 512, (n + 1) * 512)
                            gsl = slice(t0 + n * 512, t0 + (n + 1) * 512)
                            bk, rbk = nb()
                            for c in range(2):
                                p.op("pe", lambda bk=bk, c=c, h=h, tsl=tsl: E["pe"].matmul(
                                    bk[:, :], wkk[:, c, h * 128:(h + 1) * 128], ckvT[:, c, tsl],
                                    start=(c == 0), stop=(c == 1)),
                                    reads=[rwk, r_ckv[n]], writes=[rbk], ms=(c == 1))
                            io = nxt("o", 4)
                            p.op("dve", lambda bk=bk, io=io, tsl=tsl: E["dve"].tensor_tensor(
                                ost[io][:, :], bk[:, :], rkv_bc[:, tsl], ALU.mult),
                                reads=[rbk, r_rkv[n]], writes=[r_ost[io]])
                            store(knT[h, :, gsl], ost[io][:, :], r_ost[io], r_scr["knT"])
                    for cg in range(2):
                        for tt in range(NT):
                            n = tt // 4
                            bk, rbk = nb()
                            for c in range(2):
                                p.op("pe", lambda bk=bk, c=c, tt=tt, cg=cg: E["pe"].matmul(
                                    bk[:, :], ckvT[:, c, tt * 128:(tt + 1) * 128], wkv_[:, c, cg * 512:(cg + 1) * 512],
                                    start=(c == 0), stop=(c == 1)),
                                    reads=[rwk, r_ckv[n]], writes=[rbk], ms=(c == 1))
                            io = nxt("o", 4)
                            p.op("act", lambda bk=bk, io=io, tt=tt: E["act"].activation(
                                ost[io][:, :], bk[:, :], AF.Copy, scale=rkvc[:, 3 * tt + 1:3 * tt + 2]),
                                reads=[rbk, r_rkvc[tt]], writes=[r_ost[io]])
                            store(vb[t0 + tt * 128:t0 + (tt + 1) * 128, cg * 512:(cg + 1) * 512],
                                  ost[io][:, :], r_ost[io], r_scr["vb"])
                    p.barrier()

        def phase_C(l):
            with ExitStack() as sc:
                KT = [sb("KT%d" % i, [128, S], BF, sc) for i in range(2)]
                QT = [sb("QT%d" % i, [128, S], BF, sc) for i in range(2)]
                QR = [sb("QR%d" % i, [128, S], BF, sc) for i in range(2)]
                QB = [sb("QB%d" % i, [128, S], BF, sc) for i in range(2)]
                GT = [sb("GT%d" % i, [128, S], BF, sc) for i in range(2)]
                VX = [sb("VX%d" % i, [128, NB, 129], BF, sc) for i in range(2)]
                YS = [sb("YS%d" % i, [128, S], BF, sc) for i in range(2)]
                ZK = sb("ZK", [128, S], BF, sc)
                r_in = [R(), R()]
                r_ys = [R(), R()]
                r_zk = R()
                NPT = 4
                PT = [sb("PT%d" % i, [128, 512], BF, sc) for i in range(NPT)]
                r_pt = [R() for _ in range(NPT)]
                NF = 6
                fa = [sb("fa%d" % i, [128, 128], F32, sc) for i in range(NF)]
                fo = [sb("fo%d" % i, [128, 128], F32, sc) for i in range(NF)]
                fn_ = [sb("fn%d" % i, [128, 128], BF, sc) for i in range(NF)]
                fs = [sb("fs%d" % i, [128, 8], F32, sc) for i in range(NF)]
                r_f = [R() for _ in range(NF)]
                NS = 3
                SB_ = [ps("pS%d" % i, [128, 512], F32, sc) for i in range(NS)]
                r_S = [R() for _ in range(NS)]
                OB = [ps("pO%d" % i, [128, 512], F32, sc) for i in range(4)]
                r_O = [R() for _ in range(4)]
                TP = ps("pT", [128, 1024], BF, sc)
                r_T = R()
                ctr = {"s": 0, "p": 0, "o": 0, "f": 0, "t": 0}
                LOOK = 2

                def nxt(k, n):
                    i = ctr[k]
                    ctr[k] = (i + 1) % n
                    return i

                for i in range(2):
                    p.op("pool", lambda i=i: E["pool"].memset(VX[i][:, :, 128:129], 1.0), writes=[r_in[i]])
                    p.op("pool", lambda i=i: E["pool"].memset(QT[i][64:128, :], 0.0), writes=[r_in[i]])
                    p.op("pool", lambda i=i: E["pool"].memset(QB[i][0:64, :], 0.0), writes=[r_in[i]])
                    if "c_ng" in dbg:
                        p.op("pool", lambda i=i: E["pool"].memset(YS[i][:, :], 0.0), writes=[r_ys[i]])
                p.dma("sp", ZK[:, :], zkT[:, :], reads=[r_scr["zkT"]], writes=[r_zk])

                heads = [("d", h) for h in range(H)] + [("m", h) for h in range(H)]
                if "c_heads" in dbg:
                    heads = [heads[i] for i in dbg["c_heads"]]
                n_groups = dbg.get("c_ng", NB // 2)

                def load_head(hi):
                    kind, h = heads[hi]
                    b = hi % 2
                    rr = r_in[b]
                    if kind == "d":
                        p.dma("sp", KT[b][:, :], kdT[h, :, :], reads=[r_scr["kdT"]], writes=[rr])
                        p.dma("sp", QT[b][0:64, :], qdT[h, 0:64, :], reads=[r_scr["qdT"]], writes=[rr])
                        p.dma("sp", QB[b][64:128, :], qdT[h, 64:128, :], reads=[r_scr["qdT"]], writes=[rr])
                        p.dma("sp", VX[b][:, :, 0:128], vd[:, h * 128:(h + 1) * 128].rearrange("(b p) d -> p b d", p=128),
                              reads=[r_scr["vd"]], writes=[rr])
                        p.dma("sp", GT[b][:, :], gT[h, :, :], reads=[r_scr["gT"]], writes=[rr])
                    else:
                        p.dma("sp", KT[b][:, :], knT[h, :, :], reads=[r_scr["knT"]], writes=[rr])
                        p.dma("sp", QT[b][:, :], qnT[h, :, :], reads=[r_scr["qnT"]], writes=[rr])
                        p.dma("sp", QR[b][:, :], qrT[h, :, :], reads=[r_scr["qrT"]], writes=[rr])
                        p.dma("sp", VX[b][:, :, 0:128], vb[:, h * 128:(h + 1) * 128].rearrange("(b p) d -> p b d", p=128),
                              reads=[r_scr["vb"]], writes=[rr])
                        p.dma("sp", GT[b][:, :], gT[8 + h, :, :], reads=[r_scr["gT"]], writes=[rr])

                def finalize_gen(hi, lb, ob, rob):
                    kind, h = heads[hi]
                    b = hi % 2
                    f = nxt("f", NF)
                    rf = r_f[f]
                    qs = slice(lb * 128, (lb + 1) * 128)
                    if kind == "d":
                        p.op("dve", lambda: E["dve"].reciprocal(fs[f][:, 0:1], ob[:, 128:129]),
                             reads=[rob], writes=[rf])
                        p.op("dve", lambda: E["dve"].reciprocal(fs[f][:, 1:2], ob[:, 384:385]),
                             reads=[rob], writes=[rf])
                        p.op("dve", lambda: E["dve"].tensor_tensor(
                            fs[f][:, 2:3], fs[f][:, 1:2], lam[:, 4 * l + 1:4 * l + 2], ALU.mult),
                            reads=[rf], writes=[rf])
                        p.op("dve", lambda: E["dve"].tensor_scalar(
                            fa[f][:, :], ob[:, 0:128], fs[f][:, 0:1], None, ALU.mult),
                            reads=[rob, rf], writes=[rf])
                        p.op("dve", lambda: E["dve"].tensor_scalar(
                            fo[f][:, :], ob[:, 256:384], fs[f][:, 2:3], None, ALU.mult),
                            reads=[rob, rf], writes=[rf])
                        p.op("dve", lambda: E["dve"].tensor_tensor(
                            fo[f][:, :], fo[f][:, :], fa[f][:, :], ALU.add),
                            reads=[rf], writes=[rf])
                        p.op("dve", lambda: E["dve"].tensor_tensor(
                            fa[f][:, :], fo[f][:, :], fo[f][:, :], ALU.mult),
                            reads=[rf], writes=[rf])
                        p.op("dve", lambda: E["dve"].tensor_reduce(fs[f][:, 3:4], fa[f][:, :], AX.X, ALU.add),
                             reads=[rf], writes=[rf])
                        yield
                        yield
                        p.op("act", lambda: E["act"].activation(
                            fs[f][:, 4:5], fs[f][:, 3:4], AF.Ln, bias=epsb[:, 0:1], scale=1.0 / 128),
                            reads=[rf], writes=[rf])
                        yield
                        p.op("act", lambda: E["act"].activation(
                            fs[f][:, 5:6], fs[f][:, 4:5], AF.Exp, scale=-0.5),
                            reads=[rf], writes=[rf])
                        yield
                        yield
                        p.op("dve", lambda: E["dve"].tensor_scalar(
                            fa[f][:, :], fo[f][:, :], fs[f][:, 5:6], None, ALU.mult),
                            reads=[rf], writes=[rf])
                        p.op("dve", lambda: E["dve"].tensor_tensor(
                            fn_[f][:, :], fa[f][:, :], gsub[:, l * 128:(l + 1) * 128], ALU.mult),
                            reads=[rf], writes=[rf])
                    else:
                        p.op("dve", lambda: E["dve"].reciprocal(fs[f][:, 0:1], ob[:, 128:129]),
                             reads=[rob], writes=[rf])
                        p.op("dve", lambda: E["dve"].tensor_scalar(
                            fn_[f][:, :], ob[:, 0:128], fs[f][:, 0:1], None, ALU.mult),
                            reads=[rob, rf], writes=[rf])
                    yield
                    yield
                    ts_ = nxt("t", 8)
                    p.op("pe", lambda: E["pe"].transpose(TP[:, ts_ * 128:(ts_ + 1) * 128], fn_[f][:, :], identb[:, :]),
                         reads=[rf], writes=[r_T])
                    yield
                    yield
                    p.op("dve", lambda: E["dve"].tensor_tensor(
                        YS[b][:, qs], TP[:, ts_ * 128:(ts_ + 1) * 128], GT[b][:, qs], ALU.mult),
                        reads=[r_T, r_in[b]], writes=[r_ys[b]])

                pending = []

                def advance(flush=False):
                    while True:
                        for g in list(pending):
                            try:
                                next(g)
                            except StopIteration:
                                pending.remove(g)
                        if not flush or not pending:
                            break

                steps = []
                for hi in range(len(heads)):
                    gs = 2 if heads[hi][0] == "d" else 4
                    ngr = (n_groups * 2) // gs
                    for G in range(ngr):
                        lbs = tuple(range(gs * G, gs * G + gs))
                        grp = {}
                        for t in range(lbs[-1] + 1):
                            steps.append(dict(hi=hi, G=G, t=t, lbs=lbs, grp=grp,
                                              first=(G == 0 and t == 0),
                                              last=(G == ngr - 1 and t == lbs[-1])))

                def emit_S(st):
                    hi, t, lbs = st["hi"], st["t"], st["lbs"]
                    kind, h = heads[hi]
                    b = hi % 2
                    act_l = [i for i, lb in enumerate(lbs) if lb >= t]
                    q0 = lbs[act_l[0]] * 128
                    N = len(act_l) * 128
                    ks = slice(t * 128, (t + 1) * 128)
                    isb = nxt("s", NS)
                    sbk, rsb = SB_[isb], r_S[isb]
                    st.update(act_l=act_l, N=N, sbk=sbk, rsb=rsb)
                    if kind == "d":
                        for m in range(2):
                            qsrc = QT[b] if m == 0 else QB[b]
                            p.op("pe", lambda m=m, qsrc=qsrc: E["pe"].matmul(
                                sbk[:, m * 256:m * 256 + N], KT[b][:, ks], qsrc[:, q0:q0 + N], start=True, stop=True),
                                reads=[r_in[b]], writes=[rsb], ms=(m == 1))
                    else:
                        p.op("pe", lambda: E["pe"].matmul(
                            sbk[:, 0:N], KT[b][:, ks], QT[b][:, q0:q0 + N], start=True, stop=False),
                            reads=[r_in[b]], writes=[rsb], ms=False)
                        p.op("pe", lambda: E["pe"].matmul(
                            sbk[:, 0:N], ZK[:, ks], QR[b][:, q0:q0 + N], start=False, stop=True),
                            reads=[r_in[b], r_zk], writes=[rsb])

                def emit_exp(st):
                    hi, t, lbs, grp = st["hi"], st["t"], st["lbs"], st["grp"]
                    kind, h = heads[hi]
                    b = hi % 2
                    nm = 2 if kind == "d" else 1
                    scale = SC_DIFF if kind == "d" else SC_MLA
                    act_l, N, sbk, rsb = st["act_l"], st["N"], st["sbk"], st["rsb"]
                    ip = nxt("p", NPT)
                    pt, rpt = PT[ip], r_pt[ip]
                    if kind == "d":
                        sv = sbk[:, :].rearrange("p (m n) -> p m n", m=2)[:, :, 0:N]
                        ptv = pt[:, :].rearrange("p (m n) -> p m n", m=2)
                        pv_ = ptv[:, :, 0:N]
                    else:
                        sv = sbk[:, 0:N]
                        pv_ = pt[:, 0:N]
                    p.op("act", lambda: E["act"].activation(pv_, sv, AF.Exp, scale=scale),
                         reads=[rsb], writes=[rpt])
                    for i in act_l:
                        lb = lbs[i]
                        co = (i - act_l[0]) * 128
                        if kind == "d" and (t == lb or t == lb - 1):
                            kd = 0 if t == lb else 1
                            p.op("dve", lambda co=co, kd=kd: E["dve"].tensor_tensor(
                                ptv[:, :, co:co + 128], ptv[:, :, co:co + 128],
                                eb[:, 2 * h:2 * h + 2, kd, :], ALU.mult),
                                reads=[rpt], writes=[rpt])
                        elif kind == "m" and t == lb:
                            p.op("pool", lambda co=co: E["pool"].memset(pt[64:128, co:co + 64], 0.0),
                                 reads=[rpt], writes=[rpt])
                    st.update(pt=pt, rpt=rpt, ptv=(ptv if kind == "d" else None))

                def emit_pv(st):
                    hi, t, lbs, grp = st["hi"], st["t"], st["lbs"], st["grp"]
                    kind, h = heads[hi]
                    b = hi % 2
                    nm = 2 if kind == "d" else 1
                    act_l = st["act_l"]
                    pt, rpt, ptv = st["pt"], st["rpt"], st["ptv"]
                    if t == 0:
                        grp["ob"] = []
                        for lb in lbs:
                            io = nxt("o", 4)
                            grp["ob"].append((OB[io], r_O[io]))
                    for i in act_l:
                        lb = lbs[i]
                        co = (i - act_l[0]) * 128
                        ob, rob = grp["ob"][i]
                        for m in range(nm):
                            lhs = ptv[:, m, co:co + 128] if kind == "d" else pt[:, co:co + 128]
                            p.op("pe", lambda lhs=lhs, ob=ob, m=m, lb=lb: E["pe"].matmul(
                                ob[:, m * 256:m * 256 + 129], lhs, VX[b][:, t, :],
                                start=(t == 0 and m == 0), stop=(t == lb), skip_group_check=True),
                                reads=[rpt, r_in[b]], writes=[rob], ms=(m == nm - 1))
                        if t == lb:
                            pending.append(finalize_gen(hi, lb, ob, rob))

                load_head(0)
                for i in range(min(LOOK, len(steps))):
                    emit_S(steps[i])
                def handle_last(st):
                    if st["last"]:
                        advance(flush=True)
                        kind, h = heads[st["hi"]]
                        b = st["hi"] % 2
                        tile_idx = h if kind == "d" else 8 + h
                        p.dma("sp", yT[tile_idx, :, :], YS[b][:, :], reads=[r_ys[b]], writes=[r_scr["yT"]])

                for i, st in enumerate(steps):
                    if i == 0 and len(heads) > 1:
                        load_head(1)
                    if i + LOOK < len(steps):
                        emit_S(steps[i + LOOK])
                    emit_exp(st)
                    advance()
                    if i >= 1:
                        emit_pv(steps[i - 1])
                        handle_last(steps[i - 1])
                        if steps[i - 1]["last"] and st["hi"] + 1 < len(heads):
                            load_head(st["hi"] + 1)
                emit_pv(steps[-1])
                handle_last(steps[-1])
                p.barrier()

        def phase_D(l, xsrc, last):
            with ExitStack() as sd:
                wo_sb = sb("wo_sb", [128, 16, D], BF, sd); r_wo = R()
                ys = [sb("ysD%d" % i, [128, 16, 512], BF, sd) for i in range(2)]
                r_y = [R(), R()]
                xs = [sb("xsD%d" % i, [128, D], F32, sd) for i in range(3)]
                r_x = [R() for _ in range(3)]
                junk = sb("junkD", [128, D], BF, sd); r_junk = R()
                gfin = sb("gfin", [128, D], F32, sd); r_g = R()
                fsd = sb("fsd", [128, 3 * NB], F32, sd); r_fs = [R() for _ in range(NB)]
                bank = [ps("pd%d" % i, [128, 512], F32, sd) for i in range(8)]
                r_bank = [R() for _ in range(8)]
                bi = [0]
                for cg in range(4):
                    p.dma("pool", wo_sb[:, :, cg * 512:(cg + 1) * 512],
                          wo[l, :, cg * 512:(cg + 1) * 512].rearrange("(c p) n -> p c n", p=128), writes=[r_wo])
                if last:
                    p.dma("sp", gfin[:, :], gfin_in[:, :], writes=[r_g])
                for tt in range(NB):
                    n = tt // 4
                    yb = n % 2
                    if tt % 4 == 0:
                        p.dma("sp", ys[yb][:, :, :], yT[:, :, n * 512:(n + 1) * 512].rearrange("c p n -> p c n"),
                              reads=[r_scr["yT"]], writes=[r_y[yb]])
                    xb = tt % 3
                    p.dma("sp", xs[xb][:, :], xsrc[tt * 128:(tt + 1) * 128, :],
                          reads=([r_scr["x1"]] if l > 0 else []), writes=[r_x[xb]])
                    for dg in range(4):
                        i = bi[0]
                        bi[0] = (i + 1) % 8
                        bk, rbk = bank[i], r_bank[i]
                        for c in range(16):
                            p.op("pe", lambda bk=bk, c=c, yb=yb, tt=tt, dg=dg: E["pe"].matmul(
                                bk[:, :], ys[yb][:, c, (tt % 4) * 128:(tt % 4 + 1) * 128],
                                wo_sb[:, c, dg * 512:(dg + 1) * 512], start=(c == 0), stop=(c == 15)),
                                reads=[r_y[yb], r_wo], writes=[rbk], ms=(c == 15))
                        p.op("dve", lambda bk=bk, xb=xb, dg=dg: E["dve"].tensor_tensor(
                            xs[xb][:, dg * 512:(dg + 1) * 512], bk[:, :], xs[xb][:, dg * 512:(dg + 1) * 512], ALU.add),
                            reads=[rbk, r_x[xb]], writes=[r_x[xb]])
                    if not last:
                        p.dma("sp", x1[tt * 128:(tt + 1) * 128, :], xs[xb][:, :], reads=[r_x[xb]], writes=[r_scr["x1"]])
                    else:
                        c0 = 3 * tt
                        p.op("act", lambda xb=xb, c0=c0: E["act"].activation(
                            junk[:, :], xs[xb][:, :], AF.Square, accum_out=fsd[:, c0:c0 + 1]),
                            reads=[r_x[xb]], writes=[r_junk, r_fs[tt]])
                        p.op("act", lambda c0=c0: E["act"].activation(
                            fsd[:, c0 + 1:c0 + 2], fsd[:, c0:c0 + 1], AF.Sqrt, bias=epsb[:, 0:1], scale=1.0 / D),
                            reads=[r_fs[tt]], writes=[r_fs[tt]])
                        p.op("dve", lambda c0=c0: E["dve"].reciprocal(fsd[:, c0 + 2:c0 + 3], fsd[:, c0 + 1:c0 + 2]),
                             reads=[r_fs[tt]], writes=[r_fs[tt]])
                        p.op("act", lambda xb=xb, c0=c0: E["act"].activation(
                            xs[xb][:, :], xs[xb][:, :], AF.Copy, scale=fsd[:, c0 + 2:c0 + 3]),
                            reads=[r_x[xb], r_fs[tt]], writes=[r_x[xb]])
                        p.op("dve", lambda xb=xb: E["dve"].tensor_tensor(
                            xs[xb][:, :], xs[xb][:, :], gfin[:, :], ALU.mult),
                            reads=[r_x[xb], r_g], writes=[r_x[xb]])
                        p.dma("sp", out_d[tt * 128:(tt + 1) * 128, :], xs[xb][:, :], reads=[r_x[xb]], writes=[r_scr["x1"]])
                p.barrier()

        done = False
        for l in range(DEPTH):
            xsrc = x_in if l == 0 else x1
            for half in range(S // HALF):
                if not dbg.get("skipA"):
                    phase_A(l, half, xsrc)
            if stop_after == ("A", l):
                done = True
                break
            phase_C(l)
            if stop_after == ("C", l):
                done = True
                break
            phase_D(l, xsrc, last=(l == DEPTH - 1))
            if stop_after == ("D", l):
                done = True
                break
        p.barrier()
        build_program.ninst = p.ninst
    return nc


def _rel_bucket_np(rel):
    nb = 16
    max_exact = 8
    ret = (rel > 0).astype(np.int32) * nb
    n = np.abs(rel)
    nf = np.maximum(n, 1).astype(np.float32)
    large = max_exact + (np.log(nf / max_exact) / math.log(128 / max_exact) * (nb - max_exact)).astype(np.int32)
    large = np.minimum(large, nb - 1)
    return ret + np.where(n < max_exact, n, large)


def prepare_inputs(x, norm_g, w_in, diff_lambda, diff_subln_g, mla_q_norm_g, mla_w_q_b,
                   mla_kv_norm_g, mla_w_kv_b, w_out, rel_bias, final_norm_g):
    f = np.float32
    x = np.asarray(x, f)
    w_in = np.asarray(w_in, f)
    kr0 = 3840
    swap = np.concatenate([np.arange(32, 64), np.arange(0, 32)])
    colsF = np.concatenate([np.arange(0, 2048), np.arange(3072, 3840), kr0 + np.arange(64), kr0 + swap,
                            np.arange(3904, 5952)])
    assert colsF.size == NFT * 128
    wF = np.ascontiguousarray(w_in[:, :, colsF])
    wT = np.ascontiguousarray(w_in[:, :, 2048:3072])
    wqb = np.asarray(mla_w_q_b, f)
    cq_cols = []
    for h in range(H):
        b0 = h * 192
        cq_cols += [b0 + np.arange(128), b0 + 128 + np.arange(64), b0 + 128 + swap]
    wq = np.ascontiguousarray(wqb[:, :, np.concatenate(cq_cols)])
    wkvb = np.asarray(mla_w_kv_b, f)
    kc = np.concatenate([h * 256 + np.arange(128) for h in range(H)])
    vc = np.concatenate([h * 256 + 128 + np.arange(128) for h in range(H)])
    wkvk = np.ascontiguousarray(wkvb[:, :, kc])
    wkvv = np.ascontiguousarray(wkvb[:, :, vc])
    wo = np.ascontiguousarray(np.asarray(w_out, f))

    def colmajor(v, nchunk):
        v = np.asarray(v, f).reshape(DEPTH, nchunk, 128)
        return np.ascontiguousarray(v.transpose(2, 0, 1).reshape(128, DEPTH * nchunk))

    def bcast(v):
        v = np.asarray(v, f).reshape(1, -1)
        return np.ascontiguousarray(np.broadcast_to(v, (128, v.shape[1])))

    rb = np.asarray(rel_bias, f)
    k = np.arange(128)[:, None]
    q = np.arange(128)[None, :]
    idx = np.stack([_rel_bucket_np(k - q), _rel_bucket_np(k - 128 - q)], 0)
    bt = rb[idx]
    bt = np.ascontiguousarray(bt.transpose(1, 3, 0, 2).reshape(128, 16 * 2 * 128))
    cfar = bcast(rb[15, :])
    pos = np.arange(S, dtype=np.float32)
    inv_freq = (10000.0 ** (-np.arange(0, 64, 2, dtype=np.float32) / 64)).astype(np.float32)
    ang = pos[:, None] * inv_freq[None, :]
    cos, sin = np.cos(ang).astype(f).T, np.sin(ang).astype(f).T
    t1 = np.ascontiguousarray(np.concatenate([cos, cos, -sin, sin], 0))
    identf = np.eye(128, dtype=f)
    identb = np.eye(128, dtype=f).astype(ml_dtypes.bfloat16)
    dfold = (np.arange(128)[:, None] % 64 == np.arange(128)[None, :] % 64).astype(f).astype(ml_dtypes.bfloat16)
    shared = dict(
        wF=wF, wT=wT, wq=wq, wkvk=wkvk, wkvv=wkvv, wo=wo,
        g_in=colmajor(norm_g, 16), gbc=bcast(np.asarray(norm_g, f).reshape(-1)), gq=colmajor(mla_q_norm_g, 4), gkv=colmajor(mla_kv_norm_g, 2),
        gsub=bcast(np.asarray(diff_subln_g, f).reshape(-1)), gfin=bcast(final_norm_g),
        dlam=bcast(np.asarray(diff_lambda, f).reshape(-1)), cfar=cfar, bt=bt, t1=t1,
        identf=identf, identb=identb, dfold=dfold)
    in_maps = []
    for c in range(8):
        m = dict(shared)
        m["x"] = np.ascontiguousarray(x[c % 4])
        in_maps.append(m)
    return in_maps


_NC_CACHE = {}


def kernel(**inputs):
    in_maps = prepare_inputs(**inputs)
    if "nc" not in _NC_CACHE:
        _NC_CACHE["nc"] = build_program()
    res = run_bass_kernel_spmd(_NC_CACHE["nc"], in_maps, core_ids=list(range(8)))
    out = np.stack([np.asarray(res.results[c]["out"], dtype=np.float32) for c in range(4)], 0)
    return out
```

```python
import math
from contextlib import ExitStack

import numpy as np
import ml_dtypes

import concourse.bass as bass
import concourse.mybir as mybir
from concourse.bass_utils import run_bass_kernel_spmd

F32 = mybir.dt.float32
BF = mybir.dt.bfloat16
AF = mybir.ActivationFunctionType
ALU = mybir.AluOpType
AX = mybir.AxisListType

D = 2048
S = 4096
NB = S // 128
DEPTH = 2
H = 8
EPS = 1e-6
NFT = 39
HALF = 2048
SC_DIFF = 64 ** -0.5
SC_MLA = 192 ** -0.5
SAME_ENG_SYNC = True


class R:
    __slots__ = ("w", "r", "name")

    def __init__(self, name=""):
        self.w = None
        self.r = {}
        self.name = name


class Prog:
    NDS = 24

    def __init__(self, nc, es):
        self.nc = nc
        self.E = {"pe": nc.tensor, "act": nc.scalar, "dve": nc.vector, "pool": nc.gpsimd, "sp": nc.sync}
        self.csem = {e: es.enter_context(nc.semaphore("c_" + e)) for e in ("pe", "act", "dve", "pool")}
        self.ccnt = {e: 0 for e in self.csem}
        self.dsem = [es.enter_context(nc.semaphore("d%d" % i)) for i in range(self.NDS)]
        self.dcnt = [0] * self.NDS
        self.dnext = 0
        self.waited = {}
        self.ninst = 0
        self.es = es
        self.xsem = []

    def _sem(self, tag):
        if tag[0] == "c":
            return self.csem[tag[1]]
        if tag[0] == "x":
            return self.xsem[tag[1]]
        return self.dsem[tag[1]]

    def _wait(self, eng, tag):
        key = (eng, tag[0], tag[1])
        if self.waited.get(key, 0) >= tag[2]:
            return
        self.E[eng].wait_ge(self._sem(tag), tag[2])
        self.waited[key] = tag[2]

    def _deps(self, eng, reads, writes):
        deps = {}
        for r in reads:
            if r.w is not None:
                k = (r.w[0], r.w[1])
                deps[k] = max(deps.get(k, 0), r.w[2])
        for w in writes:
            if w.w is not None:
                k = (w.w[0], w.w[1])
                deps[k] = max(deps.get(k, 0), w.w[2])
            for k, v in w.r.items():
                deps[k] = max(deps.get(k, 0), v)
        for k in sorted(deps, key=str):
            if k[0] == "c" and k[1] == eng and (eng == "pe" or not SAME_ENG_SYNC):
                continue
            self._wait(eng, (k[0], k[1], deps[k]))

    def _mark(self, tag, reads, writes):
        k = (tag[0], tag[1])
        for r in reads:
            r.r[k] = max(r.r.get(k, 0), tag[2])
        for w in writes:
            w.w = tag
            w.r = {}

    def op(self, eng, fn, reads=(), writes=(), ms=True):
        self._deps(eng, reads, writes)
        inst = fn()
        self.ninst += 1
        if ms:
            self.ccnt[eng] += 1
            tag = ("c", eng, self.ccnt[eng])
            inst.then_inc(self.csem[eng], 1)
        else:
            tag = ("c", eng, self.ccnt[eng] + 1)
        self._mark(tag, reads, writes)
        return inst

    def dma(self, q, out, in_, reads=(), writes=()):
        if q == "pool":
            self._deps(q, reads, writes)
            sem = self.es.enter_context(self.nc.semaphore("x%d" % len(self.xsem)))
            self.xsem.append(sem)
            tag = ("x", len(self.xsem) - 1, 16)
            self.E[q].dma_start(out=out, in_=in_).then_inc(sem, 16)
            self.ninst += 1
            self._mark(tag, reads, writes)
            return
        i = self.dnext
        self.dnext = (self.dnext + 1) % self.NDS
        if self.dcnt[i] > 0:
            self._wait(q, ("d", i, self.dcnt[i]))
        self._deps(q, reads, writes)
        self.dcnt[i] += 16
        tag = ("d", i, self.dcnt[i])
        self.E[q].dma_start(out=out, in_=in_).then_inc(self.dsem[i], 16)
        self.ninst += 1
        self._mark(tag, reads, writes)

    def barrier(self, engines=("pe", "act", "dve", "pool", "sp")):
        tags = [("c", e, self.ccnt[e]) for e in self.csem if self.ccnt[e] > 0]
        tags += [("d", i, self.dcnt[i]) for i in range(self.NDS) if self.dcnt[i] > 0]
        tags += [("x", i, 16) for i in range(len(self.xsem))]
        for e in engines:
            for t in tags:
                if t[0] == "c" and t[1] == e:
                    continue
                self._wait(e, t)


def build_program(dbg=None):
    dbg = dbg or {}
    stop_after = dbg.get("stop", None)
    expose = dbg.get("expose", ())
    nc = bass.Bass("TRN2", target_bir_lowering=False)

    def din(name, shape, dt=F32):
        return nc.dram_tensor(name, list(shape), dt, kind="ExternalInput")

    def dscr(name, shape, dt=BF):
        if name in expose:
            return nc.dram_tensor(name, list(shape), dt, kind="ExternalOutput")
        return nc.dram_tensor(name, list(shape), dt)

    x_in = din("x", [S, D])
    wF = din("wF", [DEPTH, D, NFT * 128])
    wT = din("wT", [DEPTH, D, 1024])
    wq = din("wq", [DEPTH, 512, 2048])
    wkvk = din("wkvk", [DEPTH, 256, 1024])
    wkvv = din("wkvv", [DEPTH, 256, 1024])
    wo = din("wo", [DEPTH, D, D])
    g_in = din("g_in", [128, DEPTH * 16])
    gbc_in = din("gbc", [128, DEPTH * D])
    gq_in = din("gq", [128, DEPTH * 4])
    gkv_in = din("gkv", [128, DEPTH * 2])
    gsub_in = din("gsub", [128, DEPTH * 128])
    gfin_in = din("gfin", [128, D])
    dlam_in = din("dlam", [128, DEPTH * 256])
    cfar_in = din("cfar", [128, 16])
    bt_in = din("bt", [128, 16 * 2 * 128])
    t1_in = din("t1", [128, S])
    identf_in = din("identf", [128, 128])
    identb_in = din("identb", [128, 128], BF)
    dfold_in = din("dfold", [128, 128], BF)
    out_d = nc.dram_tensor("out", [S, D], F32, kind="ExternalOutput")

    qdT = dscr("qdT", [H, 128, S])
    kdT = dscr("kdT", [H, 128, S])
    qnT = dscr("qnT", [H, 128, S])
    qrT = dscr("qrT", [H, 128, S])
    knT = dscr("knT", [H, 128, S])
    zkT = dscr("zkT", [128, S])
    gT = dscr("gT", [16, 128, S])
    yT = dscr("yT", [16, 128, S])
    vd = dscr("vd", [S, 1024])
    vb = dscr("vb", [S, 1024])
    x1 = dscr("x1", [S, D], F32)

    with ExitStack() as es:
        p = Prog(nc, es)
        E = p.E

        uid = [0]

        def sb(name, shape, dt, stack=es):
            uid[0] += 1
            return stack.enter_context(nc.sbuf_tensor("s%d_%s" % (uid[0], name), list(shape), dt))

        def ps(name, shape, dt, stack):
            uid[0] += 1
            return stack.enter_context(nc.psum_tensor("p%d_%s" % (uid[0], name), list(shape), dt))

        identf = sb("identf", [128, 128], F32); r_const = R("const")
        identb = sb("identb", [128, 128], BF)
        dfold = sb("dfold", [128, 128], BF)
        onesb = sb("onesb", [128, 128], BF)
        onesf = sb("onesf", [128, 128], F32)
        t1 = sb("t1", [128, S], F32)
        eb = sb("eb", [128, 16, 2, 128], F32)
        gin = sb("gin", [128, DEPTH * 16], F32)
        gq = sb("gq", [128, DEPTH * 4], F32)
        gkv = sb("gkv", [128, DEPTH * 2], F32)
        gsub = sb("gsub", [128, DEPTH * 128], F32)
        dlam = sb("dlam", [128, DEPTH * 256], F32)
        cfar = sb("cfar", [128, 16], F32)
        epsb = sb("epsb", [128, 1], F32)
        lam = sb("lam", [128, 8], F32)
        lamtmp = sb("lamtmp", [128, DEPTH * 128], F32)
        for dst, src in ((identf, identf_in), (identb, identb_in), (dfold, dfold_in), (t1, t1_in),
                         (gin, g_in), (gq, gq_in), (gkv, gkv_in), (gsub, gsub_in), (dlam, dlam_in),
                         (cfar, cfar_in)):
            p.dma("sp", dst[:, :], src[:, :], writes=[r_const])
        p.dma("sp", eb[:, :, :, :].rearrange("p a b c -> p (a b c)"), bt_in[:, :], writes=[r_const])
        p.op("dve", lambda: E["dve"].memset(onesb[:, :], 1.0), writes=[r_const])
        p.op("dve", lambda: E["dve"].memset(onesf[:, :], 1.0), writes=[r_const])
        p.op("dve", lambda: E["dve"].memset(epsb[:, :], EPS), writes=[r_const])
        p.op("dve", lambda: E["dve"].tensor_scalar(cfar[:, :], cfar[:, :], -1.0, None, ALU.mult),
             reads=[r_const], writes=[r_const])
        for hm in range(16):
            p.op("act", lambda hm=hm: E["act"].activation(
                eb[:, hm, :, :], eb[:, hm, :, :], AF.Exp, bias=cfar[:, hm:hm + 1], scale=1.0),
                reads=[r_const], writes=[r_const])
        p.op("dve", lambda: E["dve"].memset(eb[64:128, :, 0, 0:64], 0.0), reads=[r_const], writes=[r_const])
        for l in range(DEPTH):
            lam_init = 0.8 - 0.6 * math.exp(-0.3 * l)
            dl = dlam[:, l * 256:(l + 1) * 256].rearrange("p (a b c) -> p a b c", a=2, b=2)
            pr = lamtmp[:, l * 128:(l + 1) * 128].rearrange("p (a c) -> p a c", a=2)
            p.op("dve", lambda dl=dl, pr=pr: E["dve"].tensor_tensor(pr, dl[:, :, 0, :], dl[:, :, 1, :], ALU.mult),
                 reads=[r_const], writes=[r_const])
            p.op("dve", lambda pr=pr, l=l: E["dve"].tensor_reduce(lam[:, 4 * l + 2:4 * l + 4], pr, AX.X, ALU.add),
                 reads=[r_const], writes=[r_const])
            p.op("act", lambda l=l: E["act"].activation(lam[:, 4 * l + 2:4 * l + 4], lam[:, 4 * l + 2:4 * l + 4], AF.Exp),
                 reads=[r_const], writes=[r_const])
            p.op("dve", lambda l=l: E["dve"].tensor_tensor(lam[:, 4 * l:4 * l + 1], lam[:, 4 * l + 2:4 * l + 3],
                                                           lam[:, 4 * l + 3:4 * l + 4], ALU.subtract),
                 reads=[r_const], writes=[r_const])
            p.op("dve", lambda l=l, li=lam_init: E["dve"].tensor_scalar(
                lam[:, 4 * l:4 * l + 1], lam[:, 4 * l:4 * l + 1], li, None, ALU.add),
                reads=[r_const], writes=[r_const])
            p.op("dve", lambda l=l: E["dve"].tensor_scalar(
                lam[:, 4 * l + 1:4 * l + 2], lam[:, 4 * l:4 * l + 1], -1.0, None, ALU.mult),
                reads=[r_const], writes=[r_const])
            p.op("dve", lambda l=l, li=lam_init: E["dve"].tensor_scalar(
                gsub[:, l * 128:(l + 1) * 128], gsub[:, l * 128:(l + 1) * 128], 1.0 - li, None, ALU.mult),
                reads=[r_const], writes=[r_const])
        p.barrier()

        r_scr = {n: R(n) for n in ("qdT", "kdT", "qnT", "qrT", "knT", "zkT", "gT", "yT", "vd", "vb", "x1")}

        def phase_A(l, half, xsrc):
            t0 = half * HALF
            NT = HALF // 128
            NCK = HALF // 512
            with ExitStack() as sa:
                xT = sb("xT", [128, 16, HALF], BF, sa); r_xT = R()
                wbuf = [sb("wbuf%d" % i, [128, 8192], BF, sa) for i in range(2)]
                r_wb = [R(), R()]
                rstd_bc = sb("rstd_bc", [128, HALF], F32, sa); r_rbc = R()
                rcol = sb("rcol", [128, 3 * NT], F32, sa); r_rcol = [R() for _ in range(NT)]
                bank = [ps("pa%d" % i, [128, 512], F32, sa) for i in range(6)]
                r_bank = [R() for _ in range(6)]
                bi = [0]

                def nb():
                    i = bi[0]
                    bi[0] = (i + 1) % 6
                    return bank[i], r_bank[i]

                with ExitStack() as s0:
                    xs = [sb("xs%d" % i, [128, D], F32, s0) for i in range(4)]
                    r_xs = [R() for _ in range(4)]
                    junk = sb("junk", [128, D], BF, s0); r_junk = R()
                    rb = sb("rb", [128, 128], F32, s0); r_rb = R()
                    g_bc = sb("g_bc", [128, D], F32, s0); r_gbc = R()
                    xg = [sb("xg%d" % i, [128, D], BF, s0) for i in range(2)]
                    r_xg = [R(), R()]
                    tpa = [ps("tpa%d" % i, [128, 1024], BF, s0) for i in range(2)]
                    r_tpa = [R(), R()]
                    tpc = [0]

                    def nxt_tp():
                        i = tpc[0]
                        tpc[0] = (i + 1) % 2
                        return i

                    p.dma("sp", g_bc[:, :], gbc_in[:, l * D:(l + 1) * D], writes=[r_gbc])
                    for tt in range(3):
                        p.dma("sp", xs[tt][:, :], xsrc[t0 + tt * 128:t0 + (tt + 1) * 128, :], writes=[r_xs[tt]])
                    rb2 = [rb, sb("rb2", [128, 128], F32, s0)]
                    r_rb2 = [r_rb, R()]
                    prev = None
                    for tt in range(NT):
                        b = tt % 4
                        if tt + 3 < NT:
                            p.dma("sp", xs[(tt + 3) % 4][:, :], xsrc[t0 + (tt + 3) * 128:t0 + (tt + 4) * 128, :],
                                  writes=[r_xs[(tt + 3) % 4]])
                        c0 = 3 * tt
                        p.op("act", lambda b=b, c0=c0: E["act"].activation(
                            junk[:, :], xs[b][:, :], AF.Square, accum_out=rcol[:, c0:c0 + 1]),
                            reads=[r_xs[b]], writes=[r_junk, r_rcol[tt]])
                        p.op("act", lambda c0=c0: E["act"].activation(
                            rcol[:, c0 + 1:c0 + 2], rcol[:, c0:c0 + 1], AF.Sqrt, bias=epsb[:, 0:1], scale=1.0 / D),
                            reads=[r_rcol[tt]], writes=[r_rcol[tt]])
                        xb_ = tt % 2
                        p.op("dve", lambda b=b, xb_=xb_: E["dve"].tensor_tensor(
                            xg[xb_][:, :], xs[b][:, :], g_bc[:, :], ALU.mult),
                            reads=[r_xs[b], r_gbc], writes=[r_xg[xb_]])
                        if prev is not None:
                            pbk, prbk, ptt = prev
                            p.op("act", lambda pbk=pbk, ptt=ptt: E["act"].copy(
                                rstd_bc[:, ptt * 128:(ptt + 1) * 128], pbk[:, 0:128]),
                                reads=[prbk], writes=[r_rbc])
                        for hb in range(2):
                            tk = nxt_tp()
                            for j in range(8):
                                c = hb * 8 + j
                                p.op("pe", lambda tk=tk, j=j, c=c, xb_=xb_: E["pe"].transpose(
                                    tpa[tk][:, j * 128:(j + 1) * 128], xg[xb_][:, c * 128:(c + 1) * 128], identb[:, :]),
                                    reads=[r_xg[xb_]], writes=[r_tpa[tk]], ms=(j == 7))
                            src = tpa[tk][:, :].rearrange("p (c n) -> p c n", c=8)
                            dst = xT[:, hb * 8:(hb + 1) * 8, tt * 128:(tt + 1) * 128]
                            if hb == 0:
                                p.op("act", lambda src=src, dst=dst: E["act"].copy(dst, src),
                                     reads=[r_tpa[tk]], writes=[R()])
                            else:
                                p.op("dve", lambda src=src, dst=dst: E["dve"].tensor_copy(dst, src),
                                     reads=[r_tpa[tk]], writes=[R()])
                        ri = tt % 2
                        p.op("dve", lambda c0=c0: E["dve"].reciprocal(rcol[:, c0 + 2:c0 + 3], rcol[:, c0 + 1:c0 + 2]),
                             reads=[r_rcol[tt]], writes=[r_rcol[tt]])
                        p.op("dve", lambda c0=c0, ri=ri: E["dve"].tensor_scalar(
                            rb2[ri][:, :], onesf[:, :], rcol[:, c0 + 2:c0 + 3], None, ALU.mult),
                            reads=[r_rcol[tt]], writes=[r_rb2[ri]])
                        bk, rbk = nb()
                        p.op("pe", lambda bk=bk, ri=ri: E["pe"].transpose(bk[:, 0:128], rb2[ri][:, :], identf[:, :]),
                             reads=[r_rb2[ri]], writes=[rbk])
                        prev = (bk, rbk, tt)
                    pbk, prbk, ptt = prev
                    p.op("act", lambda: E["act"].copy(rstd_bc[:, ptt * 128:(ptt + 1) * 128], pbk[:, 0:128]),
                         reads=[prbk], writes=[r_rbc])
                    p.barrier()

                with ExitStack() as s1:
                    cqT = sb("cqT", [128, 4, HALF], BF, s1); r_cq = [R() for _ in range(NCK)]
                    ckvT = sb("ckvT", [128, 2, HALF], BF, s1); r_ckv = [R() for _ in range(NCK)]
                    t32 = [sb("t32_%d" % i, [128, 512], F32, s1) for i in range(3)]
                    r_t32 = [R() for _ in range(3)]
                    ost = [sb("ost%d" % i, [128, 512], BF, s1) for i in range(4)]
                    r_ost = [R() for _ in range(4)]
                    sqb = [sb("sqb%d" % i, [128, 4, 512], BF, s1) for i in range(2)]
                    r_sqb = [R(), R()]
                    rq_bc = sb("rq_bc", [128, HALF], F32, s1); r_rq = [R() for _ in range(NCK)]
                    rkv_bc = sb("rkv_bc", [128, HALF], F32, s1); r_rkv = [R() for _ in range(NCK)]
                    rkvc = sb("rkvc", [128, 3 * NT], F32, s1); r_rkvc = [R() for _ in range(NT)]
                    sqkv = [sb("sqkv%d" % i, [128, 2, 512], BF, s1) for i in range(2)]
                    r_sqkv = [R(), R()]
                    ctr = {"t": 0, "o": 0, "s": 0, "w": 0}

                    def nxt(k, n):
                        i = ctr[k]
                        ctr[k] = (i + 1) % n
                        return i

                    def load_w(src_ap, view_shape):
                        i = nxt("w", 2)
                        n = 1
                        for d_ in view_shape[1:]:
                            n *= d_
                        flat = wbuf[i][:, 0:n]
                        if len(view_shape) == 3:
                            v = flat.rearrange("p (a b) -> p a b", a=view_shape[1])
                        else:
                            v = flat
                        p.dma("pool", v, src_ap, writes=[r_wb[i]])
                        return v, r_wb[i]

                    def store(dst_ap, src_ap, rsrc, rdst):
                        p.dma("sp", dst_ap, src_ap, reads=[rsrc], writes=[rdst])

                    groups = [list(range(g, min(g + 4, NFT))) for g in range(0, NFT, 4)]
                    for grp in groups:
                        c0 = grp[0] * 128
                        ncol = len(grp) * 128
                        wv, rw = load_w(
                            wF[l, :, c0:c0 + ncol].rearrange("(c p) n -> p c n", p=128), [128, 16, ncol])
                        for n in range(NCK):
                            tsl = slice(n * 512, (n + 1) * 512)
                            gsl = slice(t0 + n * 512, t0 + (n + 1) * 512)
                            for jj, ft in enumerate(grp):
                                bk, rbk = nb()
                                for c in range(16):
                                    p.op("pe", lambda bk=bk, c=c, jj=jj, tsl=tsl: E["pe"].matmul(
                                        bk[:, :], wv[:, c, jj * 128:(jj + 1) * 128], xT[:, c, tsl],
                                        start=(c == 0), stop=(c == 15)),
                                        reads=[rw, r_xT], writes=[rbk], ms=(c == 15))
                                if ft < 16 or ft == 22:
                                    io = nxt("o", 4)
                                    if ft == 22:
                                        it = nxt("t", 3)
                                        p.op("dve", lambda bk=bk, it=it, tsl=tsl: E["dve"].tensor_tensor(
                                            t32[it][:, :], bk[:, :], rstd_bc[:, tsl], ALU.mult),
                                            reads=[rbk, r_rbc], writes=[r_t32[it]])
                                        p.op("dve", lambda it=it, io=io, gsl=gsl: E["dve"].tensor_tensor(
                                            ost[io][:, :], t32[it][:, :], t1[:, gsl], ALU.mult),
                                            reads=[r_t32[it]], writes=[r_ost[io]])
                                        store(zkT[:, gsl], ost[io][:, :], r_ost[io], r_scr["zkT"])
                                    else:
                                        p.op("dve", lambda bk=bk, io=io, tsl=tsl: E["dve"].tensor_tensor(
                                            ost[io][:, :], bk[:, :], rstd_bc[:, tsl], ALU.mult),
                                            reads=[rbk, r_rbc], writes=[r_ost[io]])
                                        if ft < 8:
                                            store(qdT[ft, :, gsl], ost[io][:, :], r_ost[io], r_scr["qdT"])
                                        else:
                                            store(kdT[ft - 8, :, gsl], ost[io][:, :], r_ost[io], r_scr["kdT"])
                                elif ft >= 23:
                                    it = nxt("t", 3)
                                    io = nxt("o", 4)
                                    p.op("dve", lambda bk=bk, it=it, tsl=tsl: E["dve"].tensor_tensor(
                                        t32[it][:, :], bk[:, :], rstd_bc[:, tsl], ALU.mult),
                                        reads=[rbk, r_rbc], writes=[r_t32[it]])
                                    p.op("act", lambda it=it, io=io: E["act"].activation(
                                        ost[io][:, :], t32[it][:, :], AF.Silu),
                                        reads=[r_t32[it]], writes=[r_ost[io]])
                                    store(gT[ft - 23, :, gsl], ost[io][:, :], r_ost[io], r_scr["gT"])
                                else:
                                    it = nxt("t", 3)
                                    p.op("dve", lambda bk=bk, it=it, tsl=tsl: E["dve"].tensor_tensor(
                                        t32[it][:, :], bk[:, :], rstd_bc[:, tsl], ALU.mult),
                                        reads=[rbk, r_rbc], writes=[r_t32[it]])
                                    if ft < 20:
                                        j = ft - 16
                                        isq = n % 2
                                        p.op("act", lambda it=it, isq=isq, j=j: E["act"].activation(
                                            sqb[isq][:, j, :], t32[it][:, :], AF.Square),
                                            reads=[r_t32[it]], writes=[r_sqb[isq]])
                                        p.op("act", lambda it=it, j=j, tsl=tsl: E["act"].activation(
                                            cqT[:, j, tsl], t32[it][:, :], AF.Copy, scale=gq[:, l * 4 + j:l * 4 + j + 1]),
                                            reads=[r_t32[it]], writes=[r_cq[n]])
                                    else:
                                        j = ft - 20
                                        p.op("act", lambda it=it, j=j, n=n: E["act"].activation(
                                            sqkv[n % 2][:, j, :], t32[it][:, :], AF.Square),
                                            reads=[r_t32[it]], writes=[r_sqkv[n % 2]])
                                        p.op("act", lambda it=it, j=j, tsl=tsl: E["act"].activation(
                                            ckvT[:, j, tsl], t32[it][:, :], AF.Copy, scale=gkv[:, l * 2 + j:l * 2 + j + 1]),
                                            reads=[r_t32[it]], writes=[r_ckv[n]])
                            if grp[0] == 16:
                                isq = n % 2
                                bk, rbk = nb()
                                for j in range(4):
                                    p.op("pe", lambda bk=bk, j=j, isq=isq: E["pe"].matmul(
                                        bk[:, :], onesb[:, :], sqb[isq][:, j, :], start=(j == 0), stop=(j == 3)),
                                        reads=[r_sqb[isq]], writes=[rbk], ms=(j == 3))
                                p.op("act", lambda bk=bk, tsl=tsl: E["act"].activation(
                                    rq_bc[:, tsl], bk[:, :], AF.Sqrt, bias=epsb[:, 0:1], scale=1.0 / 512),
                                    reads=[rbk], writes=[r_rq[n]])
                                p.op("dve", lambda tsl=tsl: E["dve"].reciprocal(rq_bc[:, tsl], rq_bc[:, tsl]),
                                     reads=[r_rq[n]], writes=[r_rq[n]])
                                for j in range(4):
                                    p.op("dve", lambda j=j, tsl=tsl: E["dve"].tensor_tensor(
                                        cqT[:, j, tsl], cqT[:, j, tsl], rq_bc[:, tsl], ALU.mult),
                                        reads=[r_rq[n], r_cq[n]], writes=[r_cq[n]])
                            if grp[0] == 20:
                                isq = n % 2
                                bk, rbk = nb()
                                for j in range(2):
                                    p.op("pe", lambda bk=bk, j=j, isq=isq: E["pe"].matmul(
                                        bk[:, :], onesb[:, :], sqkv[isq][:, j, :], start=(j == 0), stop=(j == 1)),
                                        reads=[r_sqkv[isq]], writes=[rbk], ms=(j == 1))
                                p.op("act", lambda bk=bk, tsl=tsl: E["act"].activation(
                                    rkv_bc[:, tsl], bk[:, :], AF.Sqrt, bias=epsb[:, 0:1], scale=1.0 / 256),
                                    reads=[rbk], writes=[r_rkv[n]])
                                p.op("dve", lambda tsl=tsl: E["dve"].reciprocal(rkv_bc[:, tsl], rkv_bc[:, tsl]),
                                     reads=[r_rkv[n]], writes=[r_rkv[n]])
                                for j in range(2):
                                    p.op("dve", lambda j=j, tsl=tsl: E["dve"].tensor_tensor(
                                        ckvT[:, j, tsl], ckvT[:, j, tsl], rkv_bc[:, tsl], ALU.mult),
                                        reads=[r_rkv[n], r_ckv[n]], writes=[r_ckv[n]])

                    for cg in range(2):
                        wv, rw = load_w(
                            wT[l, :, cg * 512:(cg + 1) * 512].rearrange("(c p) n -> p c n", p=128), [128, 16, 512])
                        for tt in range(NT):
                            bk, rbk = nb()
                            for c in range(16):
                                p.op("pe", lambda bk=bk, c=c, tt=tt: E["pe"].matmul(
                                    bk[:, :], xT[:, c, tt * 128:(tt + 1) * 128], wv[:, c, :],
                                    start=(c == 0), stop=(c == 15)),
                                    reads=[rw, r_xT], writes=[rbk], ms=(c == 15))
                            io = nxt("o", 4)
                            p.op("act", lambda bk=bk, io=io, tt=tt: E["act"].activation(
                                ost[io][:, :], bk[:, :], AF.Copy, scale=rcol[:, 3 * tt + 2:3 * tt + 3]),
                                reads=[rbk, r_rcol[tt]], writes=[r_ost[io]])
                            store(vd[t0 + tt * 128:t0 + (tt + 1) * 128, cg * 512:(cg + 1) * 512],
                                  ost[io][:, :], r_ost[io], r_scr["vd"])

                    wqv, rwq = load_w(wq[l, :, :].rearrange("(c p) n -> p c n", p=128), [128, 4, 2048])
                    for h in range(H):
                        for n in range(NCK):
                            tsl = slice(n * 512, (n + 1) * 512)
                            gsl = slice(t0 + n * 512, t0 + (n + 1) * 512)
                            for part in range(2):
                                cs = h * 256 + part * 128
                                bk, rbk = nb()
                                for c in range(4):
                                    p.op("pe", lambda bk=bk, c=c, cs=cs, tsl=tsl: E["pe"].matmul(
                                        bk[:, :], wqv[:, c, cs:cs + 128], cqT[:, c, tsl], start=(c == 0), stop=(c == 3)),
                                        reads=[rwq, r_cq[n]], writes=[rbk], ms=(c == 3))
                                io = nxt("o", 4)
                                if part == 0:
                                    if (h + n) % 2 == 0:
                                        p.op("dve", lambda bk=bk, io=io: E["dve"].tensor_copy(ost[io][:, :], bk[:, :]),
                                             reads=[rbk], writes=[r_ost[io]])
                                    else:
                                        p.op("act", lambda bk=bk, io=io: E["act"].copy(ost[io][:, :], bk[:, :]),
                                             reads=[rbk], writes=[r_ost[io]])
                                    store(qnT[h, :, gsl], ost[io][:, :], r_ost[io], r_scr["qnT"])
                                else:
                                    p.op("dve", lambda bk=bk, io=io, gsl=gsl: E["dve"].tensor_tensor(
                                        ost[io][:, :], bk[:, :], t1[:, gsl], ALU.mult),
                                        reads=[rbk], writes=[r_ost[io]])
                                    bk2, rbk2 = nb()
                                    p.op("pe", lambda bk2=bk2, io=io: E["pe"].matmul(
                                        bk2[:, :], dfold[:, :], ost[io][:, :], start=True, stop=True),
                                        reads=[r_ost[io]], writes=[rbk2])
                                    io2 = nxt("o", 4)
                                    p.op("act", lambda bk2=bk2, io2=io2: E["act"].copy(ost[io2][:, :], bk2[:, :]),
                                         reads=[rbk2], writes=[r_ost[io2]])
                                    store(qrT[h, :, gsl], ost[io2][:, :], r_ost[io2], r_scr["qrT"])

                    i = nxt("w", 2)
                    wkk = wbuf[i][:, 0:2048].rearrange("p (a b) -> p a b", a=2)
                    wkv_ = wbuf[i][:, 2048:4096].rearrange("p (a b) -> p a b", a=2)
                    rwk = r_wb[i]
                    p.dma("pool", wkk, wkvk[l, :, :].rearrange("(c p) n -> p c n", p=128), writes=[rwk])
                    p.dma("pool", wkv_, wkvv[l, :, :].rearrange("(c p) n -> p c n", p=128), writes=[rwk])
                    for h in range(H):
                        for n in range(NCK):
                            tsl = slice(n * 512, (n + 1) * 512)
                            gsl = slice(t0 + n * 512, t0 + (n + 1) * 512)
                            bk, rbk = nb()
                            for c in range(2):
                                p.op("pe", lambda bk=bk, c=c, h=h, tsl=tsl: E["pe"].matmul(
                                    bk[:, :], wkk[:, c, h * 128:(h + 1) * 128], ckvT[:, c, tsl],
                                    start=(c == 0), stop=(c == 1)),
                                    reads=[rwk, r_ckv[n]], writes=[rbk], ms=(c == 1))
                            io = nxt("o", 4)
                            if (h + n) % 2 == 0:
                                p.op("dve", lambda bk=bk, io=io: E["dve"].tensor_copy(ost[io][:, :], bk[:, :]),
                                     reads=[rbk], writes=[r_ost[io]])
                            else:
                                p.op("act", lambda bk=bk, io=io: E["act"].copy(ost[io][:, :], bk[:, :]),
                                     reads=[rbk], writes=[r_ost[io]])
                            store(knT[h, :, gsl], ost[io][:, :], r_ost[io], r_scr["knT"])
                    for cg in range(2):
                        for tt in range(NT):
                            n = tt // 4
                            bk, rbk = nb()
                            for c in range(2):
                                p.op("pe", lambda bk=bk, c=c, tt=tt, cg=cg: E["pe"].matmul(
                                    bk[:, :], ckvT[:, c, tt * 128:(tt + 1) * 128], wkv_[:, c, cg * 512:(cg + 1) * 512],
                                    start=(c == 0), stop=(c == 1)),
                                    reads=[rwk, r_ckv[n]], writes=[rbk], ms=(c == 1))
                            io = nxt("o", 4)
                            if tt % 2 == 0:
                                p.op("act", lambda bk=bk, io=io: E["act"].copy(ost[io][:, :], bk[:, :]),
                                     reads=[rbk], writes=[r_ost[io]])
                            else:
                                p.op("dve", lambda bk=bk, io=io: E["dve"].tensor_copy(ost[io][:, :], bk[:, :]),
                                     reads=[rbk], writes=[r_ost[io]])
                            store(vb[t0 + tt * 128:t0 + (tt + 1) * 128, cg * 512:(cg + 1) * 512],
                                  ost[io][:, :], r_ost[io], r_scr["vb"])
                    p.barrier()

        def phase_C(l):
            with ExitStack() as sc:
                KT = [sb("KT%d" % i, [128, S], BF, sc) for i in range(2)]
                QT = [sb("QT%d" % i, [128, S], BF, sc) for i in range(2)]
                QR = [sb("QR%d" % i, [128, S], BF, sc) for i in range(2)]
                QB = [sb("QB%d" % i, [128, S], BF, sc) for i in range(2)]
                GT = [sb("GT%d" % i, [128, S], BF, sc) for i in range(2)]
                VX = [sb("VX%d" % i, [128, NB, 129], BF, sc) for i in range(2)]
                YS = [sb("YS%d" % i, [128, S], BF, sc) for i in range(2)]
                ZK = sb("ZK", [128, S], BF, sc)
                r_in = [R(), R()]
                r_ys = [R(), R()]
                r_zk = R()
                NPT = 4
                PT = [sb("PT%d" % i, [128, 512], BF, sc) for i in range(NPT)]
                r_pt = [R() for _ in range(NPT)]
                NF = 6
                fa = [sb("fa%d" % i, [128, 128], F32, sc) for i in range(NF)]
                fo = [sb("fo%d" % i, [128, 128], F32, sc) for i in range(NF)]
                fn_ = [sb("fn%d" % i, [128, 128], BF, sc) for i in range(NF)]
                fs = [sb("fs%d" % i, [128, 8], F32, sc) for i in range(NF)]
                r_f = [R() for _ in range(NF)]
                NS = 3
                SB_ = [ps("pS%d" % i, [128, 512], F32, sc) for i in range(NS)]
                r_S = [R() for _ in range(NS)]
                OB = [ps("pO%d" % i, [128, 512], F32, sc) for i in range(4)]
                r_O = [R() for _ in range(4)]
                TP = ps("pT", [128, 1024], BF, sc)
                r_T = R()
                ctr = {"s": 0, "p": 0, "o": 0, "f": 0, "t": 0}
                LOOK = 2

                def nxt(k, n):
                    i = ctr[k]
                    ctr[k] = (i + 1) % n
                    return i

                for i in range(2):
                    p.op("pool", lambda i=i: E["pool"].memset(VX[i][:, :, 128:129], 1.0), writes=[r_in[i]])
                    p.op("pool", lambda i=i: E["pool"].memset(QT[i][64:128, :], 0.0), writes=[r_in[i]])
                    p.op("pool", lambda i=i: E["pool"].memset(QB[i][0:64, :], 0.0), writes=[r_in[i]])
                    if "c_ng" in dbg:
                        p.op("pool", lambda i=i: E["pool"].memset(YS[i][:, :], 0.0), writes=[r_ys[i]])
                p.dma("sp", ZK[:, :], zkT[:, :], reads=[r_scr["zkT"]], writes=[r_zk])

                heads = [("d", h) for h in range(H)] + [("m", h) for h in range(H)]
                if "c_heads" in dbg:
                    heads = [heads[i] for i in dbg["c_heads"]]
                n_groups = dbg.get("c_ng", NB // 2)

                def load_head(hi):
                    kind, h = heads[hi]
                    b = hi % 2
                    rr = r_in[b]
                    if kind == "d":
                        p.dma("sp", KT[b][:, :], kdT[h, :, :], reads=[r_scr["kdT"]], writes=[rr])
                        p.dma("sp", QT[b][0:64, :], qdT[h, 0:64, :], reads=[r_scr["qdT"]], writes=[rr])
                        p.dma("sp", QB[b][64:128, :], qdT[h, 64:128, :], reads=[r_scr["qdT"]], writes=[rr])
                        p.dma("sp", VX[b][:, :, 0:128], vd[:, h * 128:(h + 1) * 128].rearrange("(b p) d -> p b d", p=128),
                              reads=[r_scr["vd"]], writes=[rr])
                        p.dma("sp", GT[b][:, :], gT[h, :, :], reads=[r_scr["gT"]], writes=[rr])
                    else:
                        p.dma("sp", KT[b][:, :], knT[h, :, :], reads=[r_scr["knT"]], writes=[rr])
                        p.dma("sp", QT[b][:, :], qnT[h, :, :], reads=[r_scr["qnT"]], writes=[rr])
                        p.dma("sp", QR[b][:, :], qrT[h, :, :], reads=[r_scr["qrT"]], writes=[rr])
                        p.dma("sp", VX[b][:, :, 0:128], vb[:, h * 128:(h + 1) * 128].rearrange("(b p) d -> p b d", p=128),
                              reads=[r_scr["vb"]], writes=[rr])
                        p.dma("sp", GT[b][:, :], gT[8 + h, :, :], reads=[r_scr["gT"]], writes=[rr])

                def finalize_gen(hi, lb, ob, rob):
                    kind, h = heads[hi]
                    b = hi % 2
                    f = nxt("f", NF)
                    rf = r_f[f]
                    qs = slice(lb * 128, (lb + 1) * 128)
                    if kind == "d":
                        p.op("dve", lambda: E["dve"].reciprocal(fs[f][:, 0:1], ob[:, 128:129]),
                             reads=[rob], writes=[rf])
                        p.op("dve", lambda: E["dve"].reciprocal(fs[f][:, 1:2], ob[:, 384:385]),
                             reads=[rob], writes=[rf])
                        p.op("dve", lambda: E["dve"].tensor_tensor(
                            fs[f][:, 2:3], fs[f][:, 1:2], lam[:, 4 * l + 1:4 * l + 2], ALU.mult),
                            reads=[rf], writes=[rf])
                        p.op("dve", lambda: E["dve"].tensor_scalar(
                            fa[f][:, :], ob[:, 0:128], fs[f][:, 0:1], None, ALU.mult),
                            reads=[rob, rf], writes=[rf])
                        p.op("dve", lambda: E["dve"].tensor_scalar(
                            fo[f][:, :], ob[:, 256:384], fs[f][:, 2:3], None, ALU.mult),
                            reads=[rob, rf], writes=[rf])
                        p.op("dve", lambda: E["dve"].tensor_tensor(
                            fo[f][:, :], fo[f][:, :], fa[f][:, :], ALU.add),
                            reads=[rf], writes=[rf])
                        p.op("dve", lambda: E["dve"].tensor_tensor(
                            fa[f][:, :], fo[f][:, :], fo[f][:, :], ALU.mult),
                            reads=[rf], writes=[rf])
                        p.op("dve", lambda: E["dve"].tensor_reduce(fs[f][:, 3:4], fa[f][:, :], AX.X, ALU.add),
                             reads=[rf], writes=[rf])
                        yield
                        yield
                        p.op("act", lambda: E["act"].activation(
                            fs[f][:, 4:5], fs[f][:, 3:4], AF.Ln, bias=epsb[:, 0:1], scale=1.0 / 128),
                            reads=[rf], writes=[rf])
                        yield
                        p.op("act", lambda: E["act"].activation(
                            fs[f][:, 5:6], fs[f][:, 4:5], AF.Exp, scale=-0.5),
                            reads=[rf], writes=[rf])
                        yield
                        yield
                        p.op("dve", lambda: E["dve"].tensor_scalar(
                            fa[f][:, :], fo[f][:, :], fs[f][:, 5:6], None, ALU.mult),
                            reads=[rf], writes=[rf])
                        p.op("dve", lambda: E["dve"].tensor_tensor(
                            fn_[f][:, :], fa[f][:, :], gsub[:, l * 128:(l + 1) * 128], ALU.mult),
                            reads=[rf], writes=[rf])
                    else:
                        p.op("dve", lambda: E["dve"].reciprocal(fs[f][:, 0:1], ob[:, 128:129]),
                             reads=[rob], writes=[rf])
                        p.op("dve", lambda: E["dve"].tensor_scalar(
                            fn_[f][:, :], ob[:, 0:128], fs[f][:, 0:1], None, ALU.mult),
                            reads=[rob, rf], writes=[rf])
                    yield
                    yield
                    ts_ = nxt("t", 8)
                    p.op("pe", lambda: E["pe"].transpose(TP[:, ts_ * 128:(ts_ + 1) * 128], fn_[f][:, :], identb[:, :]),
                         reads=[rf], writes=[r_T])
                    yield
                    yield
                    p.op("dve", lambda: E["dve"].tensor_tensor(
                        YS[b][:, qs], TP[:, ts_ * 128:(ts_ + 1) * 128], GT[b][:, qs], ALU.mult),
                        reads=[r_T, r_in[b]], writes=[r_ys[b]])

                pending = []

                def advance(flush=False):
                    while True:
                        for g in list(pending):
                            try:
                                next(g)
                            except StopIteration:
                                pending.remove(g)
                        if not flush or not pending:
                            break

                steps = []
                for hi in range(len(heads)):
                    gs = 2 if heads[hi][0] == "d" else 4
                    ngr = (n_groups * 2) // gs
                    for G in range(ngr):
                        lbs = tuple(range(gs * G, gs * G + gs))
                        grp = {}
                        for t in range(lbs[-1] + 1):
                            steps.append(dict(hi=hi, G=G, t=t, lbs=lbs, grp=grp,
                                              first=(G == 0 and t == 0),
                                              last=(G == ngr - 1 and t == lbs[-1])))

                def emit_S(st):
                    hi, t, lbs = st["hi"], st["t"], st["lbs"]
                    kind, h = heads[hi]
                    b = hi % 2
                    act_l = [i for i, lb in enumerate(lbs) if lb >= t]
                    q0 = lbs[act_l[0]] * 128
                    N = len(act_l) * 128
                    ks = slice(t * 128, (t + 1) * 128)
                    isb = nxt("s", NS)
                    sbk, rsb = SB_[isb], r_S[isb]
                    st.update(act_l=act_l, N=N, sbk=sbk, rsb=rsb)
                    if kind == "d":
                        for m in range(2):
                            qsrc = QT[b] if m == 0 else QB[b]
                            p.op("pe", lambda m=m, qsrc=qsrc: E["pe"].matmul(
                                sbk[:, m * 256:m * 256 + N], KT[b][:, ks], qsrc[:, q0:q0 + N], start=True, stop=True),
                                reads=[r_in[b]], writes=[rsb], ms=(m == 1))
                    else:
                        p.op("pe", lambda: E["pe"].matmul(
                            sbk[:, 0:N], KT[b][:, ks], QT[b][:, q0:q0 + N], start=True, stop=False),
                            reads=[r_in[b]], writes=[rsb], ms=False)
                        p.op("pe", lambda: E["pe"].matmul(
                            sbk[:, 0:N], ZK[:, ks], QR[b][:, q0:q0 + N], start=False, stop=True),
                            reads=[r_in[b], r_zk], writes=[rsb])

                def emit_exp(st):
                    hi, t, lbs, grp = st["hi"], st["t"], st["lbs"], st["grp"]
                    kind, h = heads[hi]
                    b = hi % 2
                    nm = 2 if kind == "d" else 1
                    scale = SC_DIFF if kind == "d" else SC_MLA
                    act_l, N, sbk, rsb = st["act_l"], st["N"], st["sbk"], st["rsb"]
                    ip = nxt("p", NPT)
                    pt, rpt = PT[ip], r_pt[ip]
                    if kind == "d":
                        sv = sbk[:, :].rearrange("p (m n) -> p m n", m=2)[:, :, 0:N]
                        ptv = pt[:, :].rearrange("p (m n) -> p m n", m=2)
                        pv_ = ptv[:, :, 0:N]
                    else:
                        sv = sbk[:, 0:N]
                        pv_ = pt[:, 0:N]
                    p.op("act", lambda: E["act"].activation(pv_, sv, AF.Exp, scale=scale),
                         reads=[rsb], writes=[rpt])
                    for i in act_l:
                        lb = lbs[i]
                        co = (i - act_l[0]) * 128
                        if kind == "d" and (t == lb or t == lb - 1):
                            kd = 0 if t == lb else 1
                            p.op("dve", lambda co=co, kd=kd: E["dve"].tensor_tensor(
                                ptv[:, :, co:co + 128], ptv[:, :, co:co + 128],
                                eb[:, 2 * h:2 * h + 2, kd, :], ALU.mult),
                                reads=[rpt], writes=[rpt])
                        elif kind == "m" and t == lb:
                            p.op("pool", lambda co=co: E["pool"].memset(pt[64:128, co:co + 64], 0.0),
                                 reads=[rpt], writes=[rpt])
                    st.update(pt=pt, rpt=rpt, ptv=(ptv if kind == "d" else None))

                def emit_pv(st):
                    hi, t, lbs, grp = st["hi"], st["t"], st["lbs"], st["grp"]
                    kind, h = heads[hi]
                    b = hi % 2
                    nm = 2 if kind == "d" else 1
                    act_l = st["act_l"]
                    pt, rpt, ptv = st["pt"], st["rpt"], st["ptv"]
                    if t == 0:
                        grp["ob"] = []
                        for lb in lbs:
                            io = nxt("o", 4)
                            grp["ob"].append((OB[io], r_O[io]))
                    for i in act_l:
                        lb = lbs[i]
                        co = (i - act_l[0]) * 128
                        ob, rob = grp["ob"][i]
                        for m in range(nm):
                            lhs = ptv[:, m, co:co + 128] if kind == "d" else pt[:, co:co + 128]
                            p.op("pe", lambda lhs=lhs, ob=ob, m=m, lb=lb: E["pe"].matmul(
                                ob[:, m * 256:m * 256 + 129], lhs, VX[b][:, t, :],
                                start=(t == 0 and m == 0), stop=(t == lb), skip_group_check=True),
                                reads=[rpt, r_in[b]], writes=[rob], ms=(m == nm - 1))
                        if t == lb:
                            pending.append(finalize_gen(hi, lb, ob, rob))

                load_head(0)
                for i in range(min(LOOK, len(steps))):
                    emit_S(steps[i])
                def handle_last(st):
                    if st["last"]:
                        advance(flush=True)
                        kind, h = heads[st["hi"]]
                        b = st["hi"] % 2
                        tile_idx = h if kind == "d" else 8 + h
                        p.dma("sp", yT[tile_idx, :, :], YS[b][:, :], reads=[r_ys[b]], writes=[r_scr["yT"]])

                for i, st in enumerate(steps):
                    if i == 0 and len(heads) > 1:
                        load_head(1)
                    if i + LOOK < len(steps):
                        emit_S(steps[i + LOOK])
                    emit_exp(st)
                    advance()
                    if i >= 1:
                        emit_pv(steps[i - 1])
                        handle_last(steps[i - 1])
                        if steps[i - 1]["last"] and st["hi"] + 1 < len(heads):
                            load_head(st["hi"] + 1)
                emit_pv(steps[-1])
                handle_last(steps[-1])
                p.barrier()

        def phase_D(l, xsrc, last):
            with ExitStack() as sd:
                wo_sb = sb("wo_sb", [128, 16, D], BF, sd); r_wo = R()
                ys = [sb("ysD%d" % i, [128, 16, 512], BF, sd) for i in range(2)]
                r_y = [R(), R()]
                xs = [sb("xsD%d" % i, [128, D], F32, sd) for i in range(3)]
                r_x = [R() for _ in range(3)]
                junk = sb("junkD", [128, D], BF, sd); r_junk = R()
                gfin = sb("gfin", [128, D], F32, sd); r_g = R()
                fsd = sb("fsd", [128, 3 * NB], F32, sd); r_fs = [R() for _ in range(NB)]
                bank = [ps("pd%d" % i, [128, 512], F32, sd) for i in range(8)]
                r_bank = [R() for _ in range(8)]
                bi = [0]
                for cg in range(4):
                    p.dma("pool", wo_sb[:, :, cg * 512:(cg + 1) * 512],
                          wo[l, :, cg * 512:(cg + 1) * 512].rearrange("(c p) n -> p c n", p=128), writes=[r_wo])
                if last:
                    p.dma("sp", gfin[:, :], gfin_in[:, :], writes=[r_g])
                for tt in range(NB):
                    n = tt // 4
                    yb = n % 2
                    if tt % 4 == 0:
                        p.dma("sp", ys[yb][:, :, :], yT[:, :, n * 512:(n + 1) * 512].rearrange("c p n -> p c n"),
                              reads=[r_scr["yT"]], writes=[r_y[yb]])
                    xb = tt % 3
                    p.dma("sp", xs[xb][:, :], xsrc[tt * 128:(tt + 1) * 128, :],
                          reads=([r_scr["x1"]] if l > 0 else []), writes=[r_x[xb]])
                    for dg in range(4):
                        i = bi[0]
                        bi[0] = (i + 1) % 8
                        bk, rbk = bank[i], r_bank[i]
                        for c in range(16):
                            p.op("pe", lambda bk=bk, c=c, yb=yb, tt=tt, dg=dg: E["pe"].matmul(
                                bk[:, :], ys[yb][:, c, (tt % 4) * 128:(tt % 4 + 1) * 128],
                                wo_sb[:, c, dg * 512:(dg + 1) * 512], start=(c == 0), stop=(c == 15)),
                                reads=[r_y[yb], r_wo], writes=[rbk], ms=(c == 15))
                        p.op("dve", lambda bk=bk, xb=xb, dg=dg: E["dve"].tensor_tensor(
                            xs[xb][:, dg * 512:(dg + 1) * 512], bk[:, :], xs[xb][:, dg * 512:(dg + 1) * 512], ALU.add),
                            reads=[rbk, r_x[xb]], writes=[r_x[xb]])
                    if not last:
                        p.dma("sp", x1[tt * 128:(tt + 1) * 128, :], xs[xb][:, :], reads=[r_x[xb]], writes=[r_scr["x1"]])
                    else:
                        c0 = 3 * tt
                        p.op("act", lambda xb=xb, c0=c0: E["act"].activation(
                            junk[:, :], xs[xb][:, :], AF.Square, accum_out=fsd[:, c0:c0 + 1]),
                            reads=[r_x[xb]], writes=[r_junk, r_fs[tt]])
                        p.op("act", lambda c0=c0: E["act"].activation(
                            fsd[:, c0 + 1:c0 + 2], fsd[:, c0:c0 + 1], AF.Sqrt, bias=epsb[:, 0:1], scale=1.0 / D),
                            reads=[r_fs[tt]], writes=[r_fs[tt]])
                        p.op("dve", lambda c0=c0: E["dve"].reciprocal(fsd[:, c0 + 2:c0 + 3], fsd[:, c0 + 1:c0 + 2]),
                             reads=[r_fs[tt]], writes=[r_fs[tt]])
                        p.op("act", lambda xb=xb, c0=c0: E["act"].activation(
                            xs[xb][:, :], xs[xb][:, :], AF.Copy, scale=fsd[:, c0 + 2:c0 + 3]),
                            reads=[r_x[xb], r_fs[tt]], writes=[r_x[xb]])
                        p.op("dve", lambda xb=xb: E["dve"].tensor_tensor(
                            xs[xb][:, :], xs[xb][:, :], gfin[:, :], ALU.mult),
                            reads=[r_x[xb], r_g], writes=[r_x[xb]])
                        p.dma("sp", out_d[tt * 128:(tt + 1) * 128, :], xs[xb][:, :], reads=[r_x[xb]], writes=[r_scr["x1"]])
                p.barrier()

        done = False
        for l in range(DEPTH):
            xsrc = x_in if l == 0 else x1
            for half in range(S // HALF):
                if not dbg.get("skipA"):
                    phase_A(l, half, xsrc)
            if stop_after == ("A", l):
                done = True
                break
            phase_C(l)
            if stop_after == ("C", l):
                done = True
                break
            phase_D(l, xsrc, last=(l == DEPTH - 1))
            if stop_after == ("D", l):
                done = True
                break
        p.barrier()
        build_program.ninst = p.ninst
    return nc


def _rel_bucket_np(rel):
    nb = 16
    max_exact = 8
    ret = (rel > 0).astype(np.int32) * nb
    n = np.abs(rel)
    nf = np.maximum(n, 1).astype(np.float32)
    large = max_exact + (np.log(nf / max_exact) / math.log(128 / max_exact) * (nb - max_exact)).astype(np.int32)
    large = np.minimum(large, nb - 1)
    return ret + np.where(n < max_exact, n, large)


def prepare_inputs(x, norm_g, w_in, diff_lambda, diff_subln_g, mla_q_norm_g, mla_w_q_b,
                   mla_kv_norm_g, mla_w_kv_b, w_out, rel_bias, final_norm_g):
    f = np.float32
    x = np.asarray(x, f)
    w_in = np.asarray(w_in, f)
    kr0 = 3840
    swap = np.concatenate([np.arange(32, 64), np.arange(0, 32)])
    colsF = np.concatenate([np.arange(0, 2048), np.arange(3072, 3840), kr0 + np.arange(64), kr0 + swap,
                            np.arange(3904, 5952)])
    assert colsF.size == NFT * 128
    wF = np.ascontiguousarray(w_in[:, :, colsF])
    wT = np.ascontiguousarray(w_in[:, :, 2048:3072])
    wqb = np.asarray(mla_w_q_b, f)
    cq_cols = []
    for h in range(H):
        b0 = h * 192
        cq_cols += [b0 + np.arange(128), b0 + 128 + np.arange(64), b0 + 128 + swap]
    wq = np.ascontiguousarray(wqb[:, :, np.concatenate(cq_cols)])
    wkvb = np.asarray(mla_w_kv_b, f)
    kc = np.concatenate([h * 256 + np.arange(128) for h in range(H)])
    vc = np.concatenate([h * 256 + 128 + np.arange(128) for h in range(H)])
    wkvk = np.ascontiguousarray(wkvb[:, :, kc])
    wkvv = np.ascontiguousarray(wkvb[:, :, vc])
    wo = np.ascontiguousarray(np.asarray(w_out, f))

    def colmajor(v, nchunk):
        v = np.asarray(v, f).reshape(DEPTH, nchunk, 128)
        return np.ascontiguousarray(v.transpose(2, 0, 1).reshape(128, DEPTH * nchunk))

    def bcast(v):
        v = np.asarray(v, f).reshape(1, -1)
        return np.ascontiguousarray(np.broadcast_to(v, (128, v.shape[1])))

    rb = np.asarray(rel_bias, f)
    k = np.arange(128)[:, None]
    q = np.arange(128)[None, :]
    idx = np.stack([_rel_bucket_np(k - q), _rel_bucket_np(k - 128 - q)], 0)
    bt = rb[idx]
    bt = np.ascontiguousarray(bt.transpose(1, 3, 0, 2).reshape(128, 16 * 2 * 128))
    cfar = bcast(rb[15, :])
    pos = np.arange(S, dtype=np.float32)
    inv_freq = (10000.0 ** (-np.arange(0, 64, 2, dtype=np.float32) / 64)).astype(np.float32)
    ang = pos[:, None] * inv_freq[None, :]
    cos, sin = np.cos(ang).astype(f).T, np.sin(ang).astype(f).T
    t1 = np.ascontiguousarray(np.concatenate([cos, cos, -sin, sin], 0))
    identf = np.eye(128, dtype=f)
    identb = np.eye(128, dtype=f).astype(ml_dtypes.bfloat16)
    dfold = (np.arange(128)[:, None] % 64 == np.arange(128)[None, :] % 64).astype(f).astype(ml_dtypes.bfloat16)
    shared = dict(
        wF=wF, wT=wT, wq=wq, wkvk=wkvk, wkvv=wkvv, wo=wo,
        g_in=colmajor(norm_g, 16), gbc=bcast(np.asarray(norm_g, f).reshape(-1)), gq=colmajor(mla_q_norm_g, 4), gkv=colmajor(mla_kv_norm_g, 2),
        gsub=bcast(np.asarray(diff_subln_g, f).reshape(-1)), gfin=bcast(final_norm_g),
        dlam=bcast(np.asarray(diff_lambda, f).reshape(-1)), cfar=cfar, bt=bt, t1=t1,
        identf=identf, identb=identb, dfold=dfold)
    in_maps = []
    for c in range(8):
        m = dict(shared)
        m["x"] = np.ascontiguousarray(x[c % 4])
        in_maps.append(m)
    return in_maps


_NC_CACHE = {}


def kernel(**inputs):
    in_maps = prepare_inputs(**inputs)
    if "nc" not in _NC_CACHE:
        _NC_CACHE["nc"] = build_program()
    res = run_bass_kernel_spmd(_NC_CACHE["nc"], in_maps, core_ids=list(range(8)))
    out = np.stack([np.asarray(res.results[c]["out"], dtype=np.float32) for c in range(4)], 0)
    return out
```

```python
import math
from contextlib import ExitStack

import numpy as np
import ml_dtypes

import concourse.bass as bass
import concourse.mybir as mybir
from concourse.bass_utils import run_bass_kernel_spmd

F32 = mybir.dt.float32
BF = mybir.dt.bfloat16
AF = mybir.ActivationFunctionType
ALU = mybir.AluOpType
AX = mybir.AxisListType

D = 2048
S = 4096
NB = S // 128
DEPTH = 2
H = 8
EPS = 1e-6
NFT = 39
HALF = 2048
SC_DIFF = 64 ** -0.5
SC_MLA = 192 ** -0.5
SAME_ENG_SYNC = True


class R:
    __slots__ = ("w", "r", "name")

    def __init__(self, name=""):
        self.w = None
        self.r = {}
        self.name = name


class Prog:
    NDS = 24

    def __init__(self, nc, es):
        self.nc = nc
        self.E = {"pe": nc.tensor, "act": nc.scalar, "dve": nc.vector, "pool": nc.gpsimd, "sp": nc.sync}
        self.csem = {e: es.enter_context(nc.semaphore("c_" + e)) for e in ("pe", "act", "dve", "pool")}
        self.ccnt = {e: 0 for e in self.csem}
        self.dsem = [es.enter_context(nc.semaphore("d%d" % i)) for i in range(self.NDS)]
        self.dcnt = [0] * self.NDS
        self.dnext = 0
        self.waited = {}
        self.ninst = 0
        self.es = es
        self.xsem = []

    def _sem(self, tag):
        if tag[0] == "c":
            return self.csem[tag[1]]
        if tag[0] == "x":
            return self.xsem[tag[1]]
        return self.dsem[tag[1]]

    def _wait(self, eng, tag):
        key = (eng, tag[0], tag[1])
        if self.waited.get(key, 0) >= tag[2]:
            return
        self.E[eng].wait_ge(self._sem(tag), tag[2])
        self.waited[key] = tag[2]

    def _deps(self, eng, reads, writes):
        deps = {}
        for r in reads:
            if r.w is not None:
                k = (r.w[0], r.w[1])
                deps[k] = max(deps.get(k, 0), r.w[2])
        for w in writes:
            if w.w is not None:
                k = (w.w[0], w.w[1])
                deps[k] = max(deps.get(k, 0), w.w[2])
            for k, v in w.r.items():
                deps[k] = max(deps.get(k, 0), v)
        for k in sorted(deps, key=str):
            if k[0] == "c" and k[1] == eng and (eng == "pe" or not SAME_ENG_SYNC):
                continue
            self._wait(eng, (k[0], k[1], deps[k]))

    def _mark(self, tag, reads, writes):
        k = (tag[0], tag[1])
        for r in reads:
            r.r[k] = max(r.r.get(k, 0), tag[2])
        for w in writes:
            w.w = tag
            w.r = {}

    def op(self, eng, fn, reads=(), writes=(), ms=True):
        self._deps(eng, reads, writes)
        inst = fn()
        self.ninst += 1
        if ms:
            self.ccnt[eng] += 1
            tag = ("c", eng, self.ccnt[eng])
            inst.then_inc(self.csem[eng], 1)
        else:
            tag = ("c", eng, self.ccnt[eng] + 1)
        self._mark(tag, reads, writes)
        return inst

    def dma(self, q, out, in_, reads=(), writes=()):
        if q == "pool":
            self._deps(q, reads, writes)
            sem = self.es.enter_context(self.nc.semaphore("x%d" % len(self.xsem)))
            self.xsem.append(sem)
            tag = ("x", len(self.xsem) - 1, 16)
            self.E[q].dma_start(out=out, in_=in_).then_inc(sem, 16)
            self.ninst += 1
            self._mark(tag, reads, writes)
            return
        i = self.dnext
        self.dnext = (self.dnext + 1) % self.NDS
        if self.dcnt[i] > 0:
            self._wait(q, ("d", i, self.dcnt[i]))
        self._deps(q, reads, writes)
        self.dcnt[i] += 16
        tag = ("d", i, self.dcnt[i])
        self.E[q].dma_start(out=out, in_=in_).then_inc(self.dsem[i], 16)
        self.ninst += 1
        self._mark(tag, reads, writes)

    def barrier(self, engines=("pe", "act", "dve", "pool", "sp")):
        tags = [("c", e, self.ccnt[e]) for e in self.csem if self.ccnt[e] > 0]
        tags += [("d", i, self.dcnt[i]) for i in range(self.NDS) if self.dcnt[i] > 0]
        tags += [("x", i, 16) for i in range(len(self.xsem))]
        for e in engines:
            for t in tags:
                if t[0] == "c" and t[1] == e:
                    continue
                self._wait(e, t)


def build_program(dbg=None):
    dbg = dbg or {}
    stop_after = dbg.get("stop", None)
    expose = dbg.get("expose", ())
    nc = bass.Bass("TRN2", target_bir_lowering=False)

    def din(name, shape, dt=F32):
        return nc.dram_tensor(name, list(shape), dt, kind="ExternalInput")

    def dscr(name, shape, dt=BF):
        if name in expose:
            return nc.dram_tensor(name, list(shape), dt, kind="ExternalOutput")
        return nc.dram_tensor(name, list(shape), dt)

    x_in = din("x", [S, D])
    wF = din("wF", [DEPTH, D, NFT * 128])
    wT = din("wT", [DEPTH, D, 1024])
    wq = din("wq", [DEPTH, 512, 2048])
    wkvk = din("wkvk", [DEPTH, 256, 1024])
    wkvv = din("wkvv", [DEPTH, 256, 1024])
    wo = din("wo", [DEPTH, D, D])
    g_in = din("g_in", [128, DEPTH * 16])
    gbc_in = din("gbc", [128, DEPTH * D])
    gq_in = din("gq", [128, DEPTH * 4])
    gkv_in = din("gkv", [128, DEPTH * 2])
    gsub_in = din("gsub", [128, DEPTH * 128])
    gfin_in = din("gfin", [128, D])
    dlam_in = din("dlam", [128, DEPTH * 256])
    cfar_in = din("cfar", [128, 16])
    bt_in = din("bt", [128, 16 * 2 * 128])
    t1_in = din("t1", [128, S])
    identf_in = din("identf", [128, 128])
    identb_in = din("identb", [128, 128], BF)
    dfold_in = din("dfold", [128, 128], BF)
    out_d = nc.dram_tensor("out", [S, D], F32, kind="ExternalOutput")

    qdT = dscr("qdT", [H, 128, S])
    kdT = dscr("kdT", [H, 128, S])
    qnT = dscr("qnT", [H, 128, S])
    qrT = dscr("qrT", [H, 128, S])
    knT = dscr("knT", [H, 128, S])
    zkT = dscr("zkT", [128, S])
    gT = dscr("gT", [16, 128, S])
    yT = dscr("yT", [16, 128, S])
    vd = dscr("vd", [S, 1024])
    vb = dscr("vb", [S, 1024])
    x1 = dscr("x1", [S, D], F32)

    with ExitStack() as es:
        p = Prog(nc, es)
        E = p.E

        uid = [0]

        def sb(name, shape, dt, stack=es):
            uid[0] += 1
            return stack.enter_context(nc.sbuf_tensor("s%d_%s" % (uid[0], name), list(shape), dt))

        def ps(name, shape, dt, stack):
            uid[0] += 1
            return stack.enter_context(nc.psum_tensor("p%d_%s" % (uid[0], name), list(shape), dt))

        identf = sb("identf", [128, 128], F32); r_const = R("const")
        identb = sb("identb", [128, 128], BF)
        dfold = sb("dfold", [128, 128], BF)
        onesb = sb("onesb", [128, 128], BF)
        onesf = sb("onesf", [128, 128], F32)
        t1 = sb("t1", [128, S], F32)
        eb = sb("eb", [128, 16, 2, 128], F32)
        gin = sb("gin", [128, DEPTH * 16], F32)
        gq = sb("gq", [128, DEPTH * 4], F32)
        gkv = sb("gkv", [128, DEPTH * 2], F32)
        gsub = sb("gsub", [128, DEPTH * 128], F32)
        dlam = sb("dlam", [128, DEPTH * 256], F32)
        cfar = sb("cfar", [128, 16], F32)
        epsb = sb("epsb", [128, 1], F32)
        lam = sb("lam", [128, 8], F32)
        lamtmp = sb("lamtmp", [128, DEPTH * 128], F32)
        for dst, src in ((identf, identf_in), (identb, identb_in), (dfold, dfold_in), (t1, t1_in),
                         (gin, g_in), (gq, gq_in), (gkv, gkv_in), (gsub, gsub_in), (dlam, dlam_in),
                         (cfar, cfar_in)):
            p.dma("sp", dst[:, :], src[:, :], writes=[r_const])
        p.dma("sp", eb[:, :, :, :].rearrange("p a b c -> p (a b c)"), bt_in[:, :], writes=[r_const])
        p.op("dve", lambda: E["dve"].memset(onesb[:, :], 1.0), writes=[r_const])
        p.op("dve", lambda: E["dve"].memset(onesf[:, :], 1.0), writes=[r_const])
        p.op("dve", lambda: E["dve"].memset(epsb[:, :], EPS), writes=[r_const])
        p.op("dve", lambda: E["dve"].tensor_scalar(cfar[:, :], cfar[:, :], -1.0, None, ALU.mult),
             reads=[r_const], writes=[r_const])
        for hm in range(16):
            p.op("act", lambda hm=hm: E["act"].activation(
                eb[:, hm, :, :], eb[:, hm, :, :], AF.Exp, bias=cfar[:, hm:hm + 1], scale=1.0),
                reads=[r_const], writes=[r_const])
        p.op("dve", lambda: E["dve"].memset(eb[64:128, :, 0, 0:64], 0.0), reads=[r_const], writes=[r_const])
        for l in range(DEPTH):
            lam_init = 0.8 - 0.6 * math.exp(-0.3 * l)
            dl = dlam[:, l * 256:(l + 1) * 256].rearrange("p (a b c) -> p a b c", a=2, b=2)
            pr = lamtmp[:, l * 128:(l + 1) * 128].rearrange("p (a c) -> p a c", a=2)
            p.op("dve", lambda dl=dl, pr=pr: E["dve"].tensor_tensor(pr, dl[:, :, 0, :], dl[:, :, 1, :], ALU.mult),
                 reads=[r_const], writes=[r_const])
            p.op("dve", lambda pr=pr, l=l: E["dve"].tensor_reduce(lam[:, 4 * l + 2:4 * l + 4], pr, AX.X, ALU.add),
                 reads=[r_const], writes=[r_const])
            p.op("act", lambda l=l: E["act"].activation(lam[:, 4 * l + 2:4 * l + 4], lam[:, 4 * l + 2:4 * l + 4], AF.Exp),
                 reads=[r_const], writes=[r_const])
            p.op("dve", lambda l=l: E["dve"].tensor_tensor(lam[:, 4 * l:4 * l + 1], lam[:, 4 * l + 2:4 * l + 3],
                                                           lam[:, 4 * l + 3:4 * l + 4], ALU.subtract),
                 reads=[r_const], writes=[r_const])
            p.op("dve", lambda l=l, li=lam_init: E["dve"].tensor_scalar(
                lam[:, 4 * l:4 * l + 1], lam[:, 4 * l:4 * l + 1], li, None, ALU.add),
                reads=[r_const], writes=[r_const])
            p.op("dve", lambda l=l: E["dve"].tensor_scalar(
                lam[:, 4 * l + 1:4 * l + 2], lam[:, 4 * l:4 * l + 1], -1.0, None, ALU.mult),
                reads=[r_const], writes=[r_const])
            p.op("dve", lambda l=l, li=lam_init: E["dve"].tensor_scalar(
                gsub[:, l * 128:(l + 1) * 128], gsub[:, l * 128:(l + 1) * 128], 1.0 - li, None, ALU.mult),
                reads=[r_const], writes=[r_const])
        p.barrier()

        r_scr = {n: R(n) for n in ("qdT", "kdT", "qnT", "qrT", "knT", "zkT", "gT", "yT", "vd", "vb", "x1")}

        def phase_A(l, half, xsrc):
            t0 = half * HALF
            NT = HALF // 128
            NCK = HALF // 512
            with ExitStack() as sa:
                xT = sb("xT", [128, 16, HALF], BF, sa); r_xT = R()
                wbuf = [sb("wbuf%d" % i, [128, 8192], BF, sa) for i in range(2)]
                r_wb = [R(), R()]
                rstd_bc = sb("rstd_bc", [128, HALF], F32, sa); r_rbc = R()
                rcol = sb("rcol", [128, 3 * NT], F32, sa); r_rcol = [R() for _ in range(NT)]
                bank = [ps("pa%d" % i, [128, 512], F32, sa) for i in range(6)]
                r_bank = [R() for _ in range(6)]
                bi = [0]

                def nb():
                    i = bi[0]
                    bi[0] = (i + 1) % 6
                    return bank[i], r_bank[i]

                with ExitStack() as s0:
                    xs = [sb("xs%d" % i, [128, D], F32, s0) for i in range(4)]
                    r_xs = [R() for _ in range(4)]
                    junk = sb("junk", [128, D], BF, s0); r_junk = R()
                    rb = sb("rb", [128, 128], F32, s0); r_rb = R()
                    g_bc = sb("g_bc", [128, D], F32, s0); r_gbc = R()
                    xg = [sb("xg%d" % i, [128, D], BF, s0) for i in range(2)]
                    r_xg = [R(), R()]
                    tpa = [ps("tpa%d" % i, [128, 1024], BF, s0) for i in range(2)]
                    r_tpa = [R(), R()]
                    tpc = [0]

                    def nxt_tp():
                        i = tpc[0]
                        tpc[0] = (i + 1) % 2
                        return i

                    p.dma("sp", g_bc[:, :], gbc_in[:, l * D:(l + 1) * D], writes=[r_gbc])
                    for tt in range(3):
                        p.dma("sp", xs[tt][:, :], xsrc[t0 + tt * 128:t0 + (tt + 1) * 128, :], writes=[r_xs[tt]])
                    rb2 = [rb, sb("rb2", [128, 128], F32, s0)]
                    r_rb2 = [r_rb, R()]
                    prev = None
                    for tt in range(NT):
                        b = tt % 4
                        if tt + 3 < NT:
                            p.dma("sp", xs[(tt + 3) % 4][:, :], xsrc[t0 + (tt + 3) * 128:t0 + (tt + 4) * 128, :],
                                  writes=[r_xs[(tt + 3) % 4]])
                        c0 = 3 * tt
                        p.op("act", lambda b=b, c0=c0: E["act"].activation(
                            junk[:, :], xs[b][:, :], AF.Square, accum_out=rcol[:, c0:c0 + 1]),
                            reads=[r_xs[b]], writes=[r_junk, r_rcol[tt]])
                        p.op("act", lambda c0=c0: E["act"].activation(
                            rcol[:, c0 + 1:c0 + 2], rcol[:, c0:c0 + 1], AF.Sqrt, bias=epsb[:, 0:1], scale=1.0 / D),
                            reads=[r_rcol[tt]], writes=[r_rcol[tt]])
                        xb_ = tt % 2
                        p.op("dve", lambda b=b, xb_=xb_: E["dve"].tensor_tensor(
                            xg[xb_][:, :], xs[b][:, :], g_bc[:, :], ALU.mult),
                            reads=[r_xs[b], r_gbc], writes=[r_xg[xb_]])
                        if prev is not None:
                            pbk, prbk, ptt = prev
                            p.op("act", lambda pbk=pbk, ptt=ptt: E["act"].copy(
                                rstd_bc[:, ptt * 128:(ptt + 1) * 128], pbk[:, 0:128]),
                                reads=[prbk], writes=[r_rbc])
                        for hb in range(2):
                            tk = nxt_tp()
                            for j in range(8):
                                c = hb * 8 + j
                                p.op("pe", lambda tk=tk, j=j, c=c, xb_=xb_: E["pe"].transpose(
                                    tpa[tk][:, j * 128:(j + 1) * 128], xg[xb_][:, c * 128:(c + 1) * 128], identb[:, :]),
                                    reads=[r_xg[xb_]], writes=[r_tpa[tk]], ms=(j == 7))
                            src = tpa[tk][:, :].rearrange("p (c n) -> p c n", c=8)
                            dst = xT[:, hb * 8:(hb + 1) * 8, tt * 128:(tt + 1) * 128]
                            if hb == 0:
                                p.op("act", lambda src=src, dst=dst: E["act"].copy(dst, src),
                                     reads=[r_tpa[tk]], writes=[R()])
                            else:
                                p.op("dve", lambda src=src, dst=dst: E["dve"].tensor_copy(dst, src),
                                     reads=[r_tpa[tk]], writes=[R()])
                        ri = tt % 2
                        p.op("dve", lambda c0=c0: E["dve"].reciprocal(rcol[:, c0 + 2:c0 + 3], rcol[:, c0 + 1:c0 + 2]),
                             reads=[r_rcol[tt]], writes=[r_rcol[tt]])
                        p.op("dve", lambda c0=c0, ri=ri: E["dve"].tensor_scalar(
                            rb2[ri][:, :], onesf[:, :], rcol[:, c0 + 2:c0 + 3], None, ALU.mult),
                            reads=[r_rcol[tt]], writes=[r_rb2[ri]])
                        bk, rbk = nb()
                        p.op("pe", lambda bk=bk, ri=ri: E["pe"].transpose(bk[:, 0:128], rb2[ri][:, :], identf[:, :]),
                             reads=[r_rb2[ri]], writes=[rbk])
                        prev = (bk, rbk, tt)
                    pbk, prbk, ptt = prev
                    p.op("act", lambda: E["act"].copy(rstd_bc[:, ptt * 128:(ptt + 1) * 128], pbk[:, 0:128]),
                         reads=[prbk], writes=[r_rbc])
                    p.barrier()

                with ExitStack() as s1:
                    cqT = sb("cqT", [128, 4, HALF], BF, s1); r_cq = [R() for _ in range(NCK)]
                    ckvT = sb("ckvT", [128, 2, HALF], BF, s1); r_ckv = [R() for _ in range(NCK)]
                    t32 = [sb("t32_%d" % i, [128, 512], F32, s1) for i in range(3)]
                    r_t32 = [R() for _ in range(3)]
                    ost = [sb("ost%d" % i, [128, 512], BF, s1) for i in range(4)]
                    r_ost = [R() for _ in range(4)]
                    sqb = [sb("sqb%d" % i, [128, 4, 512], BF, s1) for i in range(2)]
                    r_sqb = [R(), R()]
                    rq_bc = sb("rq_bc", [128, HALF], F32, s1); r_rq = [R() for _ in range(NCK)]
                    rkv_bc = sb("rkv_bc", [128, HALF], F32, s1); r_rkv = [R() for _ in range(NCK)]
                    rkvc = sb("rkvc", [128, 3 * NT], F32, s1); r_rkvc = [R() for _ in range(NT)]
                    sqkv = [sb("sqkv%d" % i, [128, 2, 512], BF, s1) for i in range(2)]
                    r_sqkv = [R(), R()]
                    ctr = {"t": 0, "o": 0, "s": 0, "w": 0}

                    def nxt(k, n):
                        i = ctr[k]
                        ctr[k] = (i + 1) % n
                        return i

                    def load_w(src_ap, view_shape):
                        i = nxt("w", 2)
                        n = 1
                        for d_ in view_shape[1:]:
                            n *= d_
                        flat = wbuf[i][:, 0:n]
                        if len(view_shape) == 3:
                            v = flat.rearrange("p (a b) -> p a b", a=view_shape[1])
                        else:
                            v = flat
                        p.dma("pool", v, src_ap, writes=[r_wb[i]])
                        return v, r_wb[i]

                    def store(dst_ap, src_ap, rsrc, rdst):
                        p.dma("sp", dst_ap, src_ap, reads=[rsrc], writes=[rdst])

                    groups = [list(range(g, min(g + 4, NFT))) for g in range(0, NFT, 4)]
                    for grp in groups:
                        c0 = grp[0] * 128
                        ncol = len(grp) * 128
                        wv, rw = load_w(
                            wF[l, :, c0:c0 + ncol].rearrange("(c p) n -> p c n", p=128), [128, 16, ncol])
                        for n in range(NCK):
                            tsl = slice(n * 512, (n + 1) * 512)
                            gsl = slice(t0 + n * 512, t0 + (n + 1) * 512)
                            for jj, ft in enumerate(grp):
                                bk, rbk = nb()
                                for c in range(16):
                                    p.op("pe", lambda bk=bk, c=c, jj=jj, tsl=tsl: E["pe"].matmul(
                                        bk[:, :], wv[:, c, jj * 128:(jj + 1) * 128], xT[:, c, tsl],
                                        start=(c == 0), stop=(c == 15)),
                                        reads=[rw, r_xT], writes=[rbk], ms=(c == 15))
                                if ft < 16 or ft == 22:
                                    io = nxt("o", 4)
                                    if ft == 22:
                                        it = nxt("t", 3)
                                        p.op("dve", lambda bk=bk, it=it, tsl=tsl: E["dve"].tensor_tensor(
                                            t32[it][:, :], bk[:, :], rstd_bc[:, tsl], ALU.mult),
                                            reads=[rbk, r_rbc], writes=[r_t32[it]])
                                        p.op("dve", lambda it=it, io=io, gsl=gsl: E["dve"].tensor_tensor(
                                            ost[io][:, :], t32[it][:, :], t1[:, gsl], ALU.mult),
                                            reads=[r_t32[it]], writes=[r_ost[io]])
                                        store(zkT[:, gsl], ost[io][:, :], r_ost[io], r_scr["zkT"])
                                    else:
                                        p.op("dve", lambda bk=bk, io=io, tsl=tsl: E["dve"].tensor_tensor(
                                            ost[io][:, :], bk[:, :], rstd_bc[:, tsl], ALU.mult),
                                            reads=[rbk, r_rbc], writes=[r_ost[io]])
                                        if ft < 8:
                                            store(qdT[ft, :, gsl], ost[io][:, :], r_ost[io], r_scr["qdT"])
                                        else:
                                            store(kdT[ft - 8, :, gsl], ost[io][:, :], r_ost[io], r_scr["kdT"])
                                elif ft >= 23:
                                    it = nxt("t", 3)
                                    io = nxt("o", 4)
                                    p.op("dve", lambda bk=bk, it=it, tsl=tsl: E["dve"].tensor_tensor(
                                        t32[it][:, :], bk[:, :], rstd_bc[:, tsl], ALU.mult),
                                        reads=[rbk, r_rbc], writes=[r_t32[it]])
                                    p.op("act", lambda it=it, io=io: E["act"].activation(
                                        ost[io][:, :], t32[it][:, :], AF.Silu),
                                        reads=[r_t32[it]], writes=[r_ost[io]])
                                    store(gT[ft - 23, :, gsl], ost[io][:, :], r_ost[io], r_scr["gT"])
                                else:
                                    it = nxt("t", 3)
                                    p.op("dve", lambda bk=bk, it=it, tsl=tsl: E["dve"].tensor_tensor(
                                        t32[it][:, :], bk[:, :], rstd_bc[:, tsl], ALU.mult),
                                        reads=[rbk, r_rbc], writes=[r_t32[it]])
                                    if ft < 20:
                                        j = ft - 16
                                        isq = n % 2
                                        p.op("act", lambda it=it, isq=isq, j=j: E["act"].activation(
                                            sqb[isq][:, j, :], t32[it][:, :], AF.Square),
                                            reads=[r_t32[it]], writes=[r_sqb[isq]])
                                        p.op("act", lambda it=it, j=j, tsl=tsl: E["act"].activation(
                                            cqT[:, j, tsl], t32[it][:, :], AF.Copy, scale=gq[:, l * 4 + j:l * 4 + j + 1]),
                                            reads=[r_t32[it]], writes=[r_cq[n]])
                                    else:
                                        j = ft - 20
                                        p.op("act", lambda it=it, j=j, n=n: E["act"].activation(
                                            sqkv[n % 2][:, j, :], t32[it][:, :], AF.Square),
                                            reads=[r_t32[it]], writes=[r_sqkv[n % 2]])
                                        p.op("act", lambda it=it, j=j, tsl=tsl: E["act"].activation(
                                            ckvT[:, j, tsl], t32[it][:, :], AF.Copy, scale=gkv[:, l * 2 + j:l * 2 + j + 1]),
                                            reads=[r_t32[it]], writes=[r_ckv[n]])
                            if grp[0] == 16:
                                isq = n % 2
                                bk, rbk = nb()
                                for j in range(4):
                                    p.op("pe", lambda bk=bk, j=j, isq=isq: E["pe"].matmul(
                                        bk[:, :], onesb[:, :], sqb[isq][:, j, :], start=(j == 0), stop=(j == 3)),
                                        reads=[r_sqb[isq]], writes=[rbk], ms=(j == 3))
                                p.op("act", lambda bk=bk, tsl=tsl: E["act"].activation(
                                    rq_bc[:, tsl], bk[:, :], AF.Sqrt, bias=epsb[:, 0:1], scale=1.0 / 512),
                                    reads=[rbk], writes=[r_rq[n]])
                                p.op("dve", lambda tsl=tsl: E["dve"].reciprocal(rq_bc[:, tsl], rq_bc[:, tsl]),
                                     reads=[r_rq[n]], writes=[r_rq[n]])
                                for j in range(4):
                                    p.op("dve", lambda j=j, tsl=tsl: E["dve"].tensor_tensor(
                                        cqT[:, j, tsl], cqT[:, j, tsl], rq_bc[:, tsl], ALU.mult),
                                        reads=[r_rq[n], r_cq[n]], writes=[r_cq[n]])
                            if grp[0] == 20:
                                isq = n % 2
                                bk, rbk = nb()
                                for j in range(2):
                                    p.op("pe", lambda bk=bk, j=j, isq=isq: E["pe"].matmul(
                                        bk[:, :], onesb[:, :], sqkv[isq][:, j, :], start=(j == 0), stop=(j == 1)),
                                        reads=[r_sqkv[isq]], writes=[rbk], ms=(j == 1))
                                p.op("act", lambda bk=bk, tsl=tsl: E["act"].activation(
                                    rkv_bc[:, tsl], bk[:, :], AF.Sqrt, bias=epsb[:, 0:1], scale=1.0 / 256),
                                    reads=[rbk], writes=[r_rkv[n]])
                                p.op("dve", lambda tsl=tsl: E["dve"].reciprocal(rkv_bc[:, tsl], rkv_bc[:, tsl]),
                                     reads=[r_rkv[n]], writes=[r_rkv[n]])
                                for j in range(2):
                                    p.op("dve", lambda j=j, tsl=tsl: E["dve"].tensor_tensor(
                                        ckvT[:, j, tsl], ckvT[:, j, tsl], rkv_bc[:, tsl], ALU.mult),
                                        reads=[r_rkv[n], r_ckv[n]], writes=[r_ckv[n]])

                    for cg in range(2):
                        wv, rw = load_w(
                            wT[l, :, cg * 512:(cg + 1) * 512].rearrange("(c p) n -> p c n", p=128), [128, 16, 512])
                        for tt in range(NT):
                            bk, rbk = nb()
                            for c in range(16):
                                p.op("pe", lambda bk=bk, c=c, tt=tt: E["pe"].matmul(
                                    bk[:, :], xT[:, c, tt * 128:(tt + 1) * 128], wv[:, c, :],
                                    start=(c == 0), stop=(c == 15)),
                                    reads=[rw, r_xT], writes=[rbk], ms=(c == 15))
                            io = nxt("o", 4)
                            p.op("act", lambda bk=bk, io=io, tt=tt: E["act"].activation(
                                ost[io][:, :], bk[:, :], AF.Copy, scale=rcol[:, 3 * tt + 2:3 * tt + 3]),
                                reads=[rbk, r_rcol[tt]], writes=[r_ost[io]])
                            store(vd[t0 + tt * 128:t0 + (tt + 1) * 128, cg * 512:(cg + 1) * 512],
                                  ost[io][:, :], r_ost[io], r_scr["vd"])

                    wqv, rwq = load_w(wq[l, :, :].rearrange("(c p) n -> p c n", p=128), [128, 4, 2048])
                    for h in range(H):
                        for n in range(NCK):
                            tsl = slice(n * 512, (n + 1) * 512)
                            gsl = slice(t0 + n * 512, t0 + (n + 1) * 512)
                            for part in range(2):
                                cs = h * 256 + part * 128
                                bk, rbk = nb()
                                for c in range(4):
                                    p.op("pe", lambda bk=bk, c=c, cs=cs, tsl=tsl: E["pe"].matmul(
                                        bk[:, :], wqv[:, c, cs:cs + 128], cqT[:, c, tsl], start=(c == 0), stop=(c == 3)),
                                        reads=[rwq, r_cq[n]], writes=[rbk], ms=(c == 3))
                                io = nxt("o", 4)
                                if part == 0:
                                    if (h + n) % 2 == 0:
                                        p.op("dve", lambda bk=bk, io=io: E["dve"].tensor_copy(ost[io][:, :], bk[:, :]),
                                             reads=[rbk], writes=[r_ost[io]])
                                    else:
                                        p.op("act", lambda bk=bk, io=io: E["act"].copy(ost[io][:, :], bk[:, :]),
                                             reads=[rbk], writes=[r_ost[io]])
                                    store(qnT[h, :, gsl], ost[io][:, :], r_ost[io], r_scr["qnT"])
                                else:
                                    p.op("dve", lambda bk=bk, io=io, gsl=gsl: E["dve"].tensor_tensor(
                                        ost[io][:, :], bk[:, :], t1[:, gsl], ALU.mult),
                                        reads=[rbk], writes=[r_ost[io]])
                                    bk2, rbk2 = nb()
                                    p.op("pe", lambda bk2=bk2, io=io: E["pe"].matmul(
                                        bk2[:, :], dfold[:, :], ost[io][:, :], start=True, stop=True),
                                        reads=[r_ost[io]], writes=[rbk2])
                                    io2 = nxt("o", 4)
                                    p.op("act", lambda bk2=bk2, io2=io2: E["act"].copy(ost[io2][:, :], bk2[:, :]),
                                         reads=[rbk2], writes=[r_ost[io2]])
                                    store(qrT[h, :, gsl], ost[io2][:, :], r_ost[io2], r_scr["qrT"])

                    i = nxt("w", 2)
                    wkk = wbuf[i][:, 0:2048].rearrange("p (a b) -> p a b", a=2)
                    wkv_ = wbuf[i][:, 2048:4096].rearrange("p (a b) -> p a b", a=2)
                    rwk = r_wb[i]
                    p.dma("pool", wkk, wkvk[l, :, :].rearrange("(c p) n -> p c n", p=128), writes=[rwk])
                    p.dma("pool", wkv_, wkvv[l, :, :].rearrange("(c p) n -> p c n", p=128), writes=[rwk])
                    for h in range(H):
                        for n in range(NCK):
                            tsl = slice(n * 512, (n + 1) * 512)
                            gsl = slice(t0 + n * 512, t0 + (n + 1) * 512)
                            bk, rbk = nb()
                            for c in range(2):
                                p.op("pe", lambda bk=bk, c=c, h=h, tsl=tsl: E["pe"].matmul(
                                    bk[:, :], wkk[:, c, h * 128:(h + 1) * 128], ckvT[:, c, tsl],
                                    start=(c == 0), stop=(c == 1)),
                                    reads=[rwk, r_ckv[n]], writes=[rbk], ms=(c == 1))
                            io = nxt("o", 4)
                            if (h + n) % 2 == 0:
                                p.op("dve", lambda bk=bk, io=io: E["dve"].tensor_copy(ost[io][:, :], bk[:, :]),
                                     reads=[rbk], writes=[r_ost[io]])
                            else:
                                p.op("act", lambda bk=bk, io=io: E["act"].copy(ost[io][:, :], bk[:, :]),
                                     reads=[rbk], writes=[r_ost[io]])
                            store(knT[h, :, gsl], ost[io][:, :], r_ost[io], r_scr["knT"])
                    for cg in range(2):
                        for tt in range(NT):
                            n = tt // 4
                            bk, rbk = nb()
                            for c in range(2):
                                p.op("pe", lambda bk=bk, c=c, tt=tt, cg=cg: E["pe"].matmul(
                                    bk[:, :], ckvT[:, c, tt * 128:(tt + 1) * 128], wkv_[:, c, cg * 512:(cg + 1) * 512],
                                    start=(c == 0), stop=(c == 1)),
                                    reads=[rwk, r_ckv[n]], writes=[rbk], ms=(c == 1))
                            io = nxt("o", 4)
                            if tt % 2 == 0:
                                p.op("act", lambda bk=bk, io=io: E["act"].copy(ost[io][:, :], bk[:, :]),
                                     reads=[rbk], writes=[r_ost[io]])
                            else:
                                p.op("dve", lambda bk=bk, io=io: E["dve"].tensor_copy(ost[io][:, :], bk[:, :]),
                                     reads=[rbk], writes=[r_ost[io]])
                            store(vb[t0 + tt * 128:t0 + (tt + 1) * 128, cg * 512:(cg + 1) * 512],
                                  ost[io][:, :], r_ost[io], r_scr["vb"])
                    p.barrier()

        def phase_C(l):
            with ExitStack() as sc:
                KT = [sb("KT%d" % i, [128, S], BF, sc) for i in range(2)]
                QT = [sb("QT%d" % i, [128, S], BF, sc) for i in range(2)]
                QR = [sb("QR%d" % i, [128, S], BF, sc) for i in range(2)]
                QB = [sb("QB%d" % i, [128, S], BF, sc) for i in range(2)]
                GT = [sb("GT%d" % i, [128, S], BF, sc) for i in range(2)]
                VX = [sb("VX%d" % i, [128, NB, 129], BF, sc) for i in range(2)]
                YS = [sb("YS%d" % i, [128, S], BF, sc) for i in range(2)]
                ZK = sb("ZK", [128, S], BF, sc)
                r_in = [R(), R()]
                r_ys = [R(), R()]
                r_zk = R()
                NPT = 4
                PT = [sb("PT%d" % i, [128, 512], BF, sc) for i in range(NPT)]
                r_pt = [R() for _ in range(NPT)]
                NF = 16
                fa = [sb("fa%d" % i, [128, 128], F32, sc) for i in range(NF)]
                fo = [sb("fo%d" % i, [128, 128], F32, sc) for i in range(NF)]
                fn_ = [sb("fn%d" % i, [128, 128], BF, sc) for i in range(NF)]
                fs = [sb("fs%d" % i, [128, 8], F32, sc) for i in range(NF)]
                r_f = [R() for _ in range(NF)]
                NS = 3
                SB_ = [ps("pS%d" % i, [128, 512], F32, sc) for i in range(NS)]
                r_S = [R() for _ in range(NS)]
                OB = [ps("pO%d" % i, [128, 512], F32, sc) for i in range(4)]
                r_O = [R() for _ in range(4)]
                TP = ps("pT", [128, 1024], BF, sc)
                r_T = R()
                ctr = {"s": 0, "p": 0, "o": 0, "f": 0, "t": 0}
                LOOK = 2

                def nxt(k, n):
                    i = ctr[k]
                    ctr[k] = (i + 1) % n
                    return i

                for i in range(2):
                    p.op("pool", lambda i=i: E["pool"].memset(VX[i][:, :, 128:129], 1.0), writes=[r_in[i]])
                    p.op("pool", lambda i=i: E["pool"].memset(QT[i][64:128, :], 0.0), writes=[r_in[i]])
                    p.op("pool", lambda i=i: E["pool"].memset(QB[i][0:64, :], 0.0), writes=[r_in[i]])
                    if "c_ng" in dbg:
                        p.op("pool", lambda i=i: E["pool"].memset(YS[i][:, :], 0.0), writes=[r_ys[i]])
                p.dma("sp", ZK[:, :], zkT[:, :], reads=[r_scr["zkT"]], writes=[r_zk])

                heads = [("d", h) for h in range(H)] + [("m", h) for h in range(H)]
                if "c_heads" in dbg:
                    heads = [heads[i] for i in dbg["c_heads"]]
                n_groups = dbg.get("c_ng", NB // 2)

                def load_head(hi):
                    kind, h = heads[hi]
                    b = hi % 2
                    rr = r_in[b]
                    if kind == "d":
                        p.dma("sp", KT[b][:, :], kdT[h, :, :], reads=[r_scr["kdT"]], writes=[rr])
                        p.dma("sp", QT[b][0:64, :], qdT[h, 0:64, :], reads=[r_scr["qdT"]], writes=[rr])
                        p.dma("sp", QB[b][64:128, :], qdT[h, 64:128, :], reads=[r_scr["qdT"]], writes=[rr])
                        p.dma("sp", VX[b][:, :, 0:128], vd[:, h * 128:(h + 1) * 128].rearrange("(b p) d -> p b d", p=128),
                              reads=[r_scr["vd"]], writes=[rr])
                        p.dma("sp", GT[b][:, :], gT[h, :, :], reads=[r_scr["gT"]], writes=[rr])
                    else:
                        p.dma("sp", KT[b][:, :], knT[h, :, :], reads=[r_scr["knT"]], writes=[rr])
                        p.dma("sp", QT[b][:, :], qnT[h, :, :], reads=[r_scr["qnT"]], writes=[rr])
                        p.dma("sp", QR[b][:, :], qrT[h, :, :], reads=[r_scr["qrT"]], writes=[rr])
                        p.dma("sp", VX[b][:, :, 0:128], vb[:, h * 128:(h + 1) * 128].rearrange("(b p) d -> p b d", p=128),
                              reads=[r_scr["vb"]], writes=[rr])
                        p.dma("sp", GT[b][:, :], gT[8 + h, :, :], reads=[r_scr["gT"]], writes=[rr])

                def finalize_gen(hi, lb, ob, rob):
                    kind, h = heads[hi]
                    b = hi % 2
                    f = nxt("f", NF)
                    rf = r_f[f]
                    qs = slice(lb * 128, (lb + 1) * 128)
                    if kind == "d":
                        p.op("dve", lambda: E["dve"].reciprocal(fs[f][:, 0:1], ob[:, 128:129]),
                             reads=[rob], writes=[rf])
                        p.op("dve", lambda: E["dve"].reciprocal(fs[f][:, 1:2], ob[:, 384:385]),
                             reads=[rob], writes=[rf])
                        p.op("dve", lambda: E["dve"].tensor_tensor(
                            fs[f][:, 2:3], fs[f][:, 1:2], lam[:, 4 * l + 1:4 * l + 2], ALU.mult),
                            reads=[rf], writes=[rf])
                        p.op("dve", lambda: E["dve"].tensor_scalar(
                            fa[f][:, :], ob[:, 0:128], fs[f][:, 0:1], None, ALU.mult),
                            reads=[rob, rf], writes=[rf])
                        p.op("dve", lambda: E["dve"].tensor_scalar(
                            fo[f][:, :], ob[:, 256:384], fs[f][:, 2:3], None, ALU.mult),
                            reads=[rob, rf], writes=[rf])
                        p.op("dve", lambda: E["dve"].tensor_tensor(
                            fo[f][:, :], fo[f][:, :], fa[f][:, :], ALU.add),
                            reads=[rf], writes=[rf])
                        p.op("dve", lambda: E["dve"].tensor_tensor(
                            fa[f][:, :], fo[f][:, :], fo[f][:, :], ALU.mult),
                            reads=[rf], writes=[rf])
                        p.op("dve", lambda: E["dve"].tensor_reduce(fs[f][:, 3:4], fa[f][:, :], AX.X, ALU.add),
                             reads=[rf], writes=[rf])
                        for _ in range(9):
                            yield
                        p.op("act", lambda: E["act"].activation(
                            fs[f][:, 4:5], fs[f][:, 3:4], AF.Ln, bias=epsb[:, 0:1], scale=1.0 / 128),
                            reads=[rf], writes=[rf])
                        for _ in range(2):
                            yield
                        p.op("act", lambda: E["act"].activation(
                            fs[f][:, 5:6], fs[f][:, 4:5], AF.Exp, scale=-0.5),
                            reads=[rf], writes=[rf])
                        for _ in range(3):
                            yield
                        p.op("dve", lambda: E["dve"].tensor_scalar(
                            fa[f][:, :], fo[f][:, :], fs[f][:, 5:6], None, ALU.mult),
                            reads=[rf], writes=[rf])
                        p.op("dve", lambda: E["dve"].tensor_tensor(
                            fn_[f][:, :], fa[f][:, :], gsub[:, l * 128:(l + 1) * 128], ALU.mult),
                            reads=[rf], writes=[rf])
                    else:
                        p.op("dve", lambda: E["dve"].reciprocal(fs[f][:, 0:1], ob[:, 128:129]),
                             reads=[rob], writes=[rf])
                        p.op("dve", lambda: E["dve"].tensor_scalar(
                            fn_[f][:, :], ob[:, 0:128], fs[f][:, 0:1], None, ALU.mult),
                            reads=[rob, rf], writes=[rf])
                    for _ in range(4):
                        yield
                    ts_ = nxt("t", 8)
                    p.op("pe", lambda: E["pe"].transpose(TP[:, ts_ * 128:(ts_ + 1) * 128], fn_[f][:, :], identb[:, :]),
                         reads=[rf], writes=[r_T])
                    for _ in range(3):
                        yield
                    p.op("dve", lambda: E["dve"].tensor_tensor(
                        YS[b][:, qs], TP[:, ts_ * 128:(ts_ + 1) * 128], GT[b][:, qs], ALU.mult),
                        reads=[r_T, r_in[b]], writes=[r_ys[b]])

                pending = []

                def advance(flush=False):
                    while True:
                        for g in list(pending):
                            try:
                                next(g)
                            except StopIteration:
                                pending.remove(g)
                        if not flush or not pending:
                            break

                steps = []
                for hi in range(len(heads)):
                    gs = 2 if heads[hi][0] == "d" else 4
                    ngr = (n_groups * 2) // gs
                    for G in range(ngr):
                        lbs = tuple(range(gs * G, gs * G + gs))
                        grp = {}
                        for t in range(lbs[-1] + 1):
                            steps.append(dict(hi=hi, G=G, t=t, lbs=lbs, grp=grp,
                                              first=(G == 0 and t == 0),
                                              last=(G == ngr - 1 and t == lbs[-1])))

                def emit_S(st):
                    hi, t, lbs = st["hi"], st["t"], st["lbs"]
                    kind, h = heads[hi]
                    b = hi % 2
                    act_l = [i for i, lb in enumerate(lbs) if lb >= t]
                    q0 = lbs[act_l[0]] * 128
                    N = len(act_l) * 128
                    ks = slice(t * 128, (t + 1) * 128)
                    isb = nxt("s", NS)
                    sbk, rsb = SB_[isb], r_S[isb]
                    st.update(act_l=act_l, N=N, sbk=sbk, rsb=rsb)
                    if kind == "d":
                        for m in range(2):
                            qsrc = QT[b] if m == 0 else QB[b]
                            p.op("pe", lambda m=m, qsrc=qsrc: E["pe"].matmul(
                                sbk[:, m * 256:m * 256 + N], KT[b][:, ks], qsrc[:, q0:q0 + N], start=True, stop=True),
                                reads=[r_in[b]], writes=[rsb], ms=(m == 1))
                    else:
                        p.op("pe", lambda: E["pe"].matmul(
                            sbk[:, 0:N], KT[b][:, ks], QT[b][:, q0:q0 + N], start=True, stop=False),
                            reads=[r_in[b]], writes=[rsb], ms=False)
                        p.op("pe", lambda: E["pe"].matmul(
                            sbk[:, 0:N], ZK[:, ks], QR[b][:, q0:q0 + N], start=False, stop=True),
                            reads=[r_in[b], r_zk], writes=[rsb])

                def emit_exp(st):
                    hi, t, lbs, grp = st["hi"], st["t"], st["lbs"], st["grp"]
                    kind, h = heads[hi]
                    b = hi % 2
                    nm = 2 if kind == "d" else 1
                    scale = SC_DIFF if kind == "d" else SC_MLA
                    act_l, N, sbk, rsb = st["act_l"], st["N"], st["sbk"], st["rsb"]
                    ip = nxt("p", NPT)
                    pt, rpt = PT[ip], r_pt[ip]
                    if kind == "d":
                        sv = sbk[:, :].rearrange("p (m n) -> p m n", m=2)[:, :, 0:N]
                        ptv = pt[:, :].rearrange("p (m n) -> p m n", m=2)
                        pv_ = ptv[:, :, 0:N]
                    else:
                        sv = sbk[:, 0:N]
                        pv_ = pt[:, 0:N]
                    p.op("act", lambda: E["act"].activation(pv_, sv, AF.Exp, scale=scale),
                         reads=[rsb], writes=[rpt])
                    for i in act_l:
                        lb = lbs[i]
                        co = (i - act_l[0]) * 128
                        if kind == "d" and (t == lb or t == lb - 1):
                            kd = 0 if t == lb else 1
                            p.op("dve", lambda co=co, kd=kd: E["dve"].tensor_tensor(
                                ptv[:, :, co:co + 128], ptv[:, :, co:co + 128],
                                eb[:, 2 * h:2 * h + 2, kd, :], ALU.mult),
                                reads=[rpt], writes=[rpt])
                        elif kind == "m" and t == lb:
                            p.op("pool", lambda co=co: E["pool"].memset(pt[64:128, co:co + 64], 0.0),
                                 reads=[rpt], writes=[rpt])
                    st.update(pt=pt, rpt=rpt, ptv=(ptv if kind == "d" else None))

                def emit_pv(st):
                    hi, t, lbs, grp = st["hi"], st["t"], st["lbs"], st["grp"]
                    kind, h = heads[hi]
                    b = hi % 2
                    nm = 2 if kind == "d" else 1
                    act_l = st["act_l"]
                    pt, rpt, ptv = st["pt"], st["rpt"], st["ptv"]
                    if t == 0:
                        grp["ob"] = []
                        for lb in lbs:
                            io = nxt("o", 4)
                            grp["ob"].append((OB[io], r_O[io]))
                    for i in act_l:
                        lb = lbs[i]
                        co = (i - act_l[0]) * 128
                        ob, rob = grp["ob"][i]
                        for m in range(nm):
                            lhs = ptv[:, m, co:co + 128] if kind == "d" else pt[:, co:co + 128]
                            p.op("pe", lambda lhs=lhs, ob=ob, m=m, lb=lb: E["pe"].matmul(
                                ob[:, m * 256:m * 256 + 129], lhs, VX[b][:, t, :],
                                start=(t == 0 and m == 0), stop=(t == lb), skip_group_check=True),
                                reads=[rpt, r_in[b]], writes=[rob], ms=(m == nm - 1))
                        if t == lb:
                            pending.append(finalize_gen(hi, lb, ob, rob))

                load_head(0)
                for i in range(min(LOOK, len(steps))):
                    emit_S(steps[i])
                def handle_last(st):
                    if st["last"]:
                        advance(flush=True)
                        kind, h = heads[st["hi"]]
                        b = st["hi"] % 2
                        tile_idx = h if kind == "d" else 8 + h
                        p.dma("sp", yT[tile_idx, :, :], YS[b][:, :], reads=[r_ys[b]], writes=[r_scr["yT"]])

                for i, st in enumerate(steps):
                    if i == 0 and len(heads) > 1:
                        load_head(1)
                    if i + LOOK < len(steps):
                        emit_S(steps[i + LOOK])
                    emit_exp(st)
                    advance()
                    if i >= 1:
                        emit_pv(steps[i - 1])
                        handle_last(steps[i - 1])
                        if steps[i - 1]["last"] and st["hi"] + 1 < len(heads):
                            load_head(st["hi"] + 1)
                emit_pv(steps[-1])
                handle_last(steps[-1])
                p.barrier()

        def phase_D(l, xsrc, last):
            with ExitStack() as sd:
                wo_sb = sb("wo_sb", [128, 16, D], BF, sd); r_wo = R()
                ys = [sb("ysD%d" % i, [128, 16, 512], BF, sd) for i in range(2)]
                r_y = [R(), R()]
                xs = [sb("xsD%d" % i, [128, D], F32, sd) for i in range(3)]
                r_x = [R() for _ in range(3)]
                junk = sb("junkD", [128, D], BF, sd); r_junk = R()
                gfin = sb("gfin", [128, D], F32, sd); r_g = R()
                fsd = sb("fsd", [128, 3 * NB], F32, sd); r_fs = [R() for _ in range(NB)]
                bank = [ps("pd%d" % i, [128, 512], F32, sd) for i in range(8)]
                r_bank = [R() for _ in range(8)]
                bi = [0]
                for cg in range(4):
                    p.dma("pool", wo_sb[:, :, cg * 512:(cg + 1) * 512],
                          wo[l, :, cg * 512:(cg + 1) * 512].rearrange("(c p) n -> p c n", p=128), writes=[r_wo])
                if last:
                    p.dma("sp", gfin[:, :], gfin_in[:, :], writes=[r_g])
                for tt in range(NB):
                    n = tt // 4
                    yb = n % 2
                    if tt % 4 == 0:
                        p.dma("sp", ys[yb][:, :, :], yT[:, :, n * 512:(n + 1) * 512].rearrange("c p n -> p c n"),
                              reads=[r_scr["yT"]], writes=[r_y[yb]])
                    xb = tt % 3
                    p.dma("sp", xs[xb][:, :], xsrc[tt * 128:(tt + 1) * 128, :],
                          reads=([r_scr["x1"]] if l > 0 else []), writes=[r_x[xb]])
                    for dg in range(4):
                        i = bi[0]
                        bi[0] = (i + 1) % 8
                        bk, rbk = bank[i], r_bank[i]
                        for c in range(16):
                            p.op("pe", lambda bk=bk, c=c, yb=yb, tt=tt, dg=dg: E["pe"].matmul(
                                bk[:, :], ys[yb][:, c, (tt % 4) * 128:(tt % 4 + 1) * 128],
                                wo_sb[:, c, dg * 512:(dg + 1) * 512], start=(c == 0), stop=(c == 15)),
                                reads=[r_y[yb], r_wo], writes=[rbk], ms=(c == 15))
                        p.op("dve", lambda bk=bk, xb=xb, dg=dg: E["dve"].tensor_tensor(
                            xs[xb][:, dg * 512:(dg + 1) * 512], bk[:, :], xs[xb][:, dg * 512:(dg + 1) * 512], ALU.add),
                            reads=[rbk, r_x[xb]], writes=[r_x[xb]])
                    if not last:
                        p.dma("sp", x1[tt * 128:(tt + 1) * 128, :], xs[xb][:, :], reads=[r_x[xb]], writes=[r_scr["x1"]])
                    else:
                        c0 = 3 * tt
                        p.op("act", lambda xb=xb, c0=c0: E["act"].activation(
                            junk[:, :], xs[xb][:, :], AF.Square, accum_out=fsd[:, c0:c0 + 1]),
                            reads=[r_x[xb]], writes=[r_junk, r_fs[tt]])
                        p.op("act", lambda c0=c0: E["act"].activation(
                            fsd[:, c0 + 1:c0 + 2], fsd[:, c0:c0 + 1], AF.Sqrt, bias=epsb[:, 0:1], scale=1.0 / D),
                            reads=[r_fs[tt]], writes=[r_fs[tt]])
                        p.op("dve", lambda c0=c0: E["dve"].reciprocal(fsd[:, c0 + 2:c0 + 3], fsd[:, c0 + 1:c0 + 2]),
                             reads=[r_fs[tt]], writes=[r_fs[tt]])
                        p.op("act", lambda xb=xb, c0=c0: E["act"].activation(
                            xs[xb][:, :], xs[xb][:, :], AF.Copy, scale=fsd[:, c0 + 2:c0 + 3]),
                            reads=[r_x[xb], r_fs[tt]], writes=[r_x[xb]])
                        p.op("dve", lambda xb=xb: E["dve"].tensor_tensor(
                            xs[xb][:, :], xs[xb][:, :], gfin[:, :], ALU.mult),
                            reads=[r_x[xb], r_g], writes=[r_x[xb]])
                        p.dma("sp", out_d[tt * 128:(tt + 1) * 128, :], xs[xb][:, :], reads=[r_x[xb]], writes=[r_scr["x1"]])
                p.barrier()

        done = False
        for l in range(DEPTH):
            xsrc = x_in if l == 0 else x1
            for half in range(S // HALF):
                if not dbg.get("skipA"):
                    phase_A(l, half, xsrc)
            if stop_after == ("A", l):
                done = True
                break
            phase_C(l)
            if stop_after == ("C", l):
                done = True
                break
            phase_D(l, xsrc, last=(l == DEPTH - 1))
            if stop_after == ("D", l):
                done = True
                break
        p.barrier()
        build_program.ninst = p.ninst
    return nc


def _rel_bucket_np(rel):
    nb = 16
    max_exact = 8
    ret = (rel > 0).astype(np.int32) * nb
    n = np.abs(rel)
    nf = np.maximum(n, 1).astype(np.float32)
    large = max_exact + (np.log(nf / max_exact) / math.log(128 / max_exact) * (nb - max_exact)).astype(np.int32)
    large = np.minimum(large, nb - 1)
    return ret + np.where(n < max_exact, n, large)


def prepare_inputs(x, norm_g, w_in, diff_lambda, diff_subln_g, mla_q_norm_g, mla_w_q_b,
                   mla_kv_norm_g, mla_w_kv_b, w_out, rel_bias, final_norm_g):
    f = np.float32
    x = np.asarray(x, f)
    w_in = np.asarray(w_in, f)
    kr0 = 3840
    swap = np.concatenate([np.arange(32, 64), np.arange(0, 32)])
    colsF = np.concatenate([np.arange(0, 2048), np.arange(3072, 3840), kr0 + np.arange(64), kr0 + swap,
                            np.arange(3904, 5952)])
    assert colsF.size == NFT * 128
    wF = np.ascontiguousarray(w_in[:, :, colsF])
    wT = np.ascontiguousarray(w_in[:, :, 2048:3072])
    wqb = np.asarray(mla_w_q_b, f)
    cq_cols = []
    for h in range(H):
        b0 = h * 192
        cq_cols += [b0 + np.arange(128), b0 + 128 + np.arange(64), b0 + 128 + swap]
    wq = np.ascontiguousarray(wqb[:, :, np.concatenate(cq_cols)])
    wkvb = np.asarray(mla_w_kv_b, f)
    kc = np.concatenate([h * 256 + np.arange(128) for h in range(H)])
    vc = np.concatenate([h * 256 + 128 + np.arange(128) for h in range(H)])
    wkvk = np.ascontiguousarray(wkvb[:, :, kc])
    wkvv = np.ascontiguousarray(wkvb[:, :, vc])
    wo = np.ascontiguousarray(np.asarray(w_out, f))

    def colmajor(v, nchunk):
        v = np.asarray(v, f).reshape(DEPTH, nchunk, 128)
        return np.ascontiguousarray(v.transpose(2, 0, 1).reshape(128, DEPTH * nchunk))

    def bcast(v):
        v = np.asarray(v, f).reshape(1, -1)
        return np.ascontiguousarray(np.broadcast_to(v, (128, v.shape[1])))

    rb = np.asarray(rel_bias, f)
    k = np.arange(128)[:, None]
    q = np.arange(128)[None, :]
    idx = np.stack([_rel_bucket_np(k - q), _rel_bucket_np(k - 128 - q)], 0)
    bt = rb[idx]
    bt = np.ascontiguousarray(bt.transpose(1, 3, 0, 2).reshape(128, 16 * 2 * 128))
    cfar = bcast(rb[15, :])
    pos = np.arange(S, dtype=np.float32)
    inv_freq = (10000.0 ** (-np.arange(0, 64, 2, dtype=np.float32) / 64)).astype(np.float32)
    ang = pos[:, None] * inv_freq[None, :]
    cos, sin = np.cos(ang).astype(f).T, np.sin(ang).astype(f).T
    t1 = np.ascontiguousarray(np.concatenate([cos, cos, -sin, sin], 0))
    identf = np.eye(128, dtype=f)
    identb = np.eye(128, dtype=f).astype(ml_dtypes.bfloat16)
    dfold = (np.arange(128)[:, None] % 64 == np.arange(128)[None, :] % 64).astype(f).astype(ml_dtypes.bfloat16)
    shared = dict(
        wF=wF, wT=wT, wq=wq, wkvk=wkvk, wkvv=wkvv, wo=wo,
        g_in=colmajor(norm_g, 16), gbc=bcast(np.asarray(norm_g, f).reshape(-1)), gq=colmajor(mla_q_norm_g, 4), gkv=colmajor(mla_kv_norm_g, 2),
        gsub=bcast(np.asarray(diff_subln_g, f).reshape(-1)), gfin=bcast(final_norm_g),
        dlam=bcast(np.asarray(diff_lambda, f).reshape(-1)), cfar=cfar, bt=bt, t1=t1,
        identf=identf, identb=identb, dfold=dfold)
    in_maps = []
    for c in range(8):
        m = dict(shared)
        m["x"] = np.ascontiguousarray(x[c % 4])
        in_maps.append(m)
    return in_maps


_NC_CACHE = {}


def kernel(**inputs):
    in_maps = prepare_inputs(**inputs)
    if "nc" not in _NC_CACHE:
        _NC_CACHE["nc"] = build_program()
    res = run_bass_kernel_spmd(_NC_CACHE["nc"], in_maps, core_ids=list(range(8)))
    out = np.stack([np.asarray(res.results[c]["out"], dtype=np.float32) for c in range(4)], 0)
    return out
```
